# Optimizing a Trainium2 kernel written in Bass

```python
import jax, jax.numpy as jnp
from jax import lax
import numpy as np

D_MODEL = 1024
BATCH = 8
SEQ = 4096
DEPTH = 1

GRID_W = 64
MLSTM_HEADS = 4
MLSTM_WIDTH = D_MODEL
MLSTM_HEAD_DIM = MLSTM_WIDTH // MLSTM_HEADS
MLSTM_CHUNK = 64
CONV_WIDTH = 3
NA_HEADS = 8
NA_WIDTH = D_MODEL // 2
NA_HEAD_DIM = NA_WIDTH // NA_HEADS
NA_ROWS = 8
NA_COLS = 16
Q_BLOCK_COLS = 16
K_BLOCK_COLS = 32
N_BRANCHES = 2
EPS = 1e-6
NEG = -1e30
IN_SPLITS = (MLSTM_WIDTH, MLSTM_WIDTH, MLSTM_WIDTH, MLSTM_WIDTH, MLSTM_WIDTH, 4 * MLSTM_HEADS, NA_WIDTH, NA_WIDTH, NA_WIDTH, NA_WIDTH, N_BRANCHES * D_MODEL)
IN_TOTAL = 5 * MLSTM_WIDTH + 4 * MLSTM_HEADS + 4 * NA_WIDTH + N_BRANCHES * D_MODEL

kernel_name = 'hybrid_mlstm_natten_block'


def rms_norm(x, gain):
    xf = x.astype(jnp.float32)
    y = xf * lax.rsqrt(jnp.mean(xf * xf, axis=-1, keepdims=True) + EPS)
    return (y * gain.astype(jnp.float32)).astype(x.dtype)


def centred_dwconv(u, w, b):
    s = u.shape[1]
    pad = CONV_WIDTH // 2
    up = jnp.pad(u, ((0, 0), (pad, pad), (0, 0)))
    return sum(up[:, j:j + s] * w[j] for j in range(CONV_WIDTH)) + b


def mlstm_direction(q, k, v, i_pre, log_f):
    bsz, nh, s, d = q.shape
    nc = s // MLSTM_CHUNK
    L = MLSTM_CHUNK

    def chunks(a):
        return jnp.moveaxis(a.reshape(a.shape[:2] + (nc, L) + a.shape[3:]), 2, 0)

    qc, kc, vc, ic = chunks(q), chunks(k), chunks(v), chunks(i_pre)
    bc = jnp.cumsum(chunks(log_f), axis=-1)
    lower = jnp.tril(jnp.ones((L, L), dtype=bool))

    def step(carry, inp):
        C, n, m = carry
        q_, k_, v_, i_, b_ = inp
        g_ = b_[..., -1]
        log_d = jnp.where(lower, b_[..., :, None] - b_[..., None, :] + i_[..., None, :], NEG)
        log_inter = b_ + m[..., None]
        m_t = jnp.maximum(log_inter, jnp.max(log_d, axis=-1))
        d_mat = jnp.exp(log_d - m_t[..., None])
        w_inter = jnp.exp(log_inter - m_t)
        sc = jnp.einsum('bhtd,bhsd->bhts', q_, k_) * d_mat
        num = jnp.einsum('bhts,bhsd->bhtd', sc, v_) + w_inter[..., None] * jnp.einsum('bhtd,bhde->bhte', q_, C)
        den = jnp.sum(sc, axis=-1) + w_inter * jnp.einsum('bhtd,bhd->bht', q_, n)
        h = num / jnp.maximum(jnp.abs(den), jnp.exp(-m_t))[..., None]
        log_w = g_[..., None] - b_ + i_
        m_new = jnp.maximum(g_ + m, jnp.max(log_w, axis=-1))
        w = jnp.exp(log_w - m_new[..., None])
        decay = jnp.exp(g_ + m - m_new)
        C = decay[..., None, None] * C + jnp.einsum('bhs,bhsd,bhse->bhde', w, k_, v_)
        n = decay[..., None] * n + jnp.einsum('bhs,bhsd->bhd', w, k_)
        return (C, n, m_new), h

    init = (jnp.zeros((bsz, nh, d, d), jnp.float32), jnp.zeros((bsz, nh, d), jnp.float32), jnp.zeros((bsz, nh), jnp.float32))
    _, hs = lax.scan(step, init, (qc, kc, vc, ic, bc))
    return jnp.moveaxis(hs, 0, 2).reshape(bsz, nh, s, d)


def bidirectional_mlstm(q, k, v, gates, b_igate, b_fgate):
    bsz, s, _ = q.shape

    def heads(a):
        return a.astype(jnp.float32).reshape(bsz, s, MLSTM_HEADS, MLSTM_HEAD_DIM).transpose(0, 2, 1, 3)

    qh, kh, vh = heads(q), heads(k) * (MLSTM_HEAD_DIM ** -0.5), heads(v)
    g = gates.astype(jnp.float32).reshape(bsz, s, 2, 2, MLSTM_HEADS)
    i_pre = jnp.transpose(g[:, :, :, 0] + b_igate.astype(jnp.float32), (2, 0, 3, 1))
    log_f = jnp.transpose(jax.nn.log_sigmoid(g[:, :, :, 1] + b_fgate.astype(jnp.float32)), (2, 0, 3, 1))
    h_fwd = mlstm_direction(qh, kh, vh, i_pre[0], log_f[0])
    flip = lambda a: jnp.flip(a, axis=2)
    h_bwd = flip(mlstm_direction(flip(qh), flip(kh), flip(vh), flip(i_pre[1]), flip(log_f[1])))
    h = h_fwd + h_bwd
    return h.transpose(0, 2, 1, 3).reshape(bsz, s, MLSTM_WIDTH)


def neighbourhood_attention(q, k, v, rpb):
    bsz, s, _ = q.shape
    rows = s // GRID_W
    kh = min(NA_ROWS, rows)

    def grid(a):
        return a.astype(jnp.float32).reshape(bsz, rows, GRID_W, NA_HEADS, NA_HEAD_DIM).transpose(0, 3, 1, 2, 4)

    qg, kg, vg = grid(q), grid(k), grid(v)
    n_cb = GRID_W // Q_BLOCK_COLS
    q_cols = np.arange(GRID_W).reshape(n_cb, Q_BLOCK_COLS)
    win_start = np.clip(q_cols - NA_COLS // 2, 0, GRID_W - NA_COLS)
    kb_start = np.clip(np.arange(n_cb) * Q_BLOCK_COLS - NA_COLS // 2, 0, GRID_W - K_BLOCK_COLS)
    key_cols = kb_start[:, None] + np.arange(K_BLOCK_COLS)[None, :]
    kc3 = key_cols[:, None, :]
    col_mask = jnp.asarray((kc3 >= win_start[:, :, None]) & (kc3 < win_start[:, :, None] + NA_COLS))
    dc_idx = np.clip(kc3 - q_cols[:, :, None] + NA_COLS - 1, 0, 2 * NA_COLS - 2)
    rpb_col = rpb.astype(jnp.float32)[:, :, dc_idx]
    scale = NA_HEAD_DIM ** -0.5

    def row_fn(r):
        rs = jnp.clip(r - kh // 2, 0, rows - kh)
        k_rows = lax.dynamic_slice_in_dim(kg, rs, kh, axis=2)
        v_rows = lax.dynamic_slice_in_dim(vg, rs, kh, axis=2)
        k_blk = k_rows[:, :, :, key_cols, :]
        v_blk = v_rows[:, :, :, key_cols, :].transpose(0, 1, 3, 2, 4, 5).reshape(bsz, NA_HEADS, n_cb, kh * K_BLOCK_COLS, NA_HEAD_DIM)
        q_blk = lax.dynamic_index_in_dim(qg, r, axis=2, keepdims=False).reshape(bsz, NA_HEADS, n_cb, Q_BLOCK_COLS, NA_HEAD_DIM)
        sc = jnp.einsum('bhcid,bhacjd->bhciaj', q_blk, k_blk) * scale
        dr_idx = rs + jnp.arange(kh) - r + NA_ROWS - 1
        bias = rpb_col[:, dr_idx].transpose(0, 2, 3, 1, 4)
        sc = jnp.where(col_mask[:, :, None, :], sc + bias[None], NEG)
        p = jax.nn.softmax(sc.reshape(bsz, NA_HEADS, n_cb, Q_BLOCK_COLS, kh * K_BLOCK_COLS), axis=-1)
        o = jnp.einsum('bhcin,bhcnd->bhcid', p, v_blk)
        return o.reshape(bsz, NA_HEADS, GRID_W, NA_HEAD_DIM)

    out = lax.map(row_fn, jnp.arange(rows))
    return out.transpose(1, 0, 3, 2, 4).reshape(bsz, s, NA_WIDTH)


def hybrid_mixer(h, w_in, conv_w, conv_b, b_igate, b_fgate, mlstm_norm_gain, rpb, w_proj_a, w_proj_b, b_merge):
    bsz, s, _ = h.shape
    proj = h @ w_in
    offsets = [int(o) for o in np.cumsum(IN_SPLITS)[:-1]]
    q_a, k_a, v_a, o_a, z_a, gates, q_b, k_b, v_b, z_b, g_merge = jnp.split(proj, offsets, axis=-1)
    qk = jax.nn.silu(centred_dwconv(jnp.concatenate([q_a, k_a], axis=-1), conv_w, conv_b))
    q_a, k_a = jnp.split(qk, 2, axis=-1)
    h_a = jax.nn.sigmoid(o_a.astype(jnp.float32)) * bidirectional_mlstm(q_a, k_a, v_a, gates, b_igate, b_fgate)
    h_a = rms_norm(h_a.reshape(bsz, s, MLSTM_HEADS, MLSTM_HEAD_DIM), mlstm_norm_gain.reshape(MLSTM_HEADS, MLSTM_HEAD_DIM)).reshape(bsz, s, MLSTM_WIDTH)
    y_a = h_a.astype(h.dtype) * jax.nn.silu(z_a)
    y_b = neighbourhood_attention(q_b, k_b, v_b, rpb).astype(h.dtype) * jax.nn.silu(z_b)
    gm = jax.nn.sigmoid(g_merge.reshape(bsz, s, N_BRANCHES, D_MODEL) + b_merge)
    return gm[:, :, 0] * (y_a @ w_proj_a) + gm[:, :, 1] * (y_b @ w_proj_b)


def setup_inputs(seed: int = 0) -> dict:
    key = jax.random.key(seed)
    ks = jax.random.split(key, 18)

    def nrm(k, shape, scale):
        return jax.random.normal(k, shape, jnp.float32) * scale

    return {
        'x': nrm(ks[0], (BATCH, SEQ, D_MODEL), 1.0),
        'c': nrm(ks[1], (BATCH, D_MODEL), 1.0),
        'w_ada': nrm(ks[2], (DEPTH, D_MODEL, 3 * D_MODEL), 0.5 * D_MODEL ** -0.5),
        'b_ada': nrm(ks[3], (DEPTH, 3 * D_MODEL), 0.01),
        'norm_gain': 1.0 + nrm(ks[4], (DEPTH, D_MODEL), 0.02),
        'w_in': nrm(ks[5], (DEPTH, D_MODEL, IN_TOTAL), D_MODEL ** -0.5),
        'conv_w': nrm(ks[6], (DEPTH, CONV_WIDTH, 2 * MLSTM_WIDTH), CONV_WIDTH ** -0.5),
        'conv_b': nrm(ks[7], (DEPTH, 2 * MLSTM_WIDTH), 0.01),
        'b_igate': nrm(ks[8], (DEPTH, 2, MLSTM_HEADS), 0.1),
        'b_fgate': jax.random.uniform(ks[9], (DEPTH, 2, MLSTM_HEADS), jnp.float32, 3.0, 6.0),
        'mlstm_norm_gain': 1.0 + nrm(ks[10], (DEPTH, MLSTM_WIDTH), 0.02),
        'rpb': nrm(ks[11], (DEPTH, NA_HEADS, 2 * NA_ROWS - 1, 2 * NA_COLS - 1), 0.1),
        'w_proj_a': nrm(ks[12], (DEPTH, MLSTM_WIDTH, D_MODEL), MLSTM_WIDTH ** -0.5),
        'w_proj_b': nrm(ks[13], (DEPTH, NA_WIDTH, D_MODEL), NA_WIDTH ** -0.5),
        'b_merge': nrm(ks[14], (DEPTH, N_BRANCHES, D_MODEL), 0.1),
        'w_out': nrm(ks[15], (DEPTH, D_MODEL, D_MODEL), D_MODEL ** -0.5),
        'final_gain': 1.0 + nrm(ks[16], (D_MODEL,), 0.02),
    }


def reference(x, c, w_ada, b_ada, norm_gain, w_in, conv_w, conv_b, b_igate, b_fgate, mlstm_norm_gain, rpb, w_proj_a, w_proj_b, b_merge, w_out, final_gain):
    cond = jax.nn.silu(c)
    for layer in range(DEPTH):
        mod = cond @ w_ada[layer] + b_ada[layer]
        shift, scale, gate = jnp.split(mod, 3, axis=-1)
        h = rms_norm(x, norm_gain[layer]) * (1.0 + scale[:, None, :]) + shift[:, None, :]
        merged = hybrid_mixer(h, w_in[layer], conv_w[layer], conv_b[layer], b_igate[layer], b_fgate[layer], mlstm_norm_gain[layer], rpb[layer], w_proj_a[layer], w_proj_b[layer], b_merge[layer])
        x = x + gate[:, None, :] * (merged @ w_out[layer])
    return rms_norm(x, final_gain)
```

```python
import numpy as np
import concourse.bass as bass
import concourse.mybir as mybir
from concourse.bass_utils import run_bass_kernel_spmd

F32 = mybir.dt.float32
BF16 = mybir.dt.bfloat16
ALU = mybir.AluOpType
AF = mybir.ActivationFunctionType

S_ = 4096
D = 1024
NT = 32
NBLK = 8
H = 4
DH = 256
NH = 8
NEG = -30000.0
EPS = 1e-6
ENG_NAMES = ("pe", "act", "dve", "pool", "sp")


class Sched:
    def __init__(self, nc):
        self.nc = nc
        self.ops = {e: [] for e in ENG_NAMES}
        self.count = {}
        self.last_writer = {}
        self.readers = {}
        self.seen = {e: {} for e in ENG_NAMES}
        self.sem_names = ["pe", "act", "dve", "pool"]
        self.is_dma = set()
        self.n_instr = 0

    def _deps(self, reads, writes):
        deps = set()
        for k in reads:
            w = self.last_writer.get(k)
            if w is not None:
                deps.add(w)
        for k in writes:
            w = self.last_writer.get(k)
            if w is not None:
                deps.add(w)
            deps.update(self.readers.get(k, ()))
        return deps

    def _record(self, me, reads, writes):
        for k in reads:
            self.readers.setdefault(k, []).append(me)
        for k in writes:
            self.last_writer[k] = me
            self.readers[k] = []

    def _waits(self, eng, deps):
        need = {}
        for (s, i) in deps:
            if s in self.is_dma:
                i = self.count[s] - 1
            if need.get(s, -1) < i:
                need[s] = i
        waits = []
        for s, i in need.items():
            if self.seen[eng].get(s, -1) >= i:
                continue
            self.seen[eng][s] = i
            waits.append((s, i + 1))
        return waits

    def op(self, eng, specs, reads=(), writes=()):
        if isinstance(specs, tuple):
            specs = [specs]
        waits = self._waits(eng, self._deps(reads, writes))
        idx = self.count.get(eng, 0)
        self.count[eng] = idx + 1
        self.ops[eng].append((specs, waits, (eng, 1)))
        self._record((eng, idx), reads, writes)
        self.n_instr += len(specs)

    def dma(self, queue, stream, out, in_, reads=(), writes=()):
        if stream not in self.is_dma:
            self.is_dma.add(stream)
            self.sem_names.append(stream)
            self.count[stream] = 0
        waits = self._waits(queue, self._deps(reads, writes))
        idx = self.count[stream]
        self.count[stream] = idx + 1
        self.ops[queue].append(([("dma_start", dict(out=out, in_=in_))], waits, (stream, 16)))
        self._record((stream, idx), reads, writes)
        self.n_instr += 1

    def barrier(self):
        allw = [(s, c) for s, c in self.count.items() if c > 0]
        for e in ENG_NAMES:
            waits = []
            for s, c in allw:
                if self.seen[e].get(s, -1) >= c - 1:
                    continue
                self.seen[e][s] = c - 1
                waits.append((s, c))
            if waits:
                self.ops[e].append((None, waits, None))
        self.last_writer = {}
        self.readers = {}

    def emit(self):
        import contextlib
        nc = self.nc
        self.barrier()
        with contextlib.ExitStack() as st:
            sems = {s: st.enter_context(nc.semaphore("s_" + s)) for s in self.sem_names}
            block = st.enter_context(nc.Block())

            def run(engname):
                def body(eng):
                    for specs, waits, inc in self.ops[engname]:
                        for (s, v) in waits:
                            eng.wait_ge(sems[s], v * (16 if s in self.is_dma else 1))
                        if specs is None:
                            continue
                        ins = None
                        for (m, kw) in specs:
                            ins = getattr(eng, m)(**kw)
                        ins.then_inc(sems[inc[0]], inc[1])
                return body

            block.tensor(run("pe"))
            block.scalar(run("act"))
            block.vector(run("dve"))
            block.gpsimd(run("pool"))
            block.sync(run("sp"))


class Rot:
    def __init__(self, name, tiles):
        self.name, self.tiles, self.i = name, tiles, 0

    def next(self):
        j = self.i % len(self.tiles)
        self.i += 1
        return self.tiles[j], (self.name, j)


def MM(out, lhsT, rhs, start=True, stop=True):
    return ("matmul", dict(out=out, lhsT=lhsT, rhs=rhs, start=start, stop=stop))


def TR(out, in_, identity):
    return ("transpose", dict(out=out, in_=in_, identity=identity))


def ACT(out, in_, func, **kw):
    return ("activation", dict(out=out, in_=in_, func=func, **kw))


def TT(out, in0, in1, op):
    return ("tensor_tensor", dict(out=out, in0=in0, in1=in1, op=op))


def TS(out, in0, scalar1, scalar2, op0, op1=None):
    d = dict(out=out, in0=in0, scalar1=scalar1, scalar2=scalar2, op0=op0)
    if op1 is not None:
        d["op1"] = op1
    return ("tensor_scalar", d)


def STT(out, in0, scalar, in1, op0, op1):
    return ("scalar_tensor_tensor", dict(out=out, in0=in0, scalar=scalar, in1=in1, op0=op0, op1=op1))


def CP(out, in_):
    return ("tensor_copy", dict(out=out, in_=in_))


def MS(ap, v):
    return ("memset", dict(ap=ap, constant=v))


def SCAN(out, data0, data1, initial, op0, op1):
    return ("tensor_tensor_scan", dict(out=out, data0=data0, data1=data1, initial=initial, op0=op0, op1=op1))


def build_program(stop_after=None, debug=False):
    nc = bass.Bass("TRN2", target_bir_lowering=False)
    dbg_kind = "ExternalOutput" if debug else "Internal"

    def DIN(name, shape, dt=F32):
        return nc.dram_tensor(name, list(shape), dt, kind="ExternalInput").ap()

    def DSC(name, shape, dt=BF16):
        return nc.dram_tensor(name, list(shape), dt, kind=dbg_kind).ap()

    x_d = DIN("x", [S_, D])
    c_d = DIN("c_l", [128, 8])
    wada_d = DIN("wada", [128, 8, 3072])
    rows_d = DIN("rows", [1, 5120])
    wgate_d = DIN("wgate", [128, 8, 72])
    win_d = DIN("win", [18, 128, 8, 512])
    cols_d = DIN("cols", [128, 88])
    gb_d = DIN("gb", [36, 2])
    bt2_d = DIN("bt2", [128, 4, 14, 2, 64])
    wpa_d = DIN("wpa", [128, 8, 1024])
    wpb_d = DIN("wpb", [128, 4, 1024])
    wout_d = DIN("wout", [128, 8, 1024])
    ident_d = DIN("ident", [128, 128])
    masks_d = DIN("masks", [128, 2, 128])
    sel_d = DIN("sel", [36, 8, 128])
    y_d = nc.dram_tensor("y", [S_, D], F32, kind="ExternalOutput").ap()

    QT = DSC("QT", [4, 128, 2, S_])
    KT = DSC("KT", [4, 128, 2, S_])
    VA = DSC("VA", [NT, 128, 4 * 258])
    OG = DSC("OG", [S_, D])
    ZAT = DSC("ZAT", [8, 128, S_])
    QBT = DSC("QBT", [4, 128, S_])
    KBT = DSC("KBT", [4, 128, S_])
    VBA = DSC("VBA", [NT, 128, 8 * 66])
    ZBT = DSC("ZBT", [4, 128, S_])
    GMT = DSC("GMT", [16, 128, S_])
    YAT = DSC("YAT", [8, 128, S_])
    YBT = DSC("YBT", [4, 128, S_])
    dbg = {}
    if debug:
        dbg["hT"] = nc.dram_tensor("dbg_hT", [128, 8, S_], BF16, kind="ExternalOutput").ap()
        dbg["TOKU"] = nc.dram_tensor("dbg_TOKU", [128, NT, 36], F32, kind="ExternalOutput").ap()
        dbg["TOKC"] = nc.dram_tensor("dbg_TOKC", [128, NT, 36], F32, kind="ExternalOutput").ap()
        dbg["DECB"] = nc.dram_tensor("dbg_DECB", [128, 8, NT], F32, kind="ExternalOutput").ap()
        dbg["Hacc"] = nc.dram_tensor("dbg_Hacc", [4, 128, NT, 256], F32, kind="ExternalOutput").ap()
        for nm in ("T1", "T2", "T3"):
            dbg[nm] = nc.dram_tensor("dbg_" + nm, [36, S_], F32, kind="ExternalOutput").ap()

    def dstop(tag):
        if stop_after != tag:
            return False
        S.dma("sp", "S_dbg", dbg["T1"][:, :], T1[0:36, :], reads=["T1g"])
        S.dma("sp", "S_dbg", dbg["T2"][:, :], T2[0:36, :], reads=["T2g"])
        S.dma("sp", "S_dbg", dbg["T3"][:, :], T3[0:36, :], reads=["T3g"])
        S.emit()
        return True

    SB_LO = 16512
    SB_HI = 229344
    cur = [SB_LO]

    def T(name, shape, dt):
        n = int(np.prod(shape[1:])) * (4 if dt == F32 else 2)
        n = (n + 31) // 32 * 32
        assert cur[0] + n <= SB_HI, (name, cur[0], n)
        t = nc.alloc_sbuf_tensor_at(name, list(shape), dt, offset=cur[0])
        cur[0] += n
        return t

    ps = [nc.alloc_psum_tensor("ps%d" % i, [128, 512], F32) for i in range(8)]

    def psb(i):
        return ps[i][:].bitcast(BF16)

    S = Sched(nc)

    identb = T("identb", [128, 128], BF16)
    identf = T("identf", [128, 128], F32)
    maskT = T("maskT", [128, 2, 128], F32)
    cols = T("cols", [128, 88], F32)
    TOKU = T("TOKU", [128, NT, 36], F32)
    TOKC = T("TOKC", [128, NT, 36], F32)
    DECB = T("DECB", [128, 8, NT], F32)
    gate_bc = T("gate_bc", [128, 1024], F32)
    fg_bc = T("fg_bc", [128, 1024], F32)
    ones_c = T("ones_c", [128, 128], F32)
    smallc = T("smallc", [128, 64], F32)
    REGION0 = cur[0]

    S.dma("sp", "L_const", identf[:], ident_d[:, :], writes=["identf"])
    S.dma("pool", "L_constb", identb[:], ident_d[:, :], writes=["identb"])
    S.dma("sp", "L_const", maskT[:], masks_d[:, :, :], writes=["maskT"])
    S.dma("sp", "L_const", cols[:], cols_d[:, :], writes=["cols"])
    S.op("pool", MS(ones_c[:], 1.0), writes=["ones_c"])

    cur[0] = REGION0
    modrow = T("modrow", [1, 3072], F32)
    rows = T("rows", [1, 5120], F32)
    wst = [T("wada_st%d" % i, [128, 8, 512], F32) for i in range(2)]
    A_END = cur[0]
    cur[0] = REGION0 + 65536
    G1_bc = T("G1_bc", [128, 1024], F32)
    sh_bc = T("sh_bc", [128, 1024], F32)
    c_sb = T("c_sb", [128, 8], F32)
    cond = T("cond", [128, 8], F32)
    G1row = T("G1row", [1, 1024], F32)
    assert A_END <= REGION0 + 65536

    S.dma("sp", "L_const", c_sb[:], c_d[:, :], writes=["c_sb"])
    S.dma("sp", "L_const", rows[:], rows_d[:, :], writes=["rows"])
    S.op("act", ACT(cond[:], c_sb[:], AF.Silu), reads=["c_sb"], writes=["cond"])
    wrot = Rot("wada_st", wst)
    for g in range(6):
        wt, wk = wrot.next()
        S.dma("sp", "L_wada%d" % wk[1], wt[:], wada_d[:, :, g * 512:(g + 1) * 512], writes=[wk])
        S.op("pe", [MM(ps[g % 2][0:1, :], cond[:, kc:kc + 1], wt[:, kc, :], start=(kc == 0), stop=(kc == 7)) for kc in range(8)],
             reads=["cond", wk], writes=[("ps", g % 2)])
        S.op("dve", TT(modrow[0:1, g * 512:(g + 1) * 512], ps[g % 2][0:1, :], rows[0:1, g * 512:(g + 1) * 512], ALU.add),
             reads=[("ps", g % 2), "rows"], writes=["modrow"])
    S.op("dve", STT(G1row[0:1, :], modrow[0:1, 1024:2048], 1.0, rows[0:1, 3072:4096], ALU.add, ALU.mult),
         reads=["modrow", "rows"], writes=["G1row"])
    bc_jobs = [(G1_bc, G1row[0:1, :], "G1row", "G1_bc"), (sh_bc, modrow[0:1, 0:1024], "modrow", "sh_bc"),
               (gate_bc, modrow[0:1, 2048:3072], "modrow", "gate_bc"), (fg_bc, rows[0:1, 4096:5120], "rows", "fg_bc")]
    bi = 0
    for (dst, src, skey, dkey) in bc_jobs:
        for hf in range(2):
            b = 2 + (bi % 2)
            bi += 1
            S.op("pe", MM(ps[b][:, :], ones_c[0:1, 0:128], src[:, hf * 512:(hf + 1) * 512]), reads=[skey, "ones_c"], writes=[("ps", b)])
            S.op("act", ACT(dst[:, hf * 512:(hf + 1) * 512], ps[b][:, :], AF.Copy), reads=[("ps", b)], writes=[dkey])
    S.barrier()

    cur[0] = REGION0
    hT = T("hT", [128, 8, S_], BF16)
    C_START = cur[0]
    assert C_START == REGION0 + 65536
    cur[0] = C_START + 8192 + 2 * 32
    cur[0] = (cur[0] + 4096 + 31) // 32 * 32
    xts = [T("xt%d" % i, [128, 1024], F32) for i in range(4)]
    xns = [T("xn%d" % i, [128, 1024], BF16) for i in range(3)]
    junkb = T("junkb", [128, 1024], BF16)
    xrot = Rot("xt", xts)
    xnrot = Rot("xn", xns)
    def b_stage1(tt):
        xt, xk = xrot.next()
        xn, xnk = xnrot.next()
        sc = smallc[:, (tt % 4) * 4:(tt % 4) * 4 + 4]
        sck = ("smallc", tt % 4)
        S.dma("sp", "L_xt%d" % xk[1], xt[:], x_d[tt * 128:(tt + 1) * 128, :], writes=[xk])
        S.op("act", ACT(junkb[:], xt[:], AF.Square, accum_out=sc[:, 0:1]), reads=[xk], writes=["junkb", sck])
        S.op("act", ACT(sc[:, 1:2], sc[:, 0:1], AF.Sqrt, scale=1.0 / D, bias=EPS), reads=[sck], writes=[sck])
        S.op("dve", ("reciprocal", dict(out=sc[:, 2:3], in_=sc[:, 1:2])), reads=[sck], writes=[sck])
        S.op("dve", STT(xt[:], xt[:], sc[:, 2:3], G1_bc[:], ALU.mult, ALU.mult), reads=[xk, sck, "G1_bc"], writes=[xk])
        S.op("dve" if tt % 3 != 2 else "pool", TT(xn[:], xt[:], sh_bc[:], ALU.add), reads=[xk, "sh_bc"], writes=[xnk])
        return xn, xnk

    def b_stage2(tt, xn, xnk):
        b = 6 + (tt % 2)
        S.op("pe", [TR(psb(b)[:, kc * 128:(kc + 1) * 128], xn[:, kc * 128:(kc + 1) * 128], identb[:]) for kc in range(8)],
             reads=[xnk, "identb"], writes=[("ps", b)])
        S.op("act", ACT(hT[:, :, tt * 128:(tt + 1) * 128], psb(b).rearrange("p (a b) -> p a b", a=8), AF.Copy),
             reads=[("ps", b)], writes=[("hT", tt)])

    bctx = {}
    for tt in range(NT + 1):
        if tt < NT:
            bctx[tt] = b_stage1(tt)
        if tt >= 1:
            b_stage2(tt - 1, *bctx.pop(tt - 1))
    if debug:
        S.dma("sp", "S_dbg", dbg["hT"][:, :, :], hT[:], reads=[("hT", tt) for tt in range(NT)])
    S.barrier()
    if stop_after == "B":
        S.emit()
        return nc

    cur[0] = C_START
    Wb = [T("Wb%d" % i, [128, 8, 512], BF16) for i in range(3)]
    wgb = T("wgb", [128, 8, 72], BF16)
    UA = T("UA", [128, S_ + 2], F32)
    UB = T("UB", [128, S_ + 2], F32)
    ACC = T("ACC", [128, S_], F32)
    obufs = [T("obuf%d" % i, [128, S_], BF16) for i in range(2)]
    tms = [T("tmst%d" % i, [128, 2112], BF16) for i in range(2)]
    gbc = T("gbc", [36, 2], F32)
    MPt = T("MPt", [36, NT], F32)
    MOt = T("MOt", [36, NT], F32)
    DECt = T("DECt", [36, NT], F32)
    selt = T("selt", [36, 8, 128], F32)
    C_END = cur[0]
    T1 = nc.alloc_sbuf_tensor_at("T1g", [128, S_], F32, offset=C_START + 3 * 8192 + 1152)
    T2 = nc.alloc_sbuf_tensor_at("T2g", [128, S_], F32, offset=C_START + 3 * 8192 + 1152 + 16416)
    T3 = ACC
    ONESF = nc.alloc_sbuf_tensor_at("ONESF", [128, S_], F32, offset=C_START + 3 * 8192 + 1152 + 2 * 16416 + 16384)

    wrot = Rot("Wb", Wb)
    orot = Rot("obuf", obufs)
    tmrot = Rot("tmst", tms)
    urot = Rot("U", [UA, UB])
    psrot = Rot("ps", ps[0:6])
    evtog = [0]

    S.dma("sp", "L_const", gbc[:], gb_d[:, :], writes=["gbc"])
    S.dma("sp", "L_const", selt[:], sel_d[:, :, :], writes=["selt"])
    S.dma("pool", "L_wgb", wgb[:], wgate_d[:, :, :], writes=["wgb"])
    allhT = [("hT", tt) for tt in range(NT)]

    def hkeys(tb):
        return [("hT", tb * 4 + j) for j in range(4)]

    for tb in range(NBLK):
        for gi, (Tt, col, tkey) in enumerate(((T1, 0, "T1g"), (T2, 1, "T2g"))):
            pt, pk = psrot.next()
            S.op("pe", [MM(pt[0:36, :], wgb[:, kc, gi * 36:(gi + 1) * 36], hT[:, kc, tb * 512:(tb + 1) * 512],
                           start=(kc == 0), stop=(kc == 7)) for kc in range(8)],
                 reads=["wgb"] + hkeys(tb), writes=[pk])
            S.op("act", ACT(Tt[0:36, tb * 512:(tb + 1) * 512], pt[0:36, :], AF.Identity, bias=gbc[0:36, col:col + 1]),
                 reads=[pk, "gbc"], writes=[tkey])

    if dstop("D0"):
        return nc
    r36 = slice(0, 36)
    fw = slice(0, 4)
    bw = slice(32, 36)
    S.op("act", ACT(T2[r36, :], T2[r36, :], AF.Exp, scale=-1.0), reads=["T2g"], writes=["T2g"])
    S.op("act", ACT(T2[r36, :], T2[r36, :], AF.Ln, bias=1.0), reads=["T2g"], writes=["T2g"])
    S.op("pool", MS(T3[r36, :], 0.0), writes=["T3g"])
    S.op("pool", MS(ONESF[r36, :], 1.0), writes=["ONESF"])
    S.op("pool", [MS(MPt[:], 0.0), MS(MOt[:], 0.0)], writes=["MPt", "MOt"])
    S.op("dve", SCAN(T3[fw, :], ONESF[fw, :], T2[fw, :], 0.0, ALU.mult, ALU.add),
         reads=["T2g", "ONESF"], writes=["T3g"])
    S.op("dve", SCAN(T3[bw, ::-1], ONESF[bw, :], T2[bw, ::-1], 0.0, ALU.mult, ALU.add),
         reads=["T2g", "ONESF"], writes=["T3g"])
    if dstop("D1"):
        return nc
    S.op("dve", TT(T1[r36, :], T1[r36, :], T3[r36, :], ALU.add), reads=["T1g", "T3g"], writes=["T1g"])
    S.op("dve", SCAN(T2[fw, :], ONESF[fw, :], T1[fw, :], 0.0, ALU.mult, ALU.max),
         reads=["T1g", "ONESF"], writes=["T2g"])
    S.op("dve", SCAN(T2[bw, ::-1], ONESF[bw, :], T1[bw, ::-1], 0.0, ALU.mult, ALU.max),
         reads=["T1g", "ONESF"], writes=["T2g"])
    if dstop("D2"):
        return nc
    M3 = T2[:].rearrange("p (k t) -> p k t", t=128)
    S.op("dve", [CP(MPt[fw, 1:NT], M3[fw, 0:NT - 1, 127]), CP(MOt[fw, :], M3[fw, :, 127])],
         reads=["T2g", "MPt", "MOt"], writes=["MPt", "MOt"])
    S.op("dve", [CP(MPt[bw, 0:NT - 1], M3[bw, 1:NT, 0]), CP(MOt[bw, :], M3[bw, :, 0])],
         reads=["T2g", "MPt", "MOt"], writes=["MPt", "MOt"])
    S.op("dve", TT(DECt[:], MPt[:], MOt[:], ALU.subtract), reads=["MPt", "MOt"], writes=["DECt"])
    S.op("act", ACT(DECt[:], DECt[:], AF.Exp), reads=["DECt"], writes=["DECt"])
    if dstop("D3"):
        return nc
    MPb = MPt[:].rearrange("p (k o) -> p k o", o=1).to_broadcast([36, NT, 128])
    T1v = T1[r36, :].rearrange("p (k t) -> p k t", t=128)
    T3v = T3[r36, :].rearrange("p (k t) -> p k t", t=128)
    S.op("dve", TT(T3v, T3v, MPb, ALU.subtract), reads=["T3g", "MPt"], writes=["T3g"])
    S.op("act", ACT(T3[r36, :], T3[r36, :], AF.Exp), reads=["T3g"], writes=["T3g"])
    S.op("dve", TT(T1v, T1v, MPb, ALU.subtract), reads=["T1g", "MPt"], writes=["T1g"])
    S.op("act", ACT(T1[r36, :], T1[r36, :], AF.Exp), reads=["T1g"], writes=["T1g"])
    if dstop("D4"):
        return nc
    for (src, skey, dst, dkey) in ((T1, "T1g", TOKU, "TOKU"), (T3, "T3g", TOKC, "TOKC")):
        for k0 in range(0, NT, 14):
            n = min(14, NT - k0)
            pt, pk = psrot.next()
            S.op("pe", [TR(pt[:, j * 36:(j + 1) * 36], src[0:36, (k0 + j) * 128:(k0 + j + 1) * 128], identf[0:36, 0:36]) for j in range(n)],
                 reads=[skey, "identf"], writes=[pk])
            S.op("dve", CP(dst[:, k0:k0 + n, :], pt[:, 0:n * 36].rearrange("p (a b) -> p a b", b=36)), reads=[pk], writes=[dkey])
    pt, pk = psrot.next()
    S.op("pe", [MM(pt[:, j * NT:(j + 1) * NT], selt[0:36, j, :], DECt[0:36, :]) for j in range(8)],
         reads=["selt", "DECt"], writes=[pk])
    S.op("dve", CP(DECB[:], pt[:, 0:8 * NT].rearrange("p (a b) -> p a b", b=NT)), reads=[pk], writes=["DECB"])
    if debug:
        S.dma("sp", "S_dbg", dbg["TOKU"][:, :, :], TOKU[:], reads=["TOKU"])
        S.dma("sp", "S_dbg", dbg["TOKC"][:, :, :], TOKC[:], reads=["TOKC"])
        S.dma("sp", "S_dbg", dbg["DECB"][:, :, :], DECB[:], reads=["DECB"])
    S.barrier()
    if stop_after == "D":
        S.emit()
        return nc

    S.op("pool", [MS(UA[:, 0:1], 0.0), MS(UA[:, S_ + 1:S_ + 2], 0.0), MS(UB[:, 0:1], 0.0), MS(UB[:, S_ + 1:S_ + 2], 0.0)],
         writes=[("U", 0), ("U", 1)])

    def load_w(g):
        wt, wk = wrot.next()
        S.dma("pool", "L_Wb%d" % wk[1], wt[:], win_d[g], writes=[wk])
        return wt, wk

    pend_tail = []

    def flush_tail():
        while pend_tail:
            inf = pend_tail.pop(0)
            ob, ok = orot.next()
            S.op("act", ACT(ob[:], ACC[:], AF.Silu), reads=["ACC"], writes=[ok])
            S.dma("sp", "S_obuf%d" % ok[1], inf["dst"], ob[:], reads=[ok], writes=[inf["dkey"]])

    def fm_group(g, kind, sub_info):
        wt, wk = load_w(g)
        for sub in range(4):
            info = sub_info(sub)
            if kind == "conv":
                U, uk = urot.next()
            else:
                ob, ok = orot.next()
            for tb in range(NBLK):
                pt, pk = psrot.next()
                S.op("pe", [MM(pt[:, :], wt[:, kc, sub * 128:(sub + 1) * 128], hT[:, kc, tb * 512:(tb + 1) * 512],
                               start=(kc == 0), stop=(kc == 7)) for kc in range(8)],
                     reads=[wk] + hkeys(tb), writes=[pk])
                if kind == "conv":
                    S.op("act", ACT(U[:, 1 + tb * 512:1 + (tb + 1) * 512], pt[:, :], AF.Copy), reads=[pk], writes=[uk])
                elif kind == "silu":
                    S.op("act", ACT(ob[:, tb * 512:(tb + 1) * 512], pt[:, :], AF.Silu), reads=[pk], writes=[ok])
                elif kind == "sigb":
                    S.op("act", ACT(ob[:, tb * 512:(tb + 1) * 512], pt[:, :], AF.Sigmoid, bias=info["bias"]), reads=[pk, "cols"], writes=[ok])
                elif kind == "copy":
                    evtog[0] ^= 1
                    if evtog[0]:
                        S.op("dve", TS(ob[:, tb * 512:(tb + 1) * 512], pt[:, :], info["scale"], None, ALU.mult), reads=[pk], writes=[ok])
                    else:
                        S.op("act", ACT(ob[:, tb * 512:(tb + 1) * 512], pt[:, :], AF.Copy, scale=info["scale"]), reads=[pk], writes=[ok])
            if kind == "conv":
                flush_tail()
                cg = info["cg"]
                w0 = cols[:, cg * 3 + 0:cg * 3 + 1]
                w1 = cols[:, cg * 3 + 1:cg * 3 + 2]
                w2 = cols[:, cg * 3 + 2:cg * 3 + 3]
                cb = cols[:, 48 + cg:49 + cg]
                S.op("dve", TS(ACC[:], U[:, 1:S_ + 1], w1, cb, ALU.mult, ALU.add), reads=[uk, "cols"], writes=["ACC"])
                S.op("dve", STT(ACC[:], U[:, 0:S_], w0, ACC[:], ALU.mult, ALU.add), reads=[uk, "cols", "ACC"], writes=["ACC"])
                S.op("dve", STT(ACC[:], U[:, 2:S_ + 2], w2, ACC[:], ALU.mult, ALU.add), reads=[uk, "cols", "ACC"], writes=["ACC"])
                pend_tail.append(info)
            else:
                S.dma("sp", "S_obuf%d" % ok[1], info["dst"], ob[:], reads=[ok], writes=[info["dkey"]])

    def tm_group(g, kind, col0):
        wt, wk = load_w(g)
        for t4 in range(NT // 4):
            st, sk = tmrot.next()
            if kind == "va":
                sv = st[:, 0:4 * 2 * 258].rearrange("p (t h c) -> p t h c", t=4, h=2)
                S.op("pool", [MS(sv[:, :, :, 256:257], 1.0), MS(sv[:, :, :, 257:258], 0.0)], writes=[sk])
            elif kind == "vb":
                sv = st[:, 0:4 * 8 * 66].rearrange("p (t h c) -> p t h c", t=4, h=8)
                S.op("pool", [MS(sv[:, :, :, 64:65], 1.0), MS(sv[:, :, :, 65:66], 0.0)], writes=[sk])
            else:
                sv = st[:, 0:2048].rearrange("p (t c) -> p t c", t=4)
            for j in range(4):
                tt = t4 * 4 + j
                pt, pk = psrot.next()
                S.op("pe", [MM(pt[:, :], hT[:, kc, tt * 128:(tt + 1) * 128], wt[:, kc, :], start=(kc == 0), stop=(kc == 7)) for kc in range(8)],
                     reads=[wk, ("hT", tt)], writes=[pk])
                if kind == "va":
                    S.op("dve", CP(sv[:, j, :, 0:256], pt[:, :].rearrange("p (h c) -> p h c", h=2)), reads=[pk], writes=[sk])
                elif kind == "vb":
                    S.op("dve", CP(sv[:, j, :, 0:64], pt[:, :].rearrange("p (h c) -> p h c", h=8)), reads=[pk], writes=[sk])
                else:
                    S.op("act", ACT(sv[:, j, :], pt[:, :], AF.Sigmoid), reads=[pk], writes=[sk])
            tsl = slice(t4 * 4, t4 * 4 + 4)
            if kind == "va":
                hd0 = col0
                dst = VA[tsl, :, hd0 * 258:(hd0 + 2) * 258].rearrange("t p c -> p t c")
                S.dma("sp", "S_tm%d" % sk[1], dst, st[:, 0:4 * 516].rearrange("p (t c) -> p t c", t=4), reads=[sk], writes=[("VA", t4, hd0)])
            elif kind == "vb":
                dst = VBA[tsl, :, :].rearrange("t p c -> p t c")
                S.dma("sp", "S_tm%d" % sk[1], dst, st[:, 0:4 * 528].rearrange("p (t c) -> p t c", t=4), reads=[sk], writes=[("VBA", t4)])
            else:
                dst = OG[t4 * 512:(t4 + 1) * 512, col0:col0 + 512].rearrange("(t p) c -> p t c", p=128)
                S.dma("sp", "S_tm%d" % sk[1], dst, sv, reads=[sk], writes=[("OG", t4, col0)])

    for g in (0, 1):
        fm_group(g, "conv", lambda sub, g=g: dict(cg=g * 4 + sub, dst=QT[(g * 4 + sub) // 2, :, (g * 4 + sub) % 2, :],
                                                 dkey=("QT", g * 4 + sub)))
    for g in (2, 3):
        fm_group(g, "conv", lambda sub, g=g: dict(cg=8 + (g - 2) * 4 + sub, dst=KT[((g - 2) * 4 + sub) // 2, :, ((g - 2) * 4 + sub) % 2, :],
                                                 dkey=("KT", (g - 2) * 4 + sub)))
    flush_tail()
    for g in (8, 9):
        fm_group(g, "silu", lambda sub, g=g: dict(dst=ZAT[(g - 8) * 4 + sub], dkey=("ZAT", (g - 8) * 4 + sub)))
    fm_group(13, "silu", lambda sub: dict(dst=ZBT[sub], dkey=("ZBT", sub)))
    fm_group(10, "copy", lambda sub: dict(scale=0.125, dst=QBT[sub], dkey=("QBT", sub)))
    fm_group(11, "copy", lambda sub: dict(scale=1.0, dst=KBT[sub], dkey=("KBT", sub)))
    tm_group(4, "va", 0)
    tm_group(5, "va", 2)
    tm_group(12, "vb", 0)
    tm_group(6, "og", 0)
    tm_group(7, "og", 512)
    for g in (14, 15, 16, 17):
        fm_group(g, "sigb", lambda sub, g=g: dict(bias=cols[:, 64 + (g - 14) * 4 + sub:65 + (g - 14) * 4 + sub],
                                                 dst=GMT[(g - 14) * 4 + sub], dkey=("GMT", (g - 14) * 4 + sub)))
    S.barrier()
    if stop_after == "C":
        S.emit()
        return nc

    if build_mlstm(nc, S, locals()):
        return nc
    if stop_after == "M":
        S.emit()
        return nc
    build_na(nc, S, locals())
    if stop_after == "N":
        S.emit()
        return nc
    build_final(nc, S, locals())
    S.emit()
    return nc


def build_mlstm(nc, S, E):
    T, cur, ps, psb = E["T"], E["cur"], E["ps"], E["psb"]
    identb, maskT, cols, TOKU, TOKC, DECB = E["identb"], E["maskT"], E["cols"], E["TOKU"], E["TOKC"], E["DECB"]
    QT, KT, VA, OG, ZAT, YAT = E["QT"], E["KT"], E["VA"], E["OG"], E["ZAT"], E["YAT"]
    dbg, debug = E["dbg"], E["debug"]
    cur[0] = E["REGION0"]
    qT = T("m_qT", [128, 2, S_], BF16)
    kT = T("m_kT", [128, 2, S_], BF16)
    ktok = T("m_ktok", [128, NT, 256], BF16)
    Vaug = T("m_Vaug", [128, NT, 258], BF16)
    Hacc = T("m_Hacc", [128, NT, 256], F32)
    ZATh = T("m_ZATh", [128, 2, S_], BF16)
    yaT = T("m_yaT", [128, 2, S_], BF16)
    OGt = Rot("OGt", [T("m_OGt%d" % i, [128, 4, 256], BF16) for i in range(3)])
    UVr = [Rot("UV%d" % d, [T("m_UV%d_%d" % (d, i), [128, 258], BF16) for i in range(4)]) for d in range(2)]
    Smr = [Rot("Sm%d" % d, [T("m_Sm%d_%d" % (d, i), [128, 128], BF16) for i in range(3)]) for d in range(2)]
    Zs = [T("m_Z%d" % d, [128, 2, 258], F32) for d in range(2)]
    Cbr = [Rot("Cb%d" % d, [T("m_Cb%d_%d" % (d, i), [128, 2, 258], BF16) for i in range(3)]) for d in range(2)]
    Htr = Rot("Htmp", [T("m_Htmp%d" % i, [128, 256], F32) for i in range(3)])
    hgr = Rot("hg", [T("m_hg%d" % i, [128, 256], F32) for i in range(4)])
    ytr = Rot("yatok", [T("m_yatok%d" % i, [128, 256], BF16) for i in range(4)])
    junk = T("m_junk", [128, 256], BF16)
    rcs = T("m_rcs", [128, 8, 4], F32)
    pcs = T("m_pcs", [128, 8, 4], F32)
    rci = [0]
    pci = [0]
    dcp = [((ps[4], ps[5]), [("ps", 4), ("ps", 5)]), ((ps[6], ps[7]), [("ps", 6), ("ps", 7)])]

    def loads(hd):
        S.dma("sp", "L_mq", qT[:], QT[hd], writes=["qT"])
        S.dma("sp", "L_mk", kT[:], KT[hd], writes=["kT"])
        for j0 in range(0, NT, 8):
            S.dma("sp", "L_mv", Vaug[:, j0:j0 + 8, :], VA[j0:j0 + 8, :, hd * 258:(hd + 1) * 258].rearrange("t p c -> p t c"),
                  writes=[("Vaug", j) for j in range(j0, j0 + 8)])

    loads(0)
    for hd in range(4):
        S.dma("sp", "L_mz", ZATh[:], ZAT[2 * hd:2 * hd + 2].rearrange("g p t -> p g t"), writes=["ZATh"])
        for k4 in range(8):
            kb = 6 + (k4 % 2)
            S.op("pe", [TR(psb(kb)[:, (kk * 2 + c) * 128:(kk * 2 + c + 1) * 128], kT[:, c, (k4 * 4 + kk) * 128:(k4 * 4 + kk + 1) * 128], identb[:])
                        for kk in range(4) for c in range(2)], reads=["kT", "identb"], writes=[("ps", kb)])
            S.op("act", ACT(ktok[:, k4 * 4:(k4 + 1) * 4, :].rearrange("p a b -> p (a b)"), psb(kb)[:, 0:1024], AF.Copy, scale=1.0 / 16),
                 reads=[("ps", kb)], writes=[("ktok", k4)])
        if E["stop_after"] == "M0":
            S.emit()
            return True

        ctx = {}
        chain = [dict(kprev=None) for _ in range(2)]

        def kof(d, i):
            return i if d == 0 else NT - 1 - i

        def opUV(d, i):
            k = kof(d, i)
            ucol = TOKU[:, k, d * 32 + hd:d * 32 + hd + 1]
            UVb, uvk = UVr[d].next()
            S.op("dve", TS(UVb[:], Vaug[:, k, :], ucol, None, ALU.mult), reads=[("Vaug", k), "TOKU"], writes=[uvk])
            ctx[(d, i)] = dict(k=k, ch=slice(k * 128, (k + 1) * 128), UVb=UVb, uvk=uvk, cb=None, cbk=None)

        def opST(d, i):
            c_ = ctx[(d, i)]
            stp = ps[d][:, 0:128]
            S.op("pe", [MM(stp, kT[:, c, c_["ch"]], qT[:, c, c_["ch"]], start=(c == 0), stop=(c == 1)) for c in range(2)],
                 reads=["kT", "qT"], writes=[("ps", d)])

        def opMASK(d, i):
            c_ = ctx[(d, i)]
            Sm, smk = Smr[d].next()
            S.op("dve", TT(Sm[:], ps[d][:, 0:128], maskT[:, d, :], ALU.mult), reads=[("ps", d), "maskT"], writes=[smk])
            c_["Sm"], c_["smk"] = Sm, smk

        def opDC(d, i):
            c_ = ctx[(d, i)]
            (b0, b1), dks = dcp[d]
            k = c_["k"]
            S.op("pe", [MM(b0[:, 0:258], ktok[:, k, 0:128], c_["UVb"][:]), MM(b1[:, 0:258], ktok[:, k, 128:256], c_["UVb"][:])],
                 reads=[("ktok", k // 4), c_["uvk"]], writes=dks)

        def opZ(d, i):
            c_ = ctx[(d, i)]
            (b0, b1), dks = dcp[d]
            series = d * 4 + hd
            k = c_["k"]
            Z = Zs[d]
            zk = ("Z", d)
            if i == 0:
                S.op("dve", [CP(Z[:, 0, :], b0[:, 0:258]), CP(Z[:, 1, :], b1[:, 0:258])], reads=dks, writes=[zk])
            else:
                kp = chain[d]["kprev"]
                dprev = DECB[:, series, kp:kp + 1]
                S.op("dve", [STT(Z[:, 0, :], Z[:, 0, :], dprev, b0[:, 0:258], ALU.mult, ALU.add),
                             STT(Z[:, 1, :], Z[:, 1, :], dprev, b1[:, 0:258], ALU.mult, ALU.add)],
                     reads=dks + [zk, "DECB"], writes=[zk])
            cbn, cbk = Cbr[d].next()
            dcur = DECB[:, series, k:k + 1]
            S.op("act", ACT(cbn[:].rearrange("p a b -> p (a b)"), Z[:].rearrange("p a b -> p (a b)"), AF.Copy, scale=dcur), reads=[zk, "DECB"], writes=[cbk])
            c_["cb"], c_["cbk"] = cbn, cbk
            chain[d]["kprev"] = k

        def opNP(d, i):
            c_ = ctx[(d, i)]
            first = (i == 0)
            npb, npk = ps[2 + d], ("ps", 2 + d)
            npa = npb[:, 0:258]
            specs = [MM(npa, c_["Sm"][:], c_["UVb"][:], start=True, stop=first)]
            rd = [c_["smk"], c_["uvk"]]
            if not first:
                pv = ctx[(d, i - 1)]
                specs += [MM(npa, qT[:, c, c_["ch"]], pv["cb"][:, c, :], start=False, stop=(c == 1)) for c in range(2)]
                rd += ["qT", pv["cbk"]]
            S.op("pe", specs, reads=rd, writes=[npk])

        def opOUT(d, i):
            c_ = ctx[(d, i)]
            k = c_["k"]
            npb, npk = ps[2 + d], ("ps", 2 + d)
            j = rci[0] % 8
            rci[0] += 1
            rc = rcs[:, j, :]
            rck = ("rc", j)
            den = npb[:, 256:257]
            ccol = TOKC[:, k, d * 32 + hd:d * 32 + hd + 1]
            S.op("dve", TT(rc[:, 0:1], den, ccol, ALU.max), reads=[npk, "TOKC"], writes=[rck])
            S.op("dve", STT(rc[:, 1:2], den, -1.0, rc[:, 0:1], ALU.mult, ALU.max), reads=[npk, rck], writes=[rck])
            S.op("dve", ("reciprocal", dict(out=rc[:, 2:3], in_=rc[:, 1:2])), reads=[rck], writes=[rck])
            if i < NT // 2:
                S.op("act", ACT(Hacc[:, k, :], npb[:, 0:256], AF.Copy, scale=rc[:, 2:3]), reads=[npk, rck], writes=[("Hacc", k)])
            else:
                ht, htk = Htr.next()
                S.op("act", ACT(ht[:], npb[:, 0:256], AF.Copy, scale=rc[:, 2:3]), reads=[npk, rck], writes=[htk])
                S.op("pool", TT(Hacc[:, k, :], Hacc[:, k, :], ht[:], ALU.add), reads=[htk, ("Hacc", k)], writes=[("Hacc", k)])
            if i >= 1:
                ctx.pop((d, i - 1))

        for d in range(2):
            opUV(d, 0)
        for d in range(2):
            opUV(d, 1)
        for d in range(2):
            opST(d, 0)
        for d in range(2):
            opMASK(d, 0)
        for d in range(2):
            opDC(d, 0)
        for d in range(2):
            opZ(d, 0)
        for i in range(NT):
            nx = i + 1
            if i + 2 < NT:
                opUV(0, i + 2)
                opUV(1, i + 2)
            if nx < NT:
                opST(0, nx)
                opST(1, nx)
                opMASK(0, nx)
                opMASK(1, nx)
                if nx < NT - 1:
                    opDC(0, nx)
                    opDC(1, nx)
            opNP(0, i)
            opNP(1, i)
            if nx < NT - 1:
                opZ(0, nx)
            opOUT(0, i)
            if nx < NT - 1:
                opZ(1, nx)
            opOUT(1, i)

        if debug:
            S.dma("sp", "S_dbg", dbg["Hacc"][hd], Hacc[:], reads=[("Hacc", k) for k in range(NT)])
        if hd + 1 < 4:
            loads(hd + 1)

        pctx = {}

        def P1(k):
            k4, j = k // 4, k % 4
            if j == 0:
                ogt, ogk = OGt.next()
                S.dma("pool", "L_og%d" % ogk[1], ogt[:], OG[k4 * 512:(k4 + 1) * 512, hd * 256:(hd + 1) * 256].rearrange("(t p) c -> p t c", p=128),
                      writes=[ogk])
                pctx["og"] = (ogt, ogk)
            ogt, ogk = pctx["og"]
            hg, hgk = hgr.next()
            S.op("dve", TT(hg[:], Hacc[:, k, :], ogt[:, j, :], ALU.mult), reads=[("Hacc", k), ogk], writes=[hgk])
            q = pci[0] % 8
            pci[0] += 1
            pc = pcs[:, q, :]
            pck = ("pc", q)
            S.op("act", ACT(junk[:], hg[:], AF.Square, accum_out=pc[:, 0:1]), reads=[hgk], writes=["m_junk", pck])
            S.op("act", ACT(pc[:, 1:2], pc[:, 0:1], AF.Sqrt, scale=1.0 / DH, bias=EPS), reads=[pck], writes=[pck])
            pctx[k] = (hg, hgk, pc, pck)

        def P2(k):
            k4, j = k // 4, k % 4
            hg, hgk, pc, pck = pctx.pop(k)
            S.op("dve", ("reciprocal", dict(out=pc[:, 2:3], in_=pc[:, 1:2])), reads=[pck], writes=[pck])
            yt, ytk = ytr.next()
            S.op("act", ACT(yt[:], hg[:], AF.Copy, scale=pc[:, 2:3]), reads=[hgk, pck], writes=[ytk])
            S.op("pe", [TR(psb(7)[:, c * 512 + j * 128:c * 512 + (j + 1) * 128], yt[:, c * 128:(c + 1) * 128], identb[:]) for c in range(2)],
                 reads=[ytk, "identb"], writes=[("ps", 7)])
            if j == 3:
                for c in range(2):
                    S.op("dve", STT(yaT[:, c, k4 * 512:(k4 + 1) * 512], psb(7)[:, c * 512:(c + 1) * 512], cols[:, 80 + hd * 2 + c:81 + hd * 2 + c],
                                    ZATh[:, c, k4 * 512:(k4 + 1) * 512], ALU.mult, ALU.mult),
                         reads=[("ps", 7), "cols", "ZATh"], writes=[("yaT", c)])

        for k in range(NT + 2):
            if k < NT:
                P1(k)
            if k >= 2:
                P2(k - 2)
        for c in range(2):
            S.dma("pool", "S_yaT", YAT[hd * 2 + c], yaT[:, c, :], reads=[("yaT", c)])
        if E["stop_after"] == "M2":
            S.emit()
            return True
    S.barrier()


def build_na(nc, S, E):
    T, cur, ps, psb = E["T"], E["cur"], E["ps"], E["psb"]
    identb = E["identb"]
    QBT, KBT, VBA, ZBT, YBT, bt2_d = E["QBT"], E["KBT"], E["VBA"], E["ZBT"], E["YBT"], E["bt2_d"]
    cur[0] = E["REGION0"]
    bt2b = T("n_bt2", [128, 4, 14, 2, 64], BF16)
    QBD = T("n_QBD", [128, 2, 2, S_], BF16)
    kbT = T("n_kbT", [128, 2, S_], BF16)
    ZBh = T("n_ZBh", [128, 2, S_], BF16)
    ybT = T("n_ybT", [128, 2, S_], BF16)
    VE = T("n_VE", [128, NT, 4, 66], BF16)
    VO = T("n_VO", [128, NT - 1, 4, 66], BF16)
    PTr = Rot("PT", [T("n_PT%d" % i, [128, 1024], BF16) for i in range(2)])
    otr = Rot("otok", [T("n_otok%d" % i, [64, 4, 64], BF16) for i in range(3)])
    recs = T("n_rec", [64, 4, 4], F32)
    str_ = Rot("psS", [(ps[0], ps[1]), (ps[2], ps[3])])
    pvr = Rot("psPV", [ps[4], ps[5]])
    trr = Rot("psT", [6, 7])
    ri = [0]
    NA_END = cur[0]
    wpab = T("f_wpa", [128, 8, 1024], BF16)
    wpbb = T("f_wpb", [128, 4, 1024], BF16)
    woutb = T("f_wout", [128, 8, 1024], BF16)
    wsr = Rot("f_wst", [T("f_wst%d" % i, [128, 2, 1024], F32) for i in range(1)])
    S.shared_fw = (NA_END, wpab, wpbb, woutb)
    gate_bc = E["gate_bc"]
    S.dma("pool", "L_bt2", bt2b[:], bt2_d[:, :, :, :, :], writes=["bt2b"])
    S.op("pool", [MS(QBD[0:64, :, 1, :], 0.0), MS(QBD[64:128, :, 0, :], 0.0)], writes=["QBDz"])
    for half in range(2):
        S.dma("sp", "L_nq", QBD[0:64, :, 0, :], QBT[2 * half:2 * half + 2, 0:64, :].rearrange("g p t -> p g t"), reads=["QBDz"], writes=["qbT"])
        S.dma("sp", "L_nq", QBD[64:128, :, 1, :], QBT[2 * half:2 * half + 2, 64:128, :].rearrange("g p t -> p g t"), reads=["QBDz"], writes=["qbT"])
        S.dma("sp", "L_nk", kbT[:], KBT[2 * half:2 * half + 2].rearrange("g p t -> p g t"), writes=["kbT"])
        S.dma("sp", "L_nz", ZBh[:], ZBT[2 * half:2 * half + 2].rearrange("g p t -> p g t"), writes=["ZBh"])
        c0 = half * 4 * 66
        for j0 in range(0, NT, 8):
            S.dma("sp", "L_nve", VE[:, j0:j0 + 8, :, :].rearrange("p t h c -> p t (h c)"),
                  VBA[j0:j0 + 8, :, c0:c0 + 264].rearrange("t p c -> p t c"), writes=["VE"])
        for j0 in range(0, NT - 1, 8):
            j1 = min(j0 + 8, NT - 1)
            S.dma("sp", "L_nvo", VO[0:64, j0:j1, :, :].rearrange("p t h c -> p t (h c)"),
                  VBA[j0:j1, 64:128, c0:c0 + 264].rearrange("t p c -> p t c"), writes=["VO"])
            S.dma("sp", "L_nvo", VO[64:128, j0:j1, :, :].rearrange("p t h c -> p t (h c)"),
                  VBA[j0 + 1:j1 + 1, 0:64, c0:c0 + 264].rearrange("t p c -> p t c"), writes=["VO"])
        def n_stage1(r):
            rs = min(max(r - 4, 0), 56)
            j0b = rs - r + 7
            qs = slice(r * 64, (r + 1) * 64)
            pair, sk = str_.next()
            specs = []
            for gi in range(2):
                bank = pair[gi]
                hp = half * 2 + gi
                specs.append(MM(bank[:, 0:512], identb[:], bt2b[:, hp, j0b:j0b + 7:2, :, :].rearrange("p i j q -> p i (j q)"), start=True, stop=False))
                for i in range(4):
                    tok = rs * 64 + i * 128
                    specs.append(MM(bank[:, i * 128:(i + 1) * 128], kbT[:, gi, tok:tok + 128], QBD[:, gi, :, qs], start=False, stop=(i == 3)))
            S.op("pe", specs, reads=["identb", "bt2b", "kbT", "qbT"], writes=[sk])
            PT, ptk = PTr.next()
            S.op("act", [ACT(PT[:, 0:512], pair[0][:, :], AF.Exp), ACT(PT[:, 512:1024], pair[1][:, :], AF.Exp)], reads=[sk], writes=[ptk])
            return dict(r=r, rs=rs, qs=qs, PT=PT, ptk=ptk)

        def n_stage2(c):
            rs, PT, ptk = c["rs"], c["PT"], c["ptk"]
            if rs % 2 == 0:
                Vx, vkey, tbase = VE, "VE", rs // 2
            else:
                Vx, vkey, tbase = VO, "VO", (rs - 1) // 2
            ob, ok = pvr.next()
            specs = []
            for hh in range(4):
                for i in range(4):
                    specs.append(MM(ob[0:64, hh * 66:(hh + 1) * 66], PT[:, (hh // 2) * 512 + i * 128 + (hh % 2) * 64:(hh // 2) * 512 + i * 128 + (hh % 2) * 64 + 64], Vx[:, tbase + i, hh, :],
                                    start=(i == 0), stop=(i == 3)))
            S.op("pe", specs, reads=[ptk, vkey], writes=[ok])
            q = ri[0] % 4
            ri[0] += 1
            rec = recs[:, q, :]
            rk = ("rec", q)
            ov = ob[0:64, 0:264].rearrange("p (h c) -> p h c", c=66)
            S.op("dve", ("reciprocal", dict(out=rec, in_=ov[:, :, 64])), reads=[ok], writes=[rk])
            ot, otk = otr.next()
            S.op("dve", TT(ot[:], ov[:, :, 0:64], rec.rearrange("p (h o) -> p h o", o=1).to_broadcast([64, 4, 64]), ALU.mult),
                 reads=[ok, rk], writes=[otk])
            c["ot"], c["otk"] = ot, otk

        def n_stage3(c):
            ot, otk, qs = c["ot"], c["otk"], c["qs"]
            tb_, tk = trr.next()
            otf = ot[:].rearrange("p h c -> p (h c)")
            S.op("pe", [TR(psb(tb_)[:, g * 64:(g + 1) * 64], otf[:, g * 128:(g + 1) * 128], identb[0:64, 0:64]) for g in range(2)],
                 reads=[otk, "identb"], writes=[tk])
            S.op("dve", TT(ybT[:, :, qs], psb(tb_)[:, 0:128].rearrange("p (g q) -> p g q", g=2), ZBh[:, :, qs], ALU.mult),
                 reads=[tk, "ZBh"], writes=["ybT"])

        nctx = {}
        if half == 0:
            S.dma("pool", "L_fwa", wpab[:], E["wpa_d"][:, :, :], writes=["wpab"])
            S.dma("pool", "L_fwb", wpbb[:], E["wpb_d"][:, :, :], writes=["wpbb"])
            for j in range(4):
                wt, wk = wsr.next()
                S.dma("sp", "L_fws%d" % wk[1], wt[:], E["wout_d"][:, 2 * j:2 * j + 2, :], writes=[wk])
                S.op("dve", TT(woutb[:, 2 * j:2 * j + 2, :], wt[:], gate_bc[:].rearrange("p (o n) -> p o n", o=1).to_broadcast([128, 2, 1024]), ALU.mult),
                     reads=[wk, "gate_bc"], writes=["woutb"])
        for r in range(64 + 2):
            if r < 64:
                nctx[r] = n_stage1(r)
            if 0 <= r - 1 < 64:
                n_stage2(nctx[r - 1])
            if 0 <= r - 2 < 64:
                n_stage3(nctx.pop(r - 2))
        for g in range(2):
            S.dma("sp", "S_ybT", YBT[2 * half + g], ybT[:, g, :], reads=["ybT"])
    S.barrier()


def build_final(nc, S, E):
    T, cur, ps = E["T"], E["cur"], E["ps"]
    gate_bc, fg_bc, smallc = E["gate_bc"], E["fg_bc"], E["smallc"]
    YAT, YBT, GMT, x_d, y_d = E["YAT"], E["YBT"], E["GMT"], E["x_d"], E["y_d"]
    wpa_d, wpb_d, wout_d = E["wpa_d"], E["wpb_d"], E["wout_d"]
    cur[0] = E["REGION0"]
    NA_END, wpab, wpbb, woutb = S.shared_fw
    yar = Rot("f_ya", [T("f_ya%d" % i, [128, 8, 512], BF16) for i in range(2)])
    ybr = Rot("f_yb", [T("f_yb%d" % i, [128, 4, 512], BF16) for i in range(2)])
    gmr = Rot("f_gm", [T("f_gm%d" % i, [128, 16, 512], BF16) for i in range(2)])
    t1r = Rot("f_t1", [T("f_t1%d" % i, [128, 512], F32) for i in range(2)])
    t2r = Rot("f_t2", [T("f_t2%d" % i, [128, 512], F32) for i in range(2)])
    mgr = Rot("f_mg", [T("f_mg%d" % i, [128, 8, 512], BF16) for i in range(2)])
    xr = Rot("f_xt", [T("f_xt%d" % i, [128, 1024], F32) for i in range(5)])
    x2r = Rot("f_x2", [T("f_x2%d" % i, [128, 1024], F32) for i in range(2)])
    otr = Rot("f_ot", [T("f_ot%d" % i, [128, 1024], F32) for i in range(2)])
    junk = T("f_junk", [128, 1024], BF16)
    par = Rot("psP", [ps[0], ps[1], ps[2], ps[3]])
    outr = Rot("psO", [(ps[4], ps[5]), (ps[6], ps[7])])
    assert cur[0] <= NA_END, (cur[0], NA_END)
    def f_loads(tb):
        ts = slice(tb * 512, (tb + 1) * 512)
        ya, yak = yar.next()
        yb, ybk = ybr.next()
        gm, gmk = gmr.next()
        S.dma("sp", "L_fya%d" % yak[1], ya[:], YAT[:, :, ts].rearrange("g p t -> p g t"), writes=[yak])
        S.dma("sp", "L_fyb%d" % ybk[1], yb[:], YBT[:, :, ts].rearrange("g p t -> p g t"), writes=[ybk])
        S.dma("sp", "L_fgm%d" % gmk[1], gm[:], GMT[:, :, ts].rearrange("g p t -> p g t"), writes=[gmk])
        return ya, yak, yb, ybk, gm, gmk

    fl = {0: f_loads(0)}
    mgs = {}

    def stageP(tb):
        ya, yak, yb, ybk, gm, gmk = fl.pop(tb)
        if tb + 1 < NBLK:
            fl[tb + 1] = f_loads(tb + 1)
        mg, mgk = mgr.next()
        for fg in range(8):
            fs = slice(fg * 128, (fg + 1) * 128)
            pa, pak = par.next()
            S.op("pe", [MM(pa[:, :], wpab[:, kc, fs], ya[:, kc, :], start=(kc == 0), stop=(kc == 7)) for kc in range(8)],
                 reads=["wpab", yak], writes=[pak])
            pb_, pbk = par.next()
            S.op("pe", [MM(pb_[:, :], wpbb[:, kc, fs], yb[:, kc, :], start=(kc == 0), stop=(kc == 3)) for kc in range(4)],
                 reads=["wpbb", ybk], writes=[pbk])
            t1, t1k = t1r.next()
            t2, t2k = t2r.next()
            S.op("dve", TT(t1[:], pa[:, :], gm[:, fg, :], ALU.mult), reads=[pak, gmk], writes=[t1k])
            S.op("dve", TT(t2[:], pb_[:, :], gm[:, 8 + fg, :], ALU.mult), reads=[pbk, gmk], writes=[t2k])
            S.op("pool", TT(mg[:, fg, :], t1[:], t2[:], ALU.add), reads=[t1k, t2k], writes=[(mgk, fg)])
        mgs[tb] = (mg, mgk)

    def stageO(tb):
        mg, mgk = mgs.pop(tb)
        xtl = []
        for tt in range(4):
            tile = tb * 4 + tt
            xt, xk = xr.next()
            S.dma("sp", "L_fx%d" % xk[1], xt[:], x_d[tile * 128:(tile + 1) * 128, :], writes=[xk])
            xtl.append((xt, xk))
        for tt in range(4):
            tile = tb * 4 + tt
            xt, xk = xtl[tt]
            (o0, o1), ok = outr.next()
            specs = []
            for nh, ob in enumerate((o0, o1)):
                for fg in range(8):
                    specs.append(MM(ob[:, :], mg[:, fg, tt * 128:(tt + 1) * 128], woutb[:, fg, nh * 512:(nh + 1) * 512], start=(fg == 0), stop=(fg == 7)))
            S.op("pe", specs, reads=[(mgk, fg) for fg in range(8)] + ["woutb"], writes=[ok])
            x2, x2k = x2r.next()
            S.op("dve", [TT(x2[:, 0:512], o0[:, :], xt[:, 0:512], ALU.add), TT(x2[:, 512:1024], o1[:, :], xt[:, 512:1024], ALU.add)],
                 reads=[ok, xk], writes=[x2k])
            q = tile % 4
            sc = smallc[:, q * 4:q * 4 + 4]
            sck = ("smallc", q)
            S.op("act", ACT(junk[:], x2[:], AF.Square, accum_out=sc[:, 0:1]), reads=[x2k], writes=["f_junk", sck])
            S.op("act", ACT(sc[:, 1:2], sc[:, 0:1], AF.Sqrt, scale=1.0 / D, bias=EPS), reads=[sck], writes=[sck])
            S.op("dve", ("reciprocal", dict(out=sc[:, 2:3], in_=sc[:, 1:2])), reads=[sck], writes=[sck])
            ot, otk = otr.next()
            S.op("act", ACT(ot[:], x2[:], AF.Copy, scale=sc[:, 2:3]), reads=[x2k, sck], writes=[otk])
            S.op("pool", TT(ot[:], ot[:], fg_bc[:], ALU.mult), reads=[otk, "fg_bc"], writes=[otk])
            S.dma("pool", "S_fo%d" % otk[1], y_d[tile * 128:(tile + 1) * 128, :], ot[:], reads=[otk])

    for tb in range(NBLK + 1):
        if tb < NBLK:
            stageP(tb)
        if tb >= 1:
            stageO(tb - 1)


def _shared_layouts(inp):
    f = np.float32
    w_ada = np.asarray(inp["w_ada"], f)[0]
    w_in = np.asarray(inp["w_in"], f)[0]
    sh = {}
    sh["wada"] = np.ascontiguousarray(w_ada.reshape(8, 128, 3072).transpose(1, 0, 2))
    sh["rows"] = np.ascontiguousarray(np.concatenate(
        [np.asarray(inp["b_ada"], f)[0], np.asarray(inp["norm_gain"], f)[0], np.asarray(inp["final_gain"], f)])[None, :])
    wg = np.zeros((1024, 72), f)
    gc = w_in[:, 5120:5136].reshape(1024, 2, 2, 4)
    wg[:, 0:4] = gc[:, 0, 0]
    wg[:, 32:36] = gc[:, 1, 0]
    wg[:, 36:40] = gc[:, 0, 1]
    wg[:, 68:72] = gc[:, 1, 1]
    sh["wgate"] = np.ascontiguousarray(wg.reshape(8, 128, 72).transpose(1, 0, 2))
    wl = np.concatenate([w_in[:, 0:5120], w_in[:, 5136:9232]], axis=1)
    sh["win"] = np.ascontiguousarray(wl.reshape(8, 128, 18, 512).transpose(2, 1, 0, 3))
    cols = np.zeros((128, 88), f)
    cw = np.asarray(inp["conv_w"], f)[0]
    cb = np.asarray(inp["conv_b"], f)[0]
    cols[:, 0:48] = cw.reshape(3, 16, 128).transpose(2, 1, 0).reshape(128, 48)
    cols[:, 48:64] = cb.reshape(16, 128).T
    cols[:, 64:80] = np.asarray(inp["b_merge"], f)[0].reshape(16, 128).T
    cols[:, 80:88] = np.asarray(inp["mlstm_norm_gain"], f)[0].reshape(8, 128).T
    sh["cols"] = cols
    gb = np.zeros((36, 2), f)
    bi = np.asarray(inp["b_igate"], f)[0]
    bfg = np.asarray(inp["b_fgate"], f)[0]
    gb[0:4, 0] = bi[0]
    gb[32:36, 0] = bi[1]
    gb[0:4, 1] = bfg[0]
    gb[32:36, 1] = bfg[1]
    sh["gb"] = gb
    rpb = np.asarray(inp["rpb"], f)[0]
    kc = np.arange(64)[:, None]
    qc = np.arange(64)[None, :]
    ws = np.clip(qc - 8, 0, 48)
    colok = (kc >= ws) & (kc < ws + 16)
    dcidx = np.clip(kc - qc + 15, 0, 30)
    bt2 = np.full((128, 8, 14, 64), NEG, f)
    for j in range(14):
        for half in range(2):
            dr = j - 7 + half
            tab = np.where(colok[None], rpb[:, dr + 7][:, dcidx], f(NEG))
            bt2[half * 64:(half + 1) * 64, :, j, :] = tab.transpose(1, 0, 2)
    sh["bt2"] = np.ascontiguousarray(bt2.reshape(128, 4, 2, 14, 64).transpose(0, 1, 3, 2, 4))
    sh["wpa"] = np.ascontiguousarray(np.asarray(inp["w_proj_a"], f)[0].reshape(8, 128, 1024).transpose(1, 0, 2))
    sh["wpb"] = np.ascontiguousarray(np.asarray(inp["w_proj_b"], f)[0].reshape(4, 128, 1024).transpose(1, 0, 2))
    sh["wout"] = np.ascontiguousarray(np.asarray(inp["w_out"], f)[0].reshape(8, 128, 1024).transpose(1, 0, 2))
    sh["ident"] = np.eye(128, dtype=f)
    s_i = np.arange(128)[:, None]
    t_i = np.arange(128)[None, :]
    masks = np.zeros((128, 2, 128), f)
    masks[:, 0, :] = np.where(s_i <= t_i, 1.0 / 16, 0.0)
    masks[:, 1, :] = np.where(s_i >= t_i, 1.0 / 16, 0.0)
    sh["masks"] = masks
    sel = np.zeros((36, 8, 128), f)
    for j in range(8):
        sel[(j % 4) + 32 * (j // 4), j, :] = 1.0
    sh["sel"] = sel
    return sh


def make_in_maps(inp):
    sh = _shared_layouts(inp)
    x = np.asarray(inp["x"], np.float32)
    c = np.asarray(inp["c"], np.float32)
    maps = []
    for b in range(8):
        m = dict(sh)
        m["x"] = np.ascontiguousarray(x[b])
        m["c_l"] = np.ascontiguousarray(c[b].reshape(8, 128).T)
        maps.append(m)
    return maps


_NC_CACHE = {}


def kernel(**inputs):
    if "nc" not in _NC_CACHE:
        _NC_CACHE["nc"] = build_program()
    nc = _NC_CACHE["nc"]
    in_maps = make_in_maps(inputs)
    res = run_bass_kernel_spmd(nc, in_maps, core_ids=list(range(8)))
    return np.stack([np.asarray(r["y"], np.float32) for r in res.results], axis=0)
```

```python
import numpy as np
import concourse.bass as bass
import concourse.mybir as mybir
from concourse.bass_utils import run_bass_kernel_spmd

F32 = mybir.dt.float32
BF16 = mybir.dt.bfloat16
ALU = mybir.AluOpType
AF = mybir.ActivationFunctionType

S_ = 4096
D = 1024
NT = 32
NBLK = 8
H = 4
DH = 256
NH = 8
NEG = -30000.0
EPS = 1e-6
ENG_NAMES = ("pe", "act", "dve", "pool", "sp")


class Sched:
    def __init__(self, nc):
        self.nc = nc
        self.ops = {e: [] for e in ENG_NAMES}
        self.count = {}
        self.last_writer = {}
        self.readers = {}
        self.seen = {e: {} for e in ENG_NAMES}
        self.sem_names = ["pe", "act", "dve", "pool"]
        self.is_dma = set()
        self.n_instr = 0

    def _deps(self, reads, writes):
        deps = set()
        for k in reads:
            w = self.last_writer.get(k)
            if w is not None:
                deps.add(w)
        for k in writes:
            w = self.last_writer.get(k)
            if w is not None:
                deps.add(w)
            deps.update(self.readers.get(k, ()))
        return deps

    def _record(self, me, reads, writes):
        for k in reads:
            self.readers.setdefault(k, []).append(me)
        for k in writes:
            self.last_writer[k] = me
            self.readers[k] = []

    def _waits(self, eng, deps):
        need = {}
        for (s, i) in deps:
            if s in self.is_dma:
                i = self.count[s] - 1
            if need.get(s, -1) < i:
                need[s] = i
        waits = []
        for s, i in need.items():
            if self.seen[eng].get(s, -1) >= i:
                continue
            self.seen[eng][s] = i
            waits.append((s, i + 1))
        return waits

    def op(self, eng, specs, reads=(), writes=()):
        if isinstance(specs, tuple):
            specs = [specs]
        waits = self._waits(eng, self._deps(reads, writes))
        idx = self.count.get(eng, 0)
        self.count[eng] = idx + 1
        self.ops[eng].append((specs, waits, (eng, 1)))
        self._record((eng, idx), reads, writes)
        self.n_instr += len(specs)

    def dma(self, queue, stream, out, in_, reads=(), writes=()):
        if stream not in self.is_dma:
            self.is_dma.add(stream)
            self.sem_names.append(stream)
            self.count[stream] = 0
        waits = self._waits(queue, self._deps(reads, writes))
        idx = self.count[stream]
        self.count[stream] = idx + 1
        self.ops[queue].append(([("dma_start", dict(out=out, in_=in_))], waits, (stream, 16)))
        self._record((stream, idx), reads, writes)
        self.n_instr += 1

    def barrier(self):
        allw = [(s, c) for s, c in self.count.items() if c > 0]
        for e in ENG_NAMES:
            waits = []
            for s, c in allw:
                if self.seen[e].get(s, -1) >= c - 1:
                    continue
                self.seen[e][s] = c - 1
                waits.append((s, c))
            if waits:
                self.ops[e].append((None, waits, None))
        self.last_writer = {}
        self.readers = {}

    def emit(self):
        import contextlib
        nc = self.nc
        self.barrier()
        with contextlib.ExitStack() as st:
            sems = {s: st.enter_context(nc.semaphore("s_" + s)) for s in self.sem_names}
            block = st.enter_context(nc.Block())

            def run(engname):
                def body(eng):
                    for specs, waits, inc in self.ops[engname]:
                        for (s, v) in waits:
                            eng.wait_ge(sems[s], v * (16 if s in self.is_dma else 1))
                        if specs is None:
                            continue
                        ins = None
                        for (m, kw) in specs:
                            ins = getattr(eng, m)(**kw)
                        ins.then_inc(sems[inc[0]], inc[1])
                return body

            block.tensor(run("pe"))
            block.scalar(run("act"))
            block.vector(run("dve"))
            block.gpsimd(run("pool"))
            block.sync(run("sp"))


class Rot:
    def __init__(self, name, tiles):
        self.name, self.tiles, self.i = name, tiles, 0

    def next(self):
        j = self.i % len(self.tiles)
        self.i += 1
        return self.tiles[j], (self.name, j)


def MM(out, lhsT, rhs, start=True, stop=True):
    return ("matmul", dict(out=out, lhsT=lhsT, rhs=rhs, start=start, stop=stop))


def TR(out, in_, identity):
    return ("transpose", dict(out=out, in_=in_, identity=identity))


def ACT(out, in_, func, **kw):
    return ("activation", dict(out=out, in_=in_, func=func, **kw))


def TT(out, in0, in1, op):
    return ("tensor_tensor", dict(out=out, in0=in0, in1=in1, op=op))


def TS(out, in0, scalar1, scalar2, op0, op1=None):
    d = dict(out=out, in0=in0, scalar1=scalar1, scalar2=scalar2, op0=op0)
    if op1 is not None:
        d["op1"] = op1
    return ("tensor_scalar", d)


def STT(out, in0, scalar, in1, op0, op1):
    return ("scalar_tensor_tensor", dict(out=out, in0=in0, scalar=scalar, in1=in1, op0=op0, op1=op1))


def CP(out, in_):
    return ("tensor_copy", dict(out=out, in_=in_))


def MS(ap, v):
    return ("memset", dict(ap=ap, constant=v))


def SCAN(out, data0, data1, initial, op0, op1):
    return ("tensor_tensor_scan", dict(out=out, data0=data0, data1=data1, initial=initial, op0=op0, op1=op1))


def build_program(stop_after=None, debug=False):
    nc = bass.Bass("TRN2", target_bir_lowering=False)
    dbg_kind = "ExternalOutput" if debug else "Internal"

    def DIN(name, shape, dt=F32):
        return nc.dram_tensor(name, list(shape), dt, kind="ExternalInput").ap()

    def DSC(name, shape, dt=BF16):
        return nc.dram_tensor(name, list(shape), dt, kind=dbg_kind).ap()

    x_d = DIN("x", [S_, D])
    c_d = DIN("c_l", [128, 8])
    wada_d = DIN("wada", [128, 8, 3072])
    rows_d = DIN("rows", [1, 5120])
    wgate_d = DIN("wgate", [128, 8, 72])
    win_d = DIN("win", [18, 128, 8, 512])
    cols_d = DIN("cols", [128, 88])
    gb_d = DIN("gb", [36, 2])
    bt2_d = DIN("bt2", [128, 4, 14, 2, 64])
    wpa_d = DIN("wpa", [128, 8, 1024])
    wpb_d = DIN("wpb", [128, 4, 1024])
    wout_d = DIN("wout", [128, 8, 1024])
    ident_d = DIN("ident", [128, 128])
    masks_d = DIN("masks", [128, 2, 128])
    sel_d = DIN("sel", [36, 8, 128])
    y_d = nc.dram_tensor("y", [S_, D], F32, kind="ExternalOutput").ap()

    QT = DSC("QT", [4, 128, 2, S_])
    KT = DSC("KT", [4, 128, 2, S_])
    VA = DSC("VA", [NT, 128, 4 * 258])
    OG = DSC("OG", [S_, D])
    ZAT = DSC("ZAT", [8, 128, S_])
    QBT = DSC("QBT", [4, 128, S_])
    KBT = DSC("KBT", [4, 128, S_])
    VBA = DSC("VBA", [NT, 128, 8 * 66])
    ZBT = DSC("ZBT", [4, 128, S_])
    GMT = DSC("GMT", [16, 128, S_])
    YAT = DSC("YAT", [8, 128, S_])
    YBT = DSC("YBT", [4, 128, S_])
    dbg = {}
    if debug:
        dbg["hT"] = nc.dram_tensor("dbg_hT", [128, 8, S_], BF16, kind="ExternalOutput").ap()
        dbg["TOKU"] = nc.dram_tensor("dbg_TOKU", [128, NT, 36], F32, kind="ExternalOutput").ap()
        dbg["TOKC"] = nc.dram_tensor("dbg_TOKC", [128, NT, 36], F32, kind="ExternalOutput").ap()
        dbg["DECB"] = nc.dram_tensor("dbg_DECB", [128, 8, NT], F32, kind="ExternalOutput").ap()
        dbg["Hacc"] = nc.dram_tensor("dbg_Hacc", [4, 128, NT, 256], F32, kind="ExternalOutput").ap()
        for nm in ("T1", "T2", "T3"):
            dbg[nm] = nc.dram_tensor("dbg_" + nm, [36, S_], F32, kind="ExternalOutput").ap()

    def dstop(tag):
        if stop_after != tag:
            return False
        S.dma("sp", "S_dbg", dbg["T1"][:, :], T1[0:36, :], reads=["T1g"])
        S.dma("sp", "S_dbg", dbg["T2"][:, :], T2[0:36, :], reads=["T2g"])
        S.dma("sp", "S_dbg", dbg["T3"][:, :], T3[0:36, :], reads=["T3g"])
        S.emit()
        return True

    SB_LO = 16512
    SB_HI = 229344
    cur = [SB_LO]

    def T(name, shape, dt):
        n = int(np.prod(shape[1:])) * (4 if dt == F32 else 2)
        n = (n + 31) // 32 * 32
        assert cur[0] + n <= SB_HI, (name, cur[0], n)
        t = nc.alloc_sbuf_tensor_at(name, list(shape), dt, offset=cur[0])
        cur[0] += n
        return t

    ps = [nc.alloc_psum_tensor("ps%d" % i, [128, 512], F32) for i in range(8)]

    def psb(i):
        return ps[i][:].bitcast(BF16)

    S = Sched(nc)

    identb = T("identb", [128, 128], BF16)
    identf = T("identf", [128, 128], F32)
    maskT = T("maskT", [128, 2, 128], F32)
    cols = T("cols", [128, 88], F32)
    TOKU = T("TOKU", [128, NT, 36], F32)
    TOKC = T("TOKC", [128, NT, 36], F32)
    DECB = T("DECB", [128, 8, NT], F32)
    gate_bc = T("gate_bc", [128, 1024], F32)
    fg_bc = T("fg_bc", [128, 1024], F32)
    ones_c = T("ones_c", [128, 128], F32)
    smallc = T("smallc", [128, 64], F32)
    REGION0 = cur[0]

    S.dma("sp", "L_const", identf[:], ident_d[:, :], writes=["identf"])
    S.dma("pool", "L_constb", identb[:], ident_d[:, :], writes=["identb"])
    S.dma("sp", "L_const", maskT[:], masks_d[:, :, :], writes=["maskT"])
    S.dma("sp", "L_const", cols[:], cols_d[:, :], writes=["cols"])
    S.op("pool", MS(ones_c[:], 1.0), writes=["ones_c"])

    cur[0] = REGION0
    modrow = T("modrow", [1, 3072], F32)
    rows = T("rows", [1, 5120], F32)
    wst = [T("wada_st%d" % i, [128, 8, 512], F32) for i in range(2)]
    A_END = cur[0]
    cur[0] = REGION0 + 65536
    G1_bc = T("G1_bc", [128, 1024], F32)
    sh_bc = T("sh_bc", [128, 1024], F32)
    c_sb = T("c_sb", [128, 8], F32)
    cond = T("cond", [128, 8], F32)
    G1row = T("G1row", [1, 1024], F32)
    assert A_END <= REGION0 + 65536

    S.dma("sp", "L_const", c_sb[:], c_d[:, :], writes=["c_sb"])
    S.dma("sp", "L_const", rows[:], rows_d[:, :], writes=["rows"])
    S.op("act", ACT(cond[:], c_sb[:], AF.Silu), reads=["c_sb"], writes=["cond"])
    wrot = Rot("wada_st", wst)
    for g in range(6):
        wt, wk = wrot.next()
        S.dma("sp", "L_wada%d" % wk[1], wt[:], wada_d[:, :, g * 512:(g + 1) * 512], writes=[wk])
        S.op("pe", [MM(ps[g % 2][0:1, :], cond[:, kc:kc + 1], wt[:, kc, :], start=(kc == 0), stop=(kc == 7)) for kc in range(8)],
             reads=["cond", wk], writes=[("ps", g % 2)])
        S.op("dve", TT(modrow[0:1, g * 512:(g + 1) * 512], ps[g % 2][0:1, :], rows[0:1, g * 512:(g + 1) * 512], ALU.add),
             reads=[("ps", g % 2), "rows"], writes=["modrow"])
    S.op("dve", STT(G1row[0:1, :], modrow[0:1, 1024:2048], 1.0, rows[0:1, 3072:4096], ALU.add, ALU.mult),
         reads=["modrow", "rows"], writes=["G1row"])
    bc_jobs = [(G1_bc, G1row[0:1, :], "G1row", "G1_bc"), (sh_bc, modrow[0:1, 0:1024], "modrow", "sh_bc"),
               (gate_bc, modrow[0:1, 2048:3072], "modrow", "gate_bc"), (fg_bc, rows[0:1, 4096:5120], "rows", "fg_bc")]
    bi = 0
    for (dst, src, skey, dkey) in bc_jobs:
        for hf in range(2):
            b = 2 + (bi % 2)
            bi += 1
            S.op("pe", MM(ps[b][:, :], ones_c[0:1, 0:128], src[:, hf * 512:(hf + 1) * 512]), reads=[skey, "ones_c"], writes=[("ps", b)])
            S.op("act", ACT(dst[:, hf * 512:(hf + 1) * 512], ps[b][:, :], AF.Copy), reads=[("ps", b)], writes=[dkey])
    S.barrier()

    cur[0] = REGION0
    hT = T("hT", [128, 8, S_], BF16)
    C_START = cur[0]
    assert C_START == REGION0 + 65536
    cur[0] = C_START + 8192 + 2 * 32
    cur[0] = (cur[0] + 4096 + 31) // 32 * 32
    xts = [T("xt%d" % i, [128, 1024], F32) for i in range(4)]
    xns = [T("xn%d" % i, [128, 1024], BF16) for i in range(3)]
    junkb = T("junkb", [128, 1024], BF16)
    xrot = Rot("xt", xts)
    xnrot = Rot("xn", xns)
    def b_stage1(tt):
        xt, xk = xrot.next()
        xn, xnk = xnrot.next()
        sc = smallc[:, (tt % 4) * 4:(tt % 4) * 4 + 4]
        sck = ("smallc", tt % 4)
        S.dma("sp", "L_xt%d" % xk[1], xt[:], x_d[tt * 128:(tt + 1) * 128, :], writes=[xk])
        S.op("act", ACT(junkb[:], xt[:], AF.Square, accum_out=sc[:, 0:1]), reads=[xk], writes=["junkb", sck])
        S.op("act", ACT(sc[:, 1:2], sc[:, 0:1], AF.Sqrt, scale=1.0 / D, bias=EPS), reads=[sck], writes=[sck])
        S.op("dve", ("reciprocal", dict(out=sc[:, 2:3], in_=sc[:, 1:2])), reads=[sck], writes=[sck])
        S.op("dve", STT(xt[:], xt[:], sc[:, 2:3], G1_bc[:], ALU.mult, ALU.mult), reads=[xk, sck, "G1_bc"], writes=[xk])
        S.op("dve" if tt % 3 != 2 else "pool", TT(xn[:], xt[:], sh_bc[:], ALU.add), reads=[xk, "sh_bc"], writes=[xnk])
        return xn, xnk

    def b_stage2(tt, xn, xnk):
        b = 6 + (tt % 2)
        S.op("pe", [TR(psb(b)[:, kc * 128:(kc + 1) * 128], xn[:, kc * 128:(kc + 1) * 128], identb[:]) for kc in range(8)],
             reads=[xnk, "identb"], writes=[("ps", b)])
        S.op("act", ACT(hT[:, :, tt * 128:(tt + 1) * 128], psb(b).rearrange("p (a b) -> p a b", a=8), AF.Copy),
             reads=[("ps", b)], writes=[("hT", tt)])

    bctx = {}
    for tt in range(NT + 1):
        if tt < NT:
            bctx[tt] = b_stage1(tt)
        if tt >= 1:
            b_stage2(tt - 1, *bctx.pop(tt - 1))
    if debug:
        S.dma("sp", "S_dbg", dbg["hT"][:, :, :], hT[:], reads=[("hT", tt) for tt in range(NT)])
    S.barrier()
    if stop_after == "B":
        S.emit()
        return nc

    cur[0] = C_START
    Wb = [T("Wb%d" % i, [128, 8, 512], BF16) for i in range(3)]
    wgb = T("wgb", [128, 8, 72], BF16)
    UA = T("UA", [128, S_ + 2], F32)
    UB = T("UB", [128, S_ + 2], F32)
    ACC = T("ACC", [128, S_], F32)
    obufs = [T("obuf%d" % i, [128, S_], BF16) for i in range(2)]
    tms = [T("tmst%d" % i, [128, 2112], BF16) for i in range(2)]
    gbc = T("gbc", [36, 2], F32)
    MPt = T("MPt", [36, NT], F32)
    MOt = T("MOt", [36, NT], F32)
    DECt = T("DECt", [36, NT], F32)
    selt = T("selt", [36, 8, 128], F32)
    C_END = cur[0]
    T1 = nc.alloc_sbuf_tensor_at("T1g", [128, S_], F32, offset=C_START + 3 * 8192 + 1152)
    T2 = nc.alloc_sbuf_tensor_at("T2g", [128, S_], F32, offset=C_START + 3 * 8192 + 1152 + 16416)
    T3 = ACC
    ONESF = nc.alloc_sbuf_tensor_at("ONESF", [128, S_], F32, offset=C_START + 3 * 8192 + 1152 + 2 * 16416 + 16384)

    wrot = Rot("Wb", Wb)
    orot = Rot("obuf", obufs)
    tmrot = Rot("tmst", tms)
    urot = Rot("U", [UA, UB])
    psrot = Rot("ps", ps[0:6])
    evtog = [0]

    S.dma("sp", "L_const", gbc[:], gb_d[:, :], writes=["gbc"])
    S.dma("sp", "L_const", selt[:], sel_d[:, :, :], writes=["selt"])
    S.dma("pool", "L_wgb", wgb[:], wgate_d[:, :, :], writes=["wgb"])
    allhT = [("hT", tt) for tt in range(NT)]

    def hkeys(tb):
        return [("hT", tb * 4 + j) for j in range(4)]

    for tb in range(NBLK):
        for gi, (Tt, col, tkey) in enumerate(((T1, 0, "T1g"), (T2, 1, "T2g"))):
            pt, pk = psrot.next()
            S.op("pe", [MM(pt[0:36, :], wgb[:, kc, gi * 36:(gi + 1) * 36], hT[:, kc, tb * 512:(tb + 1) * 512],
                           start=(kc == 0), stop=(kc == 7)) for kc in range(8)],
                 reads=["wgb"] + hkeys(tb), writes=[pk])
            S.op("act", ACT(Tt[0:36, tb * 512:(tb + 1) * 512], pt[0:36, :], AF.Identity, bias=gbc[0:36, col:col + 1]),
                 reads=[pk, "gbc"], writes=[tkey])

    if dstop("D0"):
        return nc
    r36 = slice(0, 36)
    fw = slice(0, 4)
    bw = slice(32, 36)
    S.op("act", ACT(T2[r36, :], T2[r36, :], AF.Exp, scale=-1.0), reads=["T2g"], writes=["T2g"])
    S.op("act", ACT(T2[r36, :], T2[r36, :], AF.Ln, bias=1.0), reads=["T2g"], writes=["T2g"])
    S.op("pool", MS(T3[r36, :], 0.0), writes=["T3g"])
    S.op("pool", MS(ONESF[r36, :], 1.0), writes=["ONESF"])
    S.op("pool", [MS(MPt[:], 0.0), MS(MOt[:], 0.0)], writes=["MPt", "MOt"])
    S.op("dve", SCAN(T3[fw, :], ONESF[fw, :], T2[fw, :], 0.0, ALU.mult, ALU.add),
         reads=["T2g", "ONESF"], writes=["T3g"])
    S.op("dve", SCAN(T3[bw, ::-1], ONESF[bw, :], T2[bw, ::-1], 0.0, ALU.mult, ALU.add),
         reads=["T2g", "ONESF"], writes=["T3g"])
    if dstop("D1"):
        return nc
    S.op("dve", TT(T1[r36, :], T1[r36, :], T3[r36, :], ALU.add), reads=["T1g", "T3g"], writes=["T1g"])
    S.op("dve", SCAN(T2[fw, :], ONESF[fw, :], T1[fw, :], 0.0, ALU.mult, ALU.max),
         reads=["T1g", "ONESF"], writes=["T2g"])
    S.op("dve", SCAN(T2[bw, ::-1], ONESF[bw, :], T1[bw, ::-1], 0.0, ALU.mult, ALU.max),
         reads=["T1g", "ONESF"], writes=["T2g"])
    if dstop("D2"):
        return nc
    M3 = T2[:].rearrange("p (k t) -> p k t", t=128)
    S.op("dve", [CP(MPt[fw, 1:NT], M3[fw, 0:NT - 1, 127]), CP(MOt[fw, :], M3[fw, :, 127])],
         reads=["T2g", "MPt", "MOt"], writes=["MPt", "MOt"])
    S.op("dve", [CP(MPt[bw, 0:NT - 1], M3[bw, 1:NT, 0]), CP(MOt[bw, :], M3[bw, :, 0])],
         reads=["T2g", "MPt", "MOt"], writes=["MPt", "MOt"])
    S.op("dve", TT(DECt[:], MPt[:], MOt[:], ALU.subtract), reads=["MPt", "MOt"], writes=["DECt"])
    S.op("act", ACT(DECt[:], DECt[:], AF.Exp), reads=["DECt"], writes=["DECt"])
    if dstop("D3"):
        return nc
    MPb = MPt[:].rearrange("p (k o) -> p k o", o=1).to_broadcast([36, NT, 128])
    T1v = T1[r36, :].rearrange("p (k t) -> p k t", t=128)
    T3v = T3[r36, :].rearrange("p (k t) -> p k t", t=128)
    S.op("dve", TT(T3v, T3v, MPb, ALU.subtract), reads=["T3g", "MPt"], writes=["T3g"])
    S.op("act", ACT(T3[r36, :], T3[r36, :], AF.Exp), reads=["T3g"], writes=["T3g"])
    S.op("dve", TT(T1v, T1v, MPb, ALU.subtract), reads=["T1g", "MPt"], writes=["T1g"])
    S.op("act", ACT(T1[r36, :], T1[r36, :], AF.Exp), reads=["T1g"], writes=["T1g"])
    if dstop("D4"):
        return nc
    for (src, skey, dst, dkey) in ((T1, "T1g", TOKU, "TOKU"), (T3, "T3g", TOKC, "TOKC")):
        for k0 in range(0, NT, 14):
            n = min(14, NT - k0)
            pt, pk = psrot.next()
            S.op("pe", [TR(pt[:, j * 36:(j + 1) * 36], src[0:36, (k0 + j) * 128:(k0 + j + 1) * 128], identf[0:36, 0:36]) for j in range(n)],
                 reads=[skey, "identf"], writes=[pk])
            S.op("dve", CP(dst[:, k0:k0 + n, :], pt[:, 0:n * 36].rearrange("p (a b) -> p a b", b=36)), reads=[pk], writes=[dkey])
    pt, pk = psrot.next()
    S.op("pe", [MM(pt[:, j * NT:(j + 1) * NT], selt[0:36, j, :], DECt[0:36, :]) for j in range(8)],
         reads=["selt", "DECt"], writes=[pk])
    S.op("dve", CP(DECB[:], pt[:, 0:8 * NT].rearrange("p (a b) -> p a b", b=NT)), reads=[pk], writes=["DECB"])
    if debug:
        S.dma("sp", "S_dbg", dbg["TOKU"][:, :, :], TOKU[:], reads=["TOKU"])
        S.dma("sp", "S_dbg", dbg["TOKC"][:, :, :], TOKC[:], reads=["TOKC"])
        S.dma("sp", "S_dbg", dbg["DECB"][:, :, :], DECB[:], reads=["DECB"])
    S.barrier()
    if stop_after == "D":
        S.emit()
        return nc

    S.op("pool", [MS(UA[:, 0:1], 0.0), MS(UA[:, S_ + 1:S_ + 2], 0.0), MS(UB[:, 0:1], 0.0), MS(UB[:, S_ + 1:S_ + 2], 0.0)],
         writes=[("U", 0), ("U", 1)])

    def load_w(g):
        wt, wk = wrot.next()
        S.dma("pool", "L_Wb%d" % wk[1], wt[:], win_d[g], writes=[wk])
        return wt, wk

    pend_tail = []

    def flush_tail():
        while pend_tail:
            inf = pend_tail.pop(0)
            ob, ok = orot.next()
            S.op("act", ACT(ob[:], ACC[:], AF.Silu), reads=["ACC"], writes=[ok])
            S.dma("sp", "S_obuf%d" % ok[1], inf["dst"], ob[:], reads=[ok], writes=[inf["dkey"]])

    def fm_group(g, kind, sub_info):
        wt, wk = load_w(g)
        for sub in range(4):
            info = sub_info(sub)
            if kind == "conv":
                U, uk = urot.next()
            else:
                ob, ok = orot.next()
            for tb in range(NBLK):
                pt, pk = psrot.next()
                S.op("pe", [MM(pt[:, :], wt[:, kc, sub * 128:(sub + 1) * 128], hT[:, kc, tb * 512:(tb + 1) * 512],
                               start=(kc == 0), stop=(kc == 7)) for kc in range(8)],
                     reads=[wk] + hkeys(tb), writes=[pk])
                if kind == "conv":
                    S.op("act", ACT(U[:, 1 + tb * 512:1 + (tb + 1) * 512], pt[:, :], AF.Copy), reads=[pk], writes=[uk])
                elif kind == "silu":
                    S.op("act", ACT(ob[:, tb * 512:(tb + 1) * 512], pt[:, :], AF.Silu), reads=[pk], writes=[ok])
                elif kind == "sigb":
                    S.op("act", ACT(ob[:, tb * 512:(tb + 1) * 512], pt[:, :], AF.Sigmoid, bias=info["bias"]), reads=[pk, "cols"], writes=[ok])
                elif kind == "copy":
                    evtog[0] ^= 1
                    if evtog[0]:
                        S.op("dve", TS(ob[:, tb * 512:(tb + 1) * 512], pt[:, :], info["scale"], None, ALU.mult), reads=[pk], writes=[ok])
                    else:
                        S.op("act", ACT(ob[:, tb * 512:(tb + 1) * 512], pt[:, :], AF.Copy, scale=info["scale"]), reads=[pk], writes=[ok])
            if kind == "conv":
                flush_tail()
                cg = info["cg"]
                w0 = cols[:, cg * 3 + 0:cg * 3 + 1]
                w1 = cols[:, cg * 3 + 1:cg * 3 + 2]
                w2 = cols[:, cg * 3 + 2:cg * 3 + 3]
                cb = cols[:, 48 + cg:49 + cg]
                S.op("dve", TS(ACC[:], U[:, 1:S_ + 1], w1, cb, ALU.mult, ALU.add), reads=[uk, "cols"], writes=["ACC"])
                S.op("dve", STT(ACC[:], U[:, 0:S_], w0, ACC[:], ALU.mult, ALU.add), reads=[uk, "cols", "ACC"], writes=["ACC"])
                S.op("dve", STT(ACC[:], U[:, 2:S_ + 2], w2, ACC[:], ALU.mult, ALU.add), reads=[uk, "cols", "ACC"], writes=["ACC"])
                pend_tail.append(info)
            else:
                S.dma("sp", "S_obuf%d" % ok[1], info["dst"], ob[:], reads=[ok], writes=[info["dkey"]])

    def tm_group(g, kind, col0):
        wt, wk = load_w(g)
        for t4 in range(NT // 4):
            st, sk = tmrot.next()
            if kind == "va":
                sv = st[:, 0:4 * 2 * 258].rearrange("p (t h c) -> p t h c", t=4, h=2)
                S.op("pool", [MS(sv[:, :, :, 256:257], 1.0), MS(sv[:, :, :, 257:258], 0.0)], writes=[sk])
            elif kind == "vb":
                sv = st[:, 0:4 * 8 * 66].rearrange("p (t h c) -> p t h c", t=4, h=8)
                S.op("pool", [MS(sv[:, :, :, 64:65], 1.0), MS(sv[:, :, :, 65:66], 0.0)], writes=[sk])
            else:
                sv = st[:, 0:2048].rearrange("p (t c) -> p t c", t=4)
            for j in range(4):
                tt = t4 * 4 + j
                pt, pk = psrot.next()
                S.op("pe", [MM(pt[:, :], hT[:, kc, tt * 128:(tt + 1) * 128], wt[:, kc, :], start=(kc == 0), stop=(kc == 7)) for kc in range(8)],
                     reads=[wk, ("hT", tt)], writes=[pk])
                if kind == "va":
                    S.op("dve", CP(sv[:, j, :, 0:256], pt[:, :].rearrange("p (h c) -> p h c", h=2)), reads=[pk], writes=[sk])
                elif kind == "vb":
                    S.op("dve", CP(sv[:, j, :, 0:64], pt[:, :].rearrange("p (h c) -> p h c", h=8)), reads=[pk], writes=[sk])
                else:
                    S.op("act", ACT(sv[:, j, :], pt[:, :], AF.Sigmoid), reads=[pk], writes=[sk])
            tsl = slice(t4 * 4, t4 * 4 + 4)
            if kind == "va":
                hd0 = col0
                dst = VA[tsl, :, hd0 * 258:(hd0 + 2) * 258].rearrange("t p c -> p t c")
                S.dma("sp", "S_tm%d" % sk[1], dst, st[:, 0:4 * 516].rearrange("p (t c) -> p t c", t=4), reads=[sk], writes=[("VA", t4, hd0)])
            elif kind == "vb":
                dst = VBA[tsl, :, :].rearrange("t p c -> p t c")
                S.dma("sp", "S_tm%d" % sk[1], dst, st[:, 0:4 * 528].rearrange("p (t c) -> p t c", t=4), reads=[sk], writes=[("VBA", t4)])
            else:
                dst = OG[t4 * 512:(t4 + 1) * 512, col0:col0 + 512].rearrange("(t p) c -> p t c", p=128)
                S.dma("sp", "S_tm%d" % sk[1], dst, sv, reads=[sk], writes=[("OG", t4, col0)])

    for g in (0, 1):
        fm_group(g, "conv", lambda sub, g=g: dict(cg=g * 4 + sub, dst=QT[(g * 4 + sub) // 2, :, (g * 4 + sub) % 2, :],
                                                 dkey=("QT", g * 4 + sub)))
    for g in (2, 3):
        fm_group(g, "conv", lambda sub, g=g: dict(cg=8 + (g - 2) * 4 + sub, dst=KT[((g - 2) * 4 + sub) // 2, :, ((g - 2) * 4 + sub) % 2, :],
                                                 dkey=("KT", (g - 2) * 4 + sub)))
    flush_tail()
    for g in (8, 9):
        fm_group(g, "silu", lambda sub, g=g: dict(dst=ZAT[(g - 8) * 4 + sub], dkey=("ZAT", (g - 8) * 4 + sub)))
    fm_group(13, "silu", lambda sub: dict(dst=ZBT[sub], dkey=("ZBT", sub)))
    fm_group(10, "copy", lambda sub: dict(scale=0.125, dst=QBT[sub], dkey=("QBT", sub)))
    fm_group(11, "copy", lambda sub: dict(scale=1.0, dst=KBT[sub], dkey=("KBT", sub)))
    tm_group(4, "va", 0)
    tm_group(5, "va", 2)
    tm_group(12, "vb", 0)
    tm_group(6, "og", 0)
    tm_group(7, "og", 512)
    for g in (14, 15, 16, 17):
        fm_group(g, "sigb", lambda sub, g=g: dict(bias=cols[:, 64 + (g - 14) * 4 + sub:65 + (g - 14) * 4 + sub],
                                                 dst=GMT[(g - 14) * 4 + sub], dkey=("GMT", (g - 14) * 4 + sub)))
    S.barrier()
    if stop_after == "C":
        S.emit()
        return nc

    if build_mlstm(nc, S, locals()):
        return nc
    if stop_after == "M":
        S.emit()
        return nc
    build_na(nc, S, locals())
    if stop_after == "N":
        S.emit()
        return nc
    build_final(nc, S, locals())
    S.emit()
    return nc


def build_mlstm(nc, S, E):
    T, cur, ps, psb = E["T"], E["cur"], E["ps"], E["psb"]
    identb, maskT, cols, TOKU, TOKC, DECB = E["identb"], E["maskT"], E["cols"], E["TOKU"], E["TOKC"], E["DECB"]
    QT, KT, VA, OG, ZAT, YAT = E["QT"], E["KT"], E["VA"], E["OG"], E["ZAT"], E["YAT"]
    dbg, debug = E["dbg"], E["debug"]
    cur[0] = E["REGION0"]
    qT = T("m_qT", [128, 2, S_], BF16)
    kT = T("m_kT", [128, 2, S_], BF16)
    ktok = T("m_ktok", [128, NT, 256], BF16)
    Vaug = T("m_Vaug", [128, NT, 258], BF16)
    Hacc = T("m_Hacc", [128, NT, 256], F32)
    ZATh = T("m_ZATh", [128, 2, S_], BF16)
    yaT = T("m_yaT", [128, 2, S_], BF16)
    OGt = Rot("OGt", [T("m_OGt%d" % i, [128, 4, 256], BF16) for i in range(3)])
    UVr = [Rot("UV%d" % d, [T("m_UV%d_%d" % (d, i), [128, 258], BF16) for i in range(4)]) for d in range(2)]
    Smr = [Rot("Sm%d" % d, [T("m_Sm%d_%d" % (d, i), [128, 128], BF16) for i in range(3)]) for d in range(2)]
    Zs = [T("m_Z%d" % d, [128, 2, 258], F32) for d in range(2)]
    Cbr = [Rot("Cb%d" % d, [T("m_Cb%d_%d" % (d, i), [128, 2, 258], BF16) for i in range(3)]) for d in range(2)]
    Htr = Rot("Htmp", [T("m_Htmp%d" % i, [128, 256], F32) for i in range(3)])
    hgr = Rot("hg", [T("m_hg%d" % i, [128, 256], F32) for i in range(4)])
    ytr = Rot("yatok", [T("m_yatok%d" % i, [128, 256], BF16) for i in range(4)])
    junk = T("m_junk", [128, 256], BF16)
    rcs = T("m_rcs", [128, 8, 4], F32)
    pcs = T("m_pcs", [128, 8, 4], F32)
    rci = [0]
    pci = [0]
    dcp = [((ps[4], ps[5]), [("ps", 4), ("ps", 5)]), ((ps[6], ps[7]), [("ps", 6), ("ps", 7)])]

    def loads(hd):
        S.dma("sp", "L_mq", qT[:], QT[hd], writes=["qT"])
        S.dma("sp", "L_mk", kT[:], KT[hd], writes=["kT"])
        for j0 in range(0, NT, 8):
            S.dma("sp", "L_mv", Vaug[:, j0:j0 + 8, :], VA[j0:j0 + 8, :, hd * 258:(hd + 1) * 258].rearrange("t p c -> p t c"),
                  writes=[("Vaug", j) for j in range(j0, j0 + 8)])

    loads(0)
    for hd in range(4):
        S.dma("sp", "L_mz", ZATh[:], ZAT[2 * hd:2 * hd + 2].rearrange("g p t -> p g t"), writes=["ZATh"])
        for k4 in range(8):
            kb = 6 + (k4 % 2)
            S.op("pe", [TR(psb(kb)[:, (kk * 2 + c) * 128:(kk * 2 + c + 1) * 128], kT[:, c, (k4 * 4 + kk) * 128:(k4 * 4 + kk + 1) * 128], identb[:])
                        for kk in range(4) for c in range(2)], reads=["kT", "identb"], writes=[("ps", kb)])
            if k4 % 2 == 0:
                S.op("act", ACT(ktok[:, k4 * 4:(k4 + 1) * 4, :].rearrange("p a b -> p (a b)"), psb(kb)[:, 0:1024], AF.Copy, scale=1.0 / 16),
                     reads=[("ps", kb)], writes=[("ktok", k4)])
            else:
                S.op("dve", TS(ktok[:, k4 * 4:(k4 + 1) * 4, :].rearrange("p a b -> p (a b)"), psb(kb)[:, 0:1024], 1.0 / 16, None, ALU.mult),
                     reads=[("ps", kb)], writes=[("ktok", k4)])
        if E["stop_after"] == "M0":
            S.emit()
            return True

        ctx = {}
        chain = [dict(kprev=None) for _ in range(2)]

        def kof(d, i):
            return i if d == 0 else NT - 1 - i

        def opUV(d, i):
            k = kof(d, i)
            ucol = TOKU[:, k, d * 32 + hd:d * 32 + hd + 1]
            UVb, uvk = UVr[d].next()
            S.op("dve", TS(UVb[:], Vaug[:, k, :], ucol, None, ALU.mult), reads=[("Vaug", k), "TOKU"], writes=[uvk])
            ctx[(d, i)] = dict(k=k, ch=slice(k * 128, (k + 1) * 128), UVb=UVb, uvk=uvk, cb=None, cbk=None)

        def opST(d, i):
            c_ = ctx[(d, i)]
            stp = ps[d][:, 0:128]
            S.op("pe", [MM(stp, kT[:, c, c_["ch"]], qT[:, c, c_["ch"]], start=(c == 0), stop=(c == 1)) for c in range(2)],
                 reads=["kT", "qT"], writes=[("ps", d)])

        def opMASK(d, i):
            c_ = ctx[(d, i)]
            Sm, smk = Smr[d].next()
            S.op("dve", TT(Sm[:], ps[d][:, 0:128], maskT[:, d, :], ALU.mult), reads=[("ps", d), "maskT"], writes=[smk])
            c_["Sm"], c_["smk"] = Sm, smk

        def opDC(d, i):
            c_ = ctx[(d, i)]
            (b0, b1), dks = dcp[d]
            k = c_["k"]
            S.op("pe", [MM(b0[:, 0:258], ktok[:, k, 0:128], c_["UVb"][:]), MM(b1[:, 0:258], ktok[:, k, 128:256], c_["UVb"][:])],
                 reads=[("ktok", k // 4), c_["uvk"]], writes=dks)

        def opZ(d, i):
            c_ = ctx[(d, i)]
            (b0, b1), dks = dcp[d]
            series = d * 4 + hd
            k = c_["k"]
            Z = Zs[d]
            zk = ("Z", d)
            if i == 0:
                S.op("dve", [CP(Z[:, 0, :], b0[:, 0:258]), CP(Z[:, 1, :], b1[:, 0:258])], reads=dks, writes=[zk])
            else:
                kp = chain[d]["kprev"]
                dprev = DECB[:, series, kp:kp + 1]
                S.op("dve", [STT(Z[:, 0, :], Z[:, 0, :], dprev, b0[:, 0:258], ALU.mult, ALU.add),
                             STT(Z[:, 1, :], Z[:, 1, :], dprev, b1[:, 0:258], ALU.mult, ALU.add)],
                     reads=dks + [zk, "DECB"], writes=[zk])
            cbn, cbk = Cbr[d].next()
            dcur = DECB[:, series, k:k + 1]
            S.op("act", ACT(cbn[:].rearrange("p a b -> p (a b)"), Z[:].rearrange("p a b -> p (a b)"), AF.Copy, scale=dcur), reads=[zk, "DECB"], writes=[cbk])
            c_["cb"], c_["cbk"] = cbn, cbk
            chain[d]["kprev"] = k

        def opNP(d, i):
            c_ = ctx[(d, i)]
            first = (i == 0)
            npb, npk = ps[2 + d], ("ps", 2 + d)
            npa = npb[:, 0:258]
            specs = [MM(npa, c_["Sm"][:], c_["UVb"][:], start=True, stop=first)]
            rd = [c_["smk"], c_["uvk"]]
            if not first:
                pv = ctx[(d, i - 1)]
                specs += [MM(npa, qT[:, c, c_["ch"]], pv["cb"][:, c, :], start=False, stop=(c == 1)) for c in range(2)]
                rd += ["qT", pv["cbk"]]
            S.op("pe", specs, reads=rd, writes=[npk])

        def opOUT(d, i):
            c_ = ctx[(d, i)]
            k = c_["k"]
            npb, npk = ps[2 + d], ("ps", 2 + d)
            j = rci[0] % 8
            rci[0] += 1
            rc = rcs[:, j, :]
            rck = ("rc", j)
            den = npb[:, 256:257]
            ccol = TOKC[:, k, d * 32 + hd:d * 32 + hd + 1]
            S.op("dve", TT(rc[:, 0:1], den, ccol, ALU.max), reads=[npk, "TOKC"], writes=[rck])
            S.op("dve", STT(rc[:, 1:2], den, -1.0, rc[:, 0:1], ALU.mult, ALU.max), reads=[npk, rck], writes=[rck])
            S.op("dve", ("reciprocal", dict(out=rc[:, 2:3], in_=rc[:, 1:2])), reads=[rck], writes=[rck])
            if i < NT // 2:
                S.op("act", ACT(Hacc[:, k, :], npb[:, 0:256], AF.Copy, scale=rc[:, 2:3]), reads=[npk, rck], writes=[("Hacc", k)])
            else:
                ht, htk = Htr.next()
                S.op("act", ACT(ht[:], npb[:, 0:256], AF.Copy, scale=rc[:, 2:3]), reads=[npk, rck], writes=[htk])
                S.op("pool", TT(Hacc[:, k, :], Hacc[:, k, :], ht[:], ALU.add), reads=[htk, ("Hacc", k)], writes=[("Hacc", k)])
            if i >= 1:
                ctx.pop((d, i - 1))

        for d in range(2):
            opUV(d, 0)
        for d in range(2):
            opUV(d, 1)
        for d in range(2):
            opST(d, 0)
        for d in range(2):
            opMASK(d, 0)
        for d in range(2):
            opDC(d, 0)
        for d in range(2):
            opZ(d, 0)
        for i in range(NT):
            nx = i + 1
            if i + 2 < NT:
                opUV(0, i + 2)
                opUV(1, i + 2)
            if nx < NT:
                opST(0, nx)
                opST(1, nx)
                opMASK(0, nx)
                opMASK(1, nx)
                if nx < NT - 1:
                    opDC(0, nx)
                    opDC(1, nx)
            opNP(0, i)
            opNP(1, i)
            if nx < NT - 1:
                opZ(0, nx)
            opOUT(0, i)
            if nx < NT - 1:
                opZ(1, nx)
            opOUT(1, i)

        if debug:
            S.dma("sp", "S_dbg", dbg["Hacc"][hd], Hacc[:], reads=[("Hacc", k) for k in range(NT)])
        if hd + 1 < 4:
            loads(hd + 1)

        pctx = {}

        def P1(k):
            k4, j = k // 4, k % 4
            if j == 0:
                ogt, ogk = OGt.next()
                S.dma("pool", "L_og%d" % ogk[1], ogt[:], OG[k4 * 512:(k4 + 1) * 512, hd * 256:(hd + 1) * 256].rearrange("(t p) c -> p t c", p=128),
                      writes=[ogk])
                pctx["og"] = (ogt, ogk)
            ogt, ogk = pctx["og"]
            hg, hgk = hgr.next()
            S.op("dve", TT(hg[:], Hacc[:, k, :], ogt[:, j, :], ALU.mult), reads=[("Hacc", k), ogk], writes=[hgk])
            q = pci[0] % 8
            pci[0] += 1
            pc = pcs[:, q, :]
            pck = ("pc", q)
            S.op("act", ACT(junk[:], hg[:], AF.Square, accum_out=pc[:, 0:1]), reads=[hgk], writes=["m_junk", pck])
            S.op("act", ACT(pc[:, 1:2], pc[:, 0:1], AF.Sqrt, scale=1.0 / DH, bias=EPS), reads=[pck], writes=[pck])
            pctx[k] = (hg, hgk, pc, pck)

        def P2(k):
            k4, j = k // 4, k % 4
            hg, hgk, pc, pck = pctx.pop(k)
            S.op("dve", ("reciprocal", dict(out=pc[:, 2:3], in_=pc[:, 1:2])), reads=[pck], writes=[pck])
            yt, ytk = ytr.next()
            S.op("dve", TS(yt[:], hg[:], pc[:, 2:3], None, ALU.mult), reads=[hgk, pck], writes=[ytk])
            S.op("pe", [TR(psb(7)[:, c * 512 + j * 128:c * 512 + (j + 1) * 128], yt[:, c * 128:(c + 1) * 128], identb[:]) for c in range(2)],
                 reads=[ytk, "identb"], writes=[("ps", 7)])
            if j == 3:
                for c in range(2):
                    S.op("dve", STT(yaT[:, c, k4 * 512:(k4 + 1) * 512], psb(7)[:, c * 512:(c + 1) * 512], cols[:, 80 + hd * 2 + c:81 + hd * 2 + c],
                                    ZATh[:, c, k4 * 512:(k4 + 1) * 512], ALU.mult, ALU.mult),
                         reads=[("ps", 7), "cols", "ZATh"], writes=[("yaT", c)])

        for k in range(NT + 2):
            if k < NT:
                P1(k)
            if k >= 2:
                P2(k - 2)
        for c in range(2):
            S.dma("pool", "S_yaT", YAT[hd * 2 + c], yaT[:, c, :], reads=[("yaT", c)])
        if E["stop_after"] == "M2":
            S.emit()
            return True
    S.barrier()


def build_na(nc, S, E):
    T, cur, ps, psb = E["T"], E["cur"], E["ps"], E["psb"]
    identb = E["identb"]
    QBT, KBT, VBA, ZBT, YBT, bt2_d = E["QBT"], E["KBT"], E["VBA"], E["ZBT"], E["YBT"], E["bt2_d"]
    cur[0] = E["REGION0"]
    bt2b = T("n_bt2", [128, 4, 14, 2, 64], BF16)
    QBD = T("n_QBD", [128, 2, 2, S_], BF16)
    kbT = T("n_kbT", [128, 2, S_], BF16)
    ZBh = T("n_ZBh", [128, 2, S_], BF16)
    ybT = T("n_ybT", [128, 2, S_], BF16)
    VE = T("n_VE", [128, NT, 4, 66], BF16)
    VO = T("n_VO", [128, NT - 1, 4, 66], BF16)
    PTr = Rot("PT", [T("n_PT%d" % i, [128, 1024], BF16) for i in range(2)])
    otr = Rot("otok", [T("n_otok%d" % i, [64, 4, 64], BF16) for i in range(3)])
    recs = T("n_rec", [64, 4, 4], F32)
    str_ = Rot("psS", [(ps[0], ps[1]), (ps[2], ps[3])])
    pvr = Rot("psPV", [ps[4], ps[5]])
    trr = Rot("psT", [6, 7])
    ri = [0]
    NA_END = cur[0]
    wpab = T("f_wpa", [128, 8, 1024], BF16)
    wpbb = T("f_wpb", [128, 4, 1024], BF16)
    woutb = T("f_wout", [128, 8, 1024], BF16)
    wsr = Rot("f_wst", [T("f_wst%d" % i, [128, 2, 1024], F32) for i in range(1)])
    S.shared_fw = (NA_END, wpab, wpbb, woutb)
    gate_bc = E["gate_bc"]
    S.dma("pool", "L_bt2", bt2b[:], bt2_d[:, :, :, :, :], writes=["bt2b"])
    S.op("pool", [MS(QBD[0:64, :, 1, :], 0.0), MS(QBD[64:128, :, 0, :], 0.0)], writes=["QBDz"])
    for half in range(2):
        S.dma("sp", "L_nq", QBD[0:64, :, 0, :], QBT[2 * half:2 * half + 2, 0:64, :].rearrange("g p t -> p g t"), reads=["QBDz"], writes=["qbT"])
        S.dma("sp", "L_nq", QBD[64:128, :, 1, :], QBT[2 * half:2 * half + 2, 64:128, :].rearrange("g p t -> p g t"), reads=["QBDz"], writes=["qbT"])
        S.dma("sp", "L_nk", kbT[:], KBT[2 * half:2 * half + 2].rearrange("g p t -> p g t"), writes=["kbT"])
        S.dma("sp", "L_nz", ZBh[:], ZBT[2 * half:2 * half + 2].rearrange("g p t -> p g t"), writes=["ZBh"])
        c0 = half * 4 * 66
        for j0 in range(0, NT, 8):
            S.dma("sp", "L_nve", VE[:, j0:j0 + 8, :, :].rearrange("p t h c -> p t (h c)"),
                  VBA[j0:j0 + 8, :, c0:c0 + 264].rearrange("t p c -> p t c"), writes=["VE"])
        for j0 in range(0, NT - 1, 8):
            j1 = min(j0 + 8, NT - 1)
            S.dma("sp", "L_nvo", VO[0:64, j0:j1, :, :].rearrange("p t h c -> p t (h c)"),
                  VBA[j0:j1, 64:128, c0:c0 + 264].rearrange("t p c -> p t c"), writes=["VO"])
            S.dma("sp", "L_nvo", VO[64:128, j0:j1, :, :].rearrange("p t h c -> p t (h c)"),
                  VBA[j0 + 1:j1 + 1, 0:64, c0:c0 + 264].rearrange("t p c -> p t c"), writes=["VO"])
        def n_stage1(r):
            rs = min(max(r - 4, 0), 56)
            j0b = rs - r + 7
            qs = slice(r * 64, (r + 1) * 64)
            pair, sk = str_.next()
            specs = []
            for gi in range(2):
                bank = pair[gi]
                hp = half * 2 + gi
                specs.append(MM(bank[:, 0:512], identb[:], bt2b[:, hp, j0b:j0b + 7:2, :, :].rearrange("p i j q -> p i (j q)"), start=True, stop=False))
                for i in range(4):
                    tok = rs * 64 + i * 128
                    specs.append(MM(bank[:, i * 128:(i + 1) * 128], kbT[:, gi, tok:tok + 128], QBD[:, gi, :, qs], start=False, stop=(i == 3)))
            S.op("pe", specs, reads=["identb", "bt2b", "kbT", "qbT"], writes=[sk])
            PT, ptk = PTr.next()
            S.op("act", [ACT(PT[:, 0:512], pair[0][:, :], AF.Exp), ACT(PT[:, 512:1024], pair[1][:, :], AF.Exp)], reads=[sk], writes=[ptk])
            return dict(r=r, rs=rs, qs=qs, PT=PT, ptk=ptk)

        def n_stage2(c):
            rs, PT, ptk = c["rs"], c["PT"], c["ptk"]
            if rs % 2 == 0:
                Vx, vkey, tbase = VE, "VE", rs // 2
            else:
                Vx, vkey, tbase = VO, "VO", (rs - 1) // 2
            ob, ok = pvr.next()
            specs = []
            for hh in range(4):
                for i in range(4):
                    specs.append(MM(ob[0:64, hh * 66:(hh + 1) * 66], PT[:, (hh // 2) * 512 + i * 128 + (hh % 2) * 64:(hh // 2) * 512 + i * 128 + (hh % 2) * 64 + 64], Vx[:, tbase + i, hh, :],
                                    start=(i == 0), stop=(i == 3)))
            S.op("pe", specs, reads=[ptk, vkey], writes=[ok])
            q = ri[0] % 4
            ri[0] += 1
            rec = recs[:, q, :]
            rk = ("rec", q)
            ov = ob[0:64, 0:264].rearrange("p (h c) -> p h c", c=66)
            S.op("dve", ("reciprocal", dict(out=rec, in_=ov[:, :, 64])), reads=[ok], writes=[rk])
            ot, otk = otr.next()
            S.op("dve", TT(ot[:], ov[:, :, 0:64], rec.rearrange("p (h o) -> p h o", o=1).to_broadcast([64, 4, 64]), ALU.mult),
                 reads=[ok, rk], writes=[otk])
            c["ot"], c["otk"] = ot, otk

        def n_stage3(c):
            ot, otk, qs = c["ot"], c["otk"], c["qs"]
            tb_, tk = trr.next()
            otf = ot[:].rearrange("p h c -> p (h c)")
            S.op("pe", [TR(psb(tb_)[:, g * 64:(g + 1) * 64], otf[:, g * 128:(g + 1) * 128], identb[0:64, 0:64]) for g in range(2)],
                 reads=[otk, "identb"], writes=[tk])
            S.op("dve", TT(ybT[:, :, qs], psb(tb_)[:, 0:128].rearrange("p (g q) -> p g q", g=2), ZBh[:, :, qs], ALU.mult),
                 reads=[tk, "ZBh"], writes=["ybT"])

        nctx = {}
        if half == 0:
            S.dma("pool", "L_fwa", wpab[:], E["wpa_d"][:, :, :], writes=["wpab"])
            S.dma("pool", "L_fwb", wpbb[:], E["wpb_d"][:, :, :], writes=["wpbb"])
            for j in range(4):
                wt, wk = wsr.next()
                S.dma("sp", "L_fws%d" % wk[1], wt[:], E["wout_d"][:, 2 * j:2 * j + 2, :], writes=[wk])
                S.op("dve", TT(woutb[:, 2 * j:2 * j + 2, :], wt[:], gate_bc[:].rearrange("p (o n) -> p o n", o=1).to_broadcast([128, 2, 1024]), ALU.mult),
                     reads=[wk, "gate_bc"], writes=["woutb"])
        for r in range(64 + 2):
            if r < 64:
                nctx[r] = n_stage1(r)
            if 0 <= r - 1 < 64:
                n_stage2(nctx[r - 1])
            if 0 <= r - 2 < 64:
                n_stage3(nctx.pop(r - 2))
        for g in range(2):
            S.dma("sp", "S_ybT", YBT[2 * half + g], ybT[:, g, :], reads=["ybT"])
    S.barrier()


def build_final(nc, S, E):
    T, cur, ps = E["T"], E["cur"], E["ps"]
    gate_bc, fg_bc, smallc = E["gate_bc"], E["fg_bc"], E["smallc"]
    YAT, YBT, GMT, x_d, y_d = E["YAT"], E["YBT"], E["GMT"], E["x_d"], E["y_d"]
    wpa_d, wpb_d, wout_d = E["wpa_d"], E["wpb_d"], E["wout_d"]
    cur[0] = E["REGION0"]
    NA_END, wpab, wpbb, woutb = S.shared_fw
    yar = Rot("f_ya", [T("f_ya%d" % i, [128, 8, 512], BF16) for i in range(2)])
    ybr = Rot("f_yb", [T("f_yb%d" % i, [128, 4, 512], BF16) for i in range(2)])
    gmr = Rot("f_gm", [T("f_gm%d" % i, [128, 16, 512], BF16) for i in range(2)])
    t1r = Rot("f_t1", [T("f_t1%d" % i, [128, 512], F32) for i in range(2)])
    t2r = Rot("f_t2", [T("f_t2%d" % i, [128, 512], F32) for i in range(2)])
    mgr = Rot("f_mg", [T("f_mg%d" % i, [128, 8, 512], BF16) for i in range(2)])
    xr = Rot("f_xt", [T("f_xt%d" % i, [128, 1024], F32) for i in range(5)])
    x2r = Rot("f_x2", [T("f_x2%d" % i, [128, 1024], F32) for i in range(2)])
    otr = Rot("f_ot", [T("f_ot%d" % i, [128, 1024], F32) for i in range(2)])
    junk = T("f_junk", [128, 1024], BF16)
    par = Rot("psP", [ps[0], ps[1], ps[2], ps[3]])
    outr = Rot("psO", [(ps[4], ps[5]), (ps[6], ps[7])])
    assert cur[0] <= NA_END, (cur[0], NA_END)
    def f_loads(tb):
        ts = slice(tb * 512, (tb + 1) * 512)
        ya, yak = yar.next()
        yb, ybk = ybr.next()
        gm, gmk = gmr.next()
        S.dma("sp", "L_fya%d" % yak[1], ya[:], YAT[:, :, ts].rearrange("g p t -> p g t"), writes=[yak])
        S.dma("sp", "L_fyb%d" % ybk[1], yb[:], YBT[:, :, ts].rearrange("g p t -> p g t"), writes=[ybk])
        S.dma("sp", "L_fgm%d" % gmk[1], gm[:], GMT[:, :, ts].rearrange("g p t -> p g t"), writes=[gmk])
        return ya, yak, yb, ybk, gm, gmk

    fl = {0: f_loads(0)}
    mgs = {}

    def stageP(tb):
        ya, yak, yb, ybk, gm, gmk = fl.pop(tb)
        if tb + 1 < NBLK:
            fl[tb + 1] = f_loads(tb + 1)
        mg, mgk = mgr.next()
        for fg in range(8):
            fs = slice(fg * 128, (fg + 1) * 128)
            pa, pak = par.next()
            S.op("pe", [MM(pa[:, :], wpab[:, kc, fs], ya[:, kc, :], start=(kc == 0), stop=(kc == 7)) for kc in range(8)],
                 reads=["wpab", yak], writes=[pak])
            pb_, pbk = par.next()
            S.op("pe", [MM(pb_[:, :], wpbb[:, kc, fs], yb[:, kc, :], start=(kc == 0), stop=(kc == 3)) for kc in range(4)],
                 reads=["wpbb", ybk], writes=[pbk])
            t1, t1k = t1r.next()
            t2, t2k = t2r.next()
            S.op("dve", TT(t1[:], pa[:, :], gm[:, fg, :], ALU.mult), reads=[pak, gmk], writes=[t1k])
            S.op("dve", TT(t2[:], pb_[:, :], gm[:, 8 + fg, :], ALU.mult), reads=[pbk, gmk], writes=[t2k])
            S.op("pool", TT(mg[:, fg, :], t1[:], t2[:], ALU.add), reads=[t1k, t2k], writes=[(mgk, fg)])
        mgs[tb] = (mg, mgk)

    def stageO(tb):
        mg, mgk = mgs.pop(tb)
        xtl = []
        for tt in range(4):
            tile = tb * 4 + tt
            xt, xk = xr.next()
            S.dma("sp", "L_fx%d" % xk[1], xt[:], x_d[tile * 128:(tile + 1) * 128, :], writes=[xk])
            xtl.append((xt, xk))
        for tt in range(4):
            tile = tb * 4 + tt
            xt, xk = xtl[tt]
            (o0, o1), ok = outr.next()
            specs = []
            for nh, ob in enumerate((o0, o1)):
                for fg in range(8):
                    specs.append(MM(ob[:, :], mg[:, fg, tt * 128:(tt + 1) * 128], woutb[:, fg, nh * 512:(nh + 1) * 512], start=(fg == 0), stop=(fg == 7)))
            S.op("pe", specs, reads=[(mgk, fg) for fg in range(8)] + ["woutb"], writes=[ok])
            x2, x2k = x2r.next()
            S.op("dve", [TT(x2[:, 0:512], o0[:, :], xt[:, 0:512], ALU.add), TT(x2[:, 512:1024], o1[:, :], xt[:, 512:1024], ALU.add)],
                 reads=[ok, xk], writes=[x2k])
            q = tile % 4
            sc = smallc[:, q * 4:q * 4 + 4]
            sck = ("smallc", q)
            S.op("act", ACT(junk[:], x2[:], AF.Square, accum_out=sc[:, 0:1]), reads=[x2k], writes=["f_junk", sck])
            S.op("act", ACT(sc[:, 1:2], sc[:, 0:1], AF.Sqrt, scale=1.0 / D, bias=EPS), reads=[sck], writes=[sck])
            S.op("dve", ("reciprocal", dict(out=sc[:, 2:3], in_=sc[:, 1:2])), reads=[sck], writes=[sck])
            ot, otk = otr.next()
            S.op("act", ACT(ot[:], x2[:], AF.Copy, scale=sc[:, 2:3]), reads=[x2k, sck], writes=[otk])
            S.op("pool", TT(ot[:], ot[:], fg_bc[:], ALU.mult), reads=[otk, "fg_bc"], writes=[otk])
            S.dma("pool", "S_fo%d" % otk[1], y_d[tile * 128:(tile + 1) * 128, :], ot[:], reads=[otk])

    for tb in range(NBLK + 1):
        if tb < NBLK:
            stageP(tb)
        if tb >= 1:
            stageO(tb - 1)


def _shared_layouts(inp):
    f = np.float32
    w_ada = np.asarray(inp["w_ada"], f)[0]
    w_in = np.asarray(inp["w_in"], f)[0]
    sh = {}
    sh["wada"] = np.ascontiguousarray(w_ada.reshape(8, 128, 3072).transpose(1, 0, 2))
    sh["rows"] = np.ascontiguousarray(np.concatenate(
        [np.asarray(inp["b_ada"], f)[0], np.asarray(inp["norm_gain"], f)[0], np.asarray(inp["final_gain"], f)])[None, :])
    wg = np.zeros((1024, 72), f)
    gc = w_in[:, 5120:5136].reshape(1024, 2, 2, 4)
    wg[:, 0:4] = gc[:, 0, 0]
    wg[:, 32:36] = gc[:, 1, 0]
    wg[:, 36:40] = gc[:, 0, 1]
    wg[:, 68:72] = gc[:, 1, 1]
    sh["wgate"] = np.ascontiguousarray(wg.reshape(8, 128, 72).transpose(1, 0, 2))
    wl = np.concatenate([w_in[:, 0:5120], w_in[:, 5136:9232]], axis=1)
    sh["win"] = np.ascontiguousarray(wl.reshape(8, 128, 18, 512).transpose(2, 1, 0, 3))
    cols = np.zeros((128, 88), f)
    cw = np.asarray(inp["conv_w"], f)[0]
    cb = np.asarray(inp["conv_b"], f)[0]
    cols[:, 0:48] = cw.reshape(3, 16, 128).transpose(2, 1, 0).reshape(128, 48)
    cols[:, 48:64] = cb.reshape(16, 128).T
    cols[:, 64:80] = np.asarray(inp["b_merge"], f)[0].reshape(16, 128).T
    cols[:, 80:88] = np.asarray(inp["mlstm_norm_gain"], f)[0].reshape(8, 128).T
    sh["cols"] = cols
    gb = np.zeros((36, 2), f)
    bi = np.asarray(inp["b_igate"], f)[0]
    bfg = np.asarray(inp["b_fgate"], f)[0]
    gb[0:4, 0] = bi[0]
    gb[32:36, 0] = bi[1]
    gb[0:4, 1] = bfg[0]
    gb[32:36, 1] = bfg[1]
    sh["gb"] = gb
    rpb = np.asarray(inp["rpb"], f)[0]
    kc = np.arange(64)[:, None]
    qc = np.arange(64)[None, :]
    ws = np.clip(qc - 8, 0, 48)
    colok = (kc >= ws) & (kc < ws + 16)
    dcidx = np.clip(kc - qc + 15, 0, 30)
    bt2 = np.full((128, 8, 14, 64), NEG, f)
    for j in range(14):
        for half in range(2):
            dr = j - 7 + half
            tab = np.where(colok[None], rpb[:, dr + 7][:, dcidx], f(NEG))
            bt2[half * 64:(half + 1) * 64, :, j, :] = tab.transpose(1, 0, 2)
    sh["bt2"] = np.ascontiguousarray(bt2.reshape(128, 4, 2, 14, 64).transpose(0, 1, 3, 2, 4))
    sh["wpa"] = np.ascontiguousarray(np.asarray(inp["w_proj_a"], f)[0].reshape(8, 128, 1024).transpose(1, 0, 2))
    sh["wpb"] = np.ascontiguousarray(np.asarray(inp["w_proj_b"], f)[0].reshape(4, 128, 1024).transpose(1, 0, 2))
    sh["wout"] = np.ascontiguousarray(np.asarray(inp["w_out"], f)[0].reshape(8, 128, 1024).transpose(1, 0, 2))
    sh["ident"] = np.eye(128, dtype=f)
    s_i = np.arange(128)[:, None]
    t_i = np.arange(128)[None, :]
    masks = np.zeros((128, 2, 128), f)
    masks[:, 0, :] = np.where(s_i <= t_i, 1.0 / 16, 0.0)
    masks[:, 1, :] = np.where(s_i >= t_i, 1.0 / 16, 0.0)
    sh["masks"] = masks
    sel = np.zeros((36, 8, 128), f)
    for j in range(8):
        sel[(j % 4) + 32 * (j // 4), j, :] = 1.0
    sh["sel"] = sel
    return sh


def make_in_maps(inp):
    sh = _shared_layouts(inp)
    x = np.asarray(inp["x"], np.float32)
    c = np.asarray(inp["c"], np.float32)
    maps = []
    for b in range(8):
        m = dict(sh)
        m["x"] = np.ascontiguousarray(x[b])
        m["c_l"] = np.ascontiguousarray(c[b].reshape(8, 128).T)
        maps.append(m)
    return maps


_NC_CACHE = {}


def kernel(**inputs):
    if "nc" not in _NC_CACHE:
        _NC_CACHE["nc"] = build_program()
    nc = _NC_CACHE["nc"]
    in_maps = make_in_maps(inputs)
    res = run_bass_kernel_spmd(nc, in_maps, core_ids=list(range(8)))
    return np.stack([np.asarray(r["y"], np.float32) for r in res.results], axis=0)
```

```python
import numpy as np
import concourse.bass as bass
import concourse.mybir as mybir
from concourse.bass_utils import run_bass_kernel_spmd

F32 = mybir.dt.float32
BF16 = mybir.dt.bfloat16
ALU = mybir.AluOpType
AF = mybir.ActivationFunctionType

S_ = 4096
D = 1024
NT = 32
NBLK = 8
H = 4
DH = 256
NH = 8
NEG = -30000.0
EPS = 1e-6
ENG_NAMES = ("pe", "act", "dve", "pool", "sp")


class Sched:
    def __init__(self, nc):
        self.nc = nc
        self.ops = {e: [] for e in ENG_NAMES}
        self.count = {}
        self.last_writer = {}
        self.readers = {}
        self.seen = {e: {} for e in ENG_NAMES}
        self.sem_names = ["pe", "act", "dve", "pool"]
        self.is_dma = set()
        self.n_instr = 0

    def _deps(self, reads, writes):
        deps = set()
        for k in reads:
            w = self.last_writer.get(k)
            if w is not None:
                deps.add(w)
        for k in writes:
            w = self.last_writer.get(k)
            if w is not None:
                deps.add(w)
            deps.update(self.readers.get(k, ()))
        return deps

    def _record(self, me, reads, writes):
        for k in reads:
            self.readers.setdefault(k, []).append(me)
        for k in writes:
            self.last_writer[k] = me
            self.readers[k] = []

    def _waits(self, eng, deps):
        need = {}
        for (s, i) in deps:
            if s in self.is_dma:
                i = self.count[s] - 1
            if need.get(s, -1) < i:
                need[s] = i
        waits = []
        for s, i in need.items():
            if self.seen[eng].get(s, -1) >= i:
                continue
            self.seen[eng][s] = i
            waits.append((s, i + 1))
        return waits

    def op(self, eng, specs, reads=(), writes=()):
        if isinstance(specs, tuple):
            specs = [specs]
        waits = self._waits(eng, self._deps(reads, writes))
        idx = self.count.get(eng, 0)
        self.count[eng] = idx + 1
        self.ops[eng].append((specs, waits, (eng, 1)))
        self._record((eng, idx), reads, writes)
        self.n_instr += len(specs)

    def dma(self, queue, stream, out, in_, reads=(), writes=()):
        if stream not in self.is_dma:
            self.is_dma.add(stream)
            self.sem_names.append(stream)
            self.count[stream] = 0
        waits = self._waits(queue, self._deps(reads, writes))
        idx = self.count[stream]
        self.count[stream] = idx + 1
        self.ops[queue].append(([("dma_start", dict(out=out, in_=in_))], waits, (stream, 16)))
        self._record((stream, idx), reads, writes)
        self.n_instr += 1

    def barrier(self):
        allw = [(s, c) for s, c in self.count.items() if c > 0]
        for e in ENG_NAMES:
            waits = []
            for s, c in allw:
                if self.seen[e].get(s, -1) >= c - 1:
                    continue
                self.seen[e][s] = c - 1
                waits.append((s, c))
            if waits:
                self.ops[e].append((None, waits, None))
        self.last_writer = {}
        self.readers = {}

    def emit(self):
        import contextlib
        nc = self.nc
        self.barrier()
        with contextlib.ExitStack() as st:
            sems = {s: st.enter_context(nc.semaphore("s_" + s)) for s in self.sem_names}
            block = st.enter_context(nc.Block())

            def run(engname):
                def body(eng):
                    for specs, waits, inc in self.ops[engname]:
                        for (s, v) in waits:
                            eng.wait_ge(sems[s], v * (16 if s in self.is_dma else 1))
                        if specs is None:
                            continue
                        ins = None
                        for (m, kw) in specs:
                            ins = getattr(eng, m)(**kw)
                        ins.then_inc(sems[inc[0]], inc[1])
                return body

            block.tensor(run("pe"))
            block.scalar(run("act"))
            block.vector(run("dve"))
            block.gpsimd(run("pool"))
            block.sync(run("sp"))


class Rot:
    def __init__(self, name, tiles):
        self.name, self.tiles, self.i = name, tiles, 0

    def next(self):
        j = self.i % len(self.tiles)
        self.i += 1
        return self.tiles[j], (self.name, j)


def MM(out, lhsT, rhs, start=True, stop=True):
    return ("matmul", dict(out=out, lhsT=lhsT, rhs=rhs, start=start, stop=stop))


def TR(out, in_, identity):
    return ("transpose", dict(out=out, in_=in_, identity=identity))


def ACT(out, in_, func, **kw):
    return ("activation", dict(out=out, in_=in_, func=func, **kw))


def TT(out, in0, in1, op):
    return ("tensor_tensor", dict(out=out, in0=in0, in1=in1, op=op))


def TS(out, in0, scalar1, scalar2, op0, op1=None):
    d = dict(out=out, in0=in0, scalar1=scalar1, scalar2=scalar2, op0=op0)
    if op1 is not None:
        d["op1"] = op1
    return ("tensor_scalar", d)


def STT(out, in0, scalar, in1, op0, op1):
    return ("scalar_tensor_tensor", dict(out=out, in0=in0, scalar=scalar, in1=in1, op0=op0, op1=op1))


def CP(out, in_):
    return ("tensor_copy", dict(out=out, in_=in_))


def MS(ap, v):
    return ("memset", dict(ap=ap, constant=v))


def SCAN(out, data0, data1, initial, op0, op1):
    return ("tensor_tensor_scan", dict(out=out, data0=data0, data1=data1, initial=initial, op0=op0, op1=op1))


def build_program(stop_after=None, debug=False):
    nc = bass.Bass("TRN2", target_bir_lowering=False)
    dbg_kind = "ExternalOutput" if debug else "Internal"

    def DIN(name, shape, dt=F32):
        return nc.dram_tensor(name, list(shape), dt, kind="ExternalInput").ap()

    def DSC(name, shape, dt=BF16):
        return nc.dram_tensor(name, list(shape), dt, kind=dbg_kind).ap()

    x_d = DIN("x", [S_, D])
    c_d = DIN("c_l", [128, 8])
    wada_d = DIN("wada", [128, 8, 3072])
    rows_d = DIN("rows", [1, 5120])
    wgate_d = DIN("wgate", [128, 8, 72])
    win_d = DIN("win", [18, 128, 8, 512])
    cols_d = DIN("cols", [128, 88])
    gb_d = DIN("gb", [36, 2])
    bt2_d = DIN("bt2", [128, 4, 14, 2, 64])
    wpa_d = DIN("wpa", [128, 8, 1024])
    wpb_d = DIN("wpb", [128, 4, 1024])
    wout_d = DIN("wout", [128, 8, 1024])
    ident_d = DIN("ident", [128, 128])
    masks_d = DIN("masks", [128, 2, 128])
    sel_d = DIN("sel", [36, 8, 128])
    y_d = nc.dram_tensor("y", [S_, D], F32, kind="ExternalOutput").ap()

    QT = DSC("QT", [4, 128, 2, S_])
    KT = DSC("KT", [4, 128, 2, S_])
    VA = DSC("VA", [NT, 128, 4 * 258])
    OG = DSC("OG", [S_, D])
    ZAT = DSC("ZAT", [8, 128, S_])
    QBT = DSC("QBT", [4, 128, S_])
    KBT = DSC("KBT", [4, 128, S_])
    VBA = DSC("VBA", [NT, 128, 8 * 66])
    ZBT = DSC("ZBT", [4, 128, S_])
    GMT = DSC("GMT", [16, 128, S_])
    YAT = DSC("YAT", [8, 128, S_])
    YBT = DSC("YBT", [4, 128, S_])
    dbg = {}
    if debug:
        dbg["hT"] = nc.dram_tensor("dbg_hT", [128, 8, S_], BF16, kind="ExternalOutput").ap()
        dbg["TOKU"] = nc.dram_tensor("dbg_TOKU", [128, NT, 36], F32, kind="ExternalOutput").ap()
        dbg["TOKC"] = nc.dram_tensor("dbg_TOKC", [128, NT, 36], F32, kind="ExternalOutput").ap()
        dbg["DECB"] = nc.dram_tensor("dbg_DECB", [128, 8, NT], F32, kind="ExternalOutput").ap()
        dbg["Hacc"] = nc.dram_tensor("dbg_Hacc", [4, 128, NT, 256], F32, kind="ExternalOutput").ap()
        for nm in ("T1", "T2", "T3"):
            dbg[nm] = nc.dram_tensor("dbg_" + nm, [36, S_], F32, kind="ExternalOutput").ap()

    def dstop(tag):
        if stop_after != tag:
            return False
        S.dma("sp", "S_dbg", dbg["T1"][:, :], T1[0:36, :], reads=["T1g"])
        S.dma("sp", "S_dbg", dbg["T2"][:, :], T2[0:36, :], reads=["T2g"])
        S.dma("sp", "S_dbg", dbg["T3"][:, :], T3[0:36, :], reads=["T3g"])
        S.emit()
        return True

    SB_LO = 16512
    SB_HI = 229344
    cur = [SB_LO]

    def T(name, shape, dt):
        n = int(np.prod(shape[1:])) * (4 if dt == F32 else 2)
        n = (n + 31) // 32 * 32
        assert cur[0] + n <= SB_HI, (name, cur[0], n)
        t = nc.alloc_sbuf_tensor_at(name, list(shape), dt, offset=cur[0])
        cur[0] += n
        return t

    ps = [nc.alloc_psum_tensor("ps%d" % i, [128, 512], F32) for i in range(8)]

    def psb(i):
        return ps[i][:].bitcast(BF16)

    S = Sched(nc)

    identb = T("identb", [128, 128], BF16)
    identf = T("identf", [128, 128], F32)
    maskT = T("maskT", [128, 2, 128], F32)
    cols = T("cols", [128, 88], F32)
    TOKU = T("TOKU", [128, NT, 36], F32)
    TOKC = T("TOKC", [128, NT, 36], F32)
    DECB = T("DECB", [128, 8, NT], F32)
    gate_bc = T("gate_bc", [128, 1024], F32)
    fg_bc = T("fg_bc", [128, 1024], F32)
    ones_c = T("ones_c", [128, 128], F32)
    smallc = T("smallc", [128, 64], F32)
    REGION0 = cur[0]

    S.dma("sp", "L_const", identf[:], ident_d[:, :], writes=["identf"])
    S.dma("pool", "L_constb", identb[:], ident_d[:, :], writes=["identb"])
    S.dma("sp", "L_const", maskT[:], masks_d[:, :, :], writes=["maskT"])
    S.dma("sp", "L_const", cols[:], cols_d[:, :], writes=["cols"])
    S.op("pool", MS(ones_c[:], 1.0), writes=["ones_c"])

    cur[0] = REGION0
    modrow = T("modrow", [1, 3072], F32)
    rows = T("rows", [1, 5120], F32)
    wst = [T("wada_st%d" % i, [128, 8, 512], F32) for i in range(2)]
    A_END = cur[0]
    cur[0] = REGION0 + 65536
    G1_bc = T("G1_bc", [128, 1024], F32)
    sh_bc = T("sh_bc", [128, 1024], F32)
    c_sb = T("c_sb", [128, 8], F32)
    cond = T("cond", [128, 8], F32)
    G1row = T("G1row", [1, 1024], F32)
    assert A_END <= REGION0 + 65536

    S.dma("sp", "L_const", c_sb[:], c_d[:, :], writes=["c_sb"])
    S.dma("sp", "L_const", rows[:], rows_d[:, :], writes=["rows"])
    S.op("act", ACT(cond[:], c_sb[:], AF.Silu), reads=["c_sb"], writes=["cond"])
    wrot = Rot("wada_st", wst)
    for g in range(6):
        wt, wk = wrot.next()
        S.dma("sp", "L_wada%d" % wk[1], wt[:], wada_d[:, :, g * 512:(g + 1) * 512], writes=[wk])
        S.op("pe", [MM(ps[g % 2][0:1, :], cond[:, kc:kc + 1], wt[:, kc, :], start=(kc == 0), stop=(kc == 7)) for kc in range(8)],
             reads=["cond", wk], writes=[("ps", g % 2)])
        S.op("dve", TT(modrow[0:1, g * 512:(g + 1) * 512], ps[g % 2][0:1, :], rows[0:1, g * 512:(g + 1) * 512], ALU.add),
             reads=[("ps", g % 2), "rows"], writes=["modrow"])
    S.op("dve", STT(G1row[0:1, :], modrow[0:1, 1024:2048], 1.0, rows[0:1, 3072:4096], ALU.add, ALU.mult),
         reads=["modrow", "rows"], writes=["G1row"])
    bc_jobs = [(G1_bc, G1row[0:1, :], "G1row", "G1_bc"), (sh_bc, modrow[0:1, 0:1024], "modrow", "sh_bc"),
               (gate_bc, modrow[0:1, 2048:3072], "modrow", "gate_bc"), (fg_bc, rows[0:1, 4096:5120], "rows", "fg_bc")]
    bi = 0
    for (dst, src, skey, dkey) in bc_jobs:
        for hf in range(2):
            b = 2 + (bi % 2)
            bi += 1
            S.op("pe", MM(ps[b][:, :], ones_c[0:1, 0:128], src[:, hf * 512:(hf + 1) * 512]), reads=[skey, "ones_c"], writes=[("ps", b)])
            S.op("act", ACT(dst[:, hf * 512:(hf + 1) * 512], ps[b][:, :], AF.Copy), reads=[("ps", b)], writes=[dkey])
    S.barrier()

    cur[0] = REGION0
    hT = T("hT", [128, 8, S_], BF16)
    C_START = cur[0]
    assert C_START == REGION0 + 65536
    cur[0] = C_START + 8192 + 2 * 32
    cur[0] = (cur[0] + 4096 + 31) // 32 * 32
    xts = [T("xt%d" % i, [128, 1024], F32) for i in range(4)]
    xns = [T("xn%d" % i, [128, 1024], BF16) for i in range(3)]
    junkb = T("junkb", [128, 1024], BF16)
    xrot = Rot("xt", xts)
    xnrot = Rot("xn", xns)
    def b_stage1(tt):
        xt, xk = xrot.next()
        xn, xnk = xnrot.next()
        sc = smallc[:, (tt % 4) * 4:(tt % 4) * 4 + 4]
        sck = ("smallc", tt % 4)
        S.dma("sp", "L_xt%d" % xk[1], xt[:], x_d[tt * 128:(tt + 1) * 128, :], writes=[xk])
        S.op("act", ACT(junkb[:], xt[:], AF.Square, accum_out=sc[:, 0:1]), reads=[xk], writes=["junkb", sck])
        S.op("act", ACT(sc[:, 1:2], sc[:, 0:1], AF.Sqrt, scale=1.0 / D, bias=EPS), reads=[sck], writes=[sck])
        S.op("dve", ("reciprocal", dict(out=sc[:, 2:3], in_=sc[:, 1:2])), reads=[sck], writes=[sck])
        S.op("dve", STT(xt[:], xt[:], sc[:, 2:3], G1_bc[:], ALU.mult, ALU.mult), reads=[xk, sck, "G1_bc"], writes=[xk])
        S.op("dve" if tt % 3 != 2 else "pool", TT(xn[:], xt[:], sh_bc[:], ALU.add), reads=[xk, "sh_bc"], writes=[xnk])
        return xn, xnk

    def b_stage2(tt, xn, xnk):
        b = 6 + (tt % 2)
        S.op("pe", [TR(psb(b)[:, kc * 128:(kc + 1) * 128], xn[:, kc * 128:(kc + 1) * 128], identb[:]) for kc in range(8)],
             reads=[xnk, "identb"], writes=[("ps", b)])
        S.op("act", ACT(hT[:, :, tt * 128:(tt + 1) * 128], psb(b).rearrange("p (a b) -> p a b", a=8), AF.Copy),
             reads=[("ps", b)], writes=[("hT", tt)])

    bctx = {}
    for tt in range(NT + 1):
        if tt < NT:
            bctx[tt] = b_stage1(tt)
        if tt >= 1:
            b_stage2(tt - 1, *bctx.pop(tt - 1))
    if debug:
        S.dma("sp", "S_dbg", dbg["hT"][:, :, :], hT[:], reads=[("hT", tt) for tt in range(NT)])
    S.barrier()
    if stop_after == "B":
        S.emit()
        return nc

    cur[0] = C_START
    Wb = [T("Wb%d" % i, [128, 8, 512], BF16) for i in range(3)]
    wgb = T("wgb", [128, 8, 72], BF16)
    UA = T("UA", [128, S_ + 2], F32)
    UB = T("UB", [128, S_ + 2], F32)
    ACC = T("ACC", [128, S_], F32)
    obufs = [T("obuf%d" % i, [128, S_], BF16) for i in range(2)]
    tms = [T("tmst%d" % i, [128, 2112], BF16) for i in range(2)]
    gbc = T("gbc", [36, 2], F32)
    MPt = T("MPt", [36, NT], F32)
    MOt = T("MOt", [36, NT], F32)
    DECt = T("DECt", [36, NT], F32)
    selt = T("selt", [36, 8, 128], F32)
    C_END = cur[0]
    T1 = nc.alloc_sbuf_tensor_at("T1g", [128, S_], F32, offset=C_START + 3 * 8192 + 1152)
    T2 = nc.alloc_sbuf_tensor_at("T2g", [128, S_], F32, offset=C_START + 3 * 8192 + 1152 + 16416)
    T3 = ACC
    ONESF = nc.alloc_sbuf_tensor_at("ONESF", [128, S_], F32, offset=C_START + 3 * 8192 + 1152 + 2 * 16416 + 16384)

    wrot = Rot("Wb", Wb)
    orot = Rot("obuf", obufs)
    tmrot = Rot("tmst", tms)
    urot = Rot("U", [UA, UB])
    psrot = Rot("ps", ps[0:6])
    evtog = [0]

    S.dma("sp", "L_const", gbc[:], gb_d[:, :], writes=["gbc"])
    S.dma("sp", "L_const", selt[:], sel_d[:, :, :], writes=["selt"])
    S.dma("pool", "L_wgb", wgb[:], wgate_d[:, :, :], writes=["wgb"])
    allhT = [("hT", tt) for tt in range(NT)]

    def hkeys(tb):
        return [("hT", tb * 4 + j) for j in range(4)]

    for tb in range(NBLK):
        for gi, (Tt, col, tkey) in enumerate(((T1, 0, "T1g"), (T2, 1, "T2g"))):
            pt, pk = psrot.next()
            S.op("pe", [MM(pt[0:36, :], wgb[:, kc, gi * 36:(gi + 1) * 36], hT[:, kc, tb * 512:(tb + 1) * 512],
                           start=(kc == 0), stop=(kc == 7)) for kc in range(8)],
                 reads=["wgb"] + hkeys(tb), writes=[pk])
            S.op("act", ACT(Tt[0:36, tb * 512:(tb + 1) * 512], pt[0:36, :], AF.Identity, bias=gbc[0:36, col:col + 1]),
                 reads=[pk, "gbc"], writes=[tkey])

    if dstop("D0"):
        return nc
    r36 = slice(0, 36)
    fw = slice(0, 4)
    bw = slice(32, 36)
    S.op("act", ACT(T2[r36, :], T2[r36, :], AF.Exp, scale=-1.0), reads=["T2g"], writes=["T2g"])
    S.op("act", ACT(T2[r36, :], T2[r36, :], AF.Ln, bias=1.0), reads=["T2g"], writes=["T2g"])
    S.op("pool", MS(T3[r36, :], 0.0), writes=["T3g"])
    S.op("pool", MS(ONESF[r36, :], 1.0), writes=["ONESF"])
    S.op("pool", [MS(MPt[:], 0.0), MS(MOt[:], 0.0)], writes=["MPt", "MOt"])
    S.op("dve", SCAN(T3[fw, :], ONESF[fw, :], T2[fw, :], 0.0, ALU.mult, ALU.add),
         reads=["T2g", "ONESF"], writes=["T3g"])
    S.op("dve", SCAN(T3[bw, ::-1], ONESF[bw, :], T2[bw, ::-1], 0.0, ALU.mult, ALU.add),
         reads=["T2g", "ONESF"], writes=["T3g"])
    if dstop("D1"):
        return nc
    S.op("dve", TT(T1[r36, :], T1[r36, :], T3[r36, :], ALU.add), reads=["T1g", "T3g"], writes=["T1g"])
    S.op("dve", SCAN(T2[fw, :], ONESF[fw, :], T1[fw, :], 0.0, ALU.mult, ALU.max),
         reads=["T1g", "ONESF"], writes=["T2g"])
    S.op("dve", SCAN(T2[bw, ::-1], ONESF[bw, :], T1[bw, ::-1], 0.0, ALU.mult, ALU.max),
         reads=["T1g", "ONESF"], writes=["T2g"])
    if dstop("D2"):
        return nc
    M3 = T2[:].rearrange("p (k t) -> p k t", t=128)
    S.op("dve", [CP(MPt[fw, 1:NT], M3[fw, 0:NT - 1, 127]), CP(MOt[fw, :], M3[fw, :, 127])],
         reads=["T2g", "MPt", "MOt"], writes=["MPt", "MOt"])
    S.op("dve", [CP(MPt[bw, 0:NT - 1], M3[bw, 1:NT, 0]), CP(MOt[bw, :], M3[bw, :, 0])],
         reads=["T2g", "MPt", "MOt"], writes=["MPt", "MOt"])
    S.op("dve", TT(DECt[:], MPt[:], MOt[:], ALU.subtract), reads=["MPt", "MOt"], writes=["DECt"])
    S.op("act", ACT(DECt[:], DECt[:], AF.Exp), reads=["DECt"], writes=["DECt"])
    if dstop("D3"):
        return nc
    MPb = MPt[:].rearrange("p (k o) -> p k o", o=1).to_broadcast([36, NT, 128])
    T1v = T1[r36, :].rearrange("p (k t) -> p k t", t=128)
    T3v = T3[r36, :].rearrange("p (k t) -> p k t", t=128)
    S.op("dve", TT(T3v, T3v, MPb, ALU.subtract), reads=["T3g", "MPt"], writes=["T3g"])
    S.op("act", ACT(T3[r36, :], T3[r36, :], AF.Exp), reads=["T3g"], writes=["T3g"])
    S.op("dve", TT(T1v, T1v, MPb, ALU.subtract), reads=["T1g", "MPt"], writes=["T1g"])
    S.op("act", ACT(T1[r36, :], T1[r36, :], AF.Exp), reads=["T1g"], writes=["T1g"])
    if dstop("D4"):
        return nc
    for (src, skey, dst, dkey) in ((T1, "T1g", TOKU, "TOKU"), (T3, "T3g", TOKC, "TOKC")):
        for k0 in range(0, NT, 14):
            n = min(14, NT - k0)
            pt, pk = psrot.next()
            S.op("pe", [TR(pt[:, j * 36:(j + 1) * 36], src[0:36, (k0 + j) * 128:(k0 + j + 1) * 128], identf[0:36, 0:36]) for j in range(n)],
                 reads=[skey, "identf"], writes=[pk])
            S.op("dve", CP(dst[:, k0:k0 + n, :], pt[:, 0:n * 36].rearrange("p (a b) -> p a b", b=36)), reads=[pk], writes=[dkey])
    pt, pk = psrot.next()
    S.op("pe", [MM(pt[:, j * NT:(j + 1) * NT], selt[0:36, j, :], DECt[0:36, :]) for j in range(8)],
         reads=["selt", "DECt"], writes=[pk])
    S.op("dve", CP(DECB[:], pt[:, 0:8 * NT].rearrange("p (a b) -> p a b", b=NT)), reads=[pk], writes=["DECB"])
    if debug:
        S.dma("sp", "S_dbg", dbg["TOKU"][:, :, :], TOKU[:], reads=["TOKU"])
        S.dma("sp", "S_dbg", dbg["TOKC"][:, :, :], TOKC[:], reads=["TOKC"])
        S.dma("sp", "S_dbg", dbg["DECB"][:, :, :], DECB[:], reads=["DECB"])
    S.barrier()
    if stop_after == "D":
        S.emit()
        return nc

    S.op("pool", [MS(UA[:, 0:1], 0.0), MS(UA[:, S_ + 1:S_ + 2], 0.0), MS(UB[:, 0:1], 0.0), MS(UB[:, S_ + 1:S_ + 2], 0.0)],
         writes=[("U", 0), ("U", 1)])

    def load_w(g):
        wt, wk = wrot.next()
        S.dma("pool", "L_Wb%d" % wk[1], wt[:], win_d[g], writes=[wk])
        return wt, wk

    pend_tail = []

    def flush_tail():
        while pend_tail:
            inf = pend_tail.pop(0)
            ob, ok = orot.next()
            S.op("act", ACT(ob[:], ACC[:], AF.Silu), reads=["ACC"], writes=[ok])
            S.dma("sp", "S_obuf%d" % ok[1], inf["dst"], ob[:], reads=[ok], writes=[inf["dkey"]])

    def fm_group(g, kind, sub_info):
        wt, wk = load_w(g)
        for sub in range(4):
            info = sub_info(sub)
            if kind == "conv":
                U, uk = urot.next()
            else:
                ob, ok = orot.next()
            for tb in range(NBLK):
                pt, pk = psrot.next()
                S.op("pe", [MM(pt[:, :], wt[:, kc, sub * 128:(sub + 1) * 128], hT[:, kc, tb * 512:(tb + 1) * 512],
                               start=(kc == 0), stop=(kc == 7)) for kc in range(8)],
                     reads=[wk] + hkeys(tb), writes=[pk])
                if kind == "conv":
                    S.op("act", ACT(U[:, 1 + tb * 512:1 + (tb + 1) * 512], pt[:, :], AF.Copy), reads=[pk], writes=[uk])
                elif kind == "silu":
                    S.op("act", ACT(ob[:, tb * 512:(tb + 1) * 512], pt[:, :], AF.Silu), reads=[pk], writes=[ok])
                elif kind == "sigb":
                    S.op("act", ACT(ob[:, tb * 512:(tb + 1) * 512], pt[:, :], AF.Sigmoid, bias=info["bias"]), reads=[pk, "cols"], writes=[ok])
                elif kind == "copy":
                    evtog[0] ^= 1
                    if evtog[0]:
                        S.op("dve", TS(ob[:, tb * 512:(tb + 1) * 512], pt[:, :], info["scale"], None, ALU.mult), reads=[pk], writes=[ok])
                    else:
                        S.op("act", ACT(ob[:, tb * 512:(tb + 1) * 512], pt[:, :], AF.Copy, scale=info["scale"]), reads=[pk], writes=[ok])
            if kind == "conv":
                flush_tail()
                cg = info["cg"]
                w0 = cols[:, cg * 3 + 0:cg * 3 + 1]
                w1 = cols[:, cg * 3 + 1:cg * 3 + 2]
                w2 = cols[:, cg * 3 + 2:cg * 3 + 3]
                cb = cols[:, 48 + cg:49 + cg]
                S.op("dve", TS(ACC[:], U[:, 1:S_ + 1], w1, cb, ALU.mult, ALU.add), reads=[uk, "cols"], writes=["ACC"])
                S.op("dve", STT(ACC[:], U[:, 0:S_], w0, ACC[:], ALU.mult, ALU.add), reads=[uk, "cols", "ACC"], writes=["ACC"])
                S.op("dve", STT(ACC[:], U[:, 2:S_ + 2], w2, ACC[:], ALU.mult, ALU.add), reads=[uk, "cols", "ACC"], writes=["ACC"])
                pend_tail.append(info)
            else:
                S.dma("sp", "S_obuf%d" % ok[1], info["dst"], ob[:], reads=[ok], writes=[info["dkey"]])

    def tm_group(g, kind, col0):
        wt, wk = load_w(g)
        for t4 in range(NT // 4):
            st, sk = tmrot.next()
            if kind == "va":
                sv = st[:, 0:4 * 2 * 258].rearrange("p (t h c) -> p t h c", t=4, h=2)
                S.op("pool", [MS(sv[:, :, :, 256:257], 1.0), MS(sv[:, :, :, 257:258], 0.0)], writes=[sk])
            elif kind == "vb":
                sv = st[:, 0:4 * 8 * 66].rearrange("p (t h c) -> p t h c", t=4, h=8)
                S.op("pool", [MS(sv[:, :, :, 64:65], 1.0), MS(sv[:, :, :, 65:66], 0.0)], writes=[sk])
            else:
                sv = st[:, 0:2048].rearrange("p (t c) -> p t c", t=4)
            for j in range(4):
                tt = t4 * 4 + j
                pt, pk = psrot.next()
                S.op("pe", [MM(pt[:, :], hT[:, kc, tt * 128:(tt + 1) * 128], wt[:, kc, :], start=(kc == 0), stop=(kc == 7)) for kc in range(8)],
                     reads=[wk, ("hT", tt)], writes=[pk])
                if kind == "va":
                    S.op("dve", CP(sv[:, j, :, 0:256], pt[:, :].rearrange("p (h c) -> p h c", h=2)), reads=[pk], writes=[sk])
                elif kind == "vb":
                    S.op("dve", CP(sv[:, j, :, 0:64], pt[:, :].rearrange("p (h c) -> p h c", h=8)), reads=[pk], writes=[sk])
                else:
                    S.op("act", ACT(sv[:, j, :], pt[:, :], AF.Sigmoid), reads=[pk], writes=[sk])
            tsl = slice(t4 * 4, t4 * 4 + 4)
            if kind == "va":
                hd0 = col0
                dst = VA[tsl, :, hd0 * 258:(hd0 + 2) * 258].rearrange("t p c -> p t c")
                S.dma("sp", "S_tm%d" % sk[1], dst, st[:, 0:4 * 516].rearrange("p (t c) -> p t c", t=4), reads=[sk], writes=[("VA", t4, hd0)])
            elif kind == "vb":
                dst = VBA[tsl, :, :].rearrange("t p c -> p t c")
                S.dma("sp", "S_tm%d" % sk[1], dst, st[:, 0:4 * 528].rearrange("p (t c) -> p t c", t=4), reads=[sk], writes=[("VBA", t4)])
            else:
                dst = OG[t4 * 512:(t4 + 1) * 512, col0:col0 + 512].rearrange("(t p) c -> p t c", p=128)
                S.dma("sp", "S_tm%d" % sk[1], dst, sv, reads=[sk], writes=[("OG", t4, col0)])

    for g in (0, 1):
        fm_group(g, "conv", lambda sub, g=g: dict(cg=g * 4 + sub, dst=QT[(g * 4 + sub) // 2, :, (g * 4 + sub) % 2, :],
                                                 dkey=("QT", g * 4 + sub)))
    for g in (2, 3):
        fm_group(g, "conv", lambda sub, g=g: dict(cg=8 + (g - 2) * 4 + sub, dst=KT[((g - 2) * 4 + sub) // 2, :, ((g - 2) * 4 + sub) % 2, :],
                                                 dkey=("KT", (g - 2) * 4 + sub)))
    flush_tail()
    for g in (8, 9):
        fm_group(g, "silu", lambda sub, g=g: dict(dst=ZAT[(g - 8) * 4 + sub], dkey=("ZAT", (g - 8) * 4 + sub)))
    fm_group(13, "silu", lambda sub: dict(dst=ZBT[sub], dkey=("ZBT", sub)))
    fm_group(10, "copy", lambda sub: dict(scale=0.125, dst=QBT[sub], dkey=("QBT", sub)))
    fm_group(11, "copy", lambda sub: dict(scale=1.0, dst=KBT[sub], dkey=("KBT", sub)))
    tm_group(4, "va", 0)
    tm_group(5, "va", 2)
    tm_group(12, "vb", 0)
    tm_group(6, "og", 0)
    tm_group(7, "og", 512)
    for g in (14, 15, 16, 17):
        fm_group(g, "sigb", lambda sub, g=g: dict(bias=cols[:, 64 + (g - 14) * 4 + sub:65 + (g - 14) * 4 + sub],
                                                 dst=GMT[(g - 14) * 4 + sub], dkey=("GMT", (g - 14) * 4 + sub)))
    S.barrier()
    if stop_after == "C":
        S.emit()
        return nc

    if build_mlstm(nc, S, locals()):
        return nc
    if stop_after == "M":
        S.emit()
        return nc
    build_na(nc, S, locals())
    if stop_after == "N":
        S.emit()
        return nc
    build_final(nc, S, locals())
    S.emit()
    return nc


def build_mlstm(nc, S, E):
    T, cur, ps, psb = E["T"], E["cur"], E["ps"], E["psb"]
    identb, maskT, cols, TOKU, TOKC, DECB = E["identb"], E["maskT"], E["cols"], E["TOKU"], E["TOKC"], E["DECB"]
    QT, KT, VA, OG, ZAT, YAT = E["QT"], E["KT"], E["VA"], E["OG"], E["ZAT"], E["YAT"]
    dbg, debug = E["dbg"], E["debug"]
    cur[0] = E["REGION0"]
    qT = T("m_qT", [128, 2, S_], BF16)
    kT = T("m_kT", [128, 2, S_], BF16)
    ktok = T("m_ktok", [128, NT, 256], BF16)
    Vaug = T("m_Vaug", [128, NT, 258], BF16)
    Hacc = T("m_Hacc", [128, NT, 256], F32)
    ZATh = T("m_ZATh", [128, 2, S_], BF16)
    yaT = T("m_yaT", [128, 2, S_], BF16)
    OGt = Rot("OGt", [T("m_OGt%d" % i, [128, 4, 256], BF16) for i in range(3)])
    UVr = [Rot("UV%d" % d, [T("m_UV%d_%d" % (d, i), [128, 258], BF16) for i in range(4)]) for d in range(2)]
    Smr = [Rot("Sm%d" % d, [T("m_Sm%d_%d" % (d, i), [128, 128], BF16) for i in range(3)]) for d in range(2)]
    Zs = [T("m_Z%d" % d, [128, 2, 258], F32) for d in range(2)]
    Cbr = [Rot("Cb%d" % d, [T("m_Cb%d_%d" % (d, i), [128, 2, 258], BF16) for i in range(3)]) for d in range(2)]
    Htr = Rot("Htmp", [T("m_Htmp%d" % i, [128, 256], F32) for i in range(3)])
    hgr = Rot("hg", [T("m_hg%d" % i, [128, 256], F32) for i in range(4)])
    ytr = Rot("yatok", [T("m_yatok%d" % i, [128, 256], BF16) for i in range(4)])
    junk = T("m_junk", [128, 256], BF16)
    rcs = T("m_rcs", [128, 8, 4], F32)
    pcs = T("m_pcs", [128, 8, 4], F32)
    rci = [0]
    pci = [0]
    dcp = [((ps[4], ps[5]), [("ps", 4), ("ps", 5)]), ((ps[6], ps[7]), [("ps", 6), ("ps", 7)])]

    def loads(hd):
        S.dma("sp", "L_mq", qT[:], QT[hd], writes=["qT"])
        S.dma("sp", "L_mk", kT[:], KT[hd], writes=["kT"])
        for j0 in range(0, NT, 8):
            S.dma("sp", "L_mv", Vaug[:, j0:j0 + 8, :], VA[j0:j0 + 8, :, hd * 258:(hd + 1) * 258].rearrange("t p c -> p t c"),
                  writes=[("Vaug", j) for j in range(j0, j0 + 8)])

    loads(0)
    for hd in range(4):
        S.dma("sp", "L_mz", ZATh[:], ZAT[2 * hd:2 * hd + 2].rearrange("g p t -> p g t"), writes=["ZATh"])
        for k4 in range(8):
            kb = 6 + (k4 % 2)
            S.op("pe", [TR(psb(kb)[:, (kk * 2 + c) * 128:(kk * 2 + c + 1) * 128], kT[:, c, (k4 * 4 + kk) * 128:(k4 * 4 + kk + 1) * 128], identb[:])
                        for kk in range(4) for c in range(2)], reads=["kT", "identb"], writes=[("ps", kb)])
            if k4 % 2 == 0:
                S.op("act", ACT(ktok[:, k4 * 4:(k4 + 1) * 4, :].rearrange("p a b -> p (a b)"), psb(kb)[:, 0:1024], AF.Copy, scale=1.0 / 16),
                     reads=[("ps", kb)], writes=[("ktok", k4)])
            else:
                S.op("dve", TS(ktok[:, k4 * 4:(k4 + 1) * 4, :].rearrange("p a b -> p (a b)"), psb(kb)[:, 0:1024], 1.0 / 16, None, ALU.mult),
                     reads=[("ps", kb)], writes=[("ktok", k4)])
        if E["stop_after"] == "M0":
            S.emit()
            return True

        ctx = {}
        chain = [dict(kprev=None) for _ in range(2)]

        def kof(d, i):
            return i if d == 0 else NT - 1 - i

        def opUV(d, i):
            k = kof(d, i)
            ucol = TOKU[:, k, d * 32 + hd:d * 32 + hd + 1]
            UVb, uvk = UVr[d].next()
            S.op("dve", TS(UVb[:], Vaug[:, k, :], ucol, None, ALU.mult), reads=[("Vaug", k), "TOKU"], writes=[uvk])
            ctx[(d, i)] = dict(k=k, ch=slice(k * 128, (k + 1) * 128), UVb=UVb, uvk=uvk, cb=None, cbk=None)

        def opST(d, i):
            c_ = ctx[(d, i)]
            stp = ps[d][:, 0:128]
            S.op("pe", [MM(stp, kT[:, c, c_["ch"]], qT[:, c, c_["ch"]], start=(c == 0), stop=(c == 1)) for c in range(2)],
                 reads=["kT", "qT"], writes=[("ps", d)])

        def opMASK(d, i):
            c_ = ctx[(d, i)]
            Sm, smk = Smr[d].next()
            S.op("dve", TT(Sm[:], ps[d][:, 0:128], maskT[:, d, :], ALU.mult), reads=[("ps", d), "maskT"], writes=[smk])
            c_["Sm"], c_["smk"] = Sm, smk

        def opDC(d, i):
            c_ = ctx[(d, i)]
            (b0, b1), dks = dcp[d]
            k = c_["k"]
            S.op("pe", [MM(b0[:, 0:258], ktok[:, k, 0:128], c_["UVb"][:]), MM(b1[:, 0:258], ktok[:, k, 128:256], c_["UVb"][:])],
                 reads=[("ktok", k // 4), c_["uvk"]], writes=dks)

        def opZ(d, i):
            c_ = ctx[(d, i)]
            (b0, b1), dks = dcp[d]
            series = d * 4 + hd
            k = c_["k"]
            Z = Zs[d]
            zk = ("Z", d)
            if i == 0:
                S.op("dve", [CP(Z[:, 0, :], b0[:, 0:258]), CP(Z[:, 1, :], b1[:, 0:258])], reads=dks, writes=[zk])
            else:
                kp = chain[d]["kprev"]
                dprev = DECB[:, series, kp:kp + 1]
                S.op("dve", [STT(Z[:, 0, :], Z[:, 0, :], dprev, b0[:, 0:258], ALU.mult, ALU.add),
                             STT(Z[:, 1, :], Z[:, 1, :], dprev, b1[:, 0:258], ALU.mult, ALU.add)],
                     reads=dks + [zk, "DECB"], writes=[zk])
            cbn, cbk = Cbr[d].next()
            dcur = DECB[:, series, k:k + 1]
            S.op("act", ACT(cbn[:].rearrange("p a b -> p (a b)"), Z[:].rearrange("p a b -> p (a b)"), AF.Copy, scale=dcur), reads=[zk, "DECB"], writes=[cbk])
            c_["cb"], c_["cbk"] = cbn, cbk
            chain[d]["kprev"] = k

        def opNP(d, i):
            c_ = ctx[(d, i)]
            first = (i == 0)
            npb, npk = ps[2 + d], ("ps", 2 + d)
            npa = npb[:, 0:258]
            specs = [MM(npa, c_["Sm"][:], c_["UVb"][:], start=True, stop=first)]
            rd = [c_["smk"], c_["uvk"]]
            if not first:
                pv = ctx[(d, i - 1)]
                specs += [MM(npa, qT[:, c, c_["ch"]], pv["cb"][:, c, :], start=False, stop=(c == 1)) for c in range(2)]
                rd += ["qT", pv["cbk"]]
            S.op("pe", specs, reads=rd, writes=[npk])

        def opOUT(d, i):
            c_ = ctx[(d, i)]
            k = c_["k"]
            npb, npk = ps[2 + d], ("ps", 2 + d)
            j = rci[0] % 8
            rci[0] += 1
            rc = rcs[:, j, :]
            rck = ("rc", j)
            den = npb[:, 256:257]
            ccol = TOKC[:, k, d * 32 + hd:d * 32 + hd + 1]
            S.op("dve", TT(rc[:, 0:1], den, ccol, ALU.max), reads=[npk, "TOKC"], writes=[rck])
            S.op("dve", STT(rc[:, 1:2], den, -1.0, rc[:, 0:1], ALU.mult, ALU.max), reads=[npk, rck], writes=[rck])
            S.op("dve", ("reciprocal", dict(out=rc[:, 2:3], in_=rc[:, 1:2])), reads=[rck], writes=[rck])
            if i < NT // 2:
                S.op("act", ACT(Hacc[:, k, :], npb[:, 0:256], AF.Copy, scale=rc[:, 2:3]), reads=[npk, rck], writes=[("Hacc", k)])
            else:
                ht, htk = Htr.next()
                S.op("act", ACT(ht[:], npb[:, 0:256], AF.Copy, scale=rc[:, 2:3]), reads=[npk, rck], writes=[htk])
                S.op("pool", TT(Hacc[:, k, :], Hacc[:, k, :], ht[:], ALU.add), reads=[htk, ("Hacc", k)], writes=[("Hacc", k)])
            if i >= 1:
                ctx.pop((d, i - 1))

        for d in range(2):
            opUV(d, 0)
        for d in range(2):
            opUV(d, 1)
        for d in range(2):
            opST(d, 0)
        for d in range(2):
            opMASK(d, 0)
        for d in range(2):
            opDC(d, 0)
        for d in range(2):
            opZ(d, 0)
        for i in range(NT):
            nx = i + 1
            if i + 2 < NT:
                opUV(0, i + 2)
                opUV(1, i + 2)
            if nx < NT:
                opST(0, nx)
                opST(1, nx)
                opMASK(0, nx)
                opMASK(1, nx)
                if nx < NT - 1:
                    opDC(0, nx)
                    opDC(1, nx)
            opNP(0, i)
            opNP(1, i)
            if nx < NT - 1:
                opZ(0, nx)
            opOUT(0, i)
            if nx < NT - 1:
                opZ(1, nx)
            opOUT(1, i)

        if debug:
            S.dma("sp", "S_dbg", dbg["Hacc"][hd], Hacc[:], reads=[("Hacc", k) for k in range(NT)])
        if hd + 1 < 4:
            loads(hd + 1)

        pctx = {}

        def P1(k):
            k4, j = k // 4, k % 4
            if j == 0:
                ogt, ogk = OGt.next()
                S.dma("pool", "L_og%d" % ogk[1], ogt[:], OG[k4 * 512:(k4 + 1) * 512, hd * 256:(hd + 1) * 256].rearrange("(t p) c -> p t c", p=128),
                      writes=[ogk])
                pctx["og"] = (ogt, ogk)
            ogt, ogk = pctx["og"]
            hg, hgk = hgr.next()
            S.op("dve", TT(hg[:], Hacc[:, k, :], ogt[:, j, :], ALU.mult), reads=[("Hacc", k), ogk], writes=[hgk])
            q = pci[0] % 8
            pci[0] += 1
            pc = pcs[:, q, :]
            pck = ("pc", q)
            S.op("act", ACT(junk[:], hg[:], AF.Square, accum_out=pc[:, 0:1]), reads=[hgk], writes=["m_junk", pck])
            S.op("act", ACT(pc[:, 1:2], pc[:, 0:1], AF.Sqrt, scale=1.0 / DH, bias=EPS), reads=[pck], writes=[pck])
            pctx[k] = (hg, hgk, pc, pck)

        def P2(k):
            k4, j = k // 4, k % 4
            hg, hgk, pc, pck = pctx.pop(k)
            S.op("dve", ("reciprocal", dict(out=pc[:, 2:3], in_=pc[:, 1:2])), reads=[pck], writes=[pck])
            yt, ytk = ytr.next()
            S.op("dve", TS(yt[:], hg[:], pc[:, 2:3], None, ALU.mult), reads=[hgk, pck], writes=[ytk])
            S.op("pe", [TR(psb(7)[:, c * 512 + j * 128:c * 512 + (j + 1) * 128], yt[:, c * 128:(c + 1) * 128], identb[:]) for c in range(2)],
                 reads=[ytk, "identb"], writes=[("ps", 7)])
            if j == 3:
                for c in range(2):
                    S.op("dve", STT(yaT[:, c, k4 * 512:(k4 + 1) * 512], psb(7)[:, c * 512:(c + 1) * 512], cols[:, 80 + hd * 2 + c:81 + hd * 2 + c],
                                    ZATh[:, c, k4 * 512:(k4 + 1) * 512], ALU.mult, ALU.mult),
                         reads=[("ps", 7), "cols", "ZATh"], writes=[("yaT", c)])

        for k in range(NT + 2):
            if k < NT:
                P1(k)
            if k >= 2:
                P2(k - 2)
        for c in range(2):
            S.dma("pool", "S_yaT", YAT[hd * 2 + c], yaT[:, c, :], reads=[("yaT", c)])
        if E["stop_after"] == "M2":
            S.emit()
            return True
    S.barrier()


def build_na(nc, S, E):
    T, cur, ps, psb = E["T"], E["cur"], E["ps"], E["psb"]
    identb = E["identb"]
    QBT, KBT, VBA, ZBT, YBT, bt2_d = E["QBT"], E["KBT"], E["VBA"], E["ZBT"], E["YBT"], E["bt2_d"]
    cur[0] = E["REGION0"]
    bt2b = T("n_bt2", [128, 4, 14, 2, 64], BF16)
    QBD = T("n_QBD", [128, 2, 2, S_], BF16)
    kbT = T("n_kbT", [128, 2, S_], BF16)
    ZBh = T("n_ZBh", [128, 2, S_], BF16)
    ybT = T("n_ybT", [128, 2, S_], BF16)
    VE = T("n_VE", [128, NT, 4, 66], BF16)
    VO = T("n_VO", [128, NT - 1, 4, 66], BF16)
    PTr = Rot("PT", [T("n_PT%d" % i, [128, 1024], BF16) for i in range(2)])
    otr = Rot("otok", [T("n_otok%d" % i, [64, 4, 64], BF16) for i in range(3)])
    recs = T("n_rec", [64, 4, 4], F32)
    str_ = Rot("psS", [(ps[0], ps[1]), (ps[2], ps[3])])
    pvr = Rot("psPV", [ps[4], ps[5]])
    trr = Rot("psT", [6, 7])
    ri = [0]
    NA_END = cur[0]
    wpab = T("f_wpa", [128, 8, 1024], BF16)
    wpbb = T("f_wpb", [128, 4, 1024], BF16)
    woutb = T("f_wout", [128, 8, 1024], BF16)
    wsr = Rot("f_wst", [T("f_wst%d" % i, [128, 2, 1024], F32) for i in range(1)])
    S.shared_fw = (NA_END, wpab, wpbb, woutb)
    gate_bc = E["gate_bc"]
    S.dma("pool", "L_bt2", bt2b[:], bt2_d[:, :, :, :, :], writes=["bt2b"])
    S.op("pool", [MS(QBD[0:64, :, 1, :], 0.0), MS(QBD[64:128, :, 0, :], 0.0)], writes=["QBDz"])
    for half in range(2):
        c0 = half * 4 * 66
        for cq in range(4):
            tsl = slice(cq * 1024, (cq + 1) * 1024)
            S.dma("sp", "L_nq%d" % cq, QBD[0:64, :, 0, tsl], QBT[2 * half:2 * half + 2, 0:64, tsl].rearrange("g p t -> p g t"), reads=["QBDz"], writes=[("qbT", cq)])
            S.dma("sp", "L_nq%d" % cq, QBD[64:128, :, 1, tsl], QBT[2 * half:2 * half + 2, 64:128, tsl].rearrange("g p t -> p g t"), reads=["QBDz"], writes=[("qbT", cq)])
            S.dma("sp", "L_nk%d" % cq, kbT[:, :, tsl], KBT[2 * half:2 * half + 2, :, tsl].rearrange("g p t -> p g t"), writes=[("kbT", cq)])
            j0 = cq * 8
            S.dma("sp", "L_nve%d" % cq, VE[:, j0:j0 + 8, :, :].rearrange("p t h c -> p t (h c)"),
                  VBA[j0:j0 + 8, :, c0:c0 + 264].rearrange("t p c -> p t c"), writes=[("VE", cq)])
            j1 = min(j0 + 8, NT - 1)
            S.dma("sp", "L_nvo%d" % cq, VO[0:64, j0:j1, :, :].rearrange("p t h c -> p t (h c)"),
                  VBA[j0:j1, 64:128, c0:c0 + 264].rearrange("t p c -> p t c"), writes=[("VO", cq)])
            S.dma("sp", "L_nvo%d" % cq, VO[64:128, j0:j1, :, :].rearrange("p t h c -> p t (h c)"),
                  VBA[j0 + 1:j1 + 1, 0:64, c0:c0 + 264].rearrange("t p c -> p t c"), writes=[("VO", cq)])
            S.dma("sp", "L_nz%d" % cq, ZBh[:, :, tsl], ZBT[2 * half:2 * half + 2, :, tsl].rearrange("g p t -> p g t"), writes=[("ZBh", cq)])

        def n_stage1(r):
            rs = min(max(r - 4, 0), 56)
            j0b = rs - r + 7
            qs = slice(r * 64, (r + 1) * 64)
            pair, sk = str_.next()
            specs = []
            for gi in range(2):
                bank = pair[gi]
                hp = half * 2 + gi
                specs.append(MM(bank[:, 0:512], identb[:], bt2b[:, hp, j0b:j0b + 7:2, :, :].rearrange("p i j q -> p i (j q)"), start=True, stop=False))
                for i in range(4):
                    tok = rs * 64 + i * 128
                    specs.append(MM(bank[:, i * 128:(i + 1) * 128], kbT[:, gi, tok:tok + 128], QBD[:, gi, :, qs], start=False, stop=(i == 3)))
            S.op("pe", specs, reads=["identb", "bt2b", ("qbT", (r * 64) // 1024)] + [("kbT", cc) for cc in sorted({(rs * 64) // 1024, (rs * 64 + 511) // 1024})], writes=[sk])
            PT, ptk = PTr.next()
            S.op("act", [ACT(PT[:, 0:512], pair[0][:, :], AF.Exp), ACT(PT[:, 512:1024], pair[1][:, :], AF.Exp)], reads=[sk], writes=[ptk])
            return dict(r=r, rs=rs, qs=qs, PT=PT, ptk=ptk)

        def n_stage2(c):
            rs, PT, ptk = c["rs"], c["PT"], c["ptk"]
            if rs % 2 == 0:
                Vx, vnm, tbase = VE, "VE", rs // 2
            else:
                Vx, vnm, tbase = VO, "VO", (rs - 1) // 2
            vkeys = [(vnm, cc) for cc in sorted({tbase // 8, (tbase + 3) // 8})]
            ob, ok = pvr.next()
            specs = []
            for hh in range(4):
                for i in range(4):
                    specs.append(MM(ob[0:64, hh * 66:(hh + 1) * 66], PT[:, (hh // 2) * 512 + i * 128 + (hh % 2) * 64:(hh // 2) * 512 + i * 128 + (hh % 2) * 64 + 64], Vx[:, tbase + i, hh, :],
                                    start=(i == 0), stop=(i == 3)))
            S.op("pe", specs, reads=[ptk] + vkeys, writes=[ok])
            q = ri[0] % 4
            ri[0] += 1
            rec = recs[:, q, :]
            rk = ("rec", q)
            ov = ob[0:64, 0:264].rearrange("p (h c) -> p h c", c=66)
            S.op("dve", ("reciprocal", dict(out=rec, in_=ov[:, :, 64])), reads=[ok], writes=[rk])
            ot, otk = otr.next()
            S.op("dve", TT(ot[:], ov[:, :, 0:64], rec.rearrange("p (h o) -> p h o", o=1).to_broadcast([64, 4, 64]), ALU.mult),
                 reads=[ok, rk], writes=[otk])
            c["ot"], c["otk"] = ot, otk

        def n_stage3(c):
            ot, otk, qs = c["ot"], c["otk"], c["qs"]
            tb_, tk = trr.next()
            otf = ot[:].rearrange("p h c -> p (h c)")
            S.op("pe", [TR(psb(tb_)[:, g * 64:(g + 1) * 64], otf[:, g * 128:(g + 1) * 128], identb[0:64, 0:64]) for g in range(2)],
                 reads=[otk, "identb"], writes=[tk])
            S.op("dve", TT(ybT[:, :, qs], psb(tb_)[:, 0:128].rearrange("p (g q) -> p g q", g=2), ZBh[:, :, qs], ALU.mult),
                 reads=[tk, ("ZBh", c["r"] * 64 // 1024)], writes=["ybT"])

        nctx = {}
        if half == 0:
            S.dma("pool", "L_fwa", wpab[:], E["wpa_d"][:, :, :], writes=["wpab"])
            S.dma("pool", "L_fwb", wpbb[:], E["wpb_d"][:, :, :], writes=["wpbb"])
            for j in range(4):
                wt, wk = wsr.next()
                S.dma("sp", "L_fws%d" % wk[1], wt[:], E["wout_d"][:, 2 * j:2 * j + 2, :], writes=[wk])
                S.op("dve", TT(woutb[:, 2 * j:2 * j + 2, :], wt[:], gate_bc[:].rearrange("p (o n) -> p o n", o=1).to_broadcast([128, 2, 1024]), ALU.mult),
                     reads=[wk, "gate_bc"], writes=["woutb"])
        for r in range(64 + 2):
            if r < 64:
                nctx[r] = n_stage1(r)
            if 0 <= r - 1 < 64:
                n_stage2(nctx[r - 1])
            if 0 <= r - 2 < 64:
                n_stage3(nctx.pop(r - 2))
        for g in range(2):
            S.dma("sp", "S_ybT", YBT[2 * half + g], ybT[:, g, :], reads=["ybT"])
    S.barrier()


def build_final(nc, S, E):
    T, cur, ps = E["T"], E["cur"], E["ps"]
    gate_bc, fg_bc, smallc = E["gate_bc"], E["fg_bc"], E["smallc"]
    YAT, YBT, GMT, x_d, y_d = E["YAT"], E["YBT"], E["GMT"], E["x_d"], E["y_d"]
    wpa_d, wpb_d, wout_d = E["wpa_d"], E["wpb_d"], E["wout_d"]
    cur[0] = E["REGION0"]
    NA_END, wpab, wpbb, woutb = S.shared_fw
    yar = Rot("f_ya", [T("f_ya%d" % i, [128, 8, 512], BF16) for i in range(2)])
    ybr = Rot("f_yb", [T("f_yb%d" % i, [128, 4, 512], BF16) for i in range(2)])
    gmr = Rot("f_gm", [T("f_gm%d" % i, [128, 16, 512], BF16) for i in range(2)])
    t1r = Rot("f_t1", [T("f_t1%d" % i, [128, 512], F32) for i in range(2)])
    t2r = Rot("f_t2", [T("f_t2%d" % i, [128, 512], F32) for i in range(2)])
    mgr = Rot("f_mg", [T("f_mg%d" % i, [128, 8, 512], BF16) for i in range(2)])
    xr = Rot("f_xt", [T("f_xt%d" % i, [128, 1024], F32) for i in range(5)])
    x2r = Rot("f_x2", [T("f_x2%d" % i, [128, 1024], F32) for i in range(2)])
    otr = Rot("f_ot", [T("f_ot%d" % i, [128, 1024], F32) for i in range(2)])
    junk = T("f_junk", [128, 1024], BF16)
    par = Rot("psP", [ps[0], ps[1], ps[2], ps[3]])
    outr = Rot("psO", [(ps[4], ps[5]), (ps[6], ps[7])])
    assert cur[0] <= NA_END, (cur[0], NA_END)
    def f_loads(tb):
        ts = slice(tb * 512, (tb + 1) * 512)
        ya, yak = yar.next()
        yb, ybk = ybr.next()
        gm, gmk = gmr.next()
        S.dma("sp", "L_fya%d" % yak[1], ya[:], YAT[:, :, ts].rearrange("g p t -> p g t"), writes=[yak])
        S.dma("sp", "L_fyb%d" % ybk[1], yb[:], YBT[:, :, ts].rearrange("g p t -> p g t"), writes=[ybk])
        S.dma("sp", "L_fgm%d" % gmk[1], gm[:], GMT[:, :, ts].rearrange("g p t -> p g t"), writes=[gmk])
        return ya, yak, yb, ybk, gm, gmk

    fl = {0: f_loads(0)}
    mgs = {}

    def stageP(tb):
        ya, yak, yb, ybk, gm, gmk = fl.pop(tb)
        if tb + 1 < NBLK:
            fl[tb + 1] = f_loads(tb + 1)
        mg, mgk = mgr.next()
        for fg in range(8):
            fs = slice(fg * 128, (fg + 1) * 128)
            pa, pak = par.next()
            S.op("pe", [MM(pa[:, :], wpab[:, kc, fs], ya[:, kc, :], start=(kc == 0), stop=(kc == 7)) for kc in range(8)],
                 reads=["wpab", yak], writes=[pak])
            pb_, pbk = par.next()
            S.op("pe", [MM(pb_[:, :], wpbb[:, kc, fs], yb[:, kc, :], start=(kc == 0), stop=(kc == 3)) for kc in range(4)],
                 reads=["wpbb", ybk], writes=[pbk])
            t1, t1k = t1r.next()
            t2, t2k = t2r.next()
            S.op("dve", TT(t1[:], pa[:, :], gm[:, fg, :], ALU.mult), reads=[pak, gmk], writes=[t1k])
            S.op("dve", TT(t2[:], pb_[:, :], gm[:, 8 + fg, :], ALU.mult), reads=[pbk, gmk], writes=[t2k])
            S.op("pool", TT(mg[:, fg, :], t1[:], t2[:], ALU.add), reads=[t1k, t2k], writes=[(mgk, fg)])
        mgs[tb] = (mg, mgk)

    def stageO(tb):
        mg, mgk = mgs.pop(tb)
        xtl = []
        for tt in range(4):
            tile = tb * 4 + tt
            xt, xk = xr.next()
            S.dma("sp", "L_fx%d" % xk[1], xt[:], x_d[tile * 128:(tile + 1) * 128, :], writes=[xk])
            xtl.append((xt, xk))
        for tt in range(4):
            tile = tb * 4 + tt
            xt, xk = xtl[tt]
            (o0, o1), ok = outr.next()
            specs = []
            for nh, ob in enumerate((o0, o1)):
                for fg in range(8):
                    specs.append(MM(ob[:, :], mg[:, fg, tt * 128:(tt + 1) * 128], woutb[:, fg, nh * 512:(nh + 1) * 512], start=(fg == 0), stop=(fg == 7)))
            S.op("pe", specs, reads=[(mgk, fg) for fg in range(8)] + ["woutb"], writes=[ok])
            x2, x2k = x2r.next()
            S.op("dve", [TT(x2[:, 0:512], o0[:, :], xt[:, 0:512], ALU.add), TT(x2[:, 512:1024], o1[:, :], xt[:, 512:1024], ALU.add)],
                 reads=[ok, xk], writes=[x2k])
            q = tile % 4
            sc = smallc[:, q * 4:q * 4 + 4]
            sck = ("smallc", q)
            S.op("act", ACT(junk[:], x2[:], AF.Square, accum_out=sc[:, 0:1]), reads=[x2k], writes=["f_junk", sck])
            S.op("act", ACT(sc[:, 1:2], sc[:, 0:1], AF.Sqrt, scale=1.0 / D, bias=EPS), reads=[sck], writes=[sck])
            S.op("dve", ("reciprocal", dict(out=sc[:, 2:3], in_=sc[:, 1:2])), reads=[sck], writes=[sck])
            ot, otk = otr.next()
            S.op("act", ACT(ot[:], x2[:], AF.Copy, scale=sc[:, 2:3]), reads=[x2k, sck], writes=[otk])
            S.op("pool", TT(ot[:], ot[:], fg_bc[:], ALU.mult), reads=[otk, "fg_bc"], writes=[otk])
            S.dma("pool", "S_fo%d" % otk[1], y_d[tile * 128:(tile + 1) * 128, :], ot[:], reads=[otk])

    for tb in range(NBLK + 1):
        if tb < NBLK:
            stageP(tb)
        if tb >= 1:
            stageO(tb - 1)


def _shared_layouts(inp):
    f = np.float32
    w_ada = np.asarray(inp["w_ada"], f)[0]
    w_in = np.asarray(inp["w_in"], f)[0]
    sh = {}
    sh["wada"] = np.ascontiguousarray(w_ada.reshape(8, 128, 3072).transpose(1, 0, 2))
    sh["rows"] = np.ascontiguousarray(np.concatenate(
        [np.asarray(inp["b_ada"], f)[0], np.asarray(inp["norm_gain"], f)[0], np.asarray(inp["final_gain"], f)])[None, :])
    wg = np.zeros((1024, 72), f)
    gc = w_in[:, 5120:5136].reshape(1024, 2, 2, 4)
    wg[:, 0:4] = gc[:, 0, 0]
    wg[:, 32:36] = gc[:, 1, 0]
    wg[:, 36:40] = gc[:, 0, 1]
    wg[:, 68:72] = gc[:, 1, 1]
    sh["wgate"] = np.ascontiguousarray(wg.reshape(8, 128, 72).transpose(1, 0, 2))
    wl = np.concatenate([w_in[:, 0:5120], w_in[:, 5136:9232]], axis=1)
    sh["win"] = np.ascontiguousarray(wl.reshape(8, 128, 18, 512).transpose(2, 1, 0, 3))
    cols = np.zeros((128, 88), f)
    cw = np.asarray(inp["conv_w"], f)[0]
    cb = np.asarray(inp["conv_b"], f)[0]
    cols[:, 0:48] = cw.reshape(3, 16, 128).transpose(2, 1, 0).reshape(128, 48)
    cols[:, 48:64] = cb.reshape(16, 128).T
    cols[:, 64:80] = np.asarray(inp["b_merge"], f)[0].reshape(16, 128).T
    cols[:, 80:88] = np.asarray(inp["mlstm_norm_gain"], f)[0].reshape(8, 128).T
    sh["cols"] = cols
    gb = np.zeros((36, 2), f)
    bi = np.asarray(inp["b_igate"], f)[0]
    bfg = np.asarray(inp["b_fgate"], f)[0]
    gb[0:4, 0] = bi[0]
    gb[32:36, 0] = bi[1]
    gb[0:4, 1] = bfg[0]
    gb[32:36, 1] = bfg[1]
    sh["gb"] = gb
    rpb = np.asarray(inp["rpb"], f)[0]
    kc = np.arange(64)[:, None]
    qc = np.arange(64)[None, :]
    ws = np.clip(qc - 8, 0, 48)
    colok = (kc >= ws) & (kc < ws + 16)
    dcidx = np.clip(kc - qc + 15, 0, 30)
    bt2 = np.full((128, 8, 14, 64), NEG, f)
    for j in range(14):
        for half in range(2):
            dr = j - 7 + half
            tab = np.where(colok[None], rpb[:, dr + 7][:, dcidx], f(NEG))
            bt2[half * 64:(half + 1) * 64, :, j, :] = tab.transpose(1, 0, 2)
    sh["bt2"] = np.ascontiguousarray(bt2.reshape(128, 4, 2, 14, 64).transpose(0, 1, 3, 2, 4))
    sh["wpa"] = np.ascontiguousarray(np.asarray(inp["w_proj_a"], f)[0].reshape(8, 128, 1024).transpose(1, 0, 2))
    sh["wpb"] = np.ascontiguousarray(np.asarray(inp["w_proj_b"], f)[0].reshape(4, 128, 1024).transpose(1, 0, 2))
    sh["wout"] = np.ascontiguousarray(np.asarray(inp["w_out"], f)[0].reshape(8, 128, 1024).transpose(1, 0, 2))
    sh["ident"] = np.eye(128, dtype=f)
    s_i = np.arange(128)[:, None]
    t_i = np.arange(128)[None, :]
    masks = np.zeros((128, 2, 128), f)
    masks[:, 0, :] = np.where(s_i <= t_i, 1.0 / 16, 0.0)
    masks[:, 1, :] = np.where(s_i >= t_i, 1.0 / 16, 0.0)
    sh["masks"] = masks
    sel = np.zeros((36, 8, 128), f)
    for j in range(8):
        sel[(j % 4) + 32 * (j // 4), j, :] = 1.0
    sh["sel"] = sel
    return sh


def make_in_maps(inp):
    sh = _shared_layouts(inp)
    x = np.asarray(inp["x"], np.float32)
    c = np.asarray(inp["c"], np.float32)
    maps = []
    for b in range(8):
        m = dict(sh)
        m["x"] = np.ascontiguousarray(x[b])
        m["c_l"] = np.ascontiguousarray(c[b].reshape(8, 128).T)
        maps.append(m)
    return maps


_NC_CACHE = {}


def kernel(**inputs):
    if "nc" not in _NC_CACHE:
        _NC_CACHE["nc"] = build_program()
    nc = _NC_CACHE["nc"]
    in_maps = make_in_maps(inputs)
    res = run_bass_kernel_spmd(nc, in_maps, core_ids=list(range(8)))
    return np.stack([np.asarray(r["y"], np.float32) for r in res.results], axis=0)
```

```python
import numpy as np
import concourse.bass as bass
import concourse.mybir as mybir
from concourse.bass_utils import run_bass_kernel_spmd

F32 = mybir.dt.float32
BF16 = mybir.dt.bfloat16
ALU = mybir.AluOpType
AF = mybir.ActivationFunctionType

S_ = 4096
D = 1024
NT = 32
NBLK = 8
H = 4
DH = 256
NH = 8
NEG = -30000.0
EPS = 1e-6
ENG_NAMES = ("pe", "act", "dve", "pool", "sp")


class Sched:
    def __init__(self, nc):
        self.nc = nc
        self.ops = {e: [] for e in ENG_NAMES}
        self.count = {}
        self.last_writer = {}
        self.readers = {}
        self.seen = {e: {} for e in ENG_NAMES}
        self.sem_names = ["pe", "act", "dve", "pool"]
        self.is_dma = set()
        self.n_instr = 0

    def _deps(self, reads, writes):
        deps = set()
        for k in reads:
            w = self.last_writer.get(k)
            if w is not None:
                deps.add(w)
        for k in writes:
            w = self.last_writer.get(k)
            if w is not None:
                deps.add(w)
            deps.update(self.readers.get(k, ()))
        return deps

    def _record(self, me, reads, writes):
        for k in reads:
            self.readers.setdefault(k, []).append(me)
        for k in writes:
            self.last_writer[k] = me
            self.readers[k] = []

    def _waits(self, eng, deps):
        need = {}
        for (s, i) in deps:
            if s in self.is_dma:
                i = self.count[s] - 1
            if need.get(s, -1) < i:
                need[s] = i
        waits = []
        for s, i in need.items():
            if self.seen[eng].get(s, -1) >= i:
                continue
            self.seen[eng][s] = i
            waits.append((s, i + 1))
        return waits

    def op(self, eng, specs, reads=(), writes=()):
        if isinstance(specs, tuple):
            specs = [specs]
        waits = self._waits(eng, self._deps(reads, writes))
        idx = self.count.get(eng, 0)
        self.count[eng] = idx + 1
        self.ops[eng].append((specs, waits, (eng, 1)))
        self._record((eng, idx), reads, writes)
        self.n_instr += len(specs)

    def dma(self, queue, stream, out, in_, reads=(), writes=()):
        if stream not in self.is_dma:
            self.is_dma.add(stream)
            self.sem_names.append(stream)
            self.count[stream] = 0
        waits = self._waits(queue, self._deps(reads, writes))
        idx = self.count[stream]
        self.count[stream] = idx + 1
        self.ops[queue].append(([("dma_start", dict(out=out, in_=in_))], waits, (stream, 16)))
        self._record((stream, idx), reads, writes)
        self.n_instr += 1

    def barrier(self):
        allw = [(s, c) for s, c in self.count.items() if c > 0]
        for e in ENG_NAMES:
            waits = []
            for s, c in allw:
                if self.seen[e].get(s, -1) >= c - 1:
                    continue
                self.seen[e][s] = c - 1
                waits.append((s, c))
            if waits:
                self.ops[e].append((None, waits, None))
        self.last_writer = {}
        self.readers = {}

    def emit(self):
        import contextlib
        nc = self.nc
        self.barrier()
        with contextlib.ExitStack() as st:
            sems = {s: st.enter_context(nc.semaphore("s_" + s)) for s in self.sem_names}
            block = st.enter_context(nc.Block())

            def run(engname):
                def body(eng):
                    for specs, waits, inc in self.ops[engname]:
                        for (s, v) in waits:
                            eng.wait_ge(sems[s], v * (16 if s in self.is_dma else 1))
                        if specs is None:
                            continue
                        ins = None
                        for (m, kw) in specs:
                            ins = getattr(eng, m)(**kw)
                        ins.then_inc(sems[inc[0]], inc[1])
                return body

            block.tensor(run("pe"))
            block.scalar(run("act"))
            block.vector(run("dve"))
            block.gpsimd(run("pool"))
            block.sync(run("sp"))


class Rot:
    def __init__(self, name, tiles):
        self.name, self.tiles, self.i = name, tiles, 0

    def next(self):
        j = self.i % len(self.tiles)
        self.i += 1
        return self.tiles[j], (self.name, j)


def MM(out, lhsT, rhs, start=True, stop=True):
    return ("matmul", dict(out=out, lhsT=lhsT, rhs=rhs, start=start, stop=stop))


def TR(out, in_, identity):
    return ("transpose", dict(out=out, in_=in_, identity=identity))


def ACT(out, in_, func, **kw):
    return ("activation", dict(out=out, in_=in_, func=func, **kw))


def TT(out, in0, in1, op):
    return ("tensor_tensor", dict(out=out, in0=in0, in1=in1, op=op))


def TS(out, in0, scalar1, scalar2, op0, op1=None):
    d = dict(out=out, in0=in0, scalar1=scalar1, scalar2=scalar2, op0=op0)
    if op1 is not None:
        d["op1"] = op1
    return ("tensor_scalar", d)


def STT(out, in0, scalar, in1, op0, op1):
    return ("scalar_tensor_tensor", dict(out=out, in0=in0, scalar=scalar, in1=in1, op0=op0, op1=op1))


def CP(out, in_):
    return ("tensor_copy", dict(out=out, in_=in_))


def MS(ap, v):
    return ("memset", dict(ap=ap, constant=v))


def SCAN(out, data0, data1, initial, op0, op1):
    return ("tensor_tensor_scan", dict(out=out, data0=data0, data1=data1, initial=initial, op0=op0, op1=op1))


def build_program(stop_after=None, debug=False):
    nc = bass.Bass("TRN2", target_bir_lowering=False)
    dbg_kind = "ExternalOutput" if debug else "Internal"

    def DIN(name, shape, dt=F32):
        return nc.dram_tensor(name, list(shape), dt, kind="ExternalInput").ap()

    def DSC(name, shape, dt=BF16):
        return nc.dram_tensor(name, list(shape), dt, kind=dbg_kind).ap()

    x_d = DIN("x", [S_, D])
    c_d = DIN("c_l", [128, 8])
    wada_d = DIN("wada", [128, 8, 3072])
    rows_d = DIN("rows", [1, 5120])
    wgate_d = DIN("wgate", [128, 8, 72])
    win_d = DIN("win", [18, 128, 8, 512])
    cols_d = DIN("cols", [128, 88])
    gb_d = DIN("gb", [36, 2])
    bt2_d = DIN("bt2", [128, 4, 14, 2, 64])
    wpa_d = DIN("wpa", [128, 8, 1024])
    wpb_d = DIN("wpb", [128, 4, 1024])
    wout_d = DIN("wout", [128, 8, 1024])
    ident_d = DIN("ident", [128, 128])
    masks_d = DIN("masks", [128, 2, 128])
    sel_d = DIN("sel", [36, 8, 128])
    y_d = nc.dram_tensor("y", [S_, D], F32, kind="ExternalOutput").ap()

    QT = DSC("QT", [4, 128, 2, S_])
    KT = DSC("KT", [4, 128, 2, S_])
    VA = DSC("VA", [NT, 128, 4 * 258])
    OG = DSC("OG", [S_, D])
    ZAT = DSC("ZAT", [8, 128, S_])
    QBT = DSC("QBT", [4, 128, S_])
    KBT = DSC("KBT", [4, 128, S_])
    VBA = DSC("VBA", [NT, 128, 8 * 66])
    ZBT = DSC("ZBT", [4, 128, S_])
    GMT = DSC("GMT", [16, 128, S_])
    YAT = DSC("YAT", [8, 128, S_])
    YBT = DSC("YBT", [4, 128, S_])
    dbg = {}
    if debug:
        dbg["hT"] = nc.dram_tensor("dbg_hT", [128, 8, S_], BF16, kind="ExternalOutput").ap()
        dbg["TOKU"] = nc.dram_tensor("dbg_TOKU", [128, NT, 36], F32, kind="ExternalOutput").ap()
        dbg["TOKC"] = nc.dram_tensor("dbg_TOKC", [128, NT, 36], F32, kind="ExternalOutput").ap()
        dbg["DECB"] = nc.dram_tensor("dbg_DECB", [128, 8, NT], F32, kind="ExternalOutput").ap()
        dbg["Hacc"] = nc.dram_tensor("dbg_Hacc", [4, 128, NT, 256], F32, kind="ExternalOutput").ap()
        for nm in ("T1", "T2", "T3"):
            dbg[nm] = nc.dram_tensor("dbg_" + nm, [36, S_], F32, kind="ExternalOutput").ap()

    def dstop(tag):
        if stop_after != tag:
            return False
        S.dma("sp", "S_dbg", dbg["T1"][:, :], T1[0:36, :], reads=["T1g"])
        S.dma("sp", "S_dbg", dbg["T2"][:, :], T2[0:36, :], reads=["T2g"])
        S.dma("sp", "S_dbg", dbg["T3"][:, :], T3[0:36, :], reads=["T3g"])
        S.emit()
        return True

    SB_LO = 16512
    SB_HI = 229344
    cur = [SB_LO]

    def T(name, shape, dt):
        n = int(np.prod(shape[1:])) * (4 if dt == F32 else 2)
        n = (n + 31) // 32 * 32
        assert cur[0] + n <= SB_HI, (name, cur[0], n)
        t = nc.alloc_sbuf_tensor_at(name, list(shape), dt, offset=cur[0])
        cur[0] += n
        return t

    ps = [nc.alloc_psum_tensor("ps%d" % i, [128, 512], F32) for i in range(8)]

    def psb(i):
        return ps[i][:].bitcast(BF16)

    S = Sched(nc)

    identb = T("identb", [128, 128], BF16)
    identf = T("identf", [128, 128], F32)
    maskT = T("maskT", [128, 2, 128], F32)
    cols = T("cols", [128, 88], F32)
    TOKU = T("TOKU", [128, NT, 36], F32)
    TOKC = T("TOKC", [128, NT, 36], F32)
    DECB = T("DECB", [128, 8, NT], F32)
    gate_bc = T("gate_bc", [128, 1024], F32)
    fg_bc = T("fg_bc", [128, 1024], F32)
    ones_c = T("ones_c", [128, 128], F32)
    smallc = T("smallc", [128, 64], F32)
    REGION0 = cur[0]

    S.dma("sp", "L_const", identf[:], ident_d[:, :], writes=["identf"])
    S.dma("pool", "L_constb", identb[:], ident_d[:, :], writes=["identb"])
    S.dma("sp", "L_const", maskT[:], masks_d[:, :, :], writes=["maskT"])
    S.dma("sp", "L_const", cols[:], cols_d[:, :], writes=["cols"])
    S.op("pool", MS(ones_c[:], 1.0), writes=["ones_c"])

    cur[0] = REGION0
    modrow = T("modrow", [1, 3072], F32)
    rows = T("rows", [1, 5120], F32)
    wst = [T("wada_st%d" % i, [128, 8, 512], F32) for i in range(2)]
    A_END = cur[0]
    cur[0] = REGION0 + 65536
    G1_bc = T("G1_bc", [128, 1024], F32)
    sh_bc = T("sh_bc", [128, 1024], F32)
    c_sb = T("c_sb", [128, 8], F32)
    cond = T("cond", [128, 8], F32)
    G1row = T("G1row", [1, 1024], F32)
    assert A_END <= REGION0 + 65536

    S.dma("sp", "L_const", c_sb[:], c_d[:, :], writes=["c_sb"])
    S.dma("sp", "L_const", rows[:], rows_d[:, :], writes=["rows"])
    S.op("act", ACT(cond[:], c_sb[:], AF.Silu), reads=["c_sb"], writes=["cond"])
    wrot = Rot("wada_st", wst)
    for g in range(6):
        wt, wk = wrot.next()
        S.dma("sp", "L_wada%d" % wk[1], wt[:], wada_d[:, :, g * 512:(g + 1) * 512], writes=[wk])
        S.op("pe", [MM(ps[g % 2][0:1, :], cond[:, kc:kc + 1], wt[:, kc, :], start=(kc == 0), stop=(kc == 7)) for kc in range(8)],
             reads=["cond", wk], writes=[("ps", g % 2)])
        S.op("dve", TT(modrow[0:1, g * 512:(g + 1) * 512], ps[g % 2][0:1, :], rows[0:1, g * 512:(g + 1) * 512], ALU.add),
             reads=[("ps", g % 2), "rows"], writes=["modrow"])
    S.op("dve", STT(G1row[0:1, :], modrow[0:1, 1024:2048], 1.0, rows[0:1, 3072:4096], ALU.add, ALU.mult),
         reads=["modrow", "rows"], writes=["G1row"])
    bc_jobs = [(G1_bc, G1row[0:1, :], "G1row", "G1_bc"), (sh_bc, modrow[0:1, 0:1024], "modrow", "sh_bc"),
               (gate_bc, modrow[0:1, 2048:3072], "modrow", "gate_bc"), (fg_bc, rows[0:1, 4096:5120], "rows", "fg_bc")]
    bi = 0
    for (dst, src, skey, dkey) in bc_jobs:
        for hf in range(2):
            b = 2 + (bi % 2)
            bi += 1
            S.op("pe", MM(ps[b][:, :], ones_c[0:1, 0:128], src[:, hf * 512:(hf + 1) * 512]), reads=[skey, "ones_c"], writes=[("ps", b)])
            S.op("act", ACT(dst[:, hf * 512:(hf + 1) * 512], ps[b][:, :], AF.Copy), reads=[("ps", b)], writes=[dkey])
    S.barrier()

    cur[0] = REGION0
    hT = T("hT", [128, 8, S_], BF16)
    C_START = cur[0]
    assert C_START == REGION0 + 65536
    cur[0] = C_START + 8192 + 2 * 32
    cur[0] = (cur[0] + 4096 + 31) // 32 * 32
    xts = [T("xt%d" % i, [128, 1024], F32) for i in range(4)]
    xns = [T("xn%d" % i, [128, 1024], BF16) for i in range(3)]
    junkb = T("junkb", [128, 1024], BF16)
    xrot = Rot("xt", xts)
    xnrot = Rot("xn", xns)
    def b_stage1(tt):
        xt, xk = xrot.next()
        xn, xnk = xnrot.next()
        sc = smallc[:, (tt % 4) * 4:(tt % 4) * 4 + 4]
        sck = ("smallc", tt % 4)
        S.dma("sp", "L_xt%d" % xk[1], xt[:], x_d[tt * 128:(tt + 1) * 128, :], writes=[xk])
        S.op("act", ACT(junkb[:], xt[:], AF.Square, accum_out=sc[:, 0:1]), reads=[xk], writes=["junkb", sck])
        S.op("act", ACT(sc[:, 1:2], sc[:, 0:1], AF.Sqrt, scale=1.0 / D, bias=EPS), reads=[sck], writes=[sck])
        S.op("dve", ("reciprocal", dict(out=sc[:, 2:3], in_=sc[:, 1:2])), reads=[sck], writes=[sck])
        S.op("dve", STT(xt[:], xt[:], sc[:, 2:3], G1_bc[:], ALU.mult, ALU.mult), reads=[xk, sck, "G1_bc"], writes=[xk])
        S.op("dve" if tt % 3 != 2 else "pool", TT(xn[:], xt[:], sh_bc[:], ALU.add), reads=[xk, "sh_bc"], writes=[xnk])
        return xn, xnk

    def b_stage2(tt, xn, xnk):
        b = 6 + (tt % 2)
        S.op("pe", [TR(psb(b)[:, kc * 128:(kc + 1) * 128], xn[:, kc * 128:(kc + 1) * 128], identb[:]) for kc in range(8)],
             reads=[xnk, "identb"], writes=[("ps", b)])
        S.op("act", ACT(hT[:, :, tt * 128:(tt + 1) * 128], psb(b).rearrange("p (a b) -> p a b", a=8), AF.Copy),
             reads=[("ps", b)], writes=[("hT", tt)])

    bctx = {}
    for tt in range(NT + 1):
        if tt < NT:
            bctx[tt] = b_stage1(tt)
        if tt >= 1:
            b_stage2(tt - 1, *bctx.pop(tt - 1))
    if debug:
        S.dma("sp", "S_dbg", dbg["hT"][:, :, :], hT[:], reads=[("hT", tt) for tt in range(NT)])
    S.barrier()
    if stop_after == "B":
        S.emit()
        return nc

    cur[0] = C_START
    Wb = [T("Wb%d" % i, [128, 8, 512], BF16) for i in range(3)]
    wgb = T("wgb", [128, 8, 72], BF16)
    UA = T("UA", [128, S_ + 2], F32)
    UB = T("UB", [128, S_ + 2], F32)
    ACC = T("ACC", [128, S_], F32)
    obufs = [T("obuf%d" % i, [128, S_], BF16) for i in range(2)]
    tms = [T("tmst%d" % i, [128, 2112], BF16) for i in range(2)]
    gbc = T("gbc", [36, 2], F32)
    MPt = T("MPt", [36, NT], F32)
    MOt = T("MOt", [36, NT], F32)
    DECt = T("DECt", [36, NT], F32)
    selt = T("selt", [36, 8, 128], F32)
    C_END = cur[0]
    T1 = nc.alloc_sbuf_tensor_at("T1g", [128, S_], F32, offset=C_START + 3 * 8192 + 1152)
    T2 = nc.alloc_sbuf_tensor_at("T2g", [128, S_], F32, offset=C_START + 3 * 8192 + 1152 + 16416)
    T3 = ACC
    ONESF = nc.alloc_sbuf_tensor_at("ONESF", [128, S_], F32, offset=C_START + 3 * 8192 + 1152 + 2 * 16416 + 16384)

    wrot = Rot("Wb", Wb)
    orot = Rot("obuf", obufs)
    tmrot = Rot("tmst", tms)
    urot = Rot("U", [UA, UB])
    psrot = Rot("ps", ps[0:6])
    evtog = [0]

    S.dma("sp", "L_const", gbc[:], gb_d[:, :], writes=["gbc"])
    S.dma("sp", "L_const", selt[:], sel_d[:, :, :], writes=["selt"])
    S.dma("pool", "L_wgb", wgb[:], wgate_d[:, :, :], writes=["wgb"])
    allhT = [("hT", tt) for tt in range(NT)]

    def hkeys(tb):
        return [("hT", tb * 4 + j) for j in range(4)]

    for tb in range(NBLK):
        for gi, (Tt, col, tkey) in enumerate(((T1, 0, "T1g"), (T2, 1, "T2g"))):
            pt, pk = psrot.next()
            S.op("pe", [MM(pt[0:36, :], wgb[:, kc, gi * 36:(gi + 1) * 36], hT[:, kc, tb * 512:(tb + 1) * 512],
                           start=(kc == 0), stop=(kc == 7)) for kc in range(8)],
                 reads=["wgb"] + hkeys(tb), writes=[pk])
            S.op("act", ACT(Tt[0:36, tb * 512:(tb + 1) * 512], pt[0:36, :], AF.Identity, bias=gbc[0:36, col:col + 1]),
                 reads=[pk, "gbc"], writes=[tkey])

    if dstop("D0"):
        return nc
    r36 = slice(0, 36)
    fw = slice(0, 4)
    bw = slice(32, 36)
    S.op("act", ACT(T2[r36, :], T2[r36, :], AF.Exp, scale=-1.0), reads=["T2g"], writes=["T2g"])
    S.op("act", ACT(T2[r36, :], T2[r36, :], AF.Ln, bias=1.0), reads=["T2g"], writes=["T2g"])
    S.op("pool", MS(T3[r36, :], 0.0), writes=["T3g"])
    S.op("pool", MS(ONESF[r36, :], 1.0), writes=["ONESF"])
    S.op("pool", [MS(MPt[:], 0.0), MS(MOt[:], 0.0)], writes=["MPt", "MOt"])
    S.op("dve", SCAN(T3[fw, :], ONESF[fw, :], T2[fw, :], 0.0, ALU.mult, ALU.add),
         reads=["T2g", "ONESF"], writes=["T3g"])
    S.op("dve", SCAN(T3[bw, ::-1], ONESF[bw, :], T2[bw, ::-1], 0.0, ALU.mult, ALU.add),
         reads=["T2g", "ONESF"], writes=["T3g"])
    if dstop("D1"):
        return nc
    S.op("dve", TT(T1[r36, :], T1[r36, :], T3[r36, :], ALU.add), reads=["T1g", "T3g"], writes=["T1g"])
    S.op("dve", SCAN(T2[fw, :], ONESF[fw, :], T1[fw, :], 0.0, ALU.mult, ALU.max),
         reads=["T1g", "ONESF"], writes=["T2g"])
    S.op("dve", SCAN(T2[bw, ::-1], ONESF[bw, :], T1[bw, ::-1], 0.0, ALU.mult, ALU.max),
         reads=["T1g", "ONESF"], writes=["T2g"])
    if dstop("D2"):
        return nc
    M3 = T2[:].rearrange("p (k t) -> p k t", t=128)
    S.op("dve", [CP(MPt[fw, 1:NT], M3[fw, 0:NT - 1, 127]), CP(MOt[fw, :], M3[fw, :, 127])],
         reads=["T2g", "MPt", "MOt"], writes=["MPt", "MOt"])
    S.op("dve", [CP(MPt[bw, 0:NT - 1], M3[bw, 1:NT, 0]), CP(MOt[bw, :], M3[bw, :, 0])],
         reads=["T2g", "MPt", "MOt"], writes=["MPt", "MOt"])
    S.op("dve", TT(DECt[:], MPt[:], MOt[:], ALU.subtract), reads=["MPt", "MOt"], writes=["DECt"])
    S.op("act", ACT(DECt[:], DECt[:], AF.Exp), reads=["DECt"], writes=["DECt"])
    if dstop("D3"):
        return nc
    MPb = MPt[:].rearrange("p (k o) -> p k o", o=1).to_broadcast([36, NT, 128])
    T1v = T1[r36, :].rearrange("p (k t) -> p k t", t=128)
    T3v = T3[r36, :].rearrange("p (k t) -> p k t", t=128)
    S.op("dve", TT(T3v, T3v, MPb, ALU.subtract), reads=["T3g", "MPt"], writes=["T3g"])
    S.op("act", ACT(T3[r36, :], T3[r36, :], AF.Exp), reads=["T3g"], writes=["T3g"])
    S.op("dve", TT(T1v, T1v, MPb, ALU.subtract), reads=["T1g", "MPt"], writes=["T1g"])
    S.op("act", ACT(T1[r36, :], T1[r36, :], AF.Exp), reads=["T1g"], writes=["T1g"])
    if dstop("D4"):
        return nc
    for (src, skey, dst, dkey) in ((T1, "T1g", TOKU, "TOKU"), (T3, "T3g", TOKC, "TOKC")):
        for k0 in range(0, NT, 14):
            n = min(14, NT - k0)
            pt, pk = psrot.next()
            S.op("pe", [TR(pt[:, j * 36:(j + 1) * 36], src[0:36, (k0 + j) * 128:(k0 + j + 1) * 128], identf[0:36, 0:36]) for j in range(n)],
                 reads=[skey, "identf"], writes=[pk])
            S.op("dve", CP(dst[:, k0:k0 + n, :], pt[:, 0:n * 36].rearrange("p (a b) -> p a b", b=36)), reads=[pk], writes=[dkey])
    pt, pk = psrot.next()
    S.op("pe", [MM(pt[:, j * NT:(j + 1) * NT], selt[0:36, j, :], DECt[0:36, :]) for j in range(8)],
         reads=["selt", "DECt"], writes=[pk])
    S.op("dve", CP(DECB[:], pt[:, 0:8 * NT].rearrange("p (a b) -> p a b", b=NT)), reads=[pk], writes=["DECB"])
    if debug:
        S.dma("sp", "S_dbg", dbg["TOKU"][:, :, :], TOKU[:], reads=["TOKU"])
        S.dma("sp", "S_dbg", dbg["TOKC"][:, :, :], TOKC[:], reads=["TOKC"])
        S.dma("sp", "S_dbg", dbg["DECB"][:, :, :], DECB[:], reads=["DECB"])
    S.barrier()
    if stop_after == "D":
        S.emit()
        return nc

    S.op("pool", [MS(UA[:, 0:1], 0.0), MS(UA[:, S_ + 1:S_ + 2], 0.0), MS(UB[:, 0:1], 0.0), MS(UB[:, S_ + 1:S_ + 2], 0.0)],
         writes=[("U", 0), ("U", 1)])

    def load_w(g):
        wt, wk = wrot.next()
        S.dma("pool", "L_Wb%d" % wk[1], wt[:], win_d[g], writes=[wk])
        return wt, wk

    pend_tail = []

    def flush_tail():
        while pend_tail:
            inf = pend_tail.pop(0)
            ob, ok = orot.next()
            S.op("act", ACT(ob[:], ACC[:], AF.Silu), reads=["ACC"], writes=[ok])
            S.dma("sp", "S_obuf%d" % ok[1], inf["dst"], ob[:], reads=[ok], writes=[inf["dkey"]])

    def fm_group(g, kind, sub_info):
        wt, wk = load_w(g)
        for sub in range(4):
            info = sub_info(sub)
            if kind == "conv":
                U, uk = urot.next()
            else:
                ob, ok = orot.next()
            for tb in range(NBLK):
                pt, pk = psrot.next()
                S.op("pe", [MM(pt[:, :], wt[:, kc, sub * 128:(sub + 1) * 128], hT[:, kc, tb * 512:(tb + 1) * 512],
                               start=(kc == 0), stop=(kc == 7)) for kc in range(8)],
                     reads=[wk] + hkeys(tb), writes=[pk])
                if kind == "conv":
                    S.op("act", ACT(U[:, 1 + tb * 512:1 + (tb + 1) * 512], pt[:, :], AF.Copy), reads=[pk], writes=[uk])
                elif kind == "silu":
                    S.op("act", ACT(ob[:, tb * 512:(tb + 1) * 512], pt[:, :], AF.Silu), reads=[pk], writes=[ok])
                elif kind == "sigb":
                    S.op("act", ACT(ob[:, tb * 512:(tb + 1) * 512], pt[:, :], AF.Sigmoid, bias=info["bias"]), reads=[pk, "cols"], writes=[ok])
                elif kind == "copy":
                    evtog[0] ^= 1
                    if evtog[0]:
                        S.op("dve", TS(ob[:, tb * 512:(tb + 1) * 512], pt[:, :], info["scale"], None, ALU.mult), reads=[pk], writes=[ok])
                    else:
                        S.op("act", ACT(ob[:, tb * 512:(tb + 1) * 512], pt[:, :], AF.Copy, scale=info["scale"]), reads=[pk], writes=[ok])
            if kind == "conv":
                flush_tail()
                cg = info["cg"]
                w0 = cols[:, cg * 3 + 0:cg * 3 + 1]
                w1 = cols[:, cg * 3 + 1:cg * 3 + 2]
                w2 = cols[:, cg * 3 + 2:cg * 3 + 3]
                cb = cols[:, 48 + cg:49 + cg]
                S.op("dve", TS(ACC[:], U[:, 1:S_ + 1], w1, cb, ALU.mult, ALU.add), reads=[uk, "cols"], writes=["ACC"])
                S.op("dve", STT(ACC[:], U[:, 0:S_], w0, ACC[:], ALU.mult, ALU.add), reads=[uk, "cols", "ACC"], writes=["ACC"])
                S.op("dve", STT(ACC[:], U[:, 2:S_ + 2], w2, ACC[:], ALU.mult, ALU.add), reads=[uk, "cols", "ACC"], writes=["ACC"])
                pend_tail.append(info)
            else:
                S.dma("sp", "S_obuf%d" % ok[1], info["dst"], ob[:], reads=[ok], writes=[info["dkey"]])

    def tm_group(g, kind, col0):
        wt, wk = load_w(g)
        for t4 in range(NT // 4):
            st, sk = tmrot.next()
            if kind == "va":
                sv = st[:, 0:4 * 2 * 258].rearrange("p (t h c) -> p t h c", t=4, h=2)
                S.op("pool", [MS(sv[:, :, :, 256:257], 1.0), MS(sv[:, :, :, 257:258], 0.0)], writes=[sk])
            elif kind == "vb":
                sv = st[:, 0:4 * 8 * 66].rearrange("p (t h c) -> p t h c", t=4, h=8)
                S.op("pool", [MS(sv[:, :, :, 64:65], 1.0), MS(sv[:, :, :, 65:66], 0.0)], writes=[sk])
            else:
                sv = st[:, 0:2048].rearrange("p (t c) -> p t c", t=4)
            for j in range(4):
                tt = t4 * 4 + j
                pt, pk = psrot.next()
                S.op("pe", [MM(pt[:, :], hT[:, kc, tt * 128:(tt + 1) * 128], wt[:, kc, :], start=(kc == 0), stop=(kc == 7)) for kc in range(8)],
                     reads=[wk, ("hT", tt)], writes=[pk])
                if kind == "va":
                    S.op("dve", CP(sv[:, j, :, 0:256], pt[:, :].rearrange("p (h c) -> p h c", h=2)), reads=[pk], writes=[sk])
                elif kind == "vb":
                    S.op("dve", CP(sv[:, j, :, 0:64], pt[:, :].rearrange("p (h c) -> p h c", h=8)), reads=[pk], writes=[sk])
                else:
                    S.op("act", ACT(sv[:, j, :], pt[:, :], AF.Sigmoid), reads=[pk], writes=[sk])
            tsl = slice(t4 * 4, t4 * 4 + 4)
            if kind == "va":
                hd0 = col0
                dst = VA[tsl, :, hd0 * 258:(hd0 + 2) * 258].rearrange("t p c -> p t c")
                S.dma("sp", "S_tm%d" % sk[1], dst, st[:, 0:4 * 516].rearrange("p (t c) -> p t c", t=4), reads=[sk], writes=[("VA", t4, hd0)])
            elif kind == "vb":
                dst = VBA[tsl, :, :].rearrange("t p c -> p t c")
                S.dma("sp", "S_tm%d" % sk[1], dst, st[:, 0:4 * 528].rearrange("p (t c) -> p t c", t=4), reads=[sk], writes=[("VBA", t4)])
            else:
                dst = OG[t4 * 512:(t4 + 1) * 512, col0:col0 + 512].rearrange("(t p) c -> p t c", p=128)
                S.dma("sp", "S_tm%d" % sk[1], dst, sv, reads=[sk], writes=[("OG", t4, col0)])

    for g in (0, 1):
        fm_group(g, "conv", lambda sub, g=g: dict(cg=g * 4 + sub, dst=QT[(g * 4 + sub) // 2, :, (g * 4 + sub) % 2, :],
                                                 dkey=("QT", g * 4 + sub)))
    for g in (2, 3):
        fm_group(g, "conv", lambda sub, g=g: dict(cg=8 + (g - 2) * 4 + sub, dst=KT[((g - 2) * 4 + sub) // 2, :, ((g - 2) * 4 + sub) % 2, :],
                                                 dkey=("KT", (g - 2) * 4 + sub)))
    flush_tail()
    for g in (8, 9):
        fm_group(g, "silu", lambda sub, g=g: dict(dst=ZAT[(g - 8) * 4 + sub], dkey=("ZAT", (g - 8) * 4 + sub)))
    fm_group(13, "silu", lambda sub: dict(dst=ZBT[sub], dkey=("ZBT", sub)))
    fm_group(10, "copy", lambda sub: dict(scale=0.125, dst=QBT[sub], dkey=("QBT", sub)))
    fm_group(11, "copy", lambda sub: dict(scale=1.0, dst=KBT[sub], dkey=("KBT", sub)))
    tm_group(4, "va", 0)
    tm_group(5, "va", 2)
    tm_group(12, "vb", 0)
    tm_group(6, "og", 0)
    tm_group(7, "og", 512)
    for g in (14, 15, 16, 17):
        fm_group(g, "sigb", lambda sub, g=g: dict(bias=cols[:, 64 + (g - 14) * 4 + sub:65 + (g - 14) * 4 + sub],
                                                 dst=GMT[(g - 14) * 4 + sub], dkey=("GMT", (g - 14) * 4 + sub)))
    S.barrier()
    if stop_after == "C":
        S.emit()
        return nc

    if build_mlstm(nc, S, locals()):
        return nc
    if stop_after == "M":
        S.emit()
        return nc
    build_na(nc, S, locals())
    if stop_after == "N":
        S.emit()
        return nc
    build_final(nc, S, locals())
    S.emit()
    return nc


def build_mlstm(nc, S, E):
    T, cur, ps, psb = E["T"], E["cur"], E["ps"], E["psb"]
    identb, maskT, cols, TOKU, TOKC, DECB = E["identb"], E["maskT"], E["cols"], E["TOKU"], E["TOKC"], E["DECB"]
    QT, KT, VA, OG, ZAT, YAT = E["QT"], E["KT"], E["VA"], E["OG"], E["ZAT"], E["YAT"]
    dbg, debug = E["dbg"], E["debug"]
    cur[0] = E["REGION0"]
    qT = T("m_qT", [128, 2, S_], BF16)
    kT = T("m_kT", [128, 2, S_], BF16)
    ktok = T("m_ktok", [128, NT, 256], BF16)
    Vaug = T("m_Vaug", [128, NT, 258], BF16)
    Hacc = T("m_Hacc", [128, NT, 256], F32)
    ZATh = T("m_ZATh", [128, 2, S_], BF16)
    yaT = T("m_yaT", [128, 2, S_], BF16)
    OGt = Rot("OGt", [T("m_OGt%d" % i, [128, 4, 256], BF16) for i in range(3)])
    UVr = [Rot("UV%d" % d, [T("m_UV%d_%d" % (d, i), [128, 258], BF16) for i in range(4)]) for d in range(2)]
    Smr = [Rot("Sm%d" % d, [T("m_Sm%d_%d" % (d, i), [128, 128], BF16) for i in range(3)]) for d in range(2)]
    Zs = [T("m_Z%d" % d, [128, 2, 258], F32) for d in range(2)]
    Cbr = [Rot("Cb%d" % d, [T("m_Cb%d_%d" % (d, i), [128, 2, 258], BF16) for i in range(3)]) for d in range(2)]
    Htr = Rot("Htmp", [T("m_Htmp%d" % i, [128, 256], F32) for i in range(3)])
    hgr = Rot("hg", [T("m_hg%d" % i, [128, 256], F32) for i in range(4)])
    ytr = Rot("yatok", [T("m_yatok%d" % i, [128, 256], BF16) for i in range(4)])
    junk = T("m_junk", [128, 256], BF16)
    rcs = T("m_rcs", [128, 8, 4], F32)
    pcs = T("m_pcs", [128, 8, 4], F32)
    rci = [0]
    pci = [0]
    dcp = [((ps[4], ps[5]), [("ps", 4), ("ps", 5)]), ((ps[6], ps[7]), [("ps", 6), ("ps", 7)])]

    def loads(hd):
        S.dma("sp", "L_mq", qT[:], QT[hd], writes=["qT"])
        S.dma("sp", "L_mk", kT[:], KT[hd], writes=["kT"])
        for j0 in range(0, NT, 8):
            S.dma("sp", "L_mv", Vaug[:, j0:j0 + 8, :], VA[j0:j0 + 8, :, hd * 258:(hd + 1) * 258].rearrange("t p c -> p t c"),
                  writes=[("Vaug", j) for j in range(j0, j0 + 8)])

    loads(0)
    for hd in range(4):
        S.dma("sp", "L_mz", ZATh[:], ZAT[2 * hd:2 * hd + 2].rearrange("g p t -> p g t"), writes=["ZATh"])
        def ktok_group(k4):
            S.op("pe", [TR(psb(1)[:, (kk * 2 + c) * 128:(kk * 2 + c + 1) * 128], kT[:, c, (k4 * 4 + kk) * 128:(k4 * 4 + kk + 1) * 128], identb[:])
                        for kk in range(4) for c in range(2)], reads=["kT", "identb"], writes=[("ps", 1)])
            if k4 % 2 == 0:
                S.op("act", ACT(ktok[:, k4 * 4:(k4 + 1) * 4, :].rearrange("p a b -> p (a b)"), psb(1)[:, 0:1024], AF.Copy, scale=1.0 / 16),
                     reads=[("ps", 1)], writes=[("ktok", k4)])
            else:
                S.op("dve", TS(ktok[:, k4 * 4:(k4 + 1) * 4, :].rearrange("p a b -> p (a b)"), psb(1)[:, 0:1024], 1.0 / 16, None, ALU.mult),
                     reads=[("ps", 1)], writes=[("ktok", k4)])

        ktok_group(0)
        ktok_group(7)
        if E["stop_after"] == "M0":
            S.emit()
            return True

        ctx = {}
        chain = [dict(kprev=None) for _ in range(2)]

        def kof(d, i):
            return i if d == 0 else NT - 1 - i

        def opUV(d, i):
            k = kof(d, i)
            ucol = TOKU[:, k, d * 32 + hd:d * 32 + hd + 1]
            UVb, uvk = UVr[d].next()
            S.op("dve", TS(UVb[:], Vaug[:, k, :], ucol, None, ALU.mult), reads=[("Vaug", k), "TOKU"], writes=[uvk])
            ctx[(d, i)] = dict(k=k, ch=slice(k * 128, (k + 1) * 128), UVb=UVb, uvk=uvk, cb=None, cbk=None)

        def opST(d, i):
            c_ = ctx[(d, i)]
            stp = ps[0][:, d * 128:(d + 1) * 128]
            S.op("pe", [MM(stp, kT[:, c, c_["ch"]], qT[:, c, c_["ch"]], start=(c == 0), stop=(c == 1)) for c in range(2)],
                 reads=["kT", "qT"], writes=[("ps", 0)])

        def opMASK(d, i):
            c_ = ctx[(d, i)]
            Sm, smk = Smr[d].next()
            S.op("dve", TT(Sm[:], ps[0][:, d * 128:(d + 1) * 128], maskT[:, d, :], ALU.mult), reads=[("ps", 0), "maskT"], writes=[smk])
            c_["Sm"], c_["smk"] = Sm, smk

        def opDC(d, i):
            c_ = ctx[(d, i)]
            (b0, b1), dks = dcp[d]
            k = c_["k"]
            S.op("pe", [MM(b0[:, 0:258], ktok[:, k, 0:128], c_["UVb"][:]), MM(b1[:, 0:258], ktok[:, k, 128:256], c_["UVb"][:])],
                 reads=[("ktok", k // 4), c_["uvk"]], writes=dks)

        def opZ(d, i):
            c_ = ctx[(d, i)]
            (b0, b1), dks = dcp[d]
            series = d * 4 + hd
            k = c_["k"]
            Z = Zs[d]
            zk = ("Z", d)
            if i == 0:
                S.op("dve", [CP(Z[:, 0, :], b0[:, 0:258]), CP(Z[:, 1, :], b1[:, 0:258])], reads=dks, writes=[zk])
            else:
                kp = chain[d]["kprev"]
                dprev = DECB[:, series, kp:kp + 1]
                S.op("dve", [STT(Z[:, 0, :], Z[:, 0, :], dprev, b0[:, 0:258], ALU.mult, ALU.add),
                             STT(Z[:, 1, :], Z[:, 1, :], dprev, b1[:, 0:258], ALU.mult, ALU.add)],
                     reads=dks + [zk, "DECB"], writes=[zk])
            cbn, cbk = Cbr[d].next()
            dcur = DECB[:, series, k:k + 1]
            S.op("act", ACT(cbn[:].rearrange("p a b -> p (a b)"), Z[:].rearrange("p a b -> p (a b)"), AF.Copy, scale=dcur), reads=[zk, "DECB"], writes=[cbk])
            c_["cb"], c_["cbk"] = cbn, cbk
            chain[d]["kprev"] = k

        def opNP(d, i):
            c_ = ctx[(d, i)]
            first = (i == 0)
            npb, npk = ps[2 + d], ("ps", 2 + d)
            npa = npb[:, 0:258]
            specs = [MM(npa, c_["Sm"][:], c_["UVb"][:], start=True, stop=first)]
            rd = [c_["smk"], c_["uvk"]]
            if not first:
                pv = ctx[(d, i - 1)]
                specs += [MM(npa, qT[:, c, c_["ch"]], pv["cb"][:, c, :], start=False, stop=(c == 1)) for c in range(2)]
                rd += ["qT", pv["cbk"]]
            S.op("pe", specs, reads=rd, writes=[npk])

        def opOUT(d, i):
            c_ = ctx[(d, i)]
            k = c_["k"]
            npb, npk = ps[2 + d], ("ps", 2 + d)
            j = rci[0] % 8
            rci[0] += 1
            rc = rcs[:, j, :]
            rck = ("rc", j)
            den = npb[:, 256:257]
            ccol = TOKC[:, k, d * 32 + hd:d * 32 + hd + 1]
            S.op("dve", TT(rc[:, 0:1], den, ccol, ALU.max), reads=[npk, "TOKC"], writes=[rck])
            S.op("dve", STT(rc[:, 1:2], den, -1.0, rc[:, 0:1], ALU.mult, ALU.max), reads=[npk, rck], writes=[rck])
            S.op("dve", ("reciprocal", dict(out=rc[:, 2:3], in_=rc[:, 1:2])), reads=[rck], writes=[rck])
            if i < NT // 2:
                S.op("act", ACT(Hacc[:, k, :], npb[:, 0:256], AF.Copy, scale=rc[:, 2:3]), reads=[npk, rck], writes=[("Hacc", k)])
            else:
                ht, htk = Htr.next()
                S.op("act", ACT(ht[:], npb[:, 0:256], AF.Copy, scale=rc[:, 2:3]), reads=[npk, rck], writes=[htk])
                S.op("pool", TT(Hacc[:, k, :], Hacc[:, k, :], ht[:], ALU.add), reads=[htk, ("Hacc", k)], writes=[("Hacc", k)])
            if i >= 1:
                ctx.pop((d, i - 1))

        for d in range(2):
            opUV(d, 0)
        for d in range(2):
            opUV(d, 1)
        for d in range(2):
            opST(d, 0)
        for d in range(2):
            opMASK(d, 0)
        for d in range(2):
            opDC(d, 0)
        for d in range(2):
            opZ(d, 0)
        for i in range(NT):
            nx = i + 1
            if i in (1, 5, 9):
                g = (i + 3) // 4
                ktok_group(g)
                ktok_group(7 - g)
            if i + 2 < NT:
                opUV(0, i + 2)
                opUV(1, i + 2)
            if nx < NT:
                opST(0, nx)
                opST(1, nx)
                opMASK(0, nx)
                opMASK(1, nx)
                if nx < NT - 1:
                    opDC(0, nx)
                    opDC(1, nx)
            opNP(0, i)
            opNP(1, i)
            if nx < NT - 1:
                opZ(0, nx)
            opOUT(0, i)
            if nx < NT - 1:
                opZ(1, nx)
            opOUT(1, i)

        if debug:
            S.dma("sp", "S_dbg", dbg["Hacc"][hd], Hacc[:], reads=[("Hacc", k) for k in range(NT)])
        if hd + 1 < 4:
            loads(hd + 1)

        pctx = {}

        def P1(k):
            k4, j = k // 4, k % 4
            if j == 0:
                ogt, ogk = OGt.next()
                S.dma("pool", "L_og%d" % ogk[1], ogt[:], OG[k4 * 512:(k4 + 1) * 512, hd * 256:(hd + 1) * 256].rearrange("(t p) c -> p t c", p=128),
                      writes=[ogk])
                pctx["og"] = (ogt, ogk)
            ogt, ogk = pctx["og"]
            hg, hgk = hgr.next()
            S.op("dve", TT(hg[:], Hacc[:, k, :], ogt[:, j, :], ALU.mult), reads=[("Hacc", k), ogk], writes=[hgk])
            q = pci[0] % 8
            pci[0] += 1
            pc = pcs[:, q, :]
            pck = ("pc", q)
            S.op("act", ACT(junk[:], hg[:], AF.Square, accum_out=pc[:, 0:1]), reads=[hgk], writes=["m_junk", pck])
            S.op("act", ACT(pc[:, 1:2], pc[:, 0:1], AF.Sqrt, scale=1.0 / DH, bias=EPS), reads=[pck], writes=[pck])
            pctx[k] = (hg, hgk, pc, pck)

        def P2(k):
            k4, j = k // 4, k % 4
            hg, hgk, pc, pck = pctx.pop(k)
            S.op("dve", ("reciprocal", dict(out=pc[:, 2:3], in_=pc[:, 1:2])), reads=[pck], writes=[pck])
            yt, ytk = ytr.next()
            S.op("dve", TS(yt[:], hg[:], pc[:, 2:3], None, ALU.mult), reads=[hgk, pck], writes=[ytk])
            S.op("pe", [TR(psb(7)[:, c * 512 + j * 128:c * 512 + (j + 1) * 128], yt[:, c * 128:(c + 1) * 128], identb[:]) for c in range(2)],
                 reads=[ytk, "identb"], writes=[("ps", 7)])
            if j == 3:
                for c in range(2):
                    S.op("dve", STT(yaT[:, c, k4 * 512:(k4 + 1) * 512], psb(7)[:, c * 512:(c + 1) * 512], cols[:, 80 + hd * 2 + c:81 + hd * 2 + c],
                                    ZATh[:, c, k4 * 512:(k4 + 1) * 512], ALU.mult, ALU.mult),
                         reads=[("ps", 7), "cols", "ZATh"], writes=[("yaT", c)])

        for k in range(NT + 2):
            if k < NT:
                P1(k)
            if k >= 2:
                P2(k - 2)
        for c in range(2):
            S.dma("pool", "S_yaT", YAT[hd * 2 + c], yaT[:, c, :], reads=[("yaT", c)])
        if E["stop_after"] == "M2":
            S.emit()
            return True
    S.barrier()


def build_na(nc, S, E):
    T, cur, ps, psb = E["T"], E["cur"], E["ps"], E["psb"]
    identb = E["identb"]
    QBT, KBT, VBA, ZBT, YBT, bt2_d = E["QBT"], E["KBT"], E["VBA"], E["ZBT"], E["YBT"], E["bt2_d"]
    cur[0] = E["REGION0"]
    bt2b = T("n_bt2", [128, 4, 14, 2, 64], BF16)
    QBD = T("n_QBD", [128, 2, 2, S_], BF16)
    kbT = T("n_kbT", [128, 2, S_], BF16)
    ZBh = T("n_ZBh", [128, 2, S_], BF16)
    ybT = T("n_ybT", [128, 2, S_], BF16)
    VE = T("n_VE", [128, NT, 4, 66], BF16)
    VO = T("n_VO", [128, NT - 1, 4, 66], BF16)
    PTr = Rot("PT", [T("n_PT%d" % i, [128, 1024], BF16) for i in range(2)])
    otr = Rot("otok", [T("n_otok%d" % i, [64, 4, 64], BF16) for i in range(3)])
    recs = T("n_rec", [64, 4, 4], F32)
    str_ = Rot("psS", [(ps[0], ps[1]), (ps[2], ps[3])])
    pvr = Rot("psPV", [ps[4], ps[5]])
    trr = Rot("psT", [6, 7])
    ri = [0]
    NA_END = cur[0]
    wpab = T("f_wpa", [128, 8, 1024], BF16)
    wpbb = T("f_wpb", [128, 4, 1024], BF16)
    woutb = T("f_wout", [128, 8, 1024], BF16)
    wsr = Rot("f_wst", [T("f_wst%d" % i, [128, 2, 1024], F32) for i in range(1)])
    S.shared_fw = (NA_END, wpab, wpbb, woutb)
    gate_bc = E["gate_bc"]
    S.dma("pool", "L_bt2", bt2b[:], bt2_d[:, :, :, :, :], writes=["bt2b"])
    S.op("pool", [MS(QBD[0:64, :, 1, :], 0.0), MS(QBD[64:128, :, 0, :], 0.0)], writes=["QBDz"])
    for half in range(2):
        c0 = half * 4 * 66
        for cq in range(4):
            tsl = slice(cq * 1024, (cq + 1) * 1024)
            S.dma("sp", "L_nq%d" % cq, QBD[0:64, :, 0, tsl], QBT[2 * half:2 * half + 2, 0:64, tsl].rearrange("g p t -> p g t"), reads=["QBDz"], writes=[("qbT", cq)])
            S.dma("sp", "L_nq%d" % cq, QBD[64:128, :, 1, tsl], QBT[2 * half:2 * half + 2, 64:128, tsl].rearrange("g p t -> p g t"), reads=["QBDz"], writes=[("qbT", cq)])
            S.dma("sp", "L_nk%d" % cq, kbT[:, :, tsl], KBT[2 * half:2 * half + 2, :, tsl].rearrange("g p t -> p g t"), writes=[("kbT", cq)])
            j0 = cq * 8
            S.dma("sp", "L_nve%d" % cq, VE[:, j0:j0 + 8, :, :].rearrange("p t h c -> p t (h c)"),
                  VBA[j0:j0 + 8, :, c0:c0 + 264].rearrange("t p c -> p t c"), writes=[("VE", cq)])
            j1 = min(j0 + 8, NT - 1)
            S.dma("sp", "L_nvo%d" % cq, VO[0:64, j0:j1, :, :].rearrange("p t h c -> p t (h c)"),
                  VBA[j0:j1, 64:128, c0:c0 + 264].rearrange("t p c -> p t c"), writes=[("VO", cq)])
            S.dma("sp", "L_nvo%d" % cq, VO[64:128, j0:j1, :, :].rearrange("p t h c -> p t (h c)"),
                  VBA[j0 + 1:j1 + 1, 0:64, c0:c0 + 264].rearrange("t p c -> p t c"), writes=[("VO", cq)])
            S.dma("sp", "L_nz%d" % cq, ZBh[:, :, tsl], ZBT[2 * half:2 * half + 2, :, tsl].rearrange("g p t -> p g t"), writes=[("ZBh", cq)])

        def n_stage1(r):
            rs = min(max(r - 4, 0), 56)
            j0b = rs - r + 7
            qs = slice(r * 64, (r + 1) * 64)
            pair, sk = str_.next()
            specs = []
            for gi in range(2):
                bank = pair[gi]
                hp = half * 2 + gi
                specs.append(MM(bank[:, 0:512], identb[:], bt2b[:, hp, j0b:j0b + 7:2, :, :].rearrange("p i j q -> p i (j q)"), start=True, stop=False))
                for i in range(4):
                    tok = rs * 64 + i * 128
                    specs.append(MM(bank[:, i * 128:(i + 1) * 128], kbT[:, gi, tok:tok + 128], QBD[:, gi, :, qs], start=False, stop=(i == 3)))
            S.op("pe", specs, reads=["identb", "bt2b", ("qbT", (r * 64) // 1024)] + [("kbT", cc) for cc in sorted({(rs * 64) // 1024, (rs * 64 + 511) // 1024})], writes=[sk])
            PT, ptk = PTr.next()
            S.op("act", [ACT(PT[:, 0:512], pair[0][:, :], AF.Exp), ACT(PT[:, 512:1024], pair[1][:, :], AF.Exp)], reads=[sk], writes=[ptk])
            return dict(r=r, rs=rs, qs=qs, PT=PT, ptk=ptk)

        def n_stage2(c):
            rs, PT, ptk = c["rs"], c["PT"], c["ptk"]
            if rs % 2 == 0:
                Vx, vnm, tbase = VE, "VE", rs // 2
            else:
                Vx, vnm, tbase = VO, "VO", (rs - 1) // 2
            vkeys = [(vnm, cc) for cc in sorted({tbase // 8, (tbase + 3) // 8})]
            ob, ok = pvr.next()
            specs = []
            for hh in range(4):
                for i in range(4):
                    specs.append(MM(ob[0:64, hh * 66:(hh + 1) * 66], PT[:, (hh // 2) * 512 + i * 128 + (hh % 2) * 64:(hh // 2) * 512 + i * 128 + (hh % 2) * 64 + 64], Vx[:, tbase + i, hh, :],
                                    start=(i == 0), stop=(i == 3)))
            S.op("pe", specs, reads=[ptk] + vkeys, writes=[ok])
            q = ri[0] % 4
            ri[0] += 1
            rec = recs[:, q, :]
            rk = ("rec", q)
            ov = ob[0:64, 0:264].rearrange("p (h c) -> p h c", c=66)
            S.op("dve", ("reciprocal", dict(out=rec, in_=ov[:, :, 64])), reads=[ok], writes=[rk])
            ot, otk = otr.next()
            S.op("dve", TT(ot[:], ov[:, :, 0:64], rec.rearrange("p (h o) -> p h o", o=1).to_broadcast([64, 4, 64]), ALU.mult),
                 reads=[ok, rk], writes=[otk])
            c["ot"], c["otk"] = ot, otk

        def n_stage3(c):
            ot, otk, qs = c["ot"], c["otk"], c["qs"]
            tb_, tk = trr.next()
            otf = ot[:].rearrange("p h c -> p (h c)")
            S.op("pe", [TR(psb(tb_)[:, g * 64:(g + 1) * 64], otf[:, g * 128:(g + 1) * 128], identb[0:64, 0:64]) for g in range(2)],
                 reads=[otk, "identb"], writes=[tk])
            S.op("dve", TT(ybT[:, :, qs], psb(tb_)[:, 0:128].rearrange("p (g q) -> p g q", g=2), ZBh[:, :, qs], ALU.mult),
                 reads=[tk, ("ZBh", c["r"] * 64 // 1024)], writes=["ybT"])

        nctx = {}
        if half == 0:
            S.dma("pool", "L_fwa", wpab[:], E["wpa_d"][:, :, :], writes=["wpab"])
            S.dma("pool", "L_fwb", wpbb[:], E["wpb_d"][:, :, :], writes=["wpbb"])
            for j in range(4):
                wt, wk = wsr.next()
                S.dma("sp", "L_fws%d" % wk[1], wt[:], E["wout_d"][:, 2 * j:2 * j + 2, :], writes=[wk])
                S.op("dve", TT(woutb[:, 2 * j:2 * j + 2, :], wt[:], gate_bc[:].rearrange("p (o n) -> p o n", o=1).to_broadcast([128, 2, 1024]), ALU.mult),
                     reads=[wk, "gate_bc"], writes=["woutb"])
        for r in range(64 + 2):
            if r < 64:
                nctx[r] = n_stage1(r)
            if 0 <= r - 1 < 64:
                n_stage2(nctx[r - 1])
            if 0 <= r - 2 < 64:
                n_stage3(nctx.pop(r - 2))
        for g in range(2):
            S.dma("sp", "S_ybT", YBT[2 * half + g], ybT[:, g, :], reads=["ybT"])
    S.barrier()


def build_final(nc, S, E):
    T, cur, ps = E["T"], E["cur"], E["ps"]
    gate_bc, fg_bc, smallc = E["gate_bc"], E["fg_bc"], E["smallc"]
    YAT, YBT, GMT, x_d, y_d = E["YAT"], E["YBT"], E["GMT"], E["x_d"], E["y_d"]
    wpa_d, wpb_d, wout_d = E["wpa_d"], E["wpb_d"], E["wout_d"]
    cur[0] = E["REGION0"]
    NA_END, wpab, wpbb, woutb = S.shared_fw
    yar = Rot("f_ya", [T("f_ya%d" % i, [128, 8, 512], BF16) for i in range(2)])
    ybr = Rot("f_yb", [T("f_yb%d" % i, [128, 4, 512], BF16) for i in range(2)])
    gmr = Rot("f_gm", [T("f_gm%d" % i, [128, 16, 512], BF16) for i in range(2)])
    t1r = Rot("f_t1", [T("f_t1%d" % i, [128, 512], F32) for i in range(2)])
    t2r = Rot("f_t2", [T("f_t2%d" % i, [128, 512], F32) for i in range(2)])
    mgr = Rot("f_mg", [T("f_mg%d" % i, [128, 8, 512], BF16) for i in range(2)])
    xr = Rot("f_xt", [T("f_xt%d" % i, [128, 1024], F32) for i in range(5)])
    x2r = Rot("f_x2", [T("f_x2%d" % i, [128, 1024], F32) for i in range(2)])
    otr = Rot("f_ot", [T("f_ot%d" % i, [128, 1024], F32) for i in range(2)])
    junk = T("f_junk", [128, 1024], BF16)
    par = Rot("psP", [ps[0], ps[1], ps[2], ps[3]])
    outr = Rot("psO", [(ps[4], ps[5]), (ps[6], ps[7])])
    assert cur[0] <= NA_END, (cur[0], NA_END)
    def f_loads(tb):
        ts = slice(tb * 512, (tb + 1) * 512)
        ya, yak = yar.next()
        yb, ybk = ybr.next()
        gm, gmk = gmr.next()
        S.dma("sp", "L_fya%d" % yak[1], ya[:], YAT[:, :, ts].rearrange("g p t -> p g t"), writes=[yak])
        S.dma("sp", "L_fyb%d" % ybk[1], yb[:], YBT[:, :, ts].rearrange("g p t -> p g t"), writes=[ybk])
        S.dma("sp", "L_fgm%d" % gmk[1], gm[:], GMT[:, :, ts].rearrange("g p t -> p g t"), writes=[gmk])
        return ya, yak, yb, ybk, gm, gmk

    fl = {0: f_loads(0)}
    mgs = {}

    def stageP(tb):
        ya, yak, yb, ybk, gm, gmk = fl.pop(tb)
        if tb + 1 < NBLK:
            fl[tb + 1] = f_loads(tb + 1)
        mg, mgk = mgr.next()
        for fg in range(8):
            fs = slice(fg * 128, (fg + 1) * 128)
            pa, pak = par.next()
            S.op("pe", [MM(pa[:, :], wpab[:, kc, fs], ya[:, kc, :], start=(kc == 0), stop=(kc == 7)) for kc in range(8)],
                 reads=["wpab", yak], writes=[pak])
            pb_, pbk = par.next()
            S.op("pe", [MM(pb_[:, :], wpbb[:, kc, fs], yb[:, kc, :], start=(kc == 0), stop=(kc == 3)) for kc in range(4)],
                 reads=["wpbb", ybk], writes=[pbk])
            t1, t1k = t1r.next()
            t2, t2k = t2r.next()
            S.op("dve", TT(t1[:], pa[:, :], gm[:, fg, :], ALU.mult), reads=[pak, gmk], writes=[t1k])
            S.op("dve", TT(t2[:], pb_[:, :], gm[:, 8 + fg, :], ALU.mult), reads=[pbk, gmk], writes=[t2k])
            S.op("pool", TT(mg[:, fg, :], t1[:], t2[:], ALU.add), reads=[t1k, t2k], writes=[(mgk, fg)])
        mgs[tb] = (mg, mgk)

    def stageO(tb):
        mg, mgk = mgs.pop(tb)
        xtl = []
        for tt in range(4):
            tile = tb * 4 + tt
            xt, xk = xr.next()
            S.dma("sp", "L_fx%d" % xk[1], xt[:], x_d[tile * 128:(tile + 1) * 128, :], writes=[xk])
            xtl.append((xt, xk))
        for tt in range(4):
            tile = tb * 4 + tt
            xt, xk = xtl[tt]
            (o0, o1), ok = outr.next()
            specs = []
            for nh, ob in enumerate((o0, o1)):
                for fg in range(8):
                    specs.append(MM(ob[:, :], mg[:, fg, tt * 128:(tt + 1) * 128], woutb[:, fg, nh * 512:(nh + 1) * 512], start=(fg == 0), stop=(fg == 7)))
            S.op("pe", specs, reads=[(mgk, fg) for fg in range(8)] + ["woutb"], writes=[ok])
            x2, x2k = x2r.next()
            S.op("dve", [TT(x2[:, 0:512], o0[:, :], xt[:, 0:512], ALU.add), TT(x2[:, 512:1024], o1[:, :], xt[:, 512:1024], ALU.add)],
                 reads=[ok, xk], writes=[x2k])
            q = tile % 4
            sc = smallc[:, q * 4:q * 4 + 4]
            sck = ("smallc", q)
            S.op("act", ACT(junk[:], x2[:], AF.Square, accum_out=sc[:, 0:1]), reads=[x2k], writes=["f_junk", sck])
            S.op("act", ACT(sc[:, 1:2], sc[:, 0:1], AF.Sqrt, scale=1.0 / D, bias=EPS), reads=[sck], writes=[sck])
            S.op("dve", ("reciprocal", dict(out=sc[:, 2:3], in_=sc[:, 1:2])), reads=[sck], writes=[sck])
            ot, otk = otr.next()
            S.op("act", ACT(ot[:], x2[:], AF.Copy, scale=sc[:, 2:3]), reads=[x2k, sck], writes=[otk])
            S.op("pool", TT(ot[:], ot[:], fg_bc[:], ALU.mult), reads=[otk, "fg_bc"], writes=[otk])
            S.dma("pool", "S_fo%d" % otk[1], y_d[tile * 128:(tile + 1) * 128, :], ot[:], reads=[otk])

    for tb in range(NBLK + 1):
        if tb < NBLK:
            stageP(tb)
        if tb >= 1:
            stageO(tb - 1)


def _shared_layouts(inp):
    f = np.float32
    w_ada = np.asarray(inp["w_ada"], f)[0]
    w_in = np.asarray(inp["w_in"], f)[0]
    sh = {}
    sh["wada"] = np.ascontiguousarray(w_ada.reshape(8, 128, 3072).transpose(1, 0, 2))
    sh["rows"] = np.ascontiguousarray(np.concatenate(
        [np.asarray(inp["b_ada"], f)[0], np.asarray(inp["norm_gain"], f)[0], np.asarray(inp["final_gain"], f)])[None, :])
    wg = np.zeros((1024, 72), f)
    gc = w_in[:, 5120:5136].reshape(1024, 2, 2, 4)
    wg[:, 0:4] = gc[:, 0, 0]
    wg[:, 32:36] = gc[:, 1, 0]
    wg[:, 36:40] = gc[:, 0, 1]
    wg[:, 68:72] = gc[:, 1, 1]
    sh["wgate"] = np.ascontiguousarray(wg.reshape(8, 128, 72).transpose(1, 0, 2))
    wl = np.concatenate([w_in[:, 0:5120], w_in[:, 5136:9232]], axis=1)
    sh["win"] = np.ascontiguousarray(wl.reshape(8, 128, 18, 512).transpose(2, 1, 0, 3))
    cols = np.zeros((128, 88), f)
    cw = np.asarray(inp["conv_w"], f)[0]
    cb = np.asarray(inp["conv_b"], f)[0]
    cols[:, 0:48] = cw.reshape(3, 16, 128).transpose(2, 1, 0).reshape(128, 48)
    cols[:, 48:64] = cb.reshape(16, 128).T
    cols[:, 64:80] = np.asarray(inp["b_merge"], f)[0].reshape(16, 128).T
    cols[:, 80:88] = np.asarray(inp["mlstm_norm_gain"], f)[0].reshape(8, 128).T
    sh["cols"] = cols
    gb = np.zeros((36, 2), f)
    bi = np.asarray(inp["b_igate"], f)[0]
    bfg = np.asarray(inp["b_fgate"], f)[0]
    gb[0:4, 0] = bi[0]
    gb[32:36, 0] = bi[1]
    gb[0:4, 1] = bfg[0]
    gb[32:36, 1] = bfg[1]
    sh["gb"] = gb
    rpb = np.asarray(inp["rpb"], f)[0]
    kc = np.arange(64)[:, None]
    qc = np.arange(64)[None, :]
    ws = np.clip(qc - 8, 0, 48)
    colok = (kc >= ws) & (kc < ws + 16)
    dcidx = np.clip(kc - qc + 15, 0, 30)
    bt2 = np.full((128, 8, 14, 64), NEG, f)
    for j in range(14):
        for half in range(2):
            dr = j - 7 + half
            tab = np.where(colok[None], rpb[:, dr + 7][:, dcidx], f(NEG))
            bt2[half * 64:(half + 1) * 64, :, j, :] = tab.transpose(1, 0, 2)
    sh["bt2"] = np.ascontiguousarray(bt2.reshape(128, 4, 2, 14, 64).transpose(0, 1, 3, 2, 4))
    sh["wpa"] = np.ascontiguousarray(np.asarray(inp["w_proj_a"], f)[0].reshape(8, 128, 1024).transpose(1, 0, 2))
    sh["wpb"] = np.ascontiguousarray(np.asarray(inp["w_proj_b"], f)[0].reshape(4, 128, 1024).transpose(1, 0, 2))
    sh["wout"] = np.ascontiguousarray(np.asarray(inp["w_out"], f)[0].reshape(8, 128, 1024).transpose(1, 0, 2))
    sh["ident"] = np.eye(128, dtype=f)
    s_i = np.arange(128)[:, None]
    t_i = np.arange(128)[None, :]
    masks = np.zeros((128, 2, 128), f)
    masks[:, 0, :] = np.where(s_i <= t_i, 1.0 / 16, 0.0)
    masks[:, 1, :] = np.where(s_i >= t_i, 1.0 / 16, 0.0)
    sh["masks"] = masks
    sel = np.zeros((36, 8, 128), f)
    for j in range(8):
        sel[(j % 4) + 32 * (j // 4), j, :] = 1.0
    sh["sel"] = sel
    return sh


def make_in_maps(inp):
    sh = _shared_layouts(inp)
    x = np.asarray(inp["x"], np.float32)
    c = np.asarray(inp["c"], np.float32)
    maps = []
    for b in range(8):
        m = dict(sh)
        m["x"] = np.ascontiguousarray(x[b])
        m["c_l"] = np.ascontiguousarray(c[b].reshape(8, 128).T)
        maps.append(m)
    return maps


_NC_CACHE = {}


def kernel(**inputs):
    if "nc" not in _NC_CACHE:
        _NC_CACHE["nc"] = build_program()
    nc = _NC_CACHE["nc"]
    in_maps = make_in_maps(inputs)
    res = run_bass_kernel_spmd(nc, in_maps, core_ids=list(range(8)))
    return np.stack([np.asarray(r["y"], np.float32) for r in res.results], axis=0)
```

```python
import numpy as np
import concourse.bass as bass
import concourse.mybir as mybir
from concourse.bass_utils import run_bass_kernel_spmd

F32 = mybir.dt.float32
BF16 = mybir.dt.bfloat16
ALU = mybir.AluOpType
AF = mybir.ActivationFunctionType

S_ = 4096
D = 1024
NT = 32
NBLK = 8
H = 4
DH = 256
NH = 8
NEG = -30000.0
EPS = 1e-6
ENG_NAMES = ("pe", "act", "dve", "pool", "sp")


class Sched:
    def __init__(self, nc):
        self.nc = nc
        self.ops = {e: [] for e in ENG_NAMES}
        self.count = {}
        self.last_writer = {}
        self.readers = {}
        self.seen = {e: {} for e in ENG_NAMES}
        self.sem_names = ["pe", "act", "dve", "pool"]
        self.is_dma = set()
        self.n_instr = 0

    def _deps(self, reads, writes):
        deps = set()
        for k in reads:
            w = self.last_writer.get(k)
            if w is not None:
                deps.add(w)
        for k in writes:
            w = self.last_writer.get(k)
            if w is not None:
                deps.add(w)
            deps.update(self.readers.get(k, ()))
        return deps

    def _record(self, me, reads, writes):
        for k in reads:
            self.readers.setdefault(k, []).append(me)
        for k in writes:
            self.last_writer[k] = me
            self.readers[k] = []

    def _waits(self, eng, deps):
        need = {}
        for (s, i) in deps:
            if s in self.is_dma:
                i = self.count[s] - 1
            if need.get(s, -1) < i:
                need[s] = i
        waits = []
        for s, i in need.items():
            if self.seen[eng].get(s, -1) >= i:
                continue
            self.seen[eng][s] = i
            waits.append((s, i + 1))
        return waits

    def op(self, eng, specs, reads=(), writes=()):
        if isinstance(specs, tuple):
            specs = [specs]
        waits = self._waits(eng, self._deps(reads, writes))
        idx = self.count.get(eng, 0)
        self.count[eng] = idx + 1
        self.ops[eng].append((specs, waits, (eng, 1)))
        self._record((eng, idx), reads, writes)
        self.n_instr += len(specs)

    def dma(self, queue, stream, out, in_, reads=(), writes=()):
        if stream not in self.is_dma:
            self.is_dma.add(stream)
            self.sem_names.append(stream)
            self.count[stream] = 0
        waits = self._waits(queue, self._deps(reads, writes))
        idx = self.count[stream]
        self.count[stream] = idx + 1
        self.ops[queue].append(([("dma_start", dict(out=out, in_=in_))], waits, (stream, 16)))
        self._record((stream, idx), reads, writes)
        self.n_instr += 1

    def barrier(self):
        allw = [(s, c) for s, c in self.count.items() if c > 0]
        for e in ENG_NAMES:
            waits = []
            for s, c in allw:
                if self.seen[e].get(s, -1) >= c - 1:
                    continue
                self.seen[e][s] = c - 1
                waits.append((s, c))
            if waits:
                self.ops[e].append((None, waits, None))
        self.last_writer = {}
        self.readers = {}

    def emit(self):
        import contextlib
        nc = self.nc
        self.barrier()
        with contextlib.ExitStack() as st:
            sems = {s: st.enter_context(nc.semaphore("s_" + s)) for s in self.sem_names}
            block = st.enter_context(nc.Block())

            def run(engname):
                def body(eng):
                    for specs, waits, inc in self.ops[engname]:
                        for (s, v) in waits:
                            eng.wait_ge(sems[s], v * (16 if s in self.is_dma else 1))
                        if specs is None:
                            continue
                        ins = None
                        for (m, kw) in specs:
                            ins = getattr(eng, m)(**kw)
                        ins.then_inc(sems[inc[0]], inc[1])
                return body

            block.tensor(run("pe"))
            block.scalar(run("act"))
            block.vector(run("dve"))
            block.gpsimd(run("pool"))
            block.sync(run("sp"))


class Rot:
    def __init__(self, name, tiles):
        self.name, self.tiles, self.i = name, tiles, 0

    def next(self):
        j = self.i % len(self.tiles)
        self.i += 1
        return self.tiles[j], (self.name, j)


def MM(out, lhsT, rhs, start=True, stop=True):
    return ("matmul", dict(out=out, lhsT=lhsT, rhs=rhs, start=start, stop=stop))


def TR(out, in_, identity):
    return ("transpose", dict(out=out, in_=in_, identity=identity))


def ACT(out, in_, func, **kw):
    return ("activation", dict(out=out, in_=in_, func=func, **kw))


def TT(out, in0, in1, op):
    return ("tensor_tensor", dict(out=out, in0=in0, in1=in1, op=op))


def TS(out, in0, scalar1, scalar2, op0, op1=None):
    d = dict(out=out, in0=in0, scalar1=scalar1, scalar2=scalar2, op0=op0)
    if op1 is not None:
        d["op1"] = op1
    return ("tensor_scalar", d)


def STT(out, in0, scalar, in1, op0, op1):
    return ("scalar_tensor_tensor", dict(out=out, in0=in0, scalar=scalar, in1=in1, op0=op0, op1=op1))


def CP(out, in_):
    return ("tensor_copy", dict(out=out, in_=in_))


def MS(ap, v):
    return ("memset", dict(ap=ap, constant=v))


def SCAN(out, data0, data1, initial, op0, op1):
    return ("tensor_tensor_scan", dict(out=out, data0=data0, data1=data1, initial=initial, op0=op0, op1=op1))


def build_program(stop_after=None, debug=False):
    nc = bass.Bass("TRN2", target_bir_lowering=False)
    dbg_kind = "ExternalOutput" if debug else "Internal"

    def DIN(name, shape, dt=F32):
        return nc.dram_tensor(name, list(shape), dt, kind="ExternalInput").ap()

    def DSC(name, shape, dt=BF16):
        return nc.dram_tensor(name, list(shape), dt, kind=dbg_kind).ap()

    x_d = DIN("x", [S_, D])
    c_d = DIN("c_l", [128, 8])
    wada_d = DIN("wada", [128, 8, 3072])
    rows_d = DIN("rows", [1, 5120])
    wgate_d = DIN("wgate", [128, 8, 72])
    win_d = DIN("win", [18, 128, 8, 512])
    cols_d = DIN("cols", [128, 88])
    gb_d = DIN("gb", [36, 2])
    bt2_d = DIN("bt2", [128, 4, 14, 2, 64])
    wpa_d = DIN("wpa", [128, 8, 1024])
    wpb_d = DIN("wpb", [128, 4, 1024])
    wout_d = DIN("wout", [128, 8, 1024])
    ident_d = DIN("ident", [128, 128])
    masks_d = DIN("masks", [128, 2, 128])
    sel_d = DIN("sel", [36, 8, 128])
    y_d = nc.dram_tensor("y", [S_, D], F32, kind="ExternalOutput").ap()

    QT = DSC("QT", [4, 128, 2, S_])
    KT = DSC("KT", [4, 128, 2, S_])
    VA = DSC("VA", [NT, 128, 4 * 258])
    OG = DSC("OG", [S_, D])
    ZAT = DSC("ZAT", [8, 128, S_])
    QBT = DSC("QBT", [4, 128, S_])
    KBT = DSC("KBT", [4, 128, S_])
    VBA = DSC("VBA", [NT, 128, 8 * 66])
    ZBT = DSC("ZBT", [4, 128, S_])
    GMT = DSC("GMT", [16, 128, S_])
    YAT = DSC("YAT", [8, 128, S_])
    YBT = DSC("YBT", [4, 128, S_])
    dbg = {}
    if debug:
        dbg["hT"] = nc.dram_tensor("dbg_hT", [128, 8, S_], BF16, kind="ExternalOutput").ap()
        dbg["TOKU"] = nc.dram_tensor("dbg_TOKU", [128, NT, 36], F32, kind="ExternalOutput").ap()
        dbg["TOKC"] = nc.dram_tensor("dbg_TOKC", [128, NT, 36], F32, kind="ExternalOutput").ap()
        dbg["DECB"] = nc.dram_tensor("dbg_DECB", [128, 8, NT], F32, kind="ExternalOutput").ap()
        dbg["Hacc"] = nc.dram_tensor("dbg_Hacc", [4, 128, NT, 256], F32, kind="ExternalOutput").ap()
        for nm in ("T1", "T2", "T3"):
            dbg[nm] = nc.dram_tensor("dbg_" + nm, [36, S_], F32, kind="ExternalOutput").ap()

    def dstop(tag):
        if stop_after != tag:
            return False
        S.dma("sp", "S_dbg", dbg["T1"][:, :], T1[0:36, :], reads=["T1g"])
        S.dma("sp", "S_dbg", dbg["T2"][:, :], T2[0:36, :], reads=["T2g"])
        S.dma("sp", "S_dbg", dbg["T3"][:, :], T3[0:36, :], reads=["T3g"])
        S.emit()
        return True

    SB_LO = 16512
    SB_HI = 229344
    cur = [SB_LO]

    def T(name, shape, dt):
        n = int(np.prod(shape[1:])) * (4 if dt == F32 else 2)
        n = (n + 31) // 32 * 32
        assert cur[0] + n <= SB_HI, (name, cur[0], n)
        t = nc.alloc_sbuf_tensor_at(name, list(shape), dt, offset=cur[0])
        cur[0] += n
        return t

    ps = [nc.alloc_psum_tensor("ps%d" % i, [128, 512], F32) for i in range(8)]

    def psb(i):
        return ps[i][:].bitcast(BF16)

    S = Sched(nc)

    identb = T("identb", [128, 128], BF16)
    identf = T("identf", [128, 128], F32)
    maskT = T("maskT", [128, 2, 128], F32)
    cols = T("cols", [128, 88], F32)
    TOKU = T("TOKU", [128, NT, 36], F32)
    TOKC = T("TOKC", [128, NT, 36], F32)
    DECB = T("DECB", [128, 8, NT], F32)
    gate_bc = T("gate_bc", [128, 1024], F32)
    fg_bc = T("fg_bc", [128, 1024], F32)
    ones_c = T("ones_c", [128, 128], F32)
    smallc = T("smallc", [128, 64], F32)
    REGION0 = cur[0]

    S.dma("sp", "L_const", identf[:], ident_d[:, :], writes=["identf"])
    S.dma("pool", "L_constb", identb[:], ident_d[:, :], writes=["identb"])
    S.dma("sp", "L_const", maskT[:], masks_d[:, :, :], writes=["maskT"])
    S.dma("sp", "L_const", cols[:], cols_d[:, :], writes=["cols"])
    S.op("pool", MS(ones_c[:], 1.0), writes=["ones_c"])

    cur[0] = REGION0
    modrow = T("modrow", [1, 3072], F32)
    rows = T("rows", [1, 5120], F32)
    wst = [T("wada_st%d" % i, [128, 8, 512], F32) for i in range(2)]
    A_END = cur[0]
    cur[0] = REGION0 + 65536
    G1_bc = T("G1_bc", [128, 1024], F32)
    sh_bc = T("sh_bc", [128, 1024], F32)
    c_sb = T("c_sb", [128, 8], F32)
    cond = T("cond", [128, 8], F32)
    G1row = T("G1row", [1, 1024], F32)
    assert A_END <= REGION0 + 65536
    wst.append(T("wada_st2", [128, 8, 512], F32))

    S.dma("sp", "L_const", c_sb[:], c_d[:, :], writes=["c_sb"])
    S.dma("sp", "L_const", rows[:], rows_d[:, :], writes=["rows"])
    S.op("act", ACT(cond[:], c_sb[:], AF.Silu), reads=["c_sb"], writes=["cond"])
    wrot = Rot("wada_st", wst)
    for g in range(6):
        wt, wk = wrot.next()
        S.dma("sp", "L_wada%d" % wk[1], wt[:], wada_d[:, :, g * 512:(g + 1) * 512], writes=[wk])
        S.op("pe", [MM(ps[g % 2][0:1, :], cond[:, kc:kc + 1], wt[:, kc, :], start=(kc == 0), stop=(kc == 7)) for kc in range(8)],
             reads=["cond", wk], writes=[("ps", g % 2)])
        S.op("dve", TT(modrow[0:1, g * 512:(g + 1) * 512], ps[g % 2][0:1, :], rows[0:1, g * 512:(g + 1) * 512], ALU.add),
             reads=[("ps", g % 2), "rows"], writes=["modrow"])
    S.op("dve", STT(G1row[0:1, :], modrow[0:1, 1024:2048], 1.0, rows[0:1, 3072:4096], ALU.add, ALU.mult),
         reads=["modrow", "rows"], writes=["G1row"])
    bc_jobs = [(G1_bc, G1row[0:1, :], "G1row", "G1_bc"), (sh_bc, modrow[0:1, 0:1024], "modrow", "sh_bc"),
               (gate_bc, modrow[0:1, 2048:3072], "modrow", "gate_bc"), (fg_bc, rows[0:1, 4096:5120], "rows", "fg_bc")]
    bi = 0
    for (dst, src, skey, dkey) in bc_jobs:
        for hf in range(2):
            b = 2 + (bi % 2)
            bi += 1
            S.op("pe", MM(ps[b][:, :], ones_c[0:1, 0:128], src[:, hf * 512:(hf + 1) * 512]), reads=[skey, "ones_c"], writes=[("ps", b)])
            S.op("act", ACT(dst[:, hf * 512:(hf + 1) * 512], ps[b][:, :], AF.Copy), reads=[("ps", b)], writes=[dkey])
    S.barrier()

    cur[0] = REGION0
    hT = T("hT", [128, 8, S_], BF16)
    C_START = cur[0]
    assert C_START == REGION0 + 65536
    cur[0] = C_START + 8192 + 2 * 32
    cur[0] = (cur[0] + 4096 + 31) // 32 * 32
    xts = [T("xt%d" % i, [128, 1024], F32) for i in range(4)]
    xns = [T("xn%d" % i, [128, 1024], BF16) for i in range(3)]
    junkb = T("junkb", [128, 1024], BF16)
    xrot = Rot("xt", xts)
    xnrot = Rot("xn", xns)
    def b_stage1(tt):
        xt, xk = xrot.next()
        xn, xnk = xnrot.next()
        sc = smallc[:, (tt % 4) * 4:(tt % 4) * 4 + 4]
        sck = ("smallc", tt % 4)
        S.dma("sp", "L_xt%d" % xk[1], xt[:], x_d[tt * 128:(tt + 1) * 128, :], writes=[xk])
        S.op("act", ACT(junkb[:], xt[:], AF.Square, accum_out=sc[:, 0:1]), reads=[xk], writes=["junkb", sck])
        S.op("act", ACT(sc[:, 1:2], sc[:, 0:1], AF.Sqrt, scale=1.0 / D, bias=EPS), reads=[sck], writes=[sck])
        S.op("dve", ("reciprocal", dict(out=sc[:, 2:3], in_=sc[:, 1:2])), reads=[sck], writes=[sck])
        S.op("dve", STT(xt[:], xt[:], sc[:, 2:3], G1_bc[:], ALU.mult, ALU.mult), reads=[xk, sck, "G1_bc"], writes=[xk])
        S.op("dve" if tt % 3 != 2 else "pool", TT(xn[:], xt[:], sh_bc[:], ALU.add), reads=[xk, "sh_bc"], writes=[xnk])
        return xn, xnk

    def b_stage2(tt, xn, xnk):
        b = 6 + (tt % 2)
        S.op("pe", [TR(psb(b)[:, kc * 128:(kc + 1) * 128], xn[:, kc * 128:(kc + 1) * 128], identb[:]) for kc in range(8)],
             reads=[xnk, "identb"], writes=[("ps", b)])
        S.op("act", ACT(hT[:, :, tt * 128:(tt + 1) * 128], psb(b).rearrange("p (a b) -> p a b", a=8), AF.Copy),
             reads=[("ps", b)], writes=[("hT", tt)])

    bctx = {}
    for tt in range(NT + 1):
        if tt < NT:
            bctx[tt] = b_stage1(tt)
        if tt >= 1:
            b_stage2(tt - 1, *bctx.pop(tt - 1))
    if debug:
        S.dma("sp", "S_dbg", dbg["hT"][:, :, :], hT[:], reads=[("hT", tt) for tt in range(NT)])
    S.barrier()
    if stop_after == "B":
        S.emit()
        return nc

    cur[0] = C_START
    Wb = [T("Wb%d" % i, [128, 8, 512], BF16) for i in range(3)]
    wgb = T("wgb", [128, 8, 72], BF16)
    UA = T("UA", [128, S_ + 2], F32)
    UB = T("UB", [128, S_ + 2], F32)
    ACC = T("ACC", [128, S_], F32)
    obufs = [T("obuf%d" % i, [128, S_], BF16) for i in range(2)]
    tms = [T("tmst%d" % i, [128, 2112], BF16) for i in range(2)]
    gbc = T("gbc", [36, 2], F32)
    MPt = T("MPt", [36, NT], F32)
    MOt = T("MOt", [36, NT], F32)
    DECt = T("DECt", [36, NT], F32)
    selt = T("selt", [36, 8, 128], F32)
    C_END = cur[0]
    T1 = nc.alloc_sbuf_tensor_at("T1g", [128, S_], F32, offset=C_START + 3 * 8192 + 1152)
    T2 = nc.alloc_sbuf_tensor_at("T2g", [128, S_], F32, offset=C_START + 3 * 8192 + 1152 + 16416)
    T3 = ACC
    ONESF = nc.alloc_sbuf_tensor_at("ONESF", [128, S_], F32, offset=C_START + 3 * 8192 + 1152 + 2 * 16416 + 16384)

    wrot = Rot("Wb", Wb)
    orot = Rot("obuf", obufs)
    tmrot = Rot("tmst", tms)
    urot = Rot("U", [UA, UB])
    psrot = Rot("ps", ps[0:6])
    evtog = [0]

    S.dma("sp", "L_const", gbc[:], gb_d[:, :], writes=["gbc"])
    S.dma("sp", "L_const", selt[:], sel_d[:, :, :], writes=["selt"])
    S.dma("pool", "L_wgb", wgb[:], wgate_d[:, :, :], writes=["wgb"])
    allhT = [("hT", tt) for tt in range(NT)]

    def hkeys(tb):
        return [("hT", tb * 4 + j) for j in range(4)]

    for tb in range(NBLK):
        for gi, (Tt, col, tkey) in enumerate(((T1, 0, "T1g"), (T2, 1, "T2g"))):
            pt, pk = psrot.next()
            S.op("pe", [MM(pt[0:36, :], wgb[:, kc, gi * 36:(gi + 1) * 36], hT[:, kc, tb * 512:(tb + 1) * 512],
                           start=(kc == 0), stop=(kc == 7)) for kc in range(8)],
                 reads=["wgb"] + hkeys(tb), writes=[pk])
            S.op("act", ACT(Tt[0:36, tb * 512:(tb + 1) * 512], pt[0:36, :], AF.Identity, bias=gbc[0:36, col:col + 1]),
                 reads=[pk, "gbc"], writes=[tkey])

    if dstop("D0"):
        return nc
    r36 = slice(0, 36)
    fw = slice(0, 4)
    bw = slice(32, 36)
    S.op("act", ACT(T2[r36, :], T2[r36, :], AF.Exp, scale=-1.0), reads=["T2g"], writes=["T2g"])
    S.op("act", ACT(T2[r36, :], T2[r36, :], AF.Ln, bias=1.0), reads=["T2g"], writes=["T2g"])
    S.op("pool", MS(T3[r36, :], 0.0), writes=["T3g"])
    S.op("pool", MS(ONESF[r36, :], 1.0), writes=["ONESF"])
    S.op("pool", [MS(MPt[:], 0.0), MS(MOt[:], 0.0)], writes=["MPt", "MOt"])
    S.op("dve", SCAN(T3[fw, :], ONESF[fw, :], T2[fw, :], 0.0, ALU.mult, ALU.add),
         reads=["T2g", "ONESF"], writes=["T3g"])
    S.op("dve", SCAN(T3[bw, ::-1], ONESF[bw, :], T2[bw, ::-1], 0.0, ALU.mult, ALU.add),
         reads=["T2g", "ONESF"], writes=["T3g"])
    if dstop("D1"):
        return nc
    S.op("dve", TT(T1[r36, :], T1[r36, :], T3[r36, :], ALU.add), reads=["T1g", "T3g"], writes=["T1g"])
    S.op("dve", SCAN(T2[fw, :], ONESF[fw, :], T1[fw, :], 0.0, ALU.mult, ALU.max),
         reads=["T1g", "ONESF"], writes=["T2g"])
    S.op("dve", SCAN(T2[bw, ::-1], ONESF[bw, :], T1[bw, ::-1], 0.0, ALU.mult, ALU.max),
         reads=["T1g", "ONESF"], writes=["T2g"])
    if dstop("D2"):
        return nc
    M3 = T2[:].rearrange("p (k t) -> p k t", t=128)
    S.op("dve", [CP(MPt[fw, 1:NT], M3[fw, 0:NT - 1, 127]), CP(MOt[fw, :], M3[fw, :, 127])],
         reads=["T2g", "MPt", "MOt"], writes=["MPt", "MOt"])
    S.op("dve", [CP(MPt[bw, 0:NT - 1], M3[bw, 1:NT, 0]), CP(MOt[bw, :], M3[bw, :, 0])],
         reads=["T2g", "MPt", "MOt"], writes=["MPt", "MOt"])
    S.op("dve", TT(DECt[:], MPt[:], MOt[:], ALU.subtract), reads=["MPt", "MOt"], writes=["DECt"])
    S.op("act", ACT(DECt[:], DECt[:], AF.Exp), reads=["DECt"], writes=["DECt"])
    if dstop("D3"):
        return nc
    MPb = MPt[:].rearrange("p (k o) -> p k o", o=1).to_broadcast([36, NT, 128])
    T1v = T1[r36, :].rearrange("p (k t) -> p k t", t=128)
    T3v = T3[r36, :].rearrange("p (k t) -> p k t", t=128)
    S.op("dve", TT(T3v, T3v, MPb, ALU.subtract), reads=["T3g", "MPt"], writes=["T3g"])
    S.op("act", ACT(T3[r36, :], T3[r36, :], AF.Exp), reads=["T3g"], writes=["T3g"])
    S.op("dve", TT(T1v, T1v, MPb, ALU.subtract), reads=["T1g", "MPt"], writes=["T1g"])
    S.op("act", ACT(T1[r36, :], T1[r36, :], AF.Exp), reads=["T1g"], writes=["T1g"])
    if dstop("D4"):
        return nc
    for (src, skey, dst, dkey) in ((T1, "T1g", TOKU, "TOKU"), (T3, "T3g", TOKC, "TOKC")):
        for k0 in range(0, NT, 14):
            n = min(14, NT - k0)
            pt, pk = psrot.next()
            S.op("pe", [TR(pt[:, j * 36:(j + 1) * 36], src[0:36, (k0 + j) * 128:(k0 + j + 1) * 128], identf[0:36, 0:36]) for j in range(n)],
                 reads=[skey, "identf"], writes=[pk])
            S.op("dve", CP(dst[:, k0:k0 + n, :], pt[:, 0:n * 36].rearrange("p (a b) -> p a b", b=36)), reads=[pk], writes=[dkey])
    pt, pk = psrot.next()
    S.op("pe", [MM(pt[:, j * NT:(j + 1) * NT], selt[0:36, j, :], DECt[0:36, :]) for j in range(8)],
         reads=["selt", "DECt"], writes=[pk])
    S.op("dve", CP(DECB[:], pt[:, 0:8 * NT].rearrange("p (a b) -> p a b", b=NT)), reads=[pk], writes=["DECB"])
    if debug:
        S.dma("sp", "S_dbg", dbg["TOKU"][:, :, :], TOKU[:], reads=["TOKU"])
        S.dma("sp", "S_dbg", dbg["TOKC"][:, :, :], TOKC[:], reads=["TOKC"])
        S.dma("sp", "S_dbg", dbg["DECB"][:, :, :], DECB[:], reads=["DECB"])
    S.barrier()
    if stop_after == "D":
        S.emit()
        return nc

    S.op("pool", [MS(UA[:, 0:1], 0.0), MS(UA[:, S_ + 1:S_ + 2], 0.0), MS(UB[:, 0:1], 0.0), MS(UB[:, S_ + 1:S_ + 2], 0.0)],
         writes=[("U", 0), ("U", 1)])

    def load_w(g):
        wt, wk = wrot.next()
        S.dma("pool", "L_Wb%d" % wk[1], wt[:], win_d[g], writes=[wk])
        return wt, wk

    pend_tail = []

    def flush_tail():
        while pend_tail:
            inf = pend_tail.pop(0)
            ob, ok = orot.next()
            S.op("act", ACT(ob[:], ACC[:], AF.Silu), reads=["ACC"], writes=[ok])
            S.dma("sp", "S_obuf%d" % ok[1], inf["dst"], ob[:], reads=[ok], writes=[inf["dkey"]])

    def fm_group(g, kind, sub_info):
        wt, wk = load_w(g)
        for sub in range(4):
            info = sub_info(sub)
            if kind == "conv":
                U, uk = urot.next()
            else:
                ob, ok = orot.next()
            for tb in range(NBLK):
                pt, pk = psrot.next()
                S.op("pe", [MM(pt[:, :], wt[:, kc, sub * 128:(sub + 1) * 128], hT[:, kc, tb * 512:(tb + 1) * 512],
                               start=(kc == 0), stop=(kc == 7)) for kc in range(8)],
                     reads=[wk] + hkeys(tb), writes=[pk])
                if kind == "conv":
                    S.op("act", ACT(U[:, 1 + tb * 512:1 + (tb + 1) * 512], pt[:, :], AF.Copy), reads=[pk], writes=[uk])
                elif kind == "silu":
                    S.op("act", ACT(ob[:, tb * 512:(tb + 1) * 512], pt[:, :], AF.Silu), reads=[pk], writes=[ok])
                elif kind == "sigb":
                    S.op("act", ACT(ob[:, tb * 512:(tb + 1) * 512], pt[:, :], AF.Sigmoid, bias=info["bias"]), reads=[pk, "cols"], writes=[ok])
                elif kind == "copy":
                    evtog[0] ^= 1
                    if evtog[0]:
                        S.op("dve", TS(ob[:, tb * 512:(tb + 1) * 512], pt[:, :], info["scale"], None, ALU.mult), reads=[pk], writes=[ok])
                    else:
                        S.op("act", ACT(ob[:, tb * 512:(tb + 1) * 512], pt[:, :], AF.Copy, scale=info["scale"]), reads=[pk], writes=[ok])
            if kind == "conv":
                flush_tail()
                cg = info["cg"]
                w0 = cols[:, cg * 3 + 0:cg * 3 + 1]
                w1 = cols[:, cg * 3 + 1:cg * 3 + 2]
                w2 = cols[:, cg * 3 + 2:cg * 3 + 3]
                cb = cols[:, 48 + cg:49 + cg]
                S.op("dve", TS(ACC[:], U[:, 1:S_ + 1], w1, cb, ALU.mult, ALU.add), reads=[uk, "cols"], writes=["ACC"])
                S.op("dve", STT(ACC[:], U[:, 0:S_], w0, ACC[:], ALU.mult, ALU.add), reads=[uk, "cols", "ACC"], writes=["ACC"])
                S.op("dve", STT(ACC[:], U[:, 2:S_ + 2], w2, ACC[:], ALU.mult, ALU.add), reads=[uk, "cols", "ACC"], writes=["ACC"])
                pend_tail.append(info)
            else:
                S.dma("sp", "S_obuf%d" % ok[1], info["dst"], ob[:], reads=[ok], writes=[info["dkey"]])

    def tm_group(g, kind, col0):
        wt, wk = load_w(g)
        for t4 in range(NT // 4):
            st, sk = tmrot.next()
            if kind == "va":
                sv = st[:, 0:4 * 2 * 258].rearrange("p (t h c) -> p t h c", t=4, h=2)
                S.op("pool", [MS(sv[:, :, :, 256:257], 1.0), MS(sv[:, :, :, 257:258], 0.0)], writes=[sk])
            elif kind == "vb":
                sv = st[:, 0:4 * 8 * 66].rearrange("p (t h c) -> p t h c", t=4, h=8)
                S.op("pool", [MS(sv[:, :, :, 64:65], 1.0), MS(sv[:, :, :, 65:66], 0.0)], writes=[sk])
            else:
                sv = st[:, 0:2048].rearrange("p (t c) -> p t c", t=4)
            for j in range(4):
                tt = t4 * 4 + j
                pt, pk = psrot.next()
                S.op("pe", [MM(pt[:, :], hT[:, kc, tt * 128:(tt + 1) * 128], wt[:, kc, :], start=(kc == 0), stop=(kc == 7)) for kc in range(8)],
                     reads=[wk, ("hT", tt)], writes=[pk])
                if kind == "va":
                    S.op("dve", CP(sv[:, j, :, 0:256], pt[:, :].rearrange("p (h c) -> p h c", h=2)), reads=[pk], writes=[sk])
                elif kind == "vb":
                    S.op("dve", CP(sv[:, j, :, 0:64], pt[:, :].rearrange("p (h c) -> p h c", h=8)), reads=[pk], writes=[sk])
                else:
                    S.op("act", ACT(sv[:, j, :], pt[:, :], AF.Sigmoid), reads=[pk], writes=[sk])
            tsl = slice(t4 * 4, t4 * 4 + 4)
            if kind == "va":
                hd0 = col0
                dst = VA[tsl, :, hd0 * 258:(hd0 + 2) * 258].rearrange("t p c -> p t c")
                S.dma("sp", "S_tm%d" % sk[1], dst, st[:, 0:4 * 516].rearrange("p (t c) -> p t c", t=4), reads=[sk], writes=[("VA", t4, hd0)])
            elif kind == "vb":
                dst = VBA[tsl, :, :].rearrange("t p c -> p t c")
                S.dma("sp", "S_tm%d" % sk[1], dst, st[:, 0:4 * 528].rearrange("p (t c) -> p t c", t=4), reads=[sk], writes=[("VBA", t4)])
            else:
                dst = OG[t4 * 512:(t4 + 1) * 512, col0:col0 + 512].rearrange("(t p) c -> p t c", p=128)
                S.dma("sp", "S_tm%d" % sk[1], dst, sv, reads=[sk], writes=[("OG", t4, col0)])

    for g in (0, 1):
        fm_group(g, "conv", lambda sub, g=g: dict(cg=g * 4 + sub, dst=QT[(g * 4 + sub) // 2, :, (g * 4 + sub) % 2, :],
                                                 dkey=("QT", g * 4 + sub)))
    for g in (2, 3):
        fm_group(g, "conv", lambda sub, g=g: dict(cg=8 + (g - 2) * 4 + sub, dst=KT[((g - 2) * 4 + sub) // 2, :, ((g - 2) * 4 + sub) % 2, :],
                                                 dkey=("KT", (g - 2) * 4 + sub)))
    flush_tail()
    for g in (8, 9):
        fm_group(g, "silu", lambda sub, g=g: dict(dst=ZAT[(g - 8) * 4 + sub], dkey=("ZAT", (g - 8) * 4 + sub)))
    fm_group(13, "silu", lambda sub: dict(dst=ZBT[sub], dkey=("ZBT", sub)))
    fm_group(10, "copy", lambda sub: dict(scale=0.125, dst=QBT[sub], dkey=("QBT", sub)))
    fm_group(11, "copy", lambda sub: dict(scale=1.0, dst=KBT[sub], dkey=("KBT", sub)))
    tm_group(4, "va", 0)
    tm_group(5, "va", 2)
    tm_group(12, "vb", 0)
    tm_group(6, "og", 0)
    tm_group(7, "og", 512)
    for g in (14, 15, 16, 17):
        fm_group(g, "sigb", lambda sub, g=g: dict(bias=cols[:, 64 + (g - 14) * 4 + sub:65 + (g - 14) * 4 + sub],
                                                 dst=GMT[(g - 14) * 4 + sub], dkey=("GMT", (g - 14) * 4 + sub)))
    S.barrier()
    if stop_after == "C":
        S.emit()
        return nc

    if build_mlstm(nc, S, locals()):
        return nc
    if stop_after == "M":
        S.emit()
        return nc
    build_na(nc, S, locals())
    if stop_after == "N":
        S.emit()
        return nc
    build_final(nc, S, locals())
    S.emit()
    return nc


def build_mlstm(nc, S, E):
    T, cur, ps, psb = E["T"], E["cur"], E["ps"], E["psb"]
    identb, maskT, cols, TOKU, TOKC, DECB = E["identb"], E["maskT"], E["cols"], E["TOKU"], E["TOKC"], E["DECB"]
    QT, KT, VA, OG, ZAT, YAT = E["QT"], E["KT"], E["VA"], E["OG"], E["ZAT"], E["YAT"]
    dbg, debug = E["dbg"], E["debug"]
    cur[0] = E["REGION0"]
    qT = T("m_qT", [128, 2, S_], BF16)
    kT = T("m_kT", [128, 2, S_], BF16)
    ktok = T("m_ktok", [128, NT, 256], BF16)
    Vaug = T("m_Vaug", [128, NT, 258], BF16)
    Hacc = T("m_Hacc", [128, NT, 256], F32)
    ZATh = T("m_ZATh", [128, 2, S_], BF16)
    yaT = T("m_yaT", [128, 2, S_], BF16)
    OGt = Rot("OGt", [T("m_OGt%d" % i, [128, 4, 256], BF16) for i in range(3)])
    UVr = [Rot("UV%d" % d, [T("m_UV%d_%d" % (d, i), [128, 258], BF16) for i in range(4)]) for d in range(2)]
    Smr = [Rot("Sm%d" % d, [T("m_Sm%d_%d" % (d, i), [128, 128], BF16) for i in range(3)]) for d in range(2)]
    Zs = [T("m_Z%d" % d, [128, 2, 258], F32) for d in range(2)]
    Cbr = [Rot("Cb%d" % d, [T("m_Cb%d_%d" % (d, i), [128, 2, 258], BF16) for i in range(3)]) for d in range(2)]
    Htr = Rot("Htmp", [T("m_Htmp%d" % i, [128, 256], F32) for i in range(3)])
    hgr = Rot("hg", [T("m_hg%d" % i, [128, 256], F32) for i in range(4)])
    ytr = Rot("yatok", [T("m_yatok%d" % i, [128, 256], BF16) for i in range(4)])
    junk = T("m_junk", [128, 256], BF16)
    rcs = T("m_rcs", [128, 8, 4], F32)
    pcs = T("m_pcs", [128, 8, 4], F32)
    rci = [0]
    pci = [0]
    dcp = [((ps[4], ps[5]), [("ps", 4), ("ps", 5)]), ((ps[6], ps[7]), [("ps", 6), ("ps", 7)])]

    def loads(hd):
        S.dma("sp", "L_mq", qT[:], QT[hd], writes=["qT"])
        S.dma("sp", "L_mk", kT[:], KT[hd], writes=["kT"])
        for j0 in range(0, NT, 8):
            S.dma("sp", "L_mv", Vaug[:, j0:j0 + 8, :], VA[j0:j0 + 8, :, hd * 258:(hd + 1) * 258].rearrange("t p c -> p t c"),
                  writes=[("Vaug", j) for j in range(j0, j0 + 8)])

    loads(0)
    for hd in range(4):
        S.dma("sp", "L_mz", ZATh[:], ZAT[2 * hd:2 * hd + 2].rearrange("g p t -> p g t"), writes=["ZATh"])
        for k4 in range(8):
            kb = 6 + (k4 % 2)
            S.op("pe", [TR(psb(kb)[:, (kk * 2 + c) * 128:(kk * 2 + c + 1) * 128], kT[:, c, (k4 * 4 + kk) * 128:(k4 * 4 + kk + 1) * 128], identb[:])
                        for kk in range(4) for c in range(2)], reads=["kT", "identb"], writes=[("ps", kb)])
            if k4 % 2 == 0:
                S.op("act", ACT(ktok[:, k4 * 4:(k4 + 1) * 4, :].rearrange("p a b -> p (a b)"), psb(kb)[:, 0:1024], AF.Copy, scale=1.0 / 16),
                     reads=[("ps", kb)], writes=[("ktok", k4)])
            else:
                S.op("dve", TS(ktok[:, k4 * 4:(k4 + 1) * 4, :].rearrange("p a b -> p (a b)"), psb(kb)[:, 0:1024], 1.0 / 16, None, ALU.mult),
                     reads=[("ps", kb)], writes=[("ktok", k4)])
        if E["stop_after"] == "M0":
            S.emit()
            return True

        ctx = {}
        chain = [dict(kprev=None) for _ in range(2)]

        def kof(d, i):
            return i if d == 0 else NT - 1 - i

        def opUV(d, i):
            k = kof(d, i)
            ucol = TOKU[:, k, d * 32 + hd:d * 32 + hd + 1]
            UVb, uvk = UVr[d].next()
            S.op("dve", TS(UVb[:], Vaug[:, k, :], ucol, None, ALU.mult), reads=[("Vaug", k), "TOKU"], writes=[uvk])
            ctx[(d, i)] = dict(k=k, ch=slice(k * 128, (k + 1) * 128), UVb=UVb, uvk=uvk, cb=None, cbk=None)

        def opST(d, i):
            c_ = ctx[(d, i)]
            stp = ps[d][:, 0:128]
            S.op("pe", [MM(stp, kT[:, c, c_["ch"]], qT[:, c, c_["ch"]], start=(c == 0), stop=(c == 1)) for c in range(2)],
                 reads=["kT", "qT"], writes=[("ps", d)])

        def opMASK(d, i):
            c_ = ctx[(d, i)]
            Sm, smk = Smr[d].next()
            S.op("dve", TT(Sm[:], ps[d][:, 0:128], maskT[:, d, :], ALU.mult), reads=[("ps", d), "maskT"], writes=[smk])
            c_["Sm"], c_["smk"] = Sm, smk

        def opDC(d, i):
            c_ = ctx[(d, i)]
            (b0, b1), dks = dcp[d]
            k = c_["k"]
            S.op("pe", [MM(b0[:, 0:258], ktok[:, k, 0:128], c_["UVb"][:]), MM(b1[:, 0:258], ktok[:, k, 128:256], c_["UVb"][:])],
                 reads=[("ktok", k // 4), c_["uvk"]], writes=dks)

        def opZ(d, i):
            c_ = ctx[(d, i)]
            (b0, b1), dks = dcp[d]
            series = d * 4 + hd
            k = c_["k"]
            Z = Zs[d]
            zk = ("Z", d)
            if i == 0:
                S.op("dve", [CP(Z[:, 0, :], b0[:, 0:258]), CP(Z[:, 1, :], b1[:, 0:258])], reads=dks, writes=[zk])
            else:
                kp = chain[d]["kprev"]
                dprev = DECB[:, series, kp:kp + 1]
                S.op("dve", [STT(Z[:, 0, :], Z[:, 0, :], dprev, b0[:, 0:258], ALU.mult, ALU.add),
                             STT(Z[:, 1, :], Z[:, 1, :], dprev, b1[:, 0:258], ALU.mult, ALU.add)],
                     reads=dks + [zk, "DECB"], writes=[zk])
            cbn, cbk = Cbr[d].next()
            dcur = DECB[:, series, k:k + 1]
            S.op("act", ACT(cbn[:].rearrange("p a b -> p (a b)"), Z[:].rearrange("p a b -> p (a b)"), AF.Copy, scale=dcur), reads=[zk, "DECB"], writes=[cbk])
            c_["cb"], c_["cbk"] = cbn, cbk
            chain[d]["kprev"] = k

        def opNP(d, i):
            c_ = ctx[(d, i)]
            first = (i == 0)
            npb, npk = ps[2 + d], ("ps", 2 + d)
            npa = npb[:, 0:258]
            specs = [MM(npa, c_["Sm"][:], c_["UVb"][:], start=True, stop=first)]
            rd = [c_["smk"], c_["uvk"]]
            if not first:
                pv = ctx[(d, i - 1)]
                specs += [MM(npa, qT[:, c, c_["ch"]], pv["cb"][:, c, :], start=False, stop=(c == 1)) for c in range(2)]
                rd += ["qT", pv["cbk"]]
            S.op("pe", specs, reads=rd, writes=[npk])

        def opOUT(d, i):
            c_ = ctx[(d, i)]
            k = c_["k"]
            npb, npk = ps[2 + d], ("ps", 2 + d)
            j = rci[0] % 8
            rci[0] += 1
            rc = rcs[:, j, :]
            rck = ("rc", j)
            den = npb[:, 256:257]
            ccol = TOKC[:, k, d * 32 + hd:d * 32 + hd + 1]
            S.op("dve", TT(rc[:, 0:1], den, ccol, ALU.max), reads=[npk, "TOKC"], writes=[rck])
            S.op("dve", STT(rc[:, 1:2], den, -1.0, rc[:, 0:1], ALU.mult, ALU.max), reads=[npk, rck], writes=[rck])
            S.op("dve", ("reciprocal", dict(out=rc[:, 2:3], in_=rc[:, 1:2])), reads=[rck], writes=[rck])
            if i < NT // 2:
                S.op("act", ACT(Hacc[:, k, :], npb[:, 0:256], AF.Copy, scale=rc[:, 2:3]), reads=[npk, rck], writes=[("Hacc", k)])
            else:
                ht, htk = Htr.next()
                S.op("act", ACT(ht[:], npb[:, 0:256], AF.Copy, scale=rc[:, 2:3]), reads=[npk, rck], writes=[htk])
                S.op("pool", TT(Hacc[:, k, :], Hacc[:, k, :], ht[:], ALU.add), reads=[htk, ("Hacc", k)], writes=[("Hacc", k)])
            if i >= 1:
                ctx.pop((d, i - 1))

        for d in range(2):
            opUV(d, 0)
        for d in range(2):
            opUV(d, 1)
        for d in range(2):
            opST(d, 0)
        for d in range(2):
            opMASK(d, 0)
        for d in range(2):
            opDC(d, 0)
        for d in range(2):
            opZ(d, 0)
        for i in range(NT):
            nx = i + 1
            if i + 2 < NT:
                opUV(0, i + 2)
                opUV(1, i + 2)
            if nx < NT:
                opST(0, nx)
                opST(1, nx)
                opMASK(0, nx)
                opMASK(1, nx)
                if nx < NT - 1:
                    opDC(0, nx)
                    opDC(1, nx)
            opNP(0, i)
            opNP(1, i)
            if nx < NT - 1:
                opZ(0, nx)
            opOUT(0, i)
            if nx < NT - 1:
                opZ(1, nx)
            opOUT(1, i)

        if debug:
            S.dma("sp", "S_dbg", dbg["Hacc"][hd], Hacc[:], reads=[("Hacc", k) for k in range(NT)])
        if hd + 1 < 4:
            loads(hd + 1)

        pctx = {}

        def P1(k):
            k4, j = k // 4, k % 4
            if j == 0:
                ogt, ogk = OGt.next()
                S.dma("pool", "L_og%d" % ogk[1], ogt[:], OG[k4 * 512:(k4 + 1) * 512, hd * 256:(hd + 1) * 256].rearrange("(t p) c -> p t c", p=128),
                      writes=[ogk])
                pctx["og"] = (ogt, ogk)
            ogt, ogk = pctx["og"]
            hg, hgk = hgr.next()
            S.op("dve", TT(hg[:], Hacc[:, k, :], ogt[:, j, :], ALU.mult), reads=[("Hacc", k), ogk], writes=[hgk])
            q = pci[0] % 8
            pci[0] += 1
            pc = pcs[:, q, :]
            pck = ("pc", q)
            S.op("act", ACT(junk[:], hg[:], AF.Square, accum_out=pc[:, 0:1]), reads=[hgk], writes=["m_junk", pck])
            S.op("act", ACT(pc[:, 1:2], pc[:, 0:1], AF.Sqrt, scale=1.0 / DH, bias=EPS), reads=[pck], writes=[pck])
            pctx[k] = (hg, hgk, pc, pck)

        def P2(k):
            k4, j = k // 4, k % 4
            hg, hgk, pc, pck = pctx.pop(k)
            S.op("dve", ("reciprocal", dict(out=pc[:, 2:3], in_=pc[:, 1:2])), reads=[pck], writes=[pck])
            yt, ytk = ytr.next()
            S.op("dve", TS(yt[:], hg[:], pc[:, 2:3], None, ALU.mult), reads=[hgk, pck], writes=[ytk])
            S.op("pe", [TR(psb(7)[:, c * 512 + j * 128:c * 512 + (j + 1) * 128], yt[:, c * 128:(c + 1) * 128], identb[:]) for c in range(2)],
                 reads=[ytk, "identb"], writes=[("ps", 7)])
            if j == 3:
                for c in range(2):
                    S.op("dve", STT(yaT[:, c, k4 * 512:(k4 + 1) * 512], psb(7)[:, c * 512:(c + 1) * 512], cols[:, 80 + hd * 2 + c:81 + hd * 2 + c],
                                    ZATh[:, c, k4 * 512:(k4 + 1) * 512], ALU.mult, ALU.mult),
                         reads=[("ps", 7), "cols", "ZATh"], writes=[("yaT", c)])

        for k in range(NT + 2):
            if k < NT:
                P1(k)
            if k >= 2:
                P2(k - 2)
        for c in range(2):
            S.dma("pool", "S_yaT", YAT[hd * 2 + c], yaT[:, c, :], reads=[("yaT", c)])
        if E["stop_after"] == "M2":
            S.emit()
            return True
    S.barrier()


def build_na(nc, S, E):
    T, cur, ps, psb = E["T"], E["cur"], E["ps"], E["psb"]
    identb = E["identb"]
    QBT, KBT, VBA, ZBT, YBT, bt2_d = E["QBT"], E["KBT"], E["VBA"], E["ZBT"], E["YBT"], E["bt2_d"]
    cur[0] = E["REGION0"]
    bt2b = T("n_bt2", [128, 4, 14, 2, 64], BF16)
    QBD = T("n_QBD", [128, 2, 2, S_], BF16)
    kbT = T("n_kbT", [128, 2, S_], BF16)
    ZBh = T("n_ZBh", [128, 2, S_], BF16)
    ybT = T("n_ybT", [128, 2, S_], BF16)
    VE = T("n_VE", [128, NT, 4, 66], BF16)
    VO = T("n_VO", [128, NT - 1, 4, 66], BF16)
    PTr = Rot("PT", [T("n_PT%d" % i, [128, 1024], BF16) for i in range(2)])
    otr = Rot("otok", [T("n_otok%d" % i, [64, 4, 64], BF16) for i in range(3)])
    recs = T("n_rec", [64, 4, 4], F32)
    str_ = Rot("psS", [(ps[0], ps[1]), (ps[2], ps[3])])
    pvr = Rot("psPV", [ps[4], ps[5]])
    trr = Rot("psT", [6, 7])
    ri = [0]
    NA_END = cur[0]
    wpab = T("f_wpa", [128, 8, 1024], BF16)
    wpbb = T("f_wpb", [128, 4, 1024], BF16)
    woutb = T("f_wout", [128, 8, 1024], BF16)
    wsr = Rot("f_wst", [T("f_wst%d" % i, [128, 2, 1024], F32) for i in range(1)])
    S.shared_fw = (NA_END, wpab, wpbb, woutb)
    gate_bc = E["gate_bc"]
    S.dma("pool", "L_bt2", bt2b[:], bt2_d[:, :, :, :, :], writes=["bt2b"])
    S.op("pool", [MS(QBD[0:64, :, 1, :], 0.0), MS(QBD[64:128, :, 0, :], 0.0)], writes=["QBDz"])
    for half in range(2):
        c0 = half * 4 * 66
        for cq in range(4):
            tsl = slice(cq * 1024, (cq + 1) * 1024)
            S.dma("sp", "L_nq%d" % cq, QBD[0:64, :, 0, tsl], QBT[2 * half:2 * half + 2, 0:64, tsl].rearrange("g p t -> p g t"), reads=["QBDz"], writes=[("qbT", cq)])
            S.dma("sp", "L_nq%d" % cq, QBD[64:128, :, 1, tsl], QBT[2 * half:2 * half + 2, 64:128, tsl].rearrange("g p t -> p g t"), reads=["QBDz"], writes=[("qbT", cq)])
            S.dma("sp", "L_nk%d" % cq, kbT[:, :, tsl], KBT[2 * half:2 * half + 2, :, tsl].rearrange("g p t -> p g t"), writes=[("kbT", cq)])
            j0 = cq * 8
            S.dma("sp", "L_nve%d" % cq, VE[:, j0:j0 + 8, :, :].rearrange("p t h c -> p t (h c)"),
                  VBA[j0:j0 + 8, :, c0:c0 + 264].rearrange("t p c -> p t c"), writes=[("VE", cq)])
            j1 = min(j0 + 8, NT - 1)
            S.dma("sp", "L_nvo%d" % cq, VO[0:64, j0:j1, :, :].rearrange("p t h c -> p t (h c)"),
                  VBA[j0:j1, 64:128, c0:c0 + 264].rearrange("t p c -> p t c"), writes=[("VO", cq)])
            S.dma("sp", "L_nvo%d" % cq, VO[64:128, j0:j1, :, :].rearrange("p t h c -> p t (h c)"),
                  VBA[j0 + 1:j1 + 1, 0:64, c0:c0 + 264].rearrange("t p c -> p t c"), writes=[("VO", cq)])
            S.dma("sp", "L_nz%d" % cq, ZBh[:, :, tsl], ZBT[2 * half:2 * half + 2, :, tsl].rearrange("g p t -> p g t"), writes=[("ZBh", cq)])

        def n_stage1(r):
            rs = min(max(r - 4, 0), 56)
            j0b = rs - r + 7
            qs = slice(r * 64, (r + 1) * 64)
            pair, sk = str_.next()
            specs = []
            for gi in range(2):
                bank = pair[gi]
                hp = half * 2 + gi
                specs.append(MM(bank[:, 0:512], identb[:], bt2b[:, hp, j0b:j0b + 7:2, :, :].rearrange("p i j q -> p i (j q)"), start=True, stop=False))
                for i in range(4):
                    tok = rs * 64 + i * 128
                    specs.append(MM(bank[:, i * 128:(i + 1) * 128], kbT[:, gi, tok:tok + 128], QBD[:, gi, :, qs], start=False, stop=(i == 3)))
            S.op("pe", specs, reads=["identb", "bt2b", ("qbT", (r * 64) // 1024)] + [("kbT", cc) for cc in sorted({(rs * 64) // 1024, (rs * 64 + 511) // 1024})], writes=[sk])
            PT, ptk = PTr.next()
            S.op("act", [ACT(PT[:, 0:512], pair[0][:, :], AF.Exp), ACT(PT[:, 512:1024], pair[1][:, :], AF.Exp)], reads=[sk], writes=[ptk])
            return dict(r=r, rs=rs, qs=qs, PT=PT, ptk=ptk)

        def n_stage2(c):
            rs, PT, ptk = c["rs"], c["PT"], c["ptk"]
            if rs % 2 == 0:
                Vx, vnm, tbase = VE, "VE", rs // 2
            else:
                Vx, vnm, tbase = VO, "VO", (rs - 1) // 2
            vkeys = [(vnm, cc) for cc in sorted({tbase // 8, (tbase + 3) // 8})]
            ob, ok = pvr.next()
            specs = []
            for hh in range(4):
                for i in range(4):
                    specs.append(MM(ob[0:64, hh * 66:(hh + 1) * 66], PT[:, (hh // 2) * 512 + i * 128 + (hh % 2) * 64:(hh // 2) * 512 + i * 128 + (hh % 2) * 64 + 64], Vx[:, tbase + i, hh, :],
                                    start=(i == 0), stop=(i == 3)))
            S.op("pe", specs, reads=[ptk] + vkeys, writes=[ok])
            q = ri[0] % 4
            ri[0] += 1
            rec = recs[:, q, :]
            rk = ("rec", q)
            ov = ob[0:64, 0:264].rearrange("p (h c) -> p h c", c=66)
            S.op("dve", ("reciprocal", dict(out=rec, in_=ov[:, :, 64])), reads=[ok], writes=[rk])
            ot, otk = otr.next()
            S.op("dve", TT(ot[:], ov[:, :, 0:64], rec.rearrange("p (h o) -> p h o", o=1).to_broadcast([64, 4, 64]), ALU.mult),
                 reads=[ok, rk], writes=[otk])
            c["ot"], c["otk"] = ot, otk

        def n_stage3(c):
            ot, otk, qs = c["ot"], c["otk"], c["qs"]
            tb_, tk = trr.next()
            otf = ot[:].rearrange("p h c -> p (h c)")
            S.op("pe", [TR(psb(tb_)[:, g * 64:(g + 1) * 64], otf[:, g * 128:(g + 1) * 128], identb[0:64, 0:64]) for g in range(2)],
                 reads=[otk, "identb"], writes=[tk])
            S.op("dve", TT(ybT[:, :, qs], psb(tb_)[:, 0:128].rearrange("p (g q) -> p g q", g=2), ZBh[:, :, qs], ALU.mult),
                 reads=[tk, ("ZBh", c["r"] * 64 // 1024)], writes=["ybT"])

        nctx = {}
        if half == 0:
            S.dma("pool", "L_fwa", wpab[:], E["wpa_d"][:, :, :], writes=["wpab"])
            S.dma("pool", "L_fwb", wpbb[:], E["wpb_d"][:, :, :], writes=["wpbb"])
            for j in range(4):
                wt, wk = wsr.next()
                S.dma("sp", "L_fws%d" % wk[1], wt[:], E["wout_d"][:, 2 * j:2 * j + 2, :], writes=[wk])
                S.op("dve", TT(woutb[:, 2 * j:2 * j + 2, :], wt[:], gate_bc[:].rearrange("p (o n) -> p o n", o=1).to_broadcast([128, 2, 1024]), ALU.mult),
                     reads=[wk, "gate_bc"], writes=["woutb"])
        for r in range(64 + 2):
            if r < 64:
                nctx[r] = n_stage1(r)
            if 0 <= r - 1 < 64:
                n_stage2(nctx[r - 1])
            if 0 <= r - 2 < 64:
                n_stage3(nctx.pop(r - 2))
        for g in range(2):
            S.dma("sp", "S_ybT", YBT[2 * half + g], ybT[:, g, :], reads=["ybT"])
    S.barrier()


def build_final(nc, S, E):
    T, cur, ps = E["T"], E["cur"], E["ps"]
    gate_bc, fg_bc, smallc = E["gate_bc"], E["fg_bc"], E["smallc"]
    YAT, YBT, GMT, x_d, y_d = E["YAT"], E["YBT"], E["GMT"], E["x_d"], E["y_d"]
    wpa_d, wpb_d, wout_d = E["wpa_d"], E["wpb_d"], E["wout_d"]
    cur[0] = E["REGION0"]
    NA_END, wpab, wpbb, woutb = S.shared_fw
    yar = Rot("f_ya", [T("f_ya%d" % i, [128, 8, 512], BF16) for i in range(2)])
    ybr = Rot("f_yb", [T("f_yb%d" % i, [128, 4, 512], BF16) for i in range(2)])
    gmr = Rot("f_gm", [T("f_gm%d" % i, [128, 16, 512], BF16) for i in range(2)])
    t1r = Rot("f_t1", [T("f_t1%d" % i, [128, 512], F32) for i in range(2)])
    t2r = Rot("f_t2", [T("f_t2%d" % i, [128, 512], F32) for i in range(2)])
    mgr = Rot("f_mg", [T("f_mg%d" % i, [128, 8, 512], BF16) for i in range(2)])
    xr = Rot("f_xt", [T("f_xt%d" % i, [128, 1024], F32) for i in range(5)])
    x2r = Rot("f_x2", [T("f_x2%d" % i, [128, 1024], F32) for i in range(2)])
    otr = Rot("f_ot", [T("f_ot%d" % i, [128, 1024], F32) for i in range(2)])
    junk = T("f_junk", [128, 1024], BF16)
    par = Rot("psP", [ps[0], ps[1], ps[2], ps[3]])
    outr = Rot("psO", [(ps[4], ps[5]), (ps[6], ps[7])])
    assert cur[0] <= NA_END, (cur[0], NA_END)
    def f_loads(tb):
        ts = slice(tb * 512, (tb + 1) * 512)
        ya, yak = yar.next()
        yb, ybk = ybr.next()
        gm, gmk = gmr.next()
        S.dma("sp", "L_fya%d" % yak[1], ya[:], YAT[:, :, ts].rearrange("g p t -> p g t"), writes=[yak])
        S.dma("sp", "L_fyb%d" % ybk[1], yb[:], YBT[:, :, ts].rearrange("g p t -> p g t"), writes=[ybk])
        S.dma("sp", "L_fgm%d" % gmk[1], gm[:], GMT[:, :, ts].rearrange("g p t -> p g t"), writes=[gmk])
        return ya, yak, yb, ybk, gm, gmk

    fl = {0: f_loads(0)}
    mgs = {}

    def stageP(tb):
        ya, yak, yb, ybk, gm, gmk = fl.pop(tb)
        if tb + 1 < NBLK:
            fl[tb + 1] = f_loads(tb + 1)
        mg, mgk = mgr.next()
        for fg in range(8):
            fs = slice(fg * 128, (fg + 1) * 128)
            pa, pak = par.next()
            S.op("pe", [MM(pa[:, :], wpab[:, kc, fs], ya[:, kc, :], start=(kc == 0), stop=(kc == 7)) for kc in range(8)],
                 reads=["wpab", yak], writes=[pak])
            pb_, pbk = par.next()
            S.op("pe", [MM(pb_[:, :], wpbb[:, kc, fs], yb[:, kc, :], start=(kc == 0), stop=(kc == 3)) for kc in range(4)],
                 reads=["wpbb", ybk], writes=[pbk])
            t1, t1k = t1r.next()
            t2, t2k = t2r.next()
            S.op("dve", TT(t1[:], pa[:, :], gm[:, fg, :], ALU.mult), reads=[pak, gmk], writes=[t1k])
            S.op("dve", TT(t2[:], pb_[:, :], gm[:, 8 + fg, :], ALU.mult), reads=[pbk, gmk], writes=[t2k])
            S.op("pool", TT(mg[:, fg, :], t1[:], t2[:], ALU.add), reads=[t1k, t2k], writes=[(mgk, fg)])
        mgs[tb] = (mg, mgk)

    def stageO(tb):
        mg, mgk = mgs.pop(tb)
        xtl = []
        for tt in range(4):
            tile = tb * 4 + tt
            xt, xk = xr.next()
            S.dma("sp", "L_fx%d" % xk[1], xt[:], x_d[tile * 128:(tile + 1) * 128, :], writes=[xk])
            xtl.append((xt, xk))
        for tt in range(4):
            tile = tb * 4 + tt
            xt, xk = xtl[tt]
            (o0, o1), ok = outr.next()
            specs = []
            for nh, ob in enumerate((o0, o1)):
                for fg in range(8):
                    specs.append(MM(ob[:, :], mg[:, fg, tt * 128:(tt + 1) * 128], woutb[:, fg, nh * 512:(nh + 1) * 512], start=(fg == 0), stop=(fg == 7)))
            S.op("pe", specs, reads=[(mgk, fg) for fg in range(8)] + ["woutb"], writes=[ok])
            x2, x2k = x2r.next()
            S.op("dve", [TT(x2[:, 0:512], o0[:, :], xt[:, 0:512], ALU.add), TT(x2[:, 512:1024], o1[:, :], xt[:, 512:1024], ALU.add)],
                 reads=[ok, xk], writes=[x2k])
            q = tile % 4
            sc = smallc[:, q * 4:q * 4 + 4]
            sck = ("smallc", q)
            S.op("act", ACT(junk[:], x2[:], AF.Square, accum_out=sc[:, 0:1]), reads=[x2k], writes=["f_junk", sck])
            S.op("act", ACT(sc[:, 1:2], sc[:, 0:1], AF.Sqrt, scale=1.0 / D, bias=EPS), reads=[sck], writes=[sck])
            S.op("dve", ("reciprocal", dict(out=sc[:, 2:3], in_=sc[:, 1:2])), reads=[sck], writes=[sck])
            ot, otk = otr.next()
            S.op("act", ACT(ot[:], x2[:], AF.Copy, scale=sc[:, 2:3]), reads=[x2k, sck], writes=[otk])
            S.op("pool", TT(ot[:], ot[:], fg_bc[:], ALU.mult), reads=[otk, "fg_bc"], writes=[otk])
            S.dma("pool", "S_fo%d" % otk[1], y_d[tile * 128:(tile + 1) * 128, :], ot[:], reads=[otk])

    for tb in range(NBLK + 1):
        if tb < NBLK:
            stageP(tb)
        if tb >= 1:
            stageO(tb - 1)


def _shared_layouts(inp):
    f = np.float32
    w_ada = np.asarray(inp["w_ada"], f)[0]
    w_in = np.asarray(inp["w_in"], f)[0]
    sh = {}
    sh["wada"] = np.ascontiguousarray(w_ada.reshape(8, 128, 3072).transpose(1, 0, 2))
    sh["rows"] = np.ascontiguousarray(np.concatenate(
        [np.asarray(inp["b_ada"], f)[0], np.asarray(inp["norm_gain"], f)[0], np.asarray(inp["final_gain"], f)])[None, :])
    wg = np.zeros((1024, 72), f)
    gc = w_in[:, 5120:5136].reshape(1024, 2, 2, 4)
    wg[:, 0:4] = gc[:, 0, 0]
    wg[:, 32:36] = gc[:, 1, 0]
    wg[:, 36:40] = gc[:, 0, 1]
    wg[:, 68:72] = gc[:, 1, 1]
    sh["wgate"] = np.ascontiguousarray(wg.reshape(8, 128, 72).transpose(1, 0, 2))
    wl = np.concatenate([w_in[:, 0:5120], w_in[:, 5136:9232]], axis=1)
    sh["win"] = np.ascontiguousarray(wl.reshape(8, 128, 18, 512).transpose(2, 1, 0, 3))
    cols = np.zeros((128, 88), f)
    cw = np.asarray(inp["conv_w"], f)[0]
    cb = np.asarray(inp["conv_b"], f)[0]
    cols[:, 0:48] = cw.reshape(3, 16, 128).transpose(2, 1, 0).reshape(128, 48)
    cols[:, 48:64] = cb.reshape(16, 128).T
    cols[:, 64:80] = np.asarray(inp["b_merge"], f)[0].reshape(16, 128).T
    cols[:, 80:88] = np.asarray(inp["mlstm_norm_gain"], f)[0].reshape(8, 128).T
    sh["cols"] = cols
    gb = np.zeros((36, 2), f)
    bi = np.asarray(inp["b_igate"], f)[0]
    bfg = np.asarray(inp["b_fgate"], f)[0]
    gb[0:4, 0] = bi[0]
    gb[32:36, 0] = bi[1]
    gb[0:4, 1] = bfg[0]
    gb[32:36, 1] = bfg[1]
    sh["gb"] = gb
    rpb = np.asarray(inp["rpb"], f)[0]
    kc = np.arange(64)[:, None]
    qc = np.arange(64)[None, :]
    ws = np.clip(qc - 8, 0, 48)
    colok = (kc >= ws) & (kc < ws + 16)
    dcidx = np.clip(kc - qc + 15, 0, 30)
    bt2 = np.full((128, 8, 14, 64), NEG, f)
    for j in range(14):
        for half in range(2):
            dr = j - 7 + half
            tab = np.where(colok[None], rpb[:, dr + 7][:, dcidx], f(NEG))
            bt2[half * 64:(half + 1) * 64, :, j, :] = tab.transpose(1, 0, 2)
    sh["bt2"] = np.ascontiguousarray(bt2.reshape(128, 4, 2, 14, 64).transpose(0, 1, 3, 2, 4))
    sh["wpa"] = np.ascontiguousarray(np.asarray(inp["w_proj_a"], f)[0].reshape(8, 128, 1024).transpose(1, 0, 2))
    sh["wpb"] = np.ascontiguousarray(np.asarray(inp["w_proj_b"], f)[0].reshape(4, 128, 1024).transpose(1, 0, 2))
    sh["wout"] = np.ascontiguousarray(np.asarray(inp["w_out"], f)[0].reshape(8, 128, 1024).transpose(1, 0, 2))
    sh["ident"] = np.eye(128, dtype=f)
    s_i = np.arange(128)[:, None]
    t_i = np.arange(128)[None, :]
    masks = np.zeros((128, 2, 128), f)
    masks[:, 0, :] = np.where(s_i <= t_i, 1.0 / 16, 0.0)
    masks[:, 1, :] = np.where(s_i >= t_i, 1.0 / 16, 0.0)
    sh["masks"] = masks
    sel = np.zeros((36, 8, 128), f)
    for j in range(8):
        sel[(j % 4) + 32 * (j // 4), j, :] = 1.0
    sh["sel"] = sel
    return sh


def make_in_maps(inp):
    sh = _shared_layouts(inp)
    x = np.asarray(inp["x"], np.float32)
    c = np.asarray(inp["c"], np.float32)
    maps = []
    for b in range(8):
        m = dict(sh)
        m["x"] = np.ascontiguousarray(x[b])
        m["c_l"] = np.ascontiguousarray(c[b].reshape(8, 128).T)
        maps.append(m)
    return maps


_NC_CACHE = {}


def kernel(**inputs):
    if "nc" not in _NC_CACHE:
        _NC_CACHE["nc"] = build_program()
    nc = _NC_CACHE["nc"]
    in_maps = make_in_maps(inputs)
    res = run_bass_kernel_spmd(nc, in_maps, core_ids=list(range(8)))
    return np.stack([np.asarray(r["y"], np.float32) for r in res.results], axis=0)
```

```python
import numpy as np
import concourse.bass as bass
import concourse.mybir as mybir
from concourse.bass_utils import run_bass_kernel_spmd

F32 = mybir.dt.float32
BF16 = mybir.dt.bfloat16
ALU = mybir.AluOpType
AF = mybir.ActivationFunctionType

S_ = 4096
D = 1024
NT = 32
NBLK = 8
H = 4
DH = 256
NH = 8
NEG = -30000.0
EPS = 1e-6
ENG_NAMES = ("pe", "act", "dve", "pool", "sp")


class Sched:
    def __init__(self, nc):
        self.nc = nc
        self.ops = {e: [] for e in ENG_NAMES}
        self.count = {}
        self.last_writer = {}
        self.readers = {}
        self.seen = {e: {} for e in ENG_NAMES}
        self.sem_names = ["pe", "act", "dve", "pool"]
        self.is_dma = set()
        self.n_instr = 0

    def _deps(self, reads, writes):
        deps = set()
        for k in reads:
            w = self.last_writer.get(k)
            if w is not None:
                deps.add(w)
        for k in writes:
            w = self.last_writer.get(k)
            if w is not None:
                deps.add(w)
            deps.update(self.readers.get(k, ()))
        return deps

    def _record(self, me, reads, writes):
        for k in reads:
            self.readers.setdefault(k, []).append(me)
        for k in writes:
            self.last_writer[k] = me
            self.readers[k] = []

    def _waits(self, eng, deps):
        need = {}
        for (s, i) in deps:
            if s in self.is_dma:
                i = self.count[s] - 1
            if need.get(s, -1) < i:
                need[s] = i
        waits = []
        for s, i in need.items():
            if self.seen[eng].get(s, -1) >= i:
                continue
            self.seen[eng][s] = i
            waits.append((s, i + 1))
        return waits

    def op(self, eng, specs, reads=(), writes=()):
        if isinstance(specs, tuple):
            specs = [specs]
        waits = self._waits(eng, self._deps(reads, writes))
        idx = self.count.get(eng, 0)
        self.count[eng] = idx + 1
        self.ops[eng].append((specs, waits, (eng, 1)))
        self._record((eng, idx), reads, writes)
        self.n_instr += len(specs)

    def dma(self, queue, stream, out, in_, reads=(), writes=()):
        if stream not in self.is_dma:
            self.is_dma.add(stream)
            self.sem_names.append(stream)
            self.count[stream] = 0
        waits = self._waits(queue, self._deps(reads, writes))
        idx = self.count[stream]
        self.count[stream] = idx + 1
        self.ops[queue].append(([("dma_start", dict(out=out, in_=in_))], waits, (stream, 16)))
        self._record((stream, idx), reads, writes)
        self.n_instr += 1

    def barrier(self):
        allw = [(s, c) for s, c in self.count.items() if c > 0]
        for e in ENG_NAMES:
            waits = []
            for s, c in allw:
                if self.seen[e].get(s, -1) >= c - 1:
                    continue
                self.seen[e][s] = c - 1
                waits.append((s, c))
            if waits:
                self.ops[e].append((None, waits, None))
        self.last_writer = {}
        self.readers = {}

    def emit(self):
        import contextlib
        nc = self.nc
        self.barrier()
        with contextlib.ExitStack() as st:
            sems = {s: st.enter_context(nc.semaphore("s_" + s)) for s in self.sem_names}
            block = st.enter_context(nc.Block())

            def run(engname):
                def body(eng):
                    for specs, waits, inc in self.ops[engname]:
                        for (s, v) in waits:
                            eng.wait_ge(sems[s], v * (16 if s in self.is_dma else 1))
                        if specs is None:
                            continue
                        ins = None
                        for (m, kw) in specs:
                            ins = getattr(eng, m)(**kw)
                        ins.then_inc(sems[inc[0]], inc[1])
                return body

            block.tensor(run("pe"))
            block.scalar(run("act"))
            block.vector(run("dve"))
            block.gpsimd(run("pool"))
            block.sync(run("sp"))


class Rot:
    def __init__(self, name, tiles):
        self.name, self.tiles, self.i = name, tiles, 0

    def next(self):
        j = self.i % len(self.tiles)
        self.i += 1
        return self.tiles[j], (self.name, j)


def MM(out, lhsT, rhs, start=True, stop=True):
    return ("matmul", dict(out=out, lhsT=lhsT, rhs=rhs, start=start, stop=stop))


def TR(out, in_, identity):
    return ("transpose", dict(out=out, in_=in_, identity=identity))


def ACT(out, in_, func, **kw):
    return ("activation", dict(out=out, in_=in_, func=func, **kw))


def TT(out, in0, in1, op):
    return ("tensor_tensor", dict(out=out, in0=in0, in1=in1, op=op))


def TS(out, in0, scalar1, scalar2, op0, op1=None):
    d = dict(out=out, in0=in0, scalar1=scalar1, scalar2=scalar2, op0=op0)
    if op1 is not None:
        d["op1"] = op1
    return ("tensor_scalar", d)


def STT(out, in0, scalar, in1, op0, op1):
    return ("scalar_tensor_tensor", dict(out=out, in0=in0, scalar=scalar, in1=in1, op0=op0, op1=op1))


def CP(out, in_):
    return ("tensor_copy", dict(out=out, in_=in_))


def MS(ap, v):
    return ("memset", dict(ap=ap, constant=v))


def SCAN(out, data0, data1, initial, op0, op1):
    return ("tensor_tensor_scan", dict(out=out, data0=data0, data1=data1, initial=initial, op0=op0, op1=op1))


def build_program(stop_after=None, debug=False):
    nc = bass.Bass("TRN2", target_bir_lowering=False)
    dbg_kind = "ExternalOutput" if debug else "Internal"

    def DIN(name, shape, dt=F32):
        return nc.dram_tensor(name, list(shape), dt, kind="ExternalInput").ap()

    def DSC(name, shape, dt=BF16):
        return nc.dram_tensor(name, list(shape), dt, kind=dbg_kind).ap()

    x_d = DIN("x", [S_, D])
    c_d = DIN("c_l", [128, 8])
    wada_d = DIN("wada", [128, 8, 3072])
    rows_d = DIN("rows", [1, 5120])
    wgate_d = DIN("wgate", [128, 8, 72])
    win_d = DIN("win", [18, 128, 8, 512])
    cols_d = DIN("cols", [128, 88])
    gb_d = DIN("gb", [36, 2])
    bt2_d = DIN("bt2", [128, 4, 14, 2, 64])
    wpa_d = DIN("wpa", [128, 8, 1024])
    wpb_d = DIN("wpb", [128, 4, 1024])
    wout_d = DIN("wout", [128, 8, 1024])
    ident_d = DIN("ident", [128, 128])
    masks_d = DIN("masks", [128, 2, 128])
    sel_d = DIN("sel", [36, 8, 128])
    y_d = nc.dram_tensor("y", [S_, D], F32, kind="ExternalOutput").ap()

    QT = DSC("QT", [4, 128, 2, S_])
    KT = DSC("KT", [4, 128, 2, S_])
    VA = DSC("VA", [NT, 128, 4 * 258])
    OG = DSC("OG", [S_, D])
    ZAT = DSC("ZAT", [8, 128, S_])
    QBT = DSC("QBT", [4, 128, S_])
    KBT = DSC("KBT", [4, 128, S_])
    VBA = DSC("VBA", [NT, 128, 8 * 66])
    ZBT = DSC("ZBT", [4, 128, S_])
    GMT = DSC("GMT", [16, 128, S_])
    YAT = DSC("YAT", [8, 128, S_])
    YBT = DSC("YBT", [4, 128, S_])
    dbg = {}
    if debug:
        dbg["hT"] = nc.dram_tensor("dbg_hT", [128, 8, S_], BF16, kind="ExternalOutput").ap()
        dbg["TOKU"] = nc.dram_tensor("dbg_TOKU", [128, NT, 36], F32, kind="ExternalOutput").ap()
        dbg["TOKC"] = nc.dram_tensor("dbg_TOKC", [128, NT, 36], F32, kind="ExternalOutput").ap()
        dbg["DECB"] = nc.dram_tensor("dbg_DECB", [128, 8, NT], F32, kind="ExternalOutput").ap()
        dbg["Hacc"] = nc.dram_tensor("dbg_Hacc", [4, 128, NT, 256], F32, kind="ExternalOutput").ap()
        for nm in ("T1", "T2", "T3"):
            dbg[nm] = nc.dram_tensor("dbg_" + nm, [36, S_], F32, kind="ExternalOutput").ap()

    def dstop(tag):
        if stop_after != tag:
            return False
        S.dma("sp", "S_dbg", dbg["T1"][:, :], T1[0:36, :], reads=["T1g"])
        S.dma("sp", "S_dbg", dbg["T2"][:, :], T2[0:36, :], reads=["T2g"])
        S.dma("sp", "S_dbg", dbg["T3"][:, :], T3[0:36, :], reads=["T3g"])
        S.emit()
        return True

    SB_LO = 16512
    SB_HI = 229344
    cur = [SB_LO]

    def T(name, shape, dt):
        n = int(np.prod(shape[1:])) * (4 if dt == F32 else 2)
        n = (n + 31) // 32 * 32
        assert cur[0] + n <= SB_HI, (name, cur[0], n)
        t = nc.alloc_sbuf_tensor_at(name, list(shape), dt, offset=cur[0])
        cur[0] += n
        return t

    ps = [nc.alloc_psum_tensor("ps%d" % i, [128, 512], F32) for i in range(8)]

    def psb(i):
        return ps[i][:].bitcast(BF16)

    S = Sched(nc)

    identb = T("identb", [128, 128], BF16)
    identf = T("identf", [128, 128], F32)
    maskT = T("maskT", [128, 2, 128], F32)
    cols = T("cols", [128, 88], F32)
    TOKU = T("TOKU", [128, NT, 36], F32)
    TOKC = T("TOKC", [128, NT, 36], F32)
    DECB = T("DECB", [128, 8, NT], F32)
    gate_bc = T("gate_bc", [128, 1024], F32)
    fg_bc = T("fg_bc", [128, 1024], F32)
    ones_c = T("ones_c", [128, 128], F32)
    smallc = T("smallc", [128, 64], F32)
    REGION0 = cur[0]

    S.dma("sp", "L_const", identf[:], ident_d[:, :], writes=["identf"])
    S.dma("pool", "L_constb", identb[:], ident_d[:, :], writes=["identb"])
    S.dma("sp", "L_const", maskT[:], masks_d[:, :, :], writes=["maskT"])
    S.dma("sp", "L_const", cols[:], cols_d[:, :], writes=["cols"])
    S.op("pool", MS(ones_c[:], 1.0), writes=["ones_c"])

    cur[0] = REGION0
    modrow = T("modrow", [1, 3072], F32)
    rows = T("rows", [1, 5120], F32)
    wst = [T("wada_st%d" % i, [128, 8, 512], F32) for i in range(2)]
    A_END = cur[0]
    cur[0] = REGION0 + 65536
    G1_bc = T("G1_bc", [128, 1024], F32)
    sh_bc = T("sh_bc", [128, 1024], F32)
    c_sb = T("c_sb", [128, 8], F32)
    cond = T("cond", [128, 8], F32)
    G1row = T("G1row", [1, 1024], F32)
    assert A_END <= REGION0 + 65536

    S.dma("sp", "L_const", c_sb[:], c_d[:, :], writes=["c_sb"])
    S.dma("sp", "L_const", rows[:], rows_d[:, :], writes=["rows"])
    S.op("act", ACT(cond[:], c_sb[:], AF.Silu), reads=["c_sb"], writes=["cond"])
    wrot = Rot("wada_st", wst)
    for g in range(6):
        wt, wk = wrot.next()
        S.dma("sp", "L_wada%d" % wk[1], wt[:], wada_d[:, :, g * 512:(g + 1) * 512], writes=[wk])
        S.op("pe", [MM(ps[g % 2][0:1, :], cond[:, kc:kc + 1], wt[:, kc, :], start=(kc == 0), stop=(kc == 7)) for kc in range(8)],
             reads=["cond", wk], writes=[("ps", g % 2)])
        S.op("dve", TT(modrow[0:1, g * 512:(g + 1) * 512], ps[g % 2][0:1, :], rows[0:1, g * 512:(g + 1) * 512], ALU.add),
             reads=[("ps", g % 2), "rows"], writes=["modrow"])
    S.op("dve", STT(G1row[0:1, :], modrow[0:1, 1024:2048], 1.0, rows[0:1, 3072:4096], ALU.add, ALU.mult),
         reads=["modrow", "rows"], writes=["G1row"])
    bc_jobs = [(G1_bc, G1row[0:1, :], "G1row", "G1_bc"), (sh_bc, modrow[0:1, 0:1024], "modrow", "sh_bc"),
               (gate_bc, modrow[0:1, 2048:3072], "modrow", "gate_bc"), (fg_bc, rows[0:1, 4096:5120], "rows", "fg_bc")]
    bi = 0
    for (dst, src, skey, dkey) in bc_jobs:
        for hf in range(2):
            b = 2 + (bi % 2)
            bi += 1
            S.op("pe", MM(ps[b][:, :], ones_c[0:1, 0:128], src[:, hf * 512:(hf + 1) * 512]), reads=[skey, "ones_c"], writes=[("ps", b)])
            S.op("act", ACT(dst[:, hf * 512:(hf + 1) * 512], ps[b][:, :], AF.Copy), reads=[("ps", b)], writes=[dkey])
    S.barrier()

    cur[0] = REGION0
    hT = T("hT", [128, 8, S_], BF16)
    C_START = cur[0]
    assert C_START == REGION0 + 65536
    cur[0] = C_START + 8192 + 2 * 32
    cur[0] = (cur[0] + 4096 + 31) // 32 * 32
    xts = [T("xt%d" % i, [128, 1024], F32) for i in range(4)]
    xns = [T("xn%d" % i, [128, 1024], BF16) for i in range(3)]
    junkb = T("junkb", [128, 1024], BF16)
    xrot = Rot("xt", xts)
    xnrot = Rot("xn", xns)
    def b_stage1(tt):
        xt, xk = xrot.next()
        xn, xnk = xnrot.next()
        sc = smallc[:, (tt % 4) * 4:(tt % 4) * 4 + 4]
        sck = ("smallc", tt % 4)
        S.dma("sp", "L_xt%d" % xk[1], xt[:], x_d[tt * 128:(tt + 1) * 128, :], writes=[xk])
        S.op("act", ACT(junkb[:], xt[:], AF.Square, accum_out=sc[:, 0:1]), reads=[xk], writes=["junkb", sck])
        S.op("act", ACT(sc[:, 1:2], sc[:, 0:1], AF.Sqrt, scale=1.0 / D, bias=EPS), reads=[sck], writes=[sck])
        S.op("dve", ("reciprocal", dict(out=sc[:, 2:3], in_=sc[:, 1:2])), reads=[sck], writes=[sck])
        S.op("dve", STT(xt[:], xt[:], sc[:, 2:3], G1_bc[:], ALU.mult, ALU.mult), reads=[xk, sck, "G1_bc"], writes=[xk])
        S.op("dve" if tt % 3 != 2 else "pool", TT(xn[:], xt[:], sh_bc[:], ALU.add), reads=[xk, "sh_bc"], writes=[xnk])
        return xn, xnk

    def b_stage2(tt, xn, xnk):
        b = 6 + (tt % 2)
        S.op("pe", [TR(psb(b)[:, kc * 128:(kc + 1) * 128], xn[:, kc * 128:(kc + 1) * 128], identb[:]) for kc in range(8)],
             reads=[xnk, "identb"], writes=[("ps", b)])
        S.op("act", ACT(hT[:, :, tt * 128:(tt + 1) * 128], psb(b).rearrange("p (a b) -> p a b", a=8), AF.Copy),
             reads=[("ps", b)], writes=[("hT", tt)])

    bctx = {}
    for tt in range(NT + 1):
        if tt < NT:
            bctx[tt] = b_stage1(tt)
        if tt >= 1:
            b_stage2(tt - 1, *bctx.pop(tt - 1))
    if debug:
        S.dma("sp", "S_dbg", dbg["hT"][:, :, :], hT[:], reads=[("hT", tt) for tt in range(NT)])
    S.barrier()
    if stop_after == "B":
        S.emit()
        return nc

    cur[0] = C_START
    Wb = [T("Wb%d" % i, [128, 8, 512], BF16) for i in range(3)]
    wgb = T("wgb", [128, 8, 72], BF16)
    UA = T("UA", [128, S_ + 2], F32)
    UB = T("UB", [128, S_ + 2], F32)
    ACC = T("ACC", [128, S_], F32)
    obufs = [T("obuf%d" % i, [128, S_], BF16) for i in range(2)]
    tms = [T("tmst%d" % i, [128, 2112], BF16) for i in range(2)]
    gbc = T("gbc", [36, 2], F32)
    MPt = T("MPt", [36, NT], F32)
    MOt = T("MOt", [36, NT], F32)
    DECt = T("DECt", [36, NT], F32)
    selt = T("selt", [36, 8, 128], F32)
    C_END = cur[0]
    T1 = nc.alloc_sbuf_tensor_at("T1g", [128, S_], F32, offset=C_START + 3 * 8192 + 1152)
    T2 = nc.alloc_sbuf_tensor_at("T2g", [128, S_], F32, offset=C_START + 3 * 8192 + 1152 + 16416)
    T3 = ACC
    ONESF = nc.alloc_sbuf_tensor_at("ONESF", [128, S_], F32, offset=C_START + 3 * 8192 + 1152 + 2 * 16416 + 16384)

    wrot = Rot("Wb", Wb)
    orot = Rot("obuf", obufs)
    tmrot = Rot("tmst", tms)
    urot = Rot("U", [UA, UB])
    psrot = Rot("ps", ps[0:8])
    evtog = [0]

    S.dma("sp", "L_const", gbc[:], gb_d[:, :], writes=["gbc"])
    S.dma("sp", "L_const", selt[:], sel_d[:, :, :], writes=["selt"])
    S.dma("pool", "L_wgb", wgb[:], wgate_d[:, :, :], writes=["wgb"])
    allhT = [("hT", tt) for tt in range(NT)]

    def hkeys(tb):
        return [("hT", tb * 4 + j) for j in range(4)]

    for tb in range(NBLK):
        for gi, (Tt, col, tkey) in enumerate(((T1, 0, "T1g"), (T2, 1, "T2g"))):
            pt, pk = psrot.next()
            S.op("pe", [MM(pt[0:36, :], wgb[:, kc, gi * 36:(gi + 1) * 36], hT[:, kc, tb * 512:(tb + 1) * 512],
                           start=(kc == 0), stop=(kc == 7)) for kc in range(8)],
                 reads=["wgb"] + hkeys(tb), writes=[pk])
            S.op("act", ACT(Tt[0:36, tb * 512:(tb + 1) * 512], pt[0:36, :], AF.Identity, bias=gbc[0:36, col:col + 1]),
                 reads=[pk, "gbc"], writes=[tkey])

    if dstop("D0"):
        return nc
    r36 = slice(0, 36)
    fw = slice(0, 4)
    bw = slice(32, 36)
    S.op("act", ACT(T2[r36, :], T2[r36, :], AF.Exp, scale=-1.0), reads=["T2g"], writes=["T2g"])
    S.op("act", ACT(T2[r36, :], T2[r36, :], AF.Ln, bias=1.0), reads=["T2g"], writes=["T2g"])
    S.op("pool", MS(T3[r36, :], 0.0), writes=["T3g"])
    S.op("pool", MS(ONESF[r36, :], 1.0), writes=["ONESF"])
    S.op("pool", [MS(MPt[:], 0.0), MS(MOt[:], 0.0)], writes=["MPt", "MOt"])
    S.op("dve", SCAN(T3[fw, :], ONESF[fw, :], T2[fw, :], 0.0, ALU.mult, ALU.add),
         reads=["T2g", "ONESF"], writes=["T3g"])
    S.op("dve", SCAN(T3[bw, ::-1], ONESF[bw, :], T2[bw, ::-1], 0.0, ALU.mult, ALU.add),
         reads=["T2g", "ONESF"], writes=["T3g"])
    if dstop("D1"):
        return nc
    S.op("dve", TT(T1[r36, :], T1[r36, :], T3[r36, :], ALU.add), reads=["T1g", "T3g"], writes=["T1g"])
    S.op("dve", SCAN(T2[fw, :], ONESF[fw, :], T1[fw, :], 0.0, ALU.mult, ALU.max),
         reads=["T1g", "ONESF"], writes=["T2g"])
    S.op("dve", SCAN(T2[bw, ::-1], ONESF[bw, :], T1[bw, ::-1], 0.0, ALU.mult, ALU.max),
         reads=["T1g", "ONESF"], writes=["T2g"])
    if dstop("D2"):
        return nc
    M3 = T2[:].rearrange("p (k t) -> p k t", t=128)
    S.op("dve", [CP(MPt[fw, 1:NT], M3[fw, 0:NT - 1, 127]), CP(MOt[fw, :], M3[fw, :, 127])],
         reads=["T2g", "MPt", "MOt"], writes=["MPt", "MOt"])
    S.op("dve", [CP(MPt[bw, 0:NT - 1], M3[bw, 1:NT, 0]), CP(MOt[bw, :], M3[bw, :, 0])],
         reads=["T2g", "MPt", "MOt"], writes=["MPt", "MOt"])
    S.op("dve", TT(DECt[:], MPt[:], MOt[:], ALU.subtract), reads=["MPt", "MOt"], writes=["DECt"])
    S.op("act", ACT(DECt[:], DECt[:], AF.Exp), reads=["DECt"], writes=["DECt"])
    if dstop("D3"):
        return nc
    MPb = MPt[:].rearrange("p (k o) -> p k o", o=1).to_broadcast([36, NT, 128])
    T1v = T1[r36, :].rearrange("p (k t) -> p k t", t=128)
    T3v = T3[r36, :].rearrange("p (k t) -> p k t", t=128)
    S.op("dve", TT(T3v, T3v, MPb, ALU.subtract), reads=["T3g", "MPt"], writes=["T3g"])
    S.op("act", ACT(T3[r36, :], T3[r36, :], AF.Exp), reads=["T3g"], writes=["T3g"])
    S.op("dve", TT(T1v, T1v, MPb, ALU.subtract), reads=["T1g", "MPt"], writes=["T1g"])
    S.op("act", ACT(T1[r36, :], T1[r36, :], AF.Exp), reads=["T1g"], writes=["T1g"])
    if dstop("D4"):
        return nc
    for (src, skey, dst, dkey) in ((T1, "T1g", TOKU, "TOKU"), (T3, "T3g", TOKC, "TOKC")):
        for k0 in range(0, NT, 14):
            n = min(14, NT - k0)
            pt, pk = psrot.next()
            S.op("pe", [TR(pt[:, j * 36:(j + 1) * 36], src[0:36, (k0 + j) * 128:(k0 + j + 1) * 128], identf[0:36, 0:36]) for j in range(n)],
                 reads=[skey, "identf"], writes=[pk])
            S.op("dve", CP(dst[:, k0:k0 + n, :], pt[:, 0:n * 36].rearrange("p (a b) -> p a b", b=36)), reads=[pk], writes=[dkey])
    pt, pk = psrot.next()
    S.op("pe", [MM(pt[:, j * NT:(j + 1) * NT], selt[0:36, j, :], DECt[0:36, :]) for j in range(8)],
         reads=["selt", "DECt"], writes=[pk])
    S.op("dve", CP(DECB[:], pt[:, 0:8 * NT].rearrange("p (a b) -> p a b", b=NT)), reads=[pk], writes=["DECB"])
    if debug:
        S.dma("sp", "S_dbg", dbg["TOKU"][:, :, :], TOKU[:], reads=["TOKU"])
        S.dma("sp", "S_dbg", dbg["TOKC"][:, :, :], TOKC[:], reads=["TOKC"])
        S.dma("sp", "S_dbg", dbg["DECB"][:, :, :], DECB[:], reads=["DECB"])
    S.barrier()
    if stop_after == "D":
        S.emit()
        return nc

    S.op("pool", [MS(UA[:, 0:1], 0.0), MS(UA[:, S_ + 1:S_ + 2], 0.0), MS(UB[:, 0:1], 0.0), MS(UB[:, S_ + 1:S_ + 2], 0.0)],
         writes=[("U", 0), ("U", 1)])

    def load_w(g):
        wt, wk = wrot.next()
        S.dma("pool", "L_Wb%d" % wk[1], wt[:], win_d[g], writes=[wk])
        return wt, wk

    pend_tail = []

    def flush_tail():
        while pend_tail:
            inf = pend_tail.pop(0)
            ob, ok = orot.next()
            S.op("act", ACT(ob[:], ACC[:], AF.Silu), reads=["ACC"], writes=[ok])
            S.dma("sp", "S_obuf%d" % ok[1], inf["dst"], ob[:], reads=[ok], writes=[inf["dkey"]])

    def fm_group(g, kind, sub_info):
        wt, wk = load_w(g)
        for sub in range(4):
            info = sub_info(sub)
            if kind == "conv":
                U, uk = urot.next()
            else:
                ob, ok = orot.next()
            for tb in range(NBLK):
                pt, pk = psrot.next()
                S.op("pe", [MM(pt[:, :], wt[:, kc, sub * 128:(sub + 1) * 128], hT[:, kc, tb * 512:(tb + 1) * 512],
                               start=(kc == 0), stop=(kc == 7)) for kc in range(8)],
                     reads=[wk] + hkeys(tb), writes=[pk])
                if kind == "conv":
                    S.op("act", ACT(U[:, 1 + tb * 512:1 + (tb + 1) * 512], pt[:, :], AF.Copy), reads=[pk], writes=[uk])
                elif kind == "silu":
                    S.op("act", ACT(ob[:, tb * 512:(tb + 1) * 512], pt[:, :], AF.Silu), reads=[pk], writes=[ok])
                elif kind == "sigb":
                    S.op("act", ACT(ob[:, tb * 512:(tb + 1) * 512], pt[:, :], AF.Sigmoid, bias=info["bias"]), reads=[pk, "cols"], writes=[ok])
                elif kind == "copy":
                    evtog[0] ^= 1
                    if evtog[0]:
                        S.op("dve", TS(ob[:, tb * 512:(tb + 1) * 512], pt[:, :], info["scale"], None, ALU.mult), reads=[pk], writes=[ok])
                    else:
                        S.op("act", ACT(ob[:, tb * 512:(tb + 1) * 512], pt[:, :], AF.Copy, scale=info["scale"]), reads=[pk], writes=[ok])
            if kind == "conv":
                flush_tail()
                cg = info["cg"]
                w0 = cols[:, cg * 3 + 0:cg * 3 + 1]
                w1 = cols[:, cg * 3 + 1:cg * 3 + 2]
                w2 = cols[:, cg * 3 + 2:cg * 3 + 3]
                cb = cols[:, 48 + cg:49 + cg]
                S.op("dve", TS(ACC[:], U[:, 1:S_ + 1], w1, cb, ALU.mult, ALU.add), reads=[uk, "cols"], writes=["ACC"])
                S.op("dve", STT(ACC[:], U[:, 0:S_], w0, ACC[:], ALU.mult, ALU.add), reads=[uk, "cols", "ACC"], writes=["ACC"])
                S.op("dve", STT(ACC[:], U[:, 2:S_ + 2], w2, ACC[:], ALU.mult, ALU.add), reads=[uk, "cols", "ACC"], writes=["ACC"])
                pend_tail.append(info)
            else:
                S.dma("sp", "S_obuf%d" % ok[1], info["dst"], ob[:], reads=[ok], writes=[info["dkey"]])

    def tm_group(g, kind, col0):
        wt, wk = load_w(g)
        for t4 in range(NT // 4):
            st, sk = tmrot.next()
            if kind == "va":
                sv = st[:, 0:4 * 2 * 258].rearrange("p (t h c) -> p t h c", t=4, h=2)
                S.op("pool", [MS(sv[:, :, :, 256:257], 1.0), MS(sv[:, :, :, 257:258], 0.0)], writes=[sk])
            elif kind == "vb":
                sv = st[:, 0:4 * 8 * 66].rearrange("p (t h c) -> p t h c", t=4, h=8)
                S.op("pool", [MS(sv[:, :, :, 64:65], 1.0), MS(sv[:, :, :, 65:66], 0.0)], writes=[sk])
            else:
                sv = st[:, 0:2048].rearrange("p (t c) -> p t c", t=4)
            for j in range(4):
                tt = t4 * 4 + j
                pt, pk = psrot.next()
                S.op("pe", [MM(pt[:, :], hT[:, kc, tt * 128:(tt + 1) * 128], wt[:, kc, :], start=(kc == 0), stop=(kc == 7)) for kc in range(8)],
                     reads=[wk, ("hT", tt)], writes=[pk])
                if kind == "va":
                    S.op("dve", CP(sv[:, j, :, 0:256], pt[:, :].rearrange("p (h c) -> p h c", h=2)), reads=[pk], writes=[sk])
                elif kind == "vb":
                    S.op("dve", CP(sv[:, j, :, 0:64], pt[:, :].rearrange("p (h c) -> p h c", h=8)), reads=[pk], writes=[sk])
                else:
                    S.op("act", ACT(sv[:, j, :], pt[:, :], AF.Sigmoid), reads=[pk], writes=[sk])
            tsl = slice(t4 * 4, t4 * 4 + 4)
            if kind == "va":
                hd0 = col0
                dst = VA[tsl, :, hd0 * 258:(hd0 + 2) * 258].rearrange("t p c -> p t c")
                S.dma("sp", "S_tm%d" % sk[1], dst, st[:, 0:4 * 516].rearrange("p (t c) -> p t c", t=4), reads=[sk], writes=[("VA", t4, hd0)])
            elif kind == "vb":
                dst = VBA[tsl, :, :].rearrange("t p c -> p t c")
                S.dma("sp", "S_tm%d" % sk[1], dst, st[:, 0:4 * 528].rearrange("p (t c) -> p t c", t=4), reads=[sk], writes=[("VBA", t4)])
            else:
                dst = OG[t4 * 512:(t4 + 1) * 512, col0:col0 + 512].rearrange("(t p) c -> p t c", p=128)
                S.dma("sp", "S_tm%d" % sk[1], dst, sv, reads=[sk], writes=[("OG", t4, col0)])

    for g in (0, 1):
        fm_group(g, "conv", lambda sub, g=g: dict(cg=g * 4 + sub, dst=QT[(g * 4 + sub) // 2, :, (g * 4 + sub) % 2, :],
                                                 dkey=("QT", g * 4 + sub)))
    for g in (2, 3):
        fm_group(g, "conv", lambda sub, g=g: dict(cg=8 + (g - 2) * 4 + sub, dst=KT[((g - 2) * 4 + sub) // 2, :, ((g - 2) * 4 + sub) % 2, :],
                                                 dkey=("KT", (g - 2) * 4 + sub)))
    flush_tail()
    for g in (8, 9):
        fm_group(g, "silu", lambda sub, g=g: dict(dst=ZAT[(g - 8) * 4 + sub], dkey=("ZAT", (g - 8) * 4 + sub)))
    fm_group(13, "silu", lambda sub: dict(dst=ZBT[sub], dkey=("ZBT", sub)))
    fm_group(10, "copy", lambda sub: dict(scale=0.125, dst=QBT[sub], dkey=("QBT", sub)))
    fm_group(11, "copy", lambda sub: dict(scale=1.0, dst=KBT[sub], dkey=("KBT", sub)))
    tm_group(4, "va", 0)
    tm_group(5, "va", 2)
    tm_group(12, "vb", 0)
    tm_group(6, "og", 0)
    tm_group(7, "og", 512)
    for g in (14, 15, 16, 17):
        fm_group(g, "sigb", lambda sub, g=g: dict(bias=cols[:, 64 + (g - 14) * 4 + sub:65 + (g - 14) * 4 + sub],
                                                 dst=GMT[(g - 14) * 4 + sub], dkey=("GMT", (g - 14) * 4 + sub)))
    S.barrier()
    if stop_after == "C":
        S.emit()
        return nc

    if build_mlstm(nc, S, locals()):
        return nc
    if stop_after == "M":
        S.emit()
        return nc
    build_na(nc, S, locals())
    if stop_after == "N":
        S.emit()
        return nc
    build_final(nc, S, locals())
    S.emit()
    return nc


def build_mlstm(nc, S, E):
    T, cur, ps, psb = E["T"], E["cur"], E["ps"], E["psb"]
    identb, maskT, cols, TOKU, TOKC, DECB = E["identb"], E["maskT"], E["cols"], E["TOKU"], E["TOKC"], E["DECB"]
    QT, KT, VA, OG, ZAT, YAT = E["QT"], E["KT"], E["VA"], E["OG"], E["ZAT"], E["YAT"]
    dbg, debug = E["dbg"], E["debug"]
    cur[0] = E["REGION0"]
    qT = T("m_qT", [128, 2, S_], BF16)
    kT = T("m_kT", [128, 2, S_], BF16)
    ktok = T("m_ktok", [128, NT, 256], BF16)
    Vaug = T("m_Vaug", [128, NT, 258], BF16)
    Hacc = T("m_Hacc", [128, NT, 256], F32)
    ZATh = T("m_ZATh", [128, 2, S_], BF16)
    yaT = T("m_yaT", [128, 2, S_], BF16)
    OGt = Rot("OGt", [T("m_OGt%d" % i, [128, 4, 256], BF16) for i in range(3)])
    UVr = [Rot("UV%d" % d, [T("m_UV%d_%d" % (d, i), [128, 258], BF16) for i in range(4)]) for d in range(2)]
    Smr = [Rot("Sm%d" % d, [T("m_Sm%d_%d" % (d, i), [128, 128], BF16) for i in range(3)]) for d in range(2)]
    Zs = [T("m_Z%d" % d, [128, 2, 258], F32) for d in range(2)]
    Cbr = [Rot("Cb%d" % d, [T("m_Cb%d_%d" % (d, i), [128, 2, 258], BF16) for i in range(3)]) for d in range(2)]
    Htr = Rot("Htmp", [T("m_Htmp%d" % i, [128, 256], F32) for i in range(3)])
    hgr = Rot("hg", [T("m_hg%d" % i, [128, 256], F32) for i in range(4)])
    ytr = Rot("yatok", [T("m_yatok%d" % i, [128, 256], BF16) for i in range(4)])
    junk = T("m_junk", [128, 256], BF16)
    rcs = T("m_rcs", [128, 8, 4], F32)
    pcs = T("m_pcs", [128, 8, 4], F32)
    rci = [0]
    pci = [0]
    dcp = [((ps[4], ps[5]), [("ps", 4), ("ps", 5)]), ((ps[6], ps[7]), [("ps", 6), ("ps", 7)])]

    def loads(hd):
        S.dma("sp", "L_mq", qT[:], QT[hd], writes=["qT"])
        S.dma("sp", "L_mk", kT[:], KT[hd], writes=["kT"])
        for j0 in range(0, NT, 8):
            S.dma("sp", "L_mv", Vaug[:, j0:j0 + 8, :], VA[j0:j0 + 8, :, hd * 258:(hd + 1) * 258].rearrange("t p c -> p t c"),
                  writes=[("Vaug", j) for j in range(j0, j0 + 8)])

    loads(0)
    for hd in range(4):
        S.dma("sp", "L_mz", ZATh[:], ZAT[2 * hd:2 * hd + 2].rearrange("g p t -> p g t"), writes=["ZATh"])
        for k4 in range(8):
            kb = 6 + (k4 % 2)
            S.op("pe", [TR(psb(kb)[:, (kk * 2 + c) * 128:(kk * 2 + c + 1) * 128], kT[:, c, (k4 * 4 + kk) * 128:(k4 * 4 + kk + 1) * 128], identb[:])
                        for kk in range(4) for c in range(2)], reads=["kT", "identb"], writes=[("ps", kb)])
            if k4 % 2 == 0:
                S.op("act", ACT(ktok[:, k4 * 4:(k4 + 1) * 4, :].rearrange("p a b -> p (a b)"), psb(kb)[:, 0:1024], AF.Copy, scale=1.0 / 16),
                     reads=[("ps", kb)], writes=[("ktok", k4)])
            else:
                S.op("dve", TS(ktok[:, k4 * 4:(k4 + 1) * 4, :].rearrange("p a b -> p (a b)"), psb(kb)[:, 0:1024], 1.0 / 16, None, ALU.mult),
                     reads=[("ps", kb)], writes=[("ktok", k4)])
        if E["stop_after"] == "M0":
            S.emit()
            return True

        ctx = {}
        chain = [dict(kprev=None) for _ in range(2)]

        def kof(d, i):
            return i if d == 0 else NT - 1 - i

        def opUV(d, i):
            k = kof(d, i)
            ucol = TOKU[:, k, d * 32 + hd:d * 32 + hd + 1]
            UVb, uvk = UVr[d].next()
            S.op("dve", TS(UVb[:], Vaug[:, k, :], ucol, None, ALU.mult), reads=[("Vaug", k), "TOKU"], writes=[uvk])
            ctx[(d, i)] = dict(k=k, ch=slice(k * 128, (k + 1) * 128), UVb=UVb, uvk=uvk, cb=None, cbk=None)

        def opST(d, i):
            c_ = ctx[(d, i)]
            stp = ps[d][:, 0:128]
            S.op("pe", [MM(stp, kT[:, c, c_["ch"]], qT[:, c, c_["ch"]], start=(c == 0), stop=(c == 1)) for c in range(2)],
                 reads=["kT", "qT"], writes=[("ps", d)])

        def opMASK(d, i):
            c_ = ctx[(d, i)]
            Sm, smk = Smr[d].next()
            S.op("dve", TT(Sm[:], ps[d][:, 0:128], maskT[:, d, :], ALU.mult), reads=[("ps", d), "maskT"], writes=[smk])
            c_["Sm"], c_["smk"] = Sm, smk

        def opDC(d, i):
            c_ = ctx[(d, i)]
            (b0, b1), dks = dcp[d]
            k = c_["k"]
            S.op("pe", [MM(b0[:, 0:258], ktok[:, k, 0:128], c_["UVb"][:]), MM(b1[:, 0:258], ktok[:, k, 128:256], c_["UVb"][:])],
                 reads=[("ktok", k // 4), c_["uvk"]], writes=dks)

        def opZ(d, i):
            c_ = ctx[(d, i)]
            (b0, b1), dks = dcp[d]
            series = d * 4 + hd
            k = c_["k"]
            Z = Zs[d]
            zk = ("Z", d)
            if i == 0:
                S.op("dve", [CP(Z[:, 0, :], b0[:, 0:258]), CP(Z[:, 1, :], b1[:, 0:258])], reads=dks, writes=[zk])
            else:
                kp = chain[d]["kprev"]
                dprev = DECB[:, series, kp:kp + 1]
                S.op("dve", [STT(Z[:, 0, :], Z[:, 0, :], dprev, b0[:, 0:258], ALU.mult, ALU.add),
                             STT(Z[:, 1, :], Z[:, 1, :], dprev, b1[:, 0:258], ALU.mult, ALU.add)],
                     reads=dks + [zk, "DECB"], writes=[zk])
            cbn, cbk = Cbr[d].next()
            dcur = DECB[:, series, k:k + 1]
            S.op("act", ACT(cbn[:].rearrange("p a b -> p (a b)"), Z[:].rearrange("p a b -> p (a b)"), AF.Copy, scale=dcur), reads=[zk, "DECB"], writes=[cbk])
            c_["cb"], c_["cbk"] = cbn, cbk
            chain[d]["kprev"] = k

        def opNP(d, i):
            c_ = ctx[(d, i)]
            first = (i == 0)
            npb, npk = ps[2 + d], ("ps", 2 + d)
            npa = npb[:, 0:258]
            specs = [MM(npa, c_["Sm"][:], c_["UVb"][:], start=True, stop=first)]
            rd = [c_["smk"], c_["uvk"]]
            if not first:
                pv = ctx[(d, i - 1)]
                specs += [MM(npa, qT[:, c, c_["ch"]], pv["cb"][:, c, :], start=False, stop=(c == 1)) for c in range(2)]
                rd += ["qT", pv["cbk"]]
            S.op("pe", specs, reads=rd, writes=[npk])

        def opOUT(d, i):
            c_ = ctx[(d, i)]
            k = c_["k"]
            npb, npk = ps[2 + d], ("ps", 2 + d)
            j = rci[0] % 8
            rci[0] += 1
            rc = rcs[:, j, :]
            rck = ("rc", j)
            den = npb[:, 256:257]
            ccol = TOKC[:, k, d * 32 + hd:d * 32 + hd + 1]
            S.op("dve", TT(rc[:, 0:1], den, ccol, ALU.max), reads=[npk, "TOKC"], writes=[rck])
            S.op("dve", STT(rc[:, 1:2], den, -1.0, rc[:, 0:1], ALU.mult, ALU.max), reads=[npk, rck], writes=[rck])
            S.op("dve", ("reciprocal", dict(out=rc[:, 2:3], in_=rc[:, 1:2])), reads=[rck], writes=[rck])
            if i < NT // 2:
                S.op("act", ACT(Hacc[:, k, :], npb[:, 0:256], AF.Copy, scale=rc[:, 2:3]), reads=[npk, rck], writes=[("Hacc", k)])
            else:
                ht, htk = Htr.next()
                S.op("act", ACT(ht[:], npb[:, 0:256], AF.Copy, scale=rc[:, 2:3]), reads=[npk, rck], writes=[htk])
                S.op("pool", TT(Hacc[:, k, :], Hacc[:, k, :], ht[:], ALU.add), reads=[htk, ("Hacc", k)], writes=[("Hacc", k)])
            if i >= 1:
                ctx.pop((d, i - 1))

        for d in range(2):
            opUV(d, 0)
        for d in range(2):
            opUV(d, 1)
        for d in range(2):
            opST(d, 0)
        for d in range(2):
            opMASK(d, 0)
        for d in range(2):
            opDC(d, 0)
        for d in range(2):
            opZ(d, 0)
        for i in range(NT):
            nx = i + 1
            if i + 2 < NT:
                opUV(0, i + 2)
                opUV(1, i + 2)
            if nx < NT:
                opST(0, nx)
                opST(1, nx)
                opMASK(0, nx)
                opMASK(1, nx)
                if nx < NT - 1:
                    opDC(0, nx)
                    opDC(1, nx)
            opNP(0, i)
            opNP(1, i)
            if nx < NT - 1:
                opZ(0, nx)
            opOUT(0, i)
            if nx < NT - 1:
                opZ(1, nx)
            opOUT(1, i)

        if debug:
            S.dma("sp", "S_dbg", dbg["Hacc"][hd], Hacc[:], reads=[("Hacc", k) for k in range(NT)])
        if hd + 1 < 4:
            loads(hd + 1)

        pctx = {}

        def P1(k):
            k4, j = k // 4, k % 4
            if j == 0:
                ogt, ogk = OGt.next()
                S.dma("pool", "L_og%d" % ogk[1], ogt[:], OG[k4 * 512:(k4 + 1) * 512, hd * 256:(hd + 1) * 256].rearrange("(t p) c -> p t c", p=128),
                      writes=[ogk])
                pctx["og"] = (ogt, ogk)
            ogt, ogk = pctx["og"]
            hg, hgk = hgr.next()
            S.op("dve", TT(hg[:], Hacc[:, k, :], ogt[:, j, :], ALU.mult), reads=[("Hacc", k), ogk], writes=[hgk])
            q = pci[0] % 8
            pci[0] += 1
            pc = pcs[:, q, :]
            pck = ("pc", q)
            S.op("act", ACT(junk[:], hg[:], AF.Square, accum_out=pc[:, 0:1]), reads=[hgk], writes=["m_junk", pck])
            S.op("act", ACT(pc[:, 1:2], pc[:, 0:1], AF.Sqrt, scale=1.0 / DH, bias=EPS), reads=[pck], writes=[pck])
            pctx[k] = (hg, hgk, pc, pck)

        def P2(k):
            k4, j = k // 4, k % 4
            hg, hgk, pc, pck = pctx.pop(k)
            S.op("dve", ("reciprocal", dict(out=pc[:, 2:3], in_=pc[:, 1:2])), reads=[pck], writes=[pck])
            yt, ytk = ytr.next()
            S.op("dve", TS(yt[:], hg[:], pc[:, 2:3], None, ALU.mult), reads=[hgk, pck], writes=[ytk])
            S.op("pe", [TR(psb(7)[:, c * 512 + j * 128:c * 512 + (j + 1) * 128], yt[:, c * 128:(c + 1) * 128], identb[:]) for c in range(2)],
                 reads=[ytk, "identb"], writes=[("ps", 7)])
            if j == 3:
                for c in range(2):
                    S.op("dve", STT(yaT[:, c, k4 * 512:(k4 + 1) * 512], psb(7)[:, c * 512:(c + 1) * 512], cols[:, 80 + hd * 2 + c:81 + hd * 2 + c],
                                    ZATh[:, c, k4 * 512:(k4 + 1) * 512], ALU.mult, ALU.mult),
                         reads=[("ps", 7), "cols", "ZATh"], writes=[("yaT", c)])

        for k in range(NT + 2):
            if k < NT:
                P1(k)
            if k >= 2:
                P2(k - 2)
        for c in range(2):
            S.dma("pool", "S_yaT", YAT[hd * 2 + c], yaT[:, c, :], reads=[("yaT", c)])
        if E["stop_after"] == "M2":
            S.emit()
            return True
    S.barrier()


def build_na(nc, S, E):
    T, cur, ps, psb = E["T"], E["cur"], E["ps"], E["psb"]
    identb = E["identb"]
    QBT, KBT, VBA, ZBT, YBT, bt2_d = E["QBT"], E["KBT"], E["VBA"], E["ZBT"], E["YBT"], E["bt2_d"]
    cur[0] = E["REGION0"]
    bt2b = T("n_bt2", [128, 4, 14, 2, 64], BF16)
    QBD = T("n_QBD", [128, 2, 2, S_], BF16)
    kbT = T("n_kbT", [128, 2, S_], BF16)
    ZBh = T("n_ZBh", [128, 2, S_], BF16)
    ybT = T("n_ybT", [128, 2, S_], BF16)
    VE = T("n_VE", [128, NT, 4, 66], BF16)
    VO = T("n_VO", [128, NT - 1, 4, 66], BF16)
    PTr = Rot("PT", [T("n_PT%d" % i, [128, 1024], BF16) for i in range(2)])
    otr = Rot("otok", [T("n_otok%d" % i, [64, 4, 64], BF16) for i in range(3)])
    recs = T("n_rec", [64, 4, 4], F32)
    str_ = Rot("psS", [(ps[0], ps[1]), (ps[2], ps[3])])
    pvr = Rot("psPV", [ps[4], ps[5]])
    trr = Rot("psT", [6, 7])
    ri = [0]
    NA_END = cur[0]
    wpab = T("f_wpa", [128, 8, 1024], BF16)
    wpbb = T("f_wpb", [128, 4, 1024], BF16)
    woutb = T("f_wout", [128, 8, 1024], BF16)
    wsr = Rot("f_wst", [T("f_wst%d" % i, [128, 2, 1024], F32) for i in range(1)])
    S.shared_fw = (NA_END, wpab, wpbb, woutb)
    gate_bc = E["gate_bc"]
    S.dma("pool", "L_bt2", bt2b[:], bt2_d[:, :, :, :, :], writes=["bt2b"])
    S.op("pool", [MS(QBD[0:64, :, 1, :], 0.0), MS(QBD[64:128, :, 0, :], 0.0)], writes=["QBDz"])
    for half in range(2):
        c0 = half * 4 * 66
        for cq in range(4):
            tsl = slice(cq * 1024, (cq + 1) * 1024)
            S.dma("sp", "L_nq%d" % cq, QBD[0:64, :, 0, tsl], QBT[2 * half:2 * half + 2, 0:64, tsl].rearrange("g p t -> p g t"), reads=["QBDz"], writes=[("qbT", cq)])
            S.dma("sp", "L_nq%d" % cq, QBD[64:128, :, 1, tsl], QBT[2 * half:2 * half + 2, 64:128, tsl].rearrange("g p t -> p g t"), reads=["QBDz"], writes=[("qbT", cq)])
            S.dma("sp", "L_nk%d" % cq, kbT[:, :, tsl], KBT[2 * half:2 * half + 2, :, tsl].rearrange("g p t -> p g t"), writes=[("kbT", cq)])
            j0 = cq * 8
            S.dma("sp", "L_nve%d" % cq, VE[:, j0:j0 + 8, :, :].rearrange("p t h c -> p t (h c)"),
                  VBA[j0:j0 + 8, :, c0:c0 + 264].rearrange("t p c -> p t c"), writes=[("VE", cq)])
            j1 = min(j0 + 8, NT - 1)
            S.dma("sp", "L_nvo%d" % cq, VO[0:64, j0:j1, :, :].rearrange("p t h c -> p t (h c)"),
                  VBA[j0:j1, 64:128, c0:c0 + 264].rearrange("t p c -> p t c"), writes=[("VO", cq)])
            S.dma("sp", "L_nvo%d" % cq, VO[64:128, j0:j1, :, :].rearrange("p t h c -> p t (h c)"),
                  VBA[j0 + 1:j1 + 1, 0:64, c0:c0 + 264].rearrange("t p c -> p t c"), writes=[("VO", cq)])
            S.dma("sp", "L_nz%d" % cq, ZBh[:, :, tsl], ZBT[2 * half:2 * half + 2, :, tsl].rearrange("g p t -> p g t"), writes=[("ZBh", cq)])

        def n_stage1(r):
            rs = min(max(r - 4, 0), 56)
            j0b = rs - r + 7
            qs = slice(r * 64, (r + 1) * 64)
            pair, sk = str_.next()
            specs = []
            for gi in range(2):
                bank = pair[gi]
                hp = half * 2 + gi
                specs.append(MM(bank[:, 0:512], identb[:], bt2b[:, hp, j0b:j0b + 7:2, :, :].rearrange("p i j q -> p i (j q)"), start=True, stop=False))
                for i in range(4):
                    tok = rs * 64 + i * 128
                    specs.append(MM(bank[:, i * 128:(i + 1) * 128], kbT[:, gi, tok:tok + 128], QBD[:, gi, :, qs], start=False, stop=(i == 3)))
            S.op("pe", specs, reads=["identb", "bt2b", ("qbT", (r * 64) // 1024)] + [("kbT", cc) for cc in sorted({(rs * 64) // 1024, (rs * 64 + 511) // 1024})], writes=[sk])
            PT, ptk = PTr.next()
            S.op("act", [ACT(PT[:, 0:512], pair[0][:, :], AF.Exp), ACT(PT[:, 512:1024], pair[1][:, :], AF.Exp)], reads=[sk], writes=[ptk])
            return dict(r=r, rs=rs, qs=qs, PT=PT, ptk=ptk)

        def n_stage2(c):
            rs, PT, ptk = c["rs"], c["PT"], c["ptk"]
            if rs % 2 == 0:
                Vx, vnm, tbase = VE, "VE", rs // 2
            else:
                Vx, vnm, tbase = VO, "VO", (rs - 1) // 2
            vkeys = [(vnm, cc) for cc in sorted({tbase // 8, (tbase + 3) // 8})]
            ob, ok = pvr.next()
            specs = []
            for hh in range(4):
                for i in range(4):
                    specs.append(MM(ob[0:64, hh * 66:(hh + 1) * 66], PT[:, (hh // 2) * 512 + i * 128 + (hh % 2) * 64:(hh // 2) * 512 + i * 128 + (hh % 2) * 64 + 64], Vx[:, tbase + i, hh, :],
                                    start=(i == 0), stop=(i == 3)))
            S.op("pe", specs, reads=[ptk] + vkeys, writes=[ok])
            q = ri[0] % 4
            ri[0] += 1
            rec = recs[:, q, :]
            rk = ("rec", q)
            ov = ob[0:64, 0:264].rearrange("p (h c) -> p h c", c=66)
            S.op("dve", ("reciprocal", dict(out=rec, in_=ov[:, :, 64])), reads=[ok], writes=[rk])
            ot, otk = otr.next()
            S.op("dve", TT(ot[:], ov[:, :, 0:64], rec.rearrange("p (h o) -> p h o", o=1).to_broadcast([64, 4, 64]), ALU.mult),
                 reads=[ok, rk], writes=[otk])
            c["ot"], c["otk"] = ot, otk

        def n_stage3(c):
            ot, otk, qs = c["ot"], c["otk"], c["qs"]
            tb_, tk = trr.next()
            otf = ot[:].rearrange("p h c -> p (h c)")
            S.op("pe", [TR(psb(tb_)[:, g * 64:(g + 1) * 64], otf[:, g * 128:(g + 1) * 128], identb[0:64, 0:64]) for g in range(2)],
                 reads=[otk, "identb"], writes=[tk])
            S.op("dve", TT(ybT[:, :, qs], psb(tb_)[:, 0:128].rearrange("p (g q) -> p g q", g=2), ZBh[:, :, qs], ALU.mult),
                 reads=[tk, ("ZBh", c["r"] * 64 // 1024)], writes=["ybT"])

        nctx = {}
        if half == 0:
            S.dma("pool", "L_fwa", wpab[:], E["wpa_d"][:, :, :], writes=["wpab"])
            S.dma("pool", "L_fwb", wpbb[:], E["wpb_d"][:, :, :], writes=["wpbb"])
            for j in range(4):
                wt, wk = wsr.next()
                S.dma("sp", "L_fws%d" % wk[1], wt[:], E["wout_d"][:, 2 * j:2 * j + 2, :], writes=[wk])
                S.op("dve", TT(woutb[:, 2 * j:2 * j + 2, :], wt[:], gate_bc[:].rearrange("p (o n) -> p o n", o=1).to_broadcast([128, 2, 1024]), ALU.mult),
                     reads=[wk, "gate_bc"], writes=["woutb"])
        for r in range(64 + 2):
            if r < 64:
                nctx[r] = n_stage1(r)
            if 0 <= r - 1 < 64:
                n_stage2(nctx[r - 1])
            if 0 <= r - 2 < 64:
                n_stage3(nctx.pop(r - 2))
        for g in range(2):
            S.dma("sp", "S_ybT", YBT[2 * half + g], ybT[:, g, :], reads=["ybT"])
    S.barrier()


def build_final(nc, S, E):
    T, cur, ps = E["T"], E["cur"], E["ps"]
    gate_bc, fg_bc, smallc = E["gate_bc"], E["fg_bc"], E["smallc"]
    YAT, YBT, GMT, x_d, y_d = E["YAT"], E["YBT"], E["GMT"], E["x_d"], E["y_d"]
    wpa_d, wpb_d, wout_d = E["wpa_d"], E["wpb_d"], E["wout_d"]
    cur[0] = E["REGION0"]
    NA_END, wpab, wpbb, woutb = S.shared_fw
    yar = Rot("f_ya", [T("f_ya%d" % i, [128, 8, 512], BF16) for i in range(2)])
    ybr = Rot("f_yb", [T("f_yb%d" % i, [128, 4, 512], BF16) for i in range(2)])
    gmr = Rot("f_gm", [T("f_gm%d" % i, [128, 16, 512], BF16) for i in range(2)])
    t1r = Rot("f_t1", [T("f_t1%d" % i, [128, 512], F32) for i in range(2)])
    t2r = Rot("f_t2", [T("f_t2%d" % i, [128, 512], F32) for i in range(2)])
    mgr = Rot("f_mg", [T("f_mg%d" % i, [128, 8, 512], BF16) for i in range(2)])
    xr = Rot("f_xt", [T("f_xt%d" % i, [128, 1024], F32) for i in range(5)])
    x2r = Rot("f_x2", [T("f_x2%d" % i, [128, 1024], F32) for i in range(2)])
    otr = Rot("f_ot", [T("f_ot%d" % i, [128, 1024], F32) for i in range(2)])
    junk = T("f_junk", [128, 1024], BF16)
    par = Rot("psP", [ps[0], ps[1], ps[2], ps[3]])
    outr = Rot("psO", [(ps[4], ps[5]), (ps[6], ps[7])])
    assert cur[0] <= NA_END, (cur[0], NA_END)
    def f_loads(tb):
        ts = slice(tb * 512, (tb + 1) * 512)
        ya, yak = yar.next()
        yb, ybk = ybr.next()
        gm, gmk = gmr.next()
        S.dma("sp", "L_fya%d" % yak[1], ya[:], YAT[:, :, ts].rearrange("g p t -> p g t"), writes=[yak])
        S.dma("sp", "L_fyb%d" % ybk[1], yb[:], YBT[:, :, ts].rearrange("g p t -> p g t"), writes=[ybk])
        S.dma("sp", "L_fgm%d" % gmk[1], gm[:], GMT[:, :, ts].rearrange("g p t -> p g t"), writes=[gmk])
        return ya, yak, yb, ybk, gm, gmk

    fl = {0: f_loads(0)}
    mgs = {}

    def stageP(tb):
        ya, yak, yb, ybk, gm, gmk = fl.pop(tb)
        if tb + 1 < NBLK:
            fl[tb + 1] = f_loads(tb + 1)
        mg, mgk = mgr.next()
        for fg in range(8):
            fs = slice(fg * 128, (fg + 1) * 128)
            pa, pak = par.next()
            S.op("pe", [MM(pa[:, :], wpab[:, kc, fs], ya[:, kc, :], start=(kc == 0), stop=(kc == 7)) for kc in range(8)],
                 reads=["wpab", yak], writes=[pak])
            pb_, pbk = par.next()
            S.op("pe", [MM(pb_[:, :], wpbb[:, kc, fs], yb[:, kc, :], start=(kc == 0), stop=(kc == 3)) for kc in range(4)],
                 reads=["wpbb", ybk], writes=[pbk])
            t1, t1k = t1r.next()
            t2, t2k = t2r.next()
            S.op("dve", TT(t1[:], pa[:, :], gm[:, fg, :], ALU.mult), reads=[pak, gmk], writes=[t1k])
            S.op("dve", TT(t2[:], pb_[:, :], gm[:, 8 + fg, :], ALU.mult), reads=[pbk, gmk], writes=[t2k])
            S.op("pool", TT(mg[:, fg, :], t1[:], t2[:], ALU.add), reads=[t1k, t2k], writes=[(mgk, fg)])
        mgs[tb] = (mg, mgk)

    def stageO(tb):
        mg, mgk = mgs.pop(tb)
        xtl = []
        for tt in range(4):
            tile = tb * 4 + tt
            xt, xk = xr.next()
            S.dma("sp", "L_fx%d" % xk[1], xt[:], x_d[tile * 128:(tile + 1) * 128, :], writes=[xk])
            xtl.append((xt, xk))
        for tt in range(4):
            tile = tb * 4 + tt
            xt, xk = xtl[tt]
            (o0, o1), ok = outr.next()
            specs = []
            for nh, ob in enumerate((o0, o1)):
                for fg in range(8):
                    specs.append(MM(ob[:, :], mg[:, fg, tt * 128:(tt + 1) * 128], woutb[:, fg, nh * 512:(nh + 1) * 512], start=(fg == 0), stop=(fg == 7)))
            S.op("pe", specs, reads=[(mgk, fg) for fg in range(8)] + ["woutb"], writes=[ok])
            x2, x2k = x2r.next()
            S.op("dve", [TT(x2[:, 0:512], o0[:, :], xt[:, 0:512], ALU.add), TT(x2[:, 512:1024], o1[:, :], xt[:, 512:1024], ALU.add)],
                 reads=[ok, xk], writes=[x2k])
            q = tile % 4
            sc = smallc[:, q * 4:q * 4 + 4]
            sck = ("smallc", q)
            S.op("act", ACT(junk[:], x2[:], AF.Square, accum_out=sc[:, 0:1]), reads=[x2k], writes=["f_junk", sck])
            S.op("act", ACT(sc[:, 1:2], sc[:, 0:1], AF.Sqrt, scale=1.0 / D, bias=EPS), reads=[sck], writes=[sck])
            S.op("dve", ("reciprocal", dict(out=sc[:, 2:3], in_=sc[:, 1:2])), reads=[sck], writes=[sck])
            ot, otk = otr.next()
            S.op("act", ACT(ot[:], x2[:], AF.Copy, scale=sc[:, 2:3]), reads=[x2k, sck], writes=[otk])
            S.op("pool", TT(ot[:], ot[:], fg_bc[:], ALU.mult), reads=[otk, "fg_bc"], writes=[otk])
            S.dma("pool", "S_fo%d" % otk[1], y_d[tile * 128:(tile + 1) * 128, :], ot[:], reads=[otk])

    for tb in range(NBLK + 1):
        if tb < NBLK:
            stageP(tb)
        if tb >= 1:
            stageO(tb - 1)


def _shared_layouts(inp):
    f = np.float32
    w_ada = np.asarray(inp["w_ada"], f)[0]
    w_in = np.asarray(inp["w_in"], f)[0]
    sh = {}
    sh["wada"] = np.ascontiguousarray(w_ada.reshape(8, 128, 3072).transpose(1, 0, 2))
    sh["rows"] = np.ascontiguousarray(np.concatenate(
        [np.asarray(inp["b_ada"], f)[0], np.asarray(inp["norm_gain"], f)[0], np.asarray(inp["final_gain"], f)])[None, :])
    wg = np.zeros((1024, 72), f)
    gc = w_in[:, 5120:5136].reshape(1024, 2, 2, 4)
    wg[:, 0:4] = gc[:, 0, 0]
    wg[:, 32:36] = gc[:, 1, 0]
    wg[:, 36:40] = gc[:, 0, 1]
    wg[:, 68:72] = gc[:, 1, 1]
    sh["wgate"] = np.ascontiguousarray(wg.reshape(8, 128, 72).transpose(1, 0, 2))
    wl = np.concatenate([w_in[:, 0:5120], w_in[:, 5136:9232]], axis=1)
    sh["win"] = np.ascontiguousarray(wl.reshape(8, 128, 18, 512).transpose(2, 1, 0, 3))
    cols = np.zeros((128, 88), f)
    cw = np.asarray(inp["conv_w"], f)[0]
    cb = np.asarray(inp["conv_b"], f)[0]
    cols[:, 0:48] = cw.reshape(3, 16, 128).transpose(2, 1, 0).reshape(128, 48)
    cols[:, 48:64] = cb.reshape(16, 128).T
    cols[:, 64:80] = np.asarray(inp["b_merge"], f)[0].reshape(16, 128).T
    cols[:, 80:88] = np.asarray(inp["mlstm_norm_gain"], f)[0].reshape(8, 128).T
    sh["cols"] = cols
    gb = np.zeros((36, 2), f)
    bi = np.asarray(inp["b_igate"], f)[0]
    bfg = np.asarray(inp["b_fgate"], f)[0]
    gb[0:4, 0] = bi[0]
    gb[32:36, 0] = bi[1]
    gb[0:4, 1] = bfg[0]
    gb[32:36, 1] = bfg[1]
    sh["gb"] = gb
    rpb = np.asarray(inp["rpb"], f)[0]
    kc = np.arange(64)[:, None]
    qc = np.arange(64)[None, :]
    ws = np.clip(qc - 8, 0, 48)
    colok = (kc >= ws) & (kc < ws + 16)
    dcidx = np.clip(kc - qc + 15, 0, 30)
    bt2 = np.full((128, 8, 14, 64), NEG, f)
    for j in range(14):
        for half in range(2):
            dr = j - 7 + half
            tab = np.where(colok[None], rpb[:, dr + 7][:, dcidx], f(NEG))
            bt2[half * 64:(half + 1) * 64, :, j, :] = tab.transpose(1, 0, 2)
    sh["bt2"] = np.ascontiguousarray(bt2.reshape(128, 4, 2, 14, 64).transpose(0, 1, 3, 2, 4))
    sh["wpa"] = np.ascontiguousarray(np.asarray(inp["w_proj_a"], f)[0].reshape(8, 128, 1024).transpose(1, 0, 2))
    sh["wpb"] = np.ascontiguousarray(np.asarray(inp["w_proj_b"], f)[0].reshape(4, 128, 1024).transpose(1, 0, 2))
    sh["wout"] = np.ascontiguousarray(np.asarray(inp["w_out"], f)[0].reshape(8, 128, 1024).transpose(1, 0, 2))
    sh["ident"] = np.eye(128, dtype=f)
    s_i = np.arange(128)[:, None]
    t_i = np.arange(128)[None, :]
    masks = np.zeros((128, 2, 128), f)
    masks[:, 0, :] = np.where(s_i <= t_i, 1.0 / 16, 0.0)
    masks[:, 1, :] = np.where(s_i >= t_i, 1.0 / 16, 0.0)
    sh["masks"] = masks
    sel = np.zeros((36, 8, 128), f)
    for j in range(8):
        sel[(j % 4) + 32 * (j // 4), j, :] = 1.0
    sh["sel"] = sel
    return sh


def make_in_maps(inp):
    sh = _shared_layouts(inp)
    x = np.asarray(inp["x"], np.float32)
    c = np.asarray(inp["c"], np.float32)
    maps = []
    for b in range(8):
        m = dict(sh)
        m["x"] = np.ascontiguousarray(x[b])
        m["c_l"] = np.ascontiguousarray(c[b].reshape(8, 128).T)
        maps.append(m)
    return maps


_NC_CACHE = {}


def kernel(**inputs):
    if "nc" not in _NC_CACHE:
        _NC_CACHE["nc"] = build_program()
    nc = _NC_CACHE["nc"]
    in_maps = make_in_maps(inputs)
    res = run_bass_kernel_spmd(nc, in_maps, core_ids=list(range(8)))
    return np.stack([np.asarray(r["y"], np.float32) for r in res.results], axis=0)
```

```python
import numpy as np
import concourse.bass as bass
import concourse.mybir as mybir
from concourse.bass_utils import run_bass_kernel_spmd

F32 = mybir.dt.float32
BF16 = mybir.dt.bfloat16
ALU = mybir.AluOpType
AF = mybir.ActivationFunctionType

S_ = 4096
D = 1024
NT = 32
NBLK = 8
H = 4
DH = 256
NH = 8
NEG = -30000.0
EPS = 1e-6
ENG_NAMES = ("pe", "act", "dve", "pool", "sp")


class Sched:
    def __init__(self, nc):
        self.nc = nc
        self.ops = {e: [] for e in ENG_NAMES}
        self.count = {}
        self.last_writer = {}
        self.readers = {}
        self.seen = {e: {} for e in ENG_NAMES}
        self.sem_names = ["pe", "act", "dve", "pool"]
        self.is_dma = set()
        self.n_instr = 0

    def _deps(self, reads, writes):
        deps = set()
        for k in reads:
            w = self.last_writer.get(k)
            if w is not None:
                deps.add(w)
        for k in writes:
            w = self.last_writer.get(k)
            if w is not None:
                deps.add(w)
            deps.update(self.readers.get(k, ()))
        return deps

    def _record(self, me, reads, writes):
        for k in reads:
            self.readers.setdefault(k, []).append(me)
        for k in writes:
            self.last_writer[k] = me
            self.readers[k] = []

    def _waits(self, eng, deps):
        need = {}
        for (s, i) in deps:
            if s in self.is_dma:
                i = self.count[s] - 1
            if need.get(s, -1) < i:
                need[s] = i
        waits = []
        for s, i in need.items():
            if self.seen[eng].get(s, -1) >= i:
                continue
            self.seen[eng][s] = i
            waits.append((s, i + 1))
        return waits

    def op(self, eng, specs, reads=(), writes=()):
        if isinstance(specs, tuple):
            specs = [specs]
        waits = self._waits(eng, self._deps(reads, writes))
        idx = self.count.get(eng, 0)
        self.count[eng] = idx + 1
        self.ops[eng].append((specs, waits, (eng, 1)))
        self._record((eng, idx), reads, writes)
        self.n_instr += len(specs)

    def dma(self, queue, stream, out, in_, reads=(), writes=()):
        if stream not in self.is_dma:
            self.is_dma.add(stream)
            self.sem_names.append(stream)
            self.count[stream] = 0
        waits = self._waits(queue, self._deps(reads, writes))
        idx = self.count[stream]
        self.count[stream] = idx + 1
        self.ops[queue].append(([("dma_start", dict(out=out, in_=in_))], waits, (stream, 16)))
        self._record((stream, idx), reads, writes)
        self.n_instr += 1

    def barrier(self):
        allw = [(s, c) for s, c in self.count.items() if c > 0]
        for e in ENG_NAMES:
            waits = []
            for s, c in allw:
                if self.seen[e].get(s, -1) >= c - 1:
                    continue
                self.seen[e][s] = c - 1
                waits.append((s, c))
            if waits:
                self.ops[e].append((None, waits, None))
        self.last_writer = {}
        self.readers = {}

    def emit(self):
        import contextlib
        nc = self.nc
        self.barrier()
        with contextlib.ExitStack() as st:
            sems = {s: st.enter_context(nc.semaphore("s_" + s)) for s in self.sem_names}
            block = st.enter_context(nc.Block())

            def run(engname):
                def body(eng):
                    for specs, waits, inc in self.ops[engname]:
                        for (s, v) in waits:
                            eng.wait_ge(sems[s], v * (16 if s in self.is_dma else 1))
                        if specs is None:
                            continue
                        ins = None
                        for (m, kw) in specs:
                            ins = getattr(eng, m)(**kw)
                        ins.then_inc(sems[inc[0]], inc[1])
                return body

            block.tensor(run("pe"))
            block.scalar(run("act"))
            block.vector(run("dve"))
            block.gpsimd(run("pool"))
            block.sync(run("sp"))


class Rot:
    def __init__(self, name, tiles):
        self.name, self.tiles, self.i = name, tiles, 0

    def next(self):
        j = self.i % len(self.tiles)
        self.i += 1
        return self.tiles[j], (self.name, j)


def MM(out, lhsT, rhs, start=True, stop=True):
    return ("matmul", dict(out=out, lhsT=lhsT, rhs=rhs, start=start, stop=stop))


def TR(out, in_, identity):
    return ("transpose", dict(out=out, in_=in_, identity=identity))


def ACT(out, in_, func, **kw):
    return ("activation", dict(out=out, in_=in_, func=func, **kw))


def TT(out, in0, in1, op):
    return ("tensor_tensor", dict(out=out, in0=in0, in1=in1, op=op))


def TS(out, in0, scalar1, scalar2, op0, op1=None):
    d = dict(out=out, in0=in0, scalar1=scalar1, scalar2=scalar2, op0=op0)
    if op1 is not None:
        d["op1"] = op1
    return ("tensor_scalar", d)


def STT(out, in0, scalar, in1, op0, op1):
    return ("scalar_tensor_tensor", dict(out=out, in0=in0, scalar=scalar, in1=in1, op0=op0, op1=op1))


def CP(out, in_):
    return ("tensor_copy", dict(out=out, in_=in_))


def MS(ap, v):
    return ("memset", dict(ap=ap, constant=v))


def SCAN(out, data0, data1, initial, op0, op1):
    return ("tensor_tensor_scan", dict(out=out, data0=data0, data1=data1, initial=initial, op0=op0, op1=op1))


def build_program(stop_after=None, debug=False):
    nc = bass.Bass("TRN2", target_bir_lowering=False)
    dbg_kind = "ExternalOutput" if debug else "Internal"

    def DIN(name, shape, dt=F32):
        return nc.dram_tensor(name, list(shape), dt, kind="ExternalInput").ap()

    def DSC(name, shape, dt=BF16):
        return nc.dram_tensor(name, list(shape), dt, kind=dbg_kind).ap()

    x_d = DIN("x", [S_, D])
    c_d = DIN("c_l", [128, 8])
    wada_d = DIN("wada", [128, 8, 3072])
    rows_d = DIN("rows", [1, 5120])
    wgate_d = DIN("wgate", [128, 8, 72])
    win_d = DIN("win", [18, 128, 8, 512])
    cols_d = DIN("cols", [128, 88])
    gb_d = DIN("gb", [36, 2])
    bt2_d = DIN("bt2", [128, 4, 14, 2, 64])
    wpa_d = DIN("wpa", [128, 8, 1024])
    wpb_d = DIN("wpb", [128, 4, 1024])
    wout_d = DIN("wout", [128, 8, 1024])
    ident_d = DIN("ident", [128, 128])
    masks_d = DIN("masks", [128, 2, 128])
    sel_d = DIN("sel", [36, 8, 128])
    y_d = nc.dram_tensor("y", [S_, D], F32, kind="ExternalOutput").ap()

    QT = DSC("QT", [4, 128, 2, S_])
    KT = DSC("KT", [4, 128, 2, S_])
    VA = DSC("VA", [NT, 128, 4 * 258])
    OG = DSC("OG", [S_, D])
    ZAT = DSC("ZAT", [8, 128, S_])
    QBT = DSC("QBT", [4, 128, S_])
    KBT = DSC("KBT", [4, 128, S_])
    VBA = DSC("VBA", [NT, 128, 8 * 66])
    ZBT = DSC("ZBT", [4, 128, S_])
    GMT = DSC("GMT", [16, 128, S_])
    YAT = DSC("YAT", [8, 128, S_])
    YBT = DSC("YBT", [4, 128, S_])
    dbg = {}
    if debug:
        dbg["hT"] = nc.dram_tensor("dbg_hT", [128, 8, S_], BF16, kind="ExternalOutput").ap()
        dbg["TOKU"] = nc.dram_tensor("dbg_TOKU", [128, NT, 36], F32, kind="ExternalOutput").ap()
        dbg["TOKC"] = nc.dram_tensor("dbg_TOKC", [128, NT, 36], F32, kind="ExternalOutput").ap()
        dbg["DECB"] = nc.dram_tensor("dbg_DECB", [128, 8, NT], F32, kind="ExternalOutput").ap()
        dbg["Hacc"] = nc.dram_tensor("dbg_Hacc", [4, 128, NT, 256], F32, kind="ExternalOutput").ap()
        for nm in ("T1", "T2", "T3"):
            dbg[nm] = nc.dram_tensor("dbg_" + nm, [36, S_], F32, kind="ExternalOutput").ap()

    def dstop(tag):
        if stop_after != tag:
            return False
        S.dma("sp", "S_dbg", dbg["T1"][:, :], T1[0:36, :], reads=["T1g"])
        S.dma("sp", "S_dbg", dbg["T2"][:, :], T2[0:36, :], reads=["T2g"])
        S.dma("sp", "S_dbg", dbg["T3"][:, :], T3[0:36, :], reads=["T3g"])
        S.emit()
        return True

    SB_LO = 16512
    SB_HI = 229344
    cur = [SB_LO]

    def T(name, shape, dt):
        n = int(np.prod(shape[1:])) * (4 if dt == F32 else 2)
        n = (n + 31) // 32 * 32
        assert cur[0] + n <= SB_HI, (name, cur[0], n)
        t = nc.alloc_sbuf_tensor_at(name, list(shape), dt, offset=cur[0])
        cur[0] += n
        return t

    ps = [nc.alloc_psum_tensor("ps%d" % i, [128, 512], F32) for i in range(8)]

    def psb(i):
        return ps[i][:].bitcast(BF16)

    S = Sched(nc)

    identb = T("identb", [128, 128], BF16)
    identf = T("identf", [128, 128], F32)
    maskT = T("maskT", [128, 2, 128], F32)
    cols = T("cols", [128, 88], F32)
    TOKU = T("TOKU", [128, NT, 36], F32)
    TOKC = T("TOKC", [128, NT, 36], F32)
    DECB = T("DECB", [128, 8, NT], F32)
    gate_bc = T("gate_bc", [128, 1024], F32)
    fg_bc = T("fg_bc", [128, 1024], F32)
    ones_c = T("ones_c", [128, 128], F32)
    smallc = T("smallc", [128, 64], F32)
    REGION0 = cur[0]

    S.dma("sp", "L_const", identf[:], ident_d[:, :], writes=["identf"])
    S.dma("pool", "L_constb", identb[:], ident_d[:, :], writes=["identb"])
    S.dma("sp", "L_const", maskT[:], masks_d[:, :, :], writes=["maskT"])
    S.dma("sp", "L_const", cols[:], cols_d[:, :], writes=["cols"])
    S.op("pool", MS(ones_c[:], 1.0), writes=["ones_c"])

    cur[0] = REGION0
    modrow = T("modrow", [1, 3072], F32)
    rows = T("rows", [1, 5120], F32)
    wst = [T("wada_st%d" % i, [128, 8, 512], F32) for i in range(2)]
    A_END = cur[0]
    cur[0] = REGION0 + 65536
    G1_bc = T("G1_bc", [128, 1024], F32)
    sh_bc = T("sh_bc", [128, 1024], F32)
    c_sb = T("c_sb", [128, 8], F32)
    cond = T("cond", [128, 8], F32)
    G1row = T("G1row", [1, 1024], F32)
    assert A_END <= REGION0 + 65536

    S.dma("sp", "L_const", c_sb[:], c_d[:, :], writes=["c_sb"])
    S.dma("sp", "L_const", rows[:], rows_d[:, :], writes=["rows"])
    S.op("act", ACT(cond[:], c_sb[:], AF.Silu), reads=["c_sb"], writes=["cond"])
    wrot = Rot("wada_st", wst)
    for g in range(6):
        wt, wk = wrot.next()
        S.dma("sp", "L_wada%d" % wk[1], wt[:], wada_d[:, :, g * 512:(g + 1) * 512], writes=[wk])
        S.op("pe", [MM(ps[g % 2][0:1, :], cond[:, kc:kc + 1], wt[:, kc, :], start=(kc == 0), stop=(kc == 7)) for kc in range(8)],
             reads=["cond", wk], writes=[("ps", g % 2)])
        S.op("dve", TT(modrow[0:1, g * 512:(g + 1) * 512], ps[g % 2][0:1, :], rows[0:1, g * 512:(g + 1) * 512], ALU.add),
             reads=[("ps", g % 2), "rows"], writes=["modrow"])
    S.op("dve", STT(G1row[0:1, :], modrow[0:1, 1024:2048], 1.0, rows[0:1, 3072:4096], ALU.add, ALU.mult),
         reads=["modrow", "rows"], writes=["G1row"])
    bc_jobs = [(G1_bc, G1row[0:1, :], "G1row", "G1_bc"), (sh_bc, modrow[0:1, 0:1024], "modrow", "sh_bc"),
               (gate_bc, modrow[0:1, 2048:3072], "modrow", "gate_bc"), (fg_bc, rows[0:1, 4096:5120], "rows", "fg_bc")]
    bi = 0
    for (dst, src, skey, dkey) in bc_jobs:
        for hf in range(2):
            b = 2 + (bi % 2)
            bi += 1
            S.op("pe", MM(ps[b][:, :], ones_c[0:1, 0:128], src[:, hf * 512:(hf + 1) * 512]), reads=[skey, "ones_c"], writes=[("ps", b)])
            S.op("act", ACT(dst[:, hf * 512:(hf + 1) * 512], ps[b][:, :], AF.Copy), reads=[("ps", b)], writes=[dkey])
    S.barrier()

    cur[0] = REGION0
    hT = T("hT", [128, 8, S_], BF16)
    C_START = cur[0]
    assert C_START == REGION0 + 65536
    cur[0] = C_START + 3 * 8192 + 1152 + 2 * 16416 + 16384
    xts = [T("xt%d" % i, [128, 1024], F32) for i in range(4)]
    xns = [T("xn%d" % i, [128, 1024], BF16) for i in range(3)]
    junkb = T("junkb", [128, 1024], BF16)
    xrot = Rot("xt", xts)
    xnrot = Rot("xn", xns)
    def b_stage1(tt):
        xt, xk = xrot.next()
        xn, xnk = xnrot.next()
        sc = smallc[:, (tt % 4) * 4:(tt % 4) * 4 + 4]
        sck = ("smallc", tt % 4)
        S.dma("sp", "L_xt%d" % xk[1], xt[:], x_d[tt * 128:(tt + 1) * 128, :], writes=[xk])
        S.op("act", ACT(junkb[:], xt[:], AF.Square, accum_out=sc[:, 0:1]), reads=[xk], writes=["junkb", sck])
        S.op("act", ACT(sc[:, 1:2], sc[:, 0:1], AF.Sqrt, scale=1.0 / D, bias=EPS), reads=[sck], writes=[sck])
        S.op("dve", ("reciprocal", dict(out=sc[:, 2:3], in_=sc[:, 1:2])), reads=[sck], writes=[sck])
        S.op("dve", STT(xt[:], xt[:], sc[:, 2:3], G1_bc[:], ALU.mult, ALU.mult), reads=[xk, sck, "G1_bc"], writes=[xk])
        S.op("dve" if tt % 3 != 2 else "pool", TT(xn[:], xt[:], sh_bc[:], ALU.add), reads=[xk, "sh_bc"], writes=[xnk])
        return xn, xnk

    def b_stage2(tt, xn, xnk):
        b = 6 + (tt % 2)
        S.op("pe", [TR(psb(b)[:, kc * 128:(kc + 1) * 128], xn[:, kc * 128:(kc + 1) * 128], identb[:]) for kc in range(8)],
             reads=[xnk, "identb"], writes=[("ps", b)])
        S.op("act", ACT(hT[:, :, tt * 128:(tt + 1) * 128], psb(b).rearrange("p (a b) -> p a b", a=8), AF.Copy),
             reads=[("ps", b)], writes=[("hT", tt)])

    bctx = {}
    for tt in range(NT + 1):
        if tt < NT:
            bctx[tt] = b_stage1(tt)
        if tt >= 1:
            b_stage2(tt - 1, *bctx.pop(tt - 1))
    if debug:
        S.dma("sp", "S_dbg", dbg["hT"][:, :, :], hT[:], reads=[("hT", tt) for tt in range(NT)])
    PB_END = cur[0]
    if stop_after == "B":
        S.emit()
        return nc

    cur[0] = C_START
    Wb = [T("Wb%d" % i, [128, 8, 512], BF16) for i in range(3)]
    wgb = T("wgb", [128, 8, 72], BF16)
    UA = T("UA", [128, S_ + 2], F32)
    UB = T("UB", [128, S_ + 2], F32)
    ACC = T("ACC", [128, S_], F32)
    obufs = [T("obuf%d" % i, [128, S_], BF16) for i in range(2)]
    tms = [T("tmst%d" % i, [128, 2112], BF16) for i in range(2)]
    assert PB_END <= cur[0], (PB_END, cur[0])
    gbc = T("gbc", [36, 2], F32)
    MPt = T("MPt", [36, NT], F32)
    MOt = T("MOt", [36, NT], F32)
    DECt = T("DECt", [36, NT], F32)
    selt = T("selt", [36, 8, 128], F32)
    C_END = cur[0]
    T1 = nc.alloc_sbuf_tensor_at("T1g", [128, S_], F32, offset=C_START + 3 * 8192 + 1152)
    T2 = nc.alloc_sbuf_tensor_at("T2g", [128, S_], F32, offset=C_START + 3 * 8192 + 1152 + 16416)
    T3 = ACC
    ONESF = nc.alloc_sbuf_tensor_at("ONESF", [128, S_], F32, offset=C_START + 3 * 8192 + 1152 + 2 * 16416 + 16384)

    wrot = Rot("Wb", Wb)
    orot = Rot("obuf", obufs)
    tmrot = Rot("tmst", tms)
    urot = Rot("U", [UA, UB])
    psrot = Rot("ps", ps[0:8])
    evtog = [0]

    S.dma("sp", "L_const", gbc[:], gb_d[:, :], writes=["gbc"])
    S.dma("sp", "L_const", selt[:], sel_d[:, :, :], writes=["selt"])
    S.dma("pool", "L_wgb", wgb[:], wgate_d[:, :, :], writes=["wgb"])
    allhT = [("hT", tt) for tt in range(NT)]

    def hkeys(tb):
        return [("hT", tb * 4 + j) for j in range(4)]

    for tb in range(NBLK):
        for gi, (Tt, col, tkey) in enumerate(((T1, 0, "T1g"), (T2, 1, "T2g"))):
            pt, pk = psrot.next()
            S.op("pe", [MM(pt[0:36, :], wgb[:, kc, gi * 36:(gi + 1) * 36], hT[:, kc, tb * 512:(tb + 1) * 512],
                           start=(kc == 0), stop=(kc == 7)) for kc in range(8)],
                 reads=["wgb"] + hkeys(tb), writes=[pk])
            S.op("act", ACT(Tt[0:36, tb * 512:(tb + 1) * 512], pt[0:36, :], AF.Identity, bias=gbc[0:36, col:col + 1]),
                 reads=[pk, "gbc"], writes=[tkey])

    if dstop("D0"):
        return nc
    r36 = slice(0, 36)
    fw = slice(0, 4)
    bw = slice(32, 36)
    S.op("act", ACT(T2[r36, :], T2[r36, :], AF.Exp, scale=-1.0), reads=["T2g"], writes=["T2g"])
    S.op("act", ACT(T2[r36, :], T2[r36, :], AF.Ln, bias=1.0), reads=["T2g"], writes=["T2g"])
    S.op("pool", MS(T3[r36, :], 0.0), writes=["T3g"])
    S.op("pool", MS(ONESF[r36, :], 1.0), reads=["T1g", "T2g"], writes=["ONESF"])
    S.op("pool", [MS(MPt[:], 0.0), MS(MOt[:], 0.0)], writes=["MPt", "MOt"])
    S.op("dve", SCAN(T3[fw, :], ONESF[fw, :], T2[fw, :], 0.0, ALU.mult, ALU.add),
         reads=["T2g", "ONESF"], writes=["T3g"])
    S.op("dve", SCAN(T3[bw, ::-1], ONESF[bw, :], T2[bw, ::-1], 0.0, ALU.mult, ALU.add),
         reads=["T2g", "ONESF"], writes=["T3g"])
    if dstop("D1"):
        return nc
    S.op("dve", TT(T1[r36, :], T1[r36, :], T3[r36, :], ALU.add), reads=["T1g", "T3g"], writes=["T1g"])
    S.op("dve", SCAN(T2[fw, :], ONESF[fw, :], T1[fw, :], 0.0, ALU.mult, ALU.max),
         reads=["T1g", "ONESF"], writes=["T2g"])
    S.op("dve", SCAN(T2[bw, ::-1], ONESF[bw, :], T1[bw, ::-1], 0.0, ALU.mult, ALU.max),
         reads=["T1g", "ONESF"], writes=["T2g"])
    if dstop("D2"):
        return nc
    M3 = T2[:].rearrange("p (k t) -> p k t", t=128)
    S.op("dve", [CP(MPt[fw, 1:NT], M3[fw, 0:NT - 1, 127]), CP(MOt[fw, :], M3[fw, :, 127])],
         reads=["T2g", "MPt", "MOt"], writes=["MPt", "MOt"])
    S.op("dve", [CP(MPt[bw, 0:NT - 1], M3[bw, 1:NT, 0]), CP(MOt[bw, :], M3[bw, :, 0])],
         reads=["T2g", "MPt", "MOt"], writes=["MPt", "MOt"])
    S.op("dve", TT(DECt[:], MPt[:], MOt[:], ALU.subtract), reads=["MPt", "MOt"], writes=["DECt"])
    S.op("act", ACT(DECt[:], DECt[:], AF.Exp), reads=["DECt"], writes=["DECt"])
    if dstop("D3"):
        return nc
    MPb = MPt[:].rearrange("p (k o) -> p k o", o=1).to_broadcast([36, NT, 128])
    T1v = T1[r36, :].rearrange("p (k t) -> p k t", t=128)
    T3v = T3[r36, :].rearrange("p (k t) -> p k t", t=128)
    S.op("dve", TT(T3v, T3v, MPb, ALU.subtract), reads=["T3g", "MPt"], writes=["T3g"])
    S.op("act", ACT(T3[r36, :], T3[r36, :], AF.Exp), reads=["T3g"], writes=["T3g"])
    S.op("dve", TT(T1v, T1v, MPb, ALU.subtract), reads=["T1g", "MPt"], writes=["T1g"])
    S.op("act", ACT(T1[r36, :], T1[r36, :], AF.Exp), reads=["T1g"], writes=["T1g"])
    if dstop("D4"):
        return nc
    for (src, skey, dst, dkey) in ((T1, "T1g", TOKU, "TOKU"), (T3, "T3g", TOKC, "TOKC")):
        for k0 in range(0, NT, 14):
            n = min(14, NT - k0)
            pt, pk = psrot.next()
            S.op("pe", [TR(pt[:, j * 36:(j + 1) * 36], src[0:36, (k0 + j) * 128:(k0 + j + 1) * 128], identf[0:36, 0:36]) for j in range(n)],
                 reads=[skey, "identf"], writes=[pk])
            S.op("dve", CP(dst[:, k0:k0 + n, :], pt[:, 0:n * 36].rearrange("p (a b) -> p a b", b=36)), reads=[pk], writes=[dkey])
    pt, pk = psrot.next()
    S.op("pe", [MM(pt[:, j * NT:(j + 1) * NT], selt[0:36, j, :], DECt[0:36, :]) for j in range(8)],
         reads=["selt", "DECt"], writes=[pk])
    S.op("dve", CP(DECB[:], pt[:, 0:8 * NT].rearrange("p (a b) -> p a b", b=NT)), reads=[pk], writes=["DECB"])
    if debug:
        S.dma("sp", "S_dbg", dbg["TOKU"][:, :, :], TOKU[:], reads=["TOKU"])
        S.dma("sp", "S_dbg", dbg["TOKC"][:, :, :], TOKC[:], reads=["TOKC"])
        S.dma("sp", "S_dbg", dbg["DECB"][:, :, :], DECB[:], reads=["DECB"])
    S.barrier()
    if stop_after == "D":
        S.emit()
        return nc

    S.op("pool", [MS(UA[:, 0:1], 0.0), MS(UA[:, S_ + 1:S_ + 2], 0.0), MS(UB[:, 0:1], 0.0), MS(UB[:, S_ + 1:S_ + 2], 0.0)],
         writes=[("U", 0), ("U", 1)])

    def load_w(g):
        wt, wk = wrot.next()
        S.dma("pool", "L_Wb%d" % wk[1], wt[:], win_d[g], writes=[wk])
        return wt, wk

    pend_tail = []

    def flush_tail():
        while pend_tail:
            inf = pend_tail.pop(0)
            ob, ok = orot.next()
            S.op("act", ACT(ob[:], ACC[:], AF.Silu), reads=["ACC"], writes=[ok])
            S.dma("sp", "S_obuf%d" % ok[1], inf["dst"], ob[:], reads=[ok], writes=[inf["dkey"]])

    def fm_group(g, kind, sub_info):
        wt, wk = load_w(g)
        for sub in range(4):
            info = sub_info(sub)
            if kind == "conv":
                U, uk = urot.next()
            else:
                ob, ok = orot.next()
            for tb in range(NBLK):
                pt, pk = psrot.next()
                S.op("pe", [MM(pt[:, :], wt[:, kc, sub * 128:(sub + 1) * 128], hT[:, kc, tb * 512:(tb + 1) * 512],
                               start=(kc == 0), stop=(kc == 7)) for kc in range(8)],
                     reads=[wk] + hkeys(tb), writes=[pk])
                if kind == "conv":
                    S.op("act", ACT(U[:, 1 + tb * 512:1 + (tb + 1) * 512], pt[:, :], AF.Copy), reads=[pk], writes=[uk])
                elif kind == "silu":
                    S.op("act", ACT(ob[:, tb * 512:(tb + 1) * 512], pt[:, :], AF.Silu), reads=[pk], writes=[ok])
                elif kind == "sigb":
                    S.op("act", ACT(ob[:, tb * 512:(tb + 1) * 512], pt[:, :], AF.Sigmoid, bias=info["bias"]), reads=[pk, "cols"], writes=[ok])
                elif kind == "copy":
                    evtog[0] ^= 1
                    if evtog[0]:
                        S.op("dve", TS(ob[:, tb * 512:(tb + 1) * 512], pt[:, :], info["scale"], None, ALU.mult), reads=[pk], writes=[ok])
                    else:
                        S.op("act", ACT(ob[:, tb * 512:(tb + 1) * 512], pt[:, :], AF.Copy, scale=info["scale"]), reads=[pk], writes=[ok])
            if kind == "conv":
                flush_tail()
                cg = info["cg"]
                w0 = cols[:, cg * 3 + 0:cg * 3 + 1]
                w1 = cols[:, cg * 3 + 1:cg * 3 + 2]
                w2 = cols[:, cg * 3 + 2:cg * 3 + 3]
                cb = cols[:, 48 + cg:49 + cg]
                S.op("dve", TS(ACC[:], U[:, 1:S_ + 1], w1, cb, ALU.mult, ALU.add), reads=[uk, "cols"], writes=["ACC"])
                S.op("dve", STT(ACC[:], U[:, 0:S_], w0, ACC[:], ALU.mult, ALU.add), reads=[uk, "cols", "ACC"], writes=["ACC"])
                S.op("dve", STT(ACC[:], U[:, 2:S_ + 2], w2, ACC[:], ALU.mult, ALU.add), reads=[uk, "cols", "ACC"], writes=["ACC"])
                pend_tail.append(info)
            else:
                S.dma("sp", "S_obuf%d" % ok[1], info["dst"], ob[:], reads=[ok], writes=[info["dkey"]])

    def tm_group(g, kind, col0):
        wt, wk = load_w(g)
        for t4 in range(NT // 4):
            st, sk = tmrot.next()
            if kind == "va":
                sv = st[:, 0:4 * 2 * 258].rearrange("p (t h c) -> p t h c", t=4, h=2)
                S.op("pool", [MS(sv[:, :, :, 256:257], 1.0), MS(sv[:, :, :, 257:258], 0.0)], writes=[sk])
            elif kind == "vb":
                sv = st[:, 0:4 * 8 * 66].rearrange("p (t h c) -> p t h c", t=4, h=8)
                S.op("pool", [MS(sv[:, :, :, 64:65], 1.0), MS(sv[:, :, :, 65:66], 0.0)], writes=[sk])
            else:
                sv = st[:, 0:2048].rearrange("p (t c) -> p t c", t=4)
            for j in range(4):
                tt = t4 * 4 + j
                pt, pk = psrot.next()
                S.op("pe", [MM(pt[:, :], hT[:, kc, tt * 128:(tt + 1) * 128], wt[:, kc, :], start=(kc == 0), stop=(kc == 7)) for kc in range(8)],
                     reads=[wk, ("hT", tt)], writes=[pk])
                if kind == "va":
                    S.op("dve", CP(sv[:, j, :, 0:256], pt[:, :].rearrange("p (h c) -> p h c", h=2)), reads=[pk], writes=[sk])
                elif kind == "vb":
                    S.op("dve", CP(sv[:, j, :, 0:64], pt[:, :].rearrange("p (h c) -> p h c", h=8)), reads=[pk], writes=[sk])
                else:
                    S.op("act", ACT(sv[:, j, :], pt[:, :], AF.Sigmoid), reads=[pk], writes=[sk])
            tsl = slice(t4 * 4, t4 * 4 + 4)
            if kind == "va":
                hd0 = col0
                dst = VA[tsl, :, hd0 * 258:(hd0 + 2) * 258].rearrange("t p c -> p t c")
                S.dma("sp", "S_tm%d" % sk[1], dst, st[:, 0:4 * 516].rearrange("p (t c) -> p t c", t=4), reads=[sk], writes=[("VA", t4, hd0)])
            elif kind == "vb":
                dst = VBA[tsl, :, :].rearrange("t p c -> p t c")
                S.dma("sp", "S_tm%d" % sk[1], dst, st[:, 0:4 * 528].rearrange("p (t c) -> p t c", t=4), reads=[sk], writes=[("VBA", t4)])
            else:
                dst = OG[t4 * 512:(t4 + 1) * 512, col0:col0 + 512].rearrange("(t p) c -> p t c", p=128)
                S.dma("sp", "S_tm%d" % sk[1], dst, sv, reads=[sk], writes=[("OG", t4, col0)])

    for g in (0, 1):
        fm_group(g, "conv", lambda sub, g=g: dict(cg=g * 4 + sub, dst=QT[(g * 4 + sub) // 2, :, (g * 4 + sub) % 2, :],
                                                 dkey=("QT", g * 4 + sub)))
    for g in (2, 3):
        fm_group(g, "conv", lambda sub, g=g: dict(cg=8 + (g - 2) * 4 + sub, dst=KT[((g - 2) * 4 + sub) // 2, :, ((g - 2) * 4 + sub) % 2, :],
                                                 dkey=("KT", (g - 2) * 4 + sub)))
    flush_tail()
    for g in (8, 9):
        fm_group(g, "silu", lambda sub, g=g: dict(dst=ZAT[(g - 8) * 4 + sub], dkey=("ZAT", (g - 8) * 4 + sub)))
    fm_group(13, "silu", lambda sub: dict(dst=ZBT[sub], dkey=("ZBT", sub)))
    fm_group(10, "copy", lambda sub: dict(scale=0.125, dst=QBT[sub], dkey=("QBT", sub)))
    fm_group(11, "copy", lambda sub: dict(scale=1.0, dst=KBT[sub], dkey=("KBT", sub)))
    tm_group(4, "va", 0)
    tm_group(5, "va", 2)
    tm_group(12, "vb", 0)
    tm_group(6, "og", 0)
    tm_group(7, "og", 512)
    for g in (14, 15, 16, 17):
        fm_group(g, "sigb", lambda sub, g=g: dict(bias=cols[:, 64 + (g - 14) * 4 + sub:65 + (g - 14) * 4 + sub],
                                                 dst=GMT[(g - 14) * 4 + sub], dkey=("GMT", (g - 14) * 4 + sub)))
    S.barrier()
    if stop_after == "C":
        S.emit()
        return nc

    if build_mlstm(nc, S, locals()):
        return nc
    if stop_after == "M":
        S.emit()
        return nc
    build_na(nc, S, locals())
    if stop_after == "N":
        S.emit()
        return nc
    build_final(nc, S, locals())
    S.emit()
    return nc


def build_mlstm(nc, S, E):
    T, cur, ps, psb = E["T"], E["cur"], E["ps"], E["psb"]
    identb, maskT, cols, TOKU, TOKC, DECB = E["identb"], E["maskT"], E["cols"], E["TOKU"], E["TOKC"], E["DECB"]
    QT, KT, VA, OG, ZAT, YAT = E["QT"], E["KT"], E["VA"], E["OG"], E["ZAT"], E["YAT"]
    dbg, debug = E["dbg"], E["debug"]
    cur[0] = E["REGION0"]
    qT = T("m_qT", [128, 2, S_], BF16)
    kT = T("m_kT", [128, 2, S_], BF16)
    ktok = T("m_ktok", [128, NT, 256], BF16)
    Vaug = T("m_Vaug", [128, NT, 258], BF16)
    Hacc = T("m_Hacc", [128, NT, 256], F32)
    ZATh = T("m_ZATh", [128, 2, S_], BF16)
    yaT = T("m_yaT", [128, 2, S_], BF16)
    OGt = Rot("OGt", [T("m_OGt%d" % i, [128, 4, 256], BF16) for i in range(3)])
    UVr = [Rot("UV%d" % d, [T("m_UV%d_%d" % (d, i), [128, 258], BF16) for i in range(4)]) for d in range(2)]
    Smr = [Rot("Sm%d" % d, [T("m_Sm%d_%d" % (d, i), [128, 128], BF16) for i in range(3)]) for d in range(2)]
    Zs = [T("m_Z%d" % d, [128, 2, 258], F32) for d in range(2)]
    Cbr = [Rot("Cb%d" % d, [T("m_Cb%d_%d" % (d, i), [128, 2, 258], BF16) for i in range(3)]) for d in range(2)]
    Htr = Rot("Htmp", [T("m_Htmp%d" % i, [128, 256], F32) for i in range(3)])
    hgr = Rot("hg", [T("m_hg%d" % i, [128, 256], F32) for i in range(4)])
    ytr = Rot("yatok", [T("m_yatok%d" % i, [128, 256], BF16) for i in range(4)])
    junk = T("m_junk", [128, 256], BF16)
    rcs = T("m_rcs", [128, 8, 4], F32)
    pcs = T("m_pcs", [128, 8, 4], F32)
    rci = [0]
    pci = [0]
    dcp = [((ps[4], ps[5]), [("ps", 4), ("ps", 5)]), ((ps[6], ps[7]), [("ps", 6), ("ps", 7)])]

    def loads(hd):
        S.dma("sp", "L_mq", qT[:], QT[hd], writes=["qT"])
        S.dma("sp", "L_mk", kT[:], KT[hd], writes=["kT"])
        for j0 in range(0, NT, 8):
            S.dma("sp", "L_mv", Vaug[:, j0:j0 + 8, :], VA[j0:j0 + 8, :, hd * 258:(hd + 1) * 258].rearrange("t p c -> p t c"),
                  writes=[("Vaug", j) for j in range(j0, j0 + 8)])

    loads(0)
    for hd in range(4):
        S.dma("sp", "L_mz", ZATh[:], ZAT[2 * hd:2 * hd + 2].rearrange("g p t -> p g t"), writes=["ZATh"])
        for k4 in range(8):
            kb = 6 + (k4 % 2)
            S.op("pe", [TR(psb(kb)[:, (kk * 2 + c) * 128:(kk * 2 + c + 1) * 128], kT[:, c, (k4 * 4 + kk) * 128:(k4 * 4 + kk + 1) * 128], identb[:])
                        for kk in range(4) for c in range(2)], reads=["kT", "identb"], writes=[("ps", kb)])
            if k4 % 2 == 0:
                S.op("act", ACT(ktok[:, k4 * 4:(k4 + 1) * 4, :].rearrange("p a b -> p (a b)"), psb(kb)[:, 0:1024], AF.Copy, scale=1.0 / 16),
                     reads=[("ps", kb)], writes=[("ktok", k4)])
            else:
                S.op("dve", TS(ktok[:, k4 * 4:(k4 + 1) * 4, :].rearrange("p a b -> p (a b)"), psb(kb)[:, 0:1024], 1.0 / 16, None, ALU.mult),
                     reads=[("ps", kb)], writes=[("ktok", k4)])
        if E["stop_after"] == "M0":
            S.emit()
            return True

        ctx = {}
        chain = [dict(kprev=None) for _ in range(2)]

        def kof(d, i):
            return i if d == 0 else NT - 1 - i

        def opUV(d, i):
            k = kof(d, i)
            ucol = TOKU[:, k, d * 32 + hd:d * 32 + hd + 1]
            UVb, uvk = UVr[d].next()
            S.op("dve", TS(UVb[:], Vaug[:, k, :], ucol, None, ALU.mult), reads=[("Vaug", k), "TOKU"], writes=[uvk])
            ctx[(d, i)] = dict(k=k, ch=slice(k * 128, (k + 1) * 128), UVb=UVb, uvk=uvk, cb=None, cbk=None)

        def opST(d, i):
            c_ = ctx[(d, i)]
            stp = ps[d][:, 0:128]
            S.op("pe", [MM(stp, kT[:, c, c_["ch"]], qT[:, c, c_["ch"]], start=(c == 0), stop=(c == 1)) for c in range(2)],
                 reads=["kT", "qT"], writes=[("ps", d)])

        def opMASK(d, i):
            c_ = ctx[(d, i)]
            Sm, smk = Smr[d].next()
            S.op("dve", TT(Sm[:], ps[d][:, 0:128], maskT[:, d, :], ALU.mult), reads=[("ps", d), "maskT"], writes=[smk])
            c_["Sm"], c_["smk"] = Sm, smk

        def opDC(d, i):
            c_ = ctx[(d, i)]
            (b0, b1), dks = dcp[d]
            k = c_["k"]
            S.op("pe", [MM(b0[:, 0:258], ktok[:, k, 0:128], c_["UVb"][:]), MM(b1[:, 0:258], ktok[:, k, 128:256], c_["UVb"][:])],
                 reads=[("ktok", k // 4), c_["uvk"]], writes=dks)

        def opZ(d, i):
            c_ = ctx[(d, i)]
            (b0, b1), dks = dcp[d]
            series = d * 4 + hd
            k = c_["k"]
            Z = Zs[d]
            zk = ("Z", d)
            if i == 0:
                S.op("dve", [CP(Z[:, 0, :], b0[:, 0:258]), CP(Z[:, 1, :], b1[:, 0:258])], reads=dks, writes=[zk])
            else:
                kp = chain[d]["kprev"]
                dprev = DECB[:, series, kp:kp + 1]
                S.op("dve", [STT(Z[:, 0, :], Z[:, 0, :], dprev, b0[:, 0:258], ALU.mult, ALU.add),
                             STT(Z[:, 1, :], Z[:, 1, :], dprev, b1[:, 0:258], ALU.mult, ALU.add)],
                     reads=dks + [zk, "DECB"], writes=[zk])
            cbn, cbk = Cbr[d].next()
            dcur = DECB[:, series, k:k + 1]
            S.op("act", ACT(cbn[:].rearrange("p a b -> p (a b)"), Z[:].rearrange("p a b -> p (a b)"), AF.Copy, scale=dcur), reads=[zk, "DECB"], writes=[cbk])
            c_["cb"], c_["cbk"] = cbn, cbk
            chain[d]["kprev"] = k

        def opNP(d, i):
            c_ = ctx[(d, i)]
            first = (i == 0)
            npb, npk = ps[2 + d], ("ps", 2 + d)
            npa = npb[:, 0:258]
            specs = [MM(npa, c_["Sm"][:], c_["UVb"][:], start=True, stop=first)]
            rd = [c_["smk"], c_["uvk"]]
            if not first:
                pv = ctx[(d, i - 1)]
                specs += [MM(npa, qT[:, c, c_["ch"]], pv["cb"][:, c, :], start=False, stop=(c == 1)) for c in range(2)]
                rd += ["qT", pv["cbk"]]
            S.op("pe", specs, reads=rd, writes=[npk])

        def opOUT(d, i):
            c_ = ctx[(d, i)]
            k = c_["k"]
            npb, npk = ps[2 + d], ("ps", 2 + d)
            j = rci[0] % 8
            rci[0] += 1
            rc = rcs[:, j, :]
            rck = ("rc", j)
            den = npb[:, 256:257]
            ccol = TOKC[:, k, d * 32 + hd:d * 32 + hd + 1]
            S.op("dve", TT(rc[:, 0:1], den, ccol, ALU.max), reads=[npk, "TOKC"], writes=[rck])
            S.op("dve", STT(rc[:, 1:2], den, -1.0, rc[:, 0:1], ALU.mult, ALU.max), reads=[npk, rck], writes=[rck])
            S.op("dve", ("reciprocal", dict(out=rc[:, 2:3], in_=rc[:, 1:2])), reads=[rck], writes=[rck])
            if i < NT // 2:
                S.op("act", ACT(Hacc[:, k, :], npb[:, 0:256], AF.Copy, scale=rc[:, 2:3]), reads=[npk, rck], writes=[("Hacc", k)])
            else:
                ht, htk = Htr.next()
                S.op("act", ACT(ht[:], npb[:, 0:256], AF.Copy, scale=rc[:, 2:3]), reads=[npk, rck], writes=[htk])
                S.op("pool", TT(Hacc[:, k, :], Hacc[:, k, :], ht[:], ALU.add), reads=[htk, ("Hacc", k)], writes=[("Hacc", k)])
            if i >= 1:
                ctx.pop((d, i - 1))

        for d in range(2):
            opUV(d, 0)
        for d in range(2):
            opUV(d, 1)
        for d in range(2):
            opST(d, 0)
        for d in range(2):
            opMASK(d, 0)
        for d in range(2):
            opDC(d, 0)
        for d in range(2):
            opZ(d, 0)
        for i in range(NT):
            nx = i + 1
            if i + 2 < NT:
                opUV(0, i + 2)
                opUV(1, i + 2)
            if nx < NT:
                opST(0, nx)
                opST(1, nx)
                opMASK(0, nx)
                opMASK(1, nx)
                if nx < NT - 1:
                    opDC(0, nx)
                    opDC(1, nx)
            opNP(0, i)
            opNP(1, i)
            if nx < NT - 1:
                opZ(0, nx)
            opOUT(0, i)
            if nx < NT - 1:
                opZ(1, nx)
            opOUT(1, i)

        if debug:
            S.dma("sp", "S_dbg", dbg["Hacc"][hd], Hacc[:], reads=[("Hacc", k) for k in range(NT)])
        if hd + 1 < 4:
            loads(hd + 1)

        pctx = {}

        def P1(k):
            k4, j = k // 4, k % 4
            if j == 0:
                ogt, ogk = OGt.next()
                S.dma("pool", "L_og%d" % ogk[1], ogt[:], OG[k4 * 512:(k4 + 1) * 512, hd * 256:(hd + 1) * 256].rearrange("(t p) c -> p t c", p=128),
                      writes=[ogk])
                pctx["og"] = (ogt, ogk)
            ogt, ogk = pctx["og"]
            hg, hgk = hgr.next()
            S.op("dve", TT(hg[:], Hacc[:, k, :], ogt[:, j, :], ALU.mult), reads=[("Hacc", k), ogk], writes=[hgk])
            q = pci[0] % 8
            pci[0] += 1
            pc = pcs[:, q, :]
            pck = ("pc", q)
            S.op("act", ACT(junk[:], hg[:], AF.Square, accum_out=pc[:, 0:1]), reads=[hgk], writes=["m_junk", pck])
            S.op("act", ACT(pc[:, 1:2], pc[:, 0:1], AF.Sqrt, scale=1.0 / DH, bias=EPS), reads=[pck], writes=[pck])
            pctx[k] = (hg, hgk, pc, pck)

        def P2(k):
            k4, j = k // 4, k % 4
            hg, hgk, pc, pck = pctx.pop(k)
            S.op("dve", ("reciprocal", dict(out=pc[:, 2:3], in_=pc[:, 1:2])), reads=[pck], writes=[pck])
            yt, ytk = ytr.next()
            S.op("dve", TS(yt[:], hg[:], pc[:, 2:3], None, ALU.mult), reads=[hgk, pck], writes=[ytk])
            S.op("pe", [TR(psb(7)[:, c * 512 + j * 128:c * 512 + (j + 1) * 128], yt[:, c * 128:(c + 1) * 128], identb[:]) for c in range(2)],
                 reads=[ytk, "identb"], writes=[("ps", 7)])
            if j == 3:
                for c in range(2):
                    S.op("dve", STT(yaT[:, c, k4 * 512:(k4 + 1) * 512], psb(7)[:, c * 512:(c + 1) * 512], cols[:, 80 + hd * 2 + c:81 + hd * 2 + c],
                                    ZATh[:, c, k4 * 512:(k4 + 1) * 512], ALU.mult, ALU.mult),
                         reads=[("ps", 7), "cols", "ZATh"], writes=[("yaT", c)])

        for k in range(NT + 2):
            if k < NT:
                P1(k)
            if k >= 2:
                P2(k - 2)
        for c in range(2):
            S.dma("pool", "S_yaT", YAT[hd * 2 + c], yaT[:, c, :], reads=[("yaT", c)])
        if E["stop_after"] == "M2":
            S.emit()
            return True
    S.barrier()


def build_na(nc, S, E):
    T, cur, ps, psb = E["T"], E["cur"], E["ps"], E["psb"]
    identb = E["identb"]
    QBT, KBT, VBA, ZBT, YBT, bt2_d = E["QBT"], E["KBT"], E["VBA"], E["ZBT"], E["YBT"], E["bt2_d"]
    cur[0] = E["REGION0"]
    bt2b = T("n_bt2", [128, 4, 14, 2, 64], BF16)
    QBD = T("n_QBD", [128, 2, 2, S_], BF16)
    kbT = T("n_kbT", [128, 2, S_], BF16)
    ZBh = T("n_ZBh", [128, 2, S_], BF16)
    ybT = T("n_ybT", [128, 2, S_], BF16)
    VE = T("n_VE", [128, NT, 4, 66], BF16)
    VO = T("n_VO", [128, NT - 1, 4, 66], BF16)
    PTr = Rot("PT", [T("n_PT%d" % i, [128, 1024], BF16) for i in range(2)])
    otr = Rot("otok", [T("n_otok%d" % i, [64, 4, 64], BF16) for i in range(3)])
    recs = T("n_rec", [64, 4, 4], F32)
    str_ = Rot("psS", [(ps[0], ps[1]), (ps[2], ps[3])])
    pvr = Rot("psPV", [ps[4], ps[5]])
    trr = Rot("psT", [6, 7])
    ri = [0]
    NA_END = cur[0]
    wpab = T("f_wpa", [128, 8, 1024], BF16)
    wpbb = T("f_wpb", [128, 4, 1024], BF16)
    woutb = T("f_wout", [128, 8, 1024], BF16)
    wsr = Rot("f_wst", [T("f_wst%d" % i, [128, 2, 1024], F32) for i in range(1)])
    S.shared_fw = (NA_END, wpab, wpbb, woutb)
    gate_bc = E["gate_bc"]
    S.dma("pool", "L_bt2", bt2b[:], bt2_d[:, :, :, :, :], writes=["bt2b"])
    S.op("pool", [MS(QBD[0:64, :, 1, :], 0.0), MS(QBD[64:128, :, 0, :], 0.0)], writes=["QBDz"])
    for half in range(2):
        c0 = half * 4 * 66
        for cq in range(4):
            tsl = slice(cq * 1024, (cq + 1) * 1024)
            S.dma("sp", "L_nq%d" % cq, QBD[0:64, :, 0, tsl], QBT[2 * half:2 * half + 2, 0:64, tsl].rearrange("g p t -> p g t"), reads=["QBDz"], writes=[("qbT", cq)])
            S.dma("sp", "L_nq%d" % cq, QBD[64:128, :, 1, tsl], QBT[2 * half:2 * half + 2, 64:128, tsl].rearrange("g p t -> p g t"), reads=["QBDz"], writes=[("qbT", cq)])
            S.dma("sp", "L_nk%d" % cq, kbT[:, :, tsl], KBT[2 * half:2 * half + 2, :, tsl].rearrange("g p t -> p g t"), writes=[("kbT", cq)])
            j0 = cq * 8
            S.dma("sp", "L_nve%d" % cq, VE[:, j0:j0 + 8, :, :].rearrange("p t h c -> p t (h c)"),
                  VBA[j0:j0 + 8, :, c0:c0 + 264].rearrange("t p c -> p t c"), writes=[("VE", cq)])
            j1 = min(j0 + 8, NT - 1)
            S.dma("sp", "L_nvo%d" % cq, VO[0:64, j0:j1, :, :].rearrange("p t h c -> p t (h c)"),
                  VBA[j0:j1, 64:128, c0:c0 + 264].rearrange("t p c -> p t c"), writes=[("VO", cq)])
            S.dma("sp", "L_nvo%d" % cq, VO[64:128, j0:j1, :, :].rearrange("p t h c -> p t (h c)"),
                  VBA[j0 + 1:j1 + 1, 0:64, c0:c0 + 264].rearrange("t p c -> p t c"), writes=[("VO", cq)])
            S.dma("sp", "L_nz%d" % cq, ZBh[:, :, tsl], ZBT[2 * half:2 * half + 2, :, tsl].rearrange("g p t -> p g t"), writes=[("ZBh", cq)])

        def n_stage1(r):
            rs = min(max(r - 4, 0), 56)
            j0b = rs - r + 7
            qs = slice(r * 64, (r + 1) * 64)
            pair, sk = str_.next()
            specs = []
            for gi in range(2):
                bank = pair[gi]
                hp = half * 2 + gi
                specs.append(MM(bank[:, 0:512], identb[:], bt2b[:, hp, j0b:j0b + 7:2, :, :].rearrange("p i j q -> p i (j q)"), start=True, stop=False))
                for i in range(4):
                    tok = rs * 64 + i * 128
                    specs.append(MM(bank[:, i * 128:(i + 1) * 128], kbT[:, gi, tok:tok + 128], QBD[:, gi, :, qs], start=False, stop=(i == 3)))
            S.op("pe", specs, reads=["identb", "bt2b", ("qbT", (r * 64) // 1024)] + [("kbT", cc) for cc in sorted({(rs * 64) // 1024, (rs * 64 + 511) // 1024})], writes=[sk])
            PT, ptk = PTr.next()
            S.op("act", [ACT(PT[:, 0:512], pair[0][:, :], AF.Exp), ACT(PT[:, 512:1024], pair[1][:, :], AF.Exp)], reads=[sk], writes=[ptk])
            return dict(r=r, rs=rs, qs=qs, PT=PT, ptk=ptk)

        def n_stage2(c):
            rs, PT, ptk = c["rs"], c["PT"], c["ptk"]
            if rs % 2 == 0:
                Vx, vnm, tbase = VE, "VE", rs // 2
            else:
                Vx, vnm, tbase = VO, "VO", (rs - 1) // 2
            vkeys = [(vnm, cc) for cc in sorted({tbase // 8, (tbase + 3) // 8})]
            ob, ok = pvr.next()
            specs = []
            for hh in range(4):
                for i in range(4):
                    specs.append(MM(ob[0:64, hh * 66:(hh + 1) * 66], PT[:, (hh // 2) * 512 + i * 128 + (hh % 2) * 64:(hh // 2) * 512 + i * 128 + (hh % 2) * 64 + 64], Vx[:, tbase + i, hh, :],
                                    start=(i == 0), stop=(i == 3)))
            S.op("pe", specs, reads=[ptk] + vkeys, writes=[ok])
            q = ri[0] % 4
            ri[0] += 1
            rec = recs[:, q, :]
            rk = ("rec", q)
            ov = ob[0:64, 0:264].rearrange("p (h c) -> p h c", c=66)
            S.op("dve", ("reciprocal", dict(out=rec, in_=ov[:, :, 64])), reads=[ok], writes=[rk])
            ot, otk = otr.next()
            S.op("dve", TT(ot[:], ov[:, :, 0:64], rec.rearrange("p (h o) -> p h o", o=1).to_broadcast([64, 4, 64]), ALU.mult),
                 reads=[ok, rk], writes=[otk])
            c["ot"], c["otk"] = ot, otk

        def n_stage3(c):
            ot, otk, qs = c["ot"], c["otk"], c["qs"]
            tb_, tk = trr.next()
            otf = ot[:].rearrange("p h c -> p (h c)")
            S.op("pe", [TR(psb(tb_)[:, g * 64:(g + 1) * 64], otf[:, g * 128:(g + 1) * 128], identb[0:64, 0:64]) for g in range(2)],
                 reads=[otk, "identb"], writes=[tk])
            S.op("dve", TT(ybT[:, :, qs], psb(tb_)[:, 0:128].rearrange("p (g q) -> p g q", g=2), ZBh[:, :, qs], ALU.mult),
                 reads=[tk, ("ZBh", c["r"] * 64 // 1024)], writes=["ybT"])

        nctx = {}
        if half == 0:
            S.dma("pool", "L_fwa", wpab[:], E["wpa_d"][:, :, :], writes=["wpab"])
            S.dma("pool", "L_fwb", wpbb[:], E["wpb_d"][:, :, :], writes=["wpbb"])
            for j in range(4):
                wt, wk = wsr.next()
                S.dma("sp", "L_fws%d" % wk[1], wt[:], E["wout_d"][:, 2 * j:2 * j + 2, :], writes=[wk])
                S.op("dve", TT(woutb[:, 2 * j:2 * j + 2, :], wt[:], gate_bc[:].rearrange("p (o n) -> p o n", o=1).to_broadcast([128, 2, 1024]), ALU.mult),
                     reads=[wk, "gate_bc"], writes=["woutb"])
        for r in range(64 + 2):
            if r < 64:
                nctx[r] = n_stage1(r)
            if 0 <= r - 1 < 64:
                n_stage2(nctx[r - 1])
            if 0 <= r - 2 < 64:
                n_stage3(nctx.pop(r - 2))
        for g in range(2):
            S.dma("sp", "S_ybT", YBT[2 * half + g], ybT[:, g, :], reads=["ybT"])
    S.barrier()


def build_final(nc, S, E):
    T, cur, ps = E["T"], E["cur"], E["ps"]
    gate_bc, fg_bc, smallc = E["gate_bc"], E["fg_bc"], E["smallc"]
    YAT, YBT, GMT, x_d, y_d = E["YAT"], E["YBT"], E["GMT"], E["x_d"], E["y_d"]
    wpa_d, wpb_d, wout_d = E["wpa_d"], E["wpb_d"], E["wout_d"]
    cur[0] = E["REGION0"]
    NA_END, wpab, wpbb, woutb = S.shared_fw
    yar = Rot("f_ya", [T("f_ya%d" % i, [128, 8, 512], BF16) for i in range(2)])
    ybr = Rot("f_yb", [T("f_yb%d" % i, [128, 4, 512], BF16) for i in range(2)])
    gmr = Rot("f_gm", [T("f_gm%d" % i, [128, 16, 512], BF16) for i in range(2)])
    t1r = Rot("f_t1", [T("f_t1%d" % i, [128, 512], F32) for i in range(2)])
    t2r = Rot("f_t2", [T("f_t2%d" % i, [128, 512], F32) for i in range(2)])
    mgr = Rot("f_mg", [T("f_mg%d" % i, [128, 8, 512], BF16) for i in range(2)])
    xr = Rot("f_xt", [T("f_xt%d" % i, [128, 1024], F32) for i in range(5)])
    x2r = Rot("f_x2", [T("f_x2%d" % i, [128, 1024], F32) for i in range(2)])
    otr = Rot("f_ot", [T("f_ot%d" % i, [128, 1024], F32) for i in range(2)])
    junk = T("f_junk", [128, 1024], BF16)
    par = Rot("psP", [ps[0], ps[1], ps[2], ps[3]])
    outr = Rot("psO", [(ps[4], ps[5]), (ps[6], ps[7])])
    assert cur[0] <= NA_END, (cur[0], NA_END)
    def f_loads(tb):
        ts = slice(tb * 512, (tb + 1) * 512)
        ya, yak = yar.next()
        yb, ybk = ybr.next()
        gm, gmk = gmr.next()
        S.dma("sp", "L_fya%d" % yak[1], ya[:], YAT[:, :, ts].rearrange("g p t -> p g t"), writes=[yak])
        S.dma("sp", "L_fyb%d" % ybk[1], yb[:], YBT[:, :, ts].rearrange("g p t -> p g t"), writes=[ybk])
        S.dma("sp", "L_fgm%d" % gmk[1], gm[:], GMT[:, :, ts].rearrange("g p t -> p g t"), writes=[gmk])
        return ya, yak, yb, ybk, gm, gmk

    fl = {0: f_loads(0)}
    mgs = {}

    def stageP(tb):
        ya, yak, yb, ybk, gm, gmk = fl.pop(tb)
        if tb + 1 < NBLK:
            fl[tb + 1] = f_loads(tb + 1)
        mg, mgk = mgr.next()
        for fg in range(8):
            fs = slice(fg * 128, (fg + 1) * 128)
            pa, pak = par.next()
            S.op("pe", [MM(pa[:, :], wpab[:, kc, fs], ya[:, kc, :], start=(kc == 0), stop=(kc == 7)) for kc in range(8)],
                 reads=["wpab", yak], writes=[pak])
            pb_, pbk = par.next()
            S.op("pe", [MM(pb_[:, :], wpbb[:, kc, fs], yb[:, kc, :], start=(kc == 0), stop=(kc == 3)) for kc in range(4)],
                 reads=["wpbb", ybk], writes=[pbk])
            t1, t1k = t1r.next()
            t2, t2k = t2r.next()
            S.op("dve", TT(t1[:], pa[:, :], gm[:, fg, :], ALU.mult), reads=[pak, gmk], writes=[t1k])
            S.op("dve", TT(t2[:], pb_[:, :], gm[:, 8 + fg, :], ALU.mult), reads=[pbk, gmk], writes=[t2k])
            S.op("pool", TT(mg[:, fg, :], t1[:], t2[:], ALU.add), reads=[t1k, t2k], writes=[(mgk, fg)])
        mgs[tb] = (mg, mgk)

    def stageO(tb):
        mg, mgk = mgs.pop(tb)
        xtl = []
        for tt in range(4):
            tile = tb * 4 + tt
            xt, xk = xr.next()
            S.dma("sp", "L_fx%d" % xk[1], xt[:], x_d[tile * 128:(tile + 1) * 128, :], writes=[xk])
            xtl.append((xt, xk))
        for tt in range(4):
            tile = tb * 4 + tt
            xt, xk = xtl[tt]
            (o0, o1), ok = outr.next()
            specs = []
            for nh, ob in enumerate((o0, o1)):
                for fg in range(8):
                    specs.append(MM(ob[:, :], mg[:, fg, tt * 128:(tt + 1) * 128], woutb[:, fg, nh * 512:(nh + 1) * 512], start=(fg == 0), stop=(fg == 7)))
            S.op("pe", specs, reads=[(mgk, fg) for fg in range(8)] + ["woutb"], writes=[ok])
            x2, x2k = x2r.next()
            S.op("dve", [TT(x2[:, 0:512], o0[:, :], xt[:, 0:512], ALU.add), TT(x2[:, 512:1024], o1[:, :], xt[:, 512:1024], ALU.add)],
                 reads=[ok, xk], writes=[x2k])
            q = tile % 4
            sc = smallc[:, q * 4:q * 4 + 4]
            sck = ("smallc", q)
            S.op("act", ACT(junk[:], x2[:], AF.Square, accum_out=sc[:, 0:1]), reads=[x2k], writes=["f_junk", sck])
            S.op("act", ACT(sc[:, 1:2], sc[:, 0:1], AF.Sqrt, scale=1.0 / D, bias=EPS), reads=[sck], writes=[sck])
            S.op("dve", ("reciprocal", dict(out=sc[:, 2:3], in_=sc[:, 1:2])), reads=[sck], writes=[sck])
            ot, otk = otr.next()
            S.op("act", ACT(ot[:], x2[:], AF.Copy, scale=sc[:, 2:3]), reads=[x2k, sck], writes=[otk])
            S.op("pool", TT(ot[:], ot[:], fg_bc[:], ALU.mult), reads=[otk, "fg_bc"], writes=[otk])
            S.dma("pool", "S_fo%d" % otk[1], y_d[tile * 128:(tile + 1) * 128, :], ot[:], reads=[otk])

    for tb in range(NBLK + 1):
        if tb < NBLK:
            stageP(tb)
        if tb >= 1:
            stageO(tb - 1)


def _shared_layouts(inp):
    f = np.float32
    w_ada = np.asarray(inp["w_ada"], f)[0]
    w_in = np.asarray(inp["w_in"], f)[0]
    sh = {}
    sh["wada"] = np.ascontiguousarray(w_ada.reshape(8, 128, 3072).transpose(1, 0, 2))
    sh["rows"] = np.ascontiguousarray(np.concatenate(
        [np.asarray(inp["b_ada"], f)[0], np.asarray(inp["norm_gain"], f)[0], np.asarray(inp["final_gain"], f)])[None, :])
    wg = np.zeros((1024, 72), f)
    gc = w_in[:, 5120:5136].reshape(1024, 2, 2, 4)
    wg[:, 0:4] = gc[:, 0, 0]
    wg[:, 32:36] = gc[:, 1, 0]
    wg[:, 36:40] = gc[:, 0, 1]
    wg[:, 68:72] = gc[:, 1, 1]
    sh["wgate"] = np.ascontiguousarray(wg.reshape(8, 128, 72).transpose(1, 0, 2))
    wl = np.concatenate([w_in[:, 0:5120], w_in[:, 5136:9232]], axis=1)
    sh["win"] = np.ascontiguousarray(wl.reshape(8, 128, 18, 512).transpose(2, 1, 0, 3))
    cols = np.zeros((128, 88), f)
    cw = np.asarray(inp["conv_w"], f)[0]
    cb = np.asarray(inp["conv_b"], f)[0]
    cols[:, 0:48] = cw.reshape(3, 16, 128).transpose(2, 1, 0).reshape(128, 48)
    cols[:, 48:64] = cb.reshape(16, 128).T
    cols[:, 64:80] = np.asarray(inp["b_merge"], f)[0].reshape(16, 128).T
    cols[:, 80:88] = np.asarray(inp["mlstm_norm_gain"], f)[0].reshape(8, 128).T
    sh["cols"] = cols
    gb = np.zeros((36, 2), f)
    bi = np.asarray(inp["b_igate"], f)[0]
    bfg = np.asarray(inp["b_fgate"], f)[0]
    gb[0:4, 0] = bi[0]
    gb[32:36, 0] = bi[1]
    gb[0:4, 1] = bfg[0]
    gb[32:36, 1] = bfg[1]
    sh["gb"] = gb
    rpb = np.asarray(inp["rpb"], f)[0]
    kc = np.arange(64)[:, None]
    qc = np.arange(64)[None, :]
    ws = np.clip(qc - 8, 0, 48)
    colok = (kc >= ws) & (kc < ws + 16)
    dcidx = np.clip(kc - qc + 15, 0, 30)
    bt2 = np.full((128, 8, 14, 64), NEG, f)
    for j in range(14):
        for half in range(2):
            dr = j - 7 + half
            tab = np.where(colok[None], rpb[:, dr + 7][:, dcidx], f(NEG))
            bt2[half * 64:(half + 1) * 64, :, j, :] = tab.transpose(1, 0, 2)
    sh["bt2"] = np.ascontiguousarray(bt2.reshape(128, 4, 2, 14, 64).transpose(0, 1, 3, 2, 4))
    sh["wpa"] = np.ascontiguousarray(np.asarray(inp["w_proj_a"], f)[0].reshape(8, 128, 1024).transpose(1, 0, 2))
    sh["wpb"] = np.ascontiguousarray(np.asarray(inp["w_proj_b"], f)[0].reshape(4, 128, 1024).transpose(1, 0, 2))
    sh["wout"] = np.ascontiguousarray(np.asarray(inp["w_out"], f)[0].reshape(8, 128, 1024).transpose(1, 0, 2))
    sh["ident"] = np.eye(128, dtype=f)
    s_i = np.arange(128)[:, None]
    t_i = np.arange(128)[None, :]
    masks = np.zeros((128, 2, 128), f)
    masks[:, 0, :] = np.where(s_i <= t_i, 1.0 / 16, 0.0)
    masks[:, 1, :] = np.where(s_i >= t_i, 1.0 / 16, 0.0)
    sh["masks"] = masks
    sel = np.zeros((36, 8, 128), f)
    for j in range(8):
        sel[(j % 4) + 32 * (j // 4), j, :] = 1.0
    sh["sel"] = sel
    return sh


def make_in_maps(inp):
    sh = _shared_layouts(inp)
    x = np.asarray(inp["x"], np.float32)
    c = np.asarray(inp["c"], np.float32)
    maps = []
    for b in range(8):
        m = dict(sh)
        m["x"] = np.ascontiguousarray(x[b])
        m["c_l"] = np.ascontiguousarray(c[b].reshape(8, 128).T)
        maps.append(m)
    return maps


_NC_CACHE = {}


def kernel(**inputs):
    if "nc" not in _NC_CACHE:
        _NC_CACHE["nc"] = build_program()
    nc = _NC_CACHE["nc"]
    in_maps = make_in_maps(inputs)
    res = run_bass_kernel_spmd(nc, in_maps, core_ids=list(range(8)))
    return np.stack([np.asarray(r["y"], np.float32) for r in res.results], axis=0)
```

```python
import numpy as np
import concourse.bass as bass
import concourse.mybir as mybir
from concourse.bass_utils import run_bass_kernel_spmd

F32 = mybir.dt.float32
BF16 = mybir.dt.bfloat16
ALU = mybir.AluOpType
AF = mybir.ActivationFunctionType

S_ = 4096
D = 1024
NT = 32
NBLK = 8
H = 4
DH = 256
NH = 8
NEG = -30000.0
EPS = 1e-6
ENG_NAMES = ("pe", "act", "dve", "pool", "sp")


class Sched:
    def __init__(self, nc):
        self.nc = nc
        self.ops = {e: [] for e in ENG_NAMES}
        self.count = {}
        self.last_writer = {}
        self.readers = {}
        self.seen = {e: {} for e in ENG_NAMES}
        self.sem_names = ["pe", "act", "dve", "pool"]
        self.is_dma = set()
        self.n_instr = 0

    def _deps(self, reads, writes):
        deps = set()
        for k in reads:
            w = self.last_writer.get(k)
            if w is not None:
                deps.add(w)
        for k in writes:
            w = self.last_writer.get(k)
            if w is not None:
                deps.add(w)
            deps.update(self.readers.get(k, ()))
        return deps

    def _record(self, me, reads, writes):
        for k in reads:
            self.readers.setdefault(k, []).append(me)
        for k in writes:
            self.last_writer[k] = me
            self.readers[k] = []

    def _waits(self, eng, deps):
        need = {}
        for (s, i) in deps:
            if s in self.is_dma:
                i = self.count[s] - 1
            if need.get(s, -1) < i:
                need[s] = i
        waits = []
        for s, i in need.items():
            if self.seen[eng].get(s, -1) >= i:
                continue
            self.seen[eng][s] = i
            waits.append((s, i + 1))
        return waits

    def op(self, eng, specs, reads=(), writes=()):
        if isinstance(specs, tuple):
            specs = [specs]
        waits = self._waits(eng, self._deps(reads, writes))
        idx = self.count.get(eng, 0)
        self.count[eng] = idx + 1
        self.ops[eng].append((specs, waits, (eng, 1)))
        self._record((eng, idx), reads, writes)
        self.n_instr += len(specs)

    def dma(self, queue, stream, out, in_, reads=(), writes=()):
        if stream not in self.is_dma:
            self.is_dma.add(stream)
            self.sem_names.append(stream)
            self.count[stream] = 0
        waits = self._waits(queue, self._deps(reads, writes))
        idx = self.count[stream]
        self.count[stream] = idx + 1
        self.ops[queue].append(([("dma_start", dict(out=out, in_=in_))], waits, (stream, 16)))
        self._record((stream, idx), reads, writes)
        self.n_instr += 1

    def barrier(self):
        allw = [(s, c) for s, c in self.count.items() if c > 0]
        for e in ENG_NAMES:
            waits = []
            for s, c in allw:
                if self.seen[e].get(s, -1) >= c - 1:
                    continue
                self.seen[e][s] = c - 1
                waits.append((s, c))
            if waits:
                self.ops[e].append((None, waits, None))
        self.last_writer = {}
        self.readers = {}

    def emit(self):
        import contextlib
        nc = self.nc
        self.barrier()
        with contextlib.ExitStack() as st:
            sems = {s: st.enter_context(nc.semaphore("s_" + s)) for s in self.sem_names}
            block = st.enter_context(nc.Block())

            def run(engname):
                def body(eng):
                    for specs, waits, inc in self.ops[engname]:
                        for (s, v) in waits:
                            eng.wait_ge(sems[s], v * (16 if s in self.is_dma else 1))
                        if specs is None:
                            continue
                        ins = None
                        for (m, kw) in specs:
                            ins = getattr(eng, m)(**kw)
                        ins.then_inc(sems[inc[0]], inc[1])
                return body

            block.tensor(run("pe"))
            block.scalar(run("act"))
            block.vector(run("dve"))
            block.gpsimd(run("pool"))
            block.sync(run("sp"))


class Rot:
    def __init__(self, name, tiles):
        self.name, self.tiles, self.i = name, tiles, 0

    def next(self):
        j = self.i % len(self.tiles)
        self.i += 1
        return self.tiles[j], (self.name, j)


def MM(out, lhsT, rhs, start=True, stop=True):
    return ("matmul", dict(out=out, lhsT=lhsT, rhs=rhs, start=start, stop=stop))


def TR(out, in_, identity):
    return ("transpose", dict(out=out, in_=in_, identity=identity))


def ACT(out, in_, func, **kw):
    return ("activation", dict(out=out, in_=in_, func=func, **kw))


def TT(out, in0, in1, op):
    return ("tensor_tensor", dict(out=out, in0=in0, in1=in1, op=op))


def TS(out, in0, scalar1, scalar2, op0, op1=None):
    d = dict(out=out, in0=in0, scalar1=scalar1, scalar2=scalar2, op0=op0)
    if op1 is not None:
        d["op1"] = op1
    return ("tensor_scalar", d)


def STT(out, in0, scalar, in1, op0, op1):
    return ("scalar_tensor_tensor", dict(out=out, in0=in0, scalar=scalar, in1=in1, op0=op0, op1=op1))


def CP(out, in_):
    return ("tensor_copy", dict(out=out, in_=in_))


def MS(ap, v):
    return ("memset", dict(ap=ap, constant=v))


def SCAN(out, data0, data1, initial, op0, op1):
    return ("tensor_tensor_scan", dict(out=out, data0=data0, data1=data1, initial=initial, op0=op0, op1=op1))


def build_program(stop_after=None, debug=False):
    nc = bass.Bass("TRN2", target_bir_lowering=False)
    dbg_kind = "ExternalOutput" if debug else "Internal"

    def DIN(name, shape, dt=F32):
        return nc.dram_tensor(name, list(shape), dt, kind="ExternalInput").ap()

    def DSC(name, shape, dt=BF16):
        return nc.dram_tensor(name, list(shape), dt, kind=dbg_kind).ap()

    x_d = DIN("x", [S_, D])
    c_d = DIN("c_l", [128, 8])
    wada_d = DIN("wada", [128, 8, 3072])
    rows_d = DIN("rows", [1, 5120])
    wgate_d = DIN("wgate", [128, 8, 72])
    win_d = DIN("win", [18, 128, 8, 512])
    cols_d = DIN("cols", [128, 88])
    gb_d = DIN("gb", [36, 2])
    bt2_d = DIN("bt2", [128, 4, 14, 2, 64])
    wpa_d = DIN("wpa", [128, 8, 1024])
    wpb_d = DIN("wpb", [128, 4, 1024])
    wout_d = DIN("wout", [128, 8, 1024])
    ident_d = DIN("ident", [128, 128])
    masks_d = DIN("masks", [128, 2, 128])
    sel_d = DIN("sel", [36, 8, 128])
    y_d = nc.dram_tensor("y", [S_, D], F32, kind="ExternalOutput").ap()

    QT = DSC("QT", [4, 128, 2, S_])
    KT = DSC("KT", [4, 128, 2, S_])
    VA = DSC("VA", [NT, 128, 4 * 258])
    OG = DSC("OG", [S_, D])
    ZAT = DSC("ZAT", [8, 128, S_])
    QBT = DSC("QBT", [4, 128, S_])
    KBT = DSC("KBT", [4, 128, S_])
    VBA = DSC("VBA", [NT, 128, 8 * 66])
    ZBT = DSC("ZBT", [4, 128, S_])
    GMT = DSC("GMT", [16, 128, S_])
    YAT = DSC("YAT", [8, 128, S_])
    YBT = DSC("YBT", [4, 128, S_])
    dbg = {}
    if debug:
        dbg["hT"] = nc.dram_tensor("dbg_hT", [128, 8, S_], BF16, kind="ExternalOutput").ap()
        dbg["TOKU"] = nc.dram_tensor("dbg_TOKU", [128, NT, 36], F32, kind="ExternalOutput").ap()
        dbg["TOKC"] = nc.dram_tensor("dbg_TOKC", [128, NT, 36], F32, kind="ExternalOutput").ap()
        dbg["DECB"] = nc.dram_tensor("dbg_DECB", [128, 8, NT], F32, kind="ExternalOutput").ap()
        dbg["Hacc"] = nc.dram_tensor("dbg_Hacc", [4, 128, NT, 256], F32, kind="ExternalOutput").ap()
        for nm in ("T1", "T2", "T3"):
            dbg[nm] = nc.dram_tensor("dbg_" + nm, [36, S_], F32, kind="ExternalOutput").ap()

    def dstop(tag):
        if stop_after != tag:
            return False
        S.dma("sp", "S_dbg", dbg["T1"][:, :], T1[0:36, :], reads=["T1g"])
        S.dma("sp", "S_dbg", dbg["T2"][:, :], T2[0:36, :], reads=["T2g"])
        S.dma("sp", "S_dbg", dbg["T3"][:, :], T3[0:36, :], reads=["T3g"])
        S.emit()
        return True

    SB_LO = 16512
    SB_HI = 229344
    cur = [SB_LO]

    def T(name, shape, dt):
        n = int(np.prod(shape[1:])) * (4 if dt == F32 else 2)
        n = (n + 31) // 32 * 32
        assert cur[0] + n <= SB_HI, (name, cur[0], n)
        t = nc.alloc_sbuf_tensor_at(name, list(shape), dt, offset=cur[0])
        cur[0] += n
        return t

    ps = [nc.alloc_psum_tensor("ps%d" % i, [128, 512], F32) for i in range(8)]

    def psb(i):
        return ps[i][:].bitcast(BF16)

    S = Sched(nc)

    identb = T("identb", [128, 128], BF16)
    identf = T("identf", [128, 128], F32)
    maskT = T("maskT", [128, 2, 128], F32)
    cols = T("cols", [128, 88], F32)
    TOKU = T("TOKU", [128, NT, 36], F32)
    TOKC = T("TOKC", [128, NT, 36], F32)
    DECB = T("DECB", [128, 8, NT], F32)
    gate_bc = T("gate_bc", [128, 1024], F32)
    fg_bc = T("fg_bc", [128, 1024], F32)
    ones_c = T("ones_c", [128, 128], F32)
    smallc = T("smallc", [128, 64], F32)
    REGION0 = cur[0]

    S.dma("sp", "L_const", identf[:], ident_d[:, :], writes=["identf"])
    S.dma("pool", "L_constb", identb[:], ident_d[:, :], writes=["identb"])
    S.dma("sp", "L_const", maskT[:], masks_d[:, :, :], writes=["maskT"])
    S.dma("sp", "L_const", cols[:], cols_d[:, :], writes=["cols"])
    S.op("pool", MS(ones_c[:], 1.0), writes=["ones_c"])

    cur[0] = REGION0
    modrow = T("modrow", [1, 3072], F32)
    rows = T("rows", [1, 5120], F32)
    wst = [T("wada_st%d" % i, [128, 8, 512], F32) for i in range(2)]
    A_END = cur[0]
    cur[0] = REGION0 + 65536
    G1_bc = T("G1_bc", [128, 1024], F32)
    sh_bc = T("sh_bc", [128, 1024], F32)
    c_sb = T("c_sb", [128, 8], F32)
    cond = T("cond", [128, 8], F32)
    G1row = T("G1row", [1, 1024], F32)
    assert A_END <= REGION0 + 65536

    S.dma("sp", "L_const", c_sb[:], c_d[:, :], writes=["c_sb"])
    S.dma("sp", "L_const", rows[:], rows_d[:, :], writes=["rows"])
    S.op("act", ACT(cond[:], c_sb[:], AF.Silu), reads=["c_sb"], writes=["cond"])
    wrot = Rot("wada_st", wst)
    for g in range(6):
        wt, wk = wrot.next()
        S.dma("sp", "L_wada%d" % wk[1], wt[:], wada_d[:, :, g * 512:(g + 1) * 512], writes=[wk])
        S.op("pe", [MM(ps[g % 2][0:1, :], cond[:, kc:kc + 1], wt[:, kc, :], start=(kc == 0), stop=(kc == 7)) for kc in range(8)],
             reads=["cond", wk], writes=[("ps", g % 2)])
        S.op("dve", TT(modrow[0:1, g * 512:(g + 1) * 512], ps[g % 2][0:1, :], rows[0:1, g * 512:(g + 1) * 512], ALU.add),
             reads=[("ps", g % 2), "rows"], writes=["modrow"])
    S.op("dve", STT(G1row[0:1, :], modrow[0:1, 1024:2048], 1.0, rows[0:1, 3072:4096], ALU.add, ALU.mult),
         reads=["modrow", "rows"], writes=["G1row"])
    bc_jobs = [(G1_bc, G1row[0:1, :], "G1row", "G1_bc"), (sh_bc, modrow[0:1, 0:1024], "modrow", "sh_bc"),
               (gate_bc, modrow[0:1, 2048:3072], "modrow", "gate_bc"), (fg_bc, rows[0:1, 4096:5120], "rows", "fg_bc")]
    bi = 0
    for (dst, src, skey, dkey) in bc_jobs:
        for hf in range(2):
            b = 2 + (bi % 2)
            bi += 1
            S.op("pe", MM(ps[b][:, :], ones_c[0:1, 0:128], src[:, hf * 512:(hf + 1) * 512]), reads=[skey, "ones_c"], writes=[("ps", b)])
            S.op("act", ACT(dst[:, hf * 512:(hf + 1) * 512], ps[b][:, :], AF.Copy), reads=[("ps", b)], writes=[dkey])
    S.barrier()

    cur[0] = REGION0
    hT = T("hT", [128, 8, S_], BF16)
    C_START = cur[0]
    assert C_START == REGION0 + 65536
    cur[0] = C_START + 8192 + 2 * 32
    cur[0] = (cur[0] + 4096 + 31) // 32 * 32
    xts = [T("xt%d" % i, [128, 1024], F32) for i in range(4)]
    xns = [T("xn%d" % i, [128, 1024], BF16) for i in range(3)]
    junkb = T("junkb", [128, 1024], BF16)
    xrot = Rot("xt", xts)
    xnrot = Rot("xn", xns)
    def b_stage1(tt):
        xt, xk = xrot.next()
        xn, xnk = xnrot.next()
        sc = smallc[:, (tt % 4) * 4:(tt % 4) * 4 + 4]
        sck = ("smallc", tt % 4)
        S.dma("sp", "L_xt%d" % xk[1], xt[:], x_d[tt * 128:(tt + 1) * 128, :], writes=[xk])
        S.op("act", ACT(junkb[:], xt[:], AF.Square, accum_out=sc[:, 0:1]), reads=[xk], writes=["junkb", sck])
        S.op("act", ACT(sc[:, 1:2], sc[:, 0:1], AF.Sqrt, scale=1.0 / D, bias=EPS), reads=[sck], writes=[sck])
        S.op("dve", ("reciprocal", dict(out=sc[:, 2:3], in_=sc[:, 1:2])), reads=[sck], writes=[sck])
        S.op("dve", STT(xt[:], xt[:], sc[:, 2:3], G1_bc[:], ALU.mult, ALU.mult), reads=[xk, sck, "G1_bc"], writes=[xk])
        S.op("dve" if tt % 3 != 2 else "pool", TT(xn[:], xt[:], sh_bc[:], ALU.add), reads=[xk, "sh_bc"], writes=[xnk])
        return xn, xnk

    def b_stage2(tt, xn, xnk):
        b = 6 + (tt % 2)
        S.op("pe", [TR(psb(b)[:, kc * 128:(kc + 1) * 128], xn[:, kc * 128:(kc + 1) * 128], identb[:]) for kc in range(8)],
             reads=[xnk, "identb"], writes=[("ps", b)])
        S.op("act", ACT(hT[:, :, tt * 128:(tt + 1) * 128], psb(b).rearrange("p (a b) -> p a b", a=8), AF.Copy),
             reads=[("ps", b)], writes=[("hT", tt)])

    bctx = {}
    for tt in range(NT + 1):
        if tt < NT:
            bctx[tt] = b_stage1(tt)
        if tt >= 1:
            b_stage2(tt - 1, *bctx.pop(tt - 1))
    if debug:
        S.dma("sp", "S_dbg", dbg["hT"][:, :, :], hT[:], reads=[("hT", tt) for tt in range(NT)])
    S.barrier()
    if stop_after == "B":
        S.emit()
        return nc

    cur[0] = C_START
    Wb = [T("Wb%d" % i, [128, 8, 512], BF16) for i in range(3)]
    wgb = T("wgb", [128, 8, 72], BF16)
    UA = T("UA", [128, S_ + 2], F32)
    UB = T("UB", [128, S_ + 2], F32)
    ACC = T("ACC", [128, S_], F32)
    obufs = [T("obuf%d" % i, [128, S_], BF16) for i in range(2)]
    tms = [T("tmst%d" % i, [128, 2112], BF16) for i in range(2)]
    gbc = T("gbc", [36, 2], F32)
    MPt = T("MPt", [36, NT], F32)
    MOt = T("MOt", [36, NT], F32)
    DECt = T("DECt", [36, NT], F32)
    selt = T("selt", [36, 8, 128], F32)
    C_END = cur[0]
    T1 = nc.alloc_sbuf_tensor_at("T1g", [128, S_], F32, offset=C_START + 3 * 8192 + 1152)
    T2 = nc.alloc_sbuf_tensor_at("T2g", [128, S_], F32, offset=C_START + 3 * 8192 + 1152 + 16416)
    T3 = ACC
    ONESF = nc.alloc_sbuf_tensor_at("ONESF", [128, S_], F32, offset=C_START + 3 * 8192 + 1152 + 2 * 16416 + 16384)

    wrot = Rot("Wb", Wb)
    orot = Rot("obuf", obufs)
    tmrot = Rot("tmst", tms)
    urot = Rot("U", [UA, UB])
    psrot = Rot("ps", ps[0:8])
    evtog = [0]

    S.dma("sp", "L_const", gbc[:], gb_d[:, :], writes=["gbc"])
    S.dma("sp", "L_const", selt[:], sel_d[:, :, :], writes=["selt"])
    S.dma("pool", "L_wgb", wgb[:], wgate_d[:, :, :], writes=["wgb"])
    allhT = [("hT", tt) for tt in range(NT)]

    def hkeys(tb):
        return [("hT", tb * 4 + j) for j in range(4)]

    for tb in range(NBLK):
        for gi, (Tt, col, tkey) in enumerate(((T1, 0, "T1g"), (T2, 1, "T2g"))):
            pt, pk = psrot.next()
            S.op("pe", [MM(pt[0:36, :], wgb[:, kc, gi * 36:(gi + 1) * 36], hT[:, kc, tb * 512:(tb + 1) * 512],
                           start=(kc == 0), stop=(kc == 7)) for kc in range(8)],
                 reads=["wgb"] + hkeys(tb), writes=[pk])
            S.op("act", ACT(Tt[0:36, tb * 512:(tb + 1) * 512], pt[0:36, :], AF.Identity, bias=gbc[0:36, col:col + 1]),
                 reads=[pk, "gbc"], writes=[tkey])

    if dstop("D0"):
        return nc
    r36 = slice(0, 36)
    fw = slice(0, 4)
    bw = slice(32, 36)
    S.op("act", ACT(T2[r36, :], T2[r36, :], AF.Exp, scale=-1.0), reads=["T2g"], writes=["T2g"])
    S.op("act", ACT(T2[r36, :], T2[r36, :], AF.Ln, bias=1.0), reads=["T2g"], writes=["T2g"])
    S.op("pool", MS(T3[r36, :], 0.0), writes=["T3g"])
    S.op("pool", MS(ONESF[r36, :], 1.0), writes=["ONESF"])
    S.op("pool", [MS(MPt[:], 0.0), MS(MOt[:], 0.0)], writes=["MPt", "MOt"])
    S.op("dve", SCAN(T3[fw, :], ONESF[fw, :], T2[fw, :], 0.0, ALU.mult, ALU.add),
         reads=["T2g", "ONESF"], writes=["T3g"])
    S.op("dve", SCAN(T3[bw, ::-1], ONESF[bw, :], T2[bw, ::-1], 0.0, ALU.mult, ALU.add),
         reads=["T2g", "ONESF"], writes=["T3g"])
    if dstop("D1"):
        return nc
    S.op("dve", TT(T1[r36, :], T1[r36, :], T3[r36, :], ALU.add), reads=["T1g", "T3g"], writes=["T1g"])
    S.op("dve", SCAN(T2[fw, :], ONESF[fw, :], T1[fw, :], 0.0, ALU.mult, ALU.max),
         reads=["T1g", "ONESF"], writes=["T2g"])
    S.op("dve", SCAN(T2[bw, ::-1], ONESF[bw, :], T1[bw, ::-1], 0.0, ALU.mult, ALU.max),
         reads=["T1g", "ONESF"], writes=["T2g"])
    if dstop("D2"):
        return nc
    M3 = T2[:].rearrange("p (k t) -> p k t", t=128)
    S.op("dve", [CP(MPt[fw, 1:NT], M3[fw, 0:NT - 1, 127]), CP(MOt[fw, :], M3[fw, :, 127])],
         reads=["T2g", "MPt", "MOt"], writes=["MPt", "MOt"])
    S.op("dve", [CP(MPt[bw, 0:NT - 1], M3[bw, 1:NT, 0]), CP(MOt[bw, :], M3[bw, :, 0])],
         reads=["T2g", "MPt", "MOt"], writes=["MPt", "MOt"])
    S.op("dve", TT(DECt[:], MPt[:], MOt[:], ALU.subtract), reads=["MPt", "MOt"], writes=["DECt"])
    S.op("act", ACT(DECt[:], DECt[:], AF.Exp), reads=["DECt"], writes=["DECt"])
    if dstop("D3"):
        return nc
    MPb = MPt[:].rearrange("p (k o) -> p k o", o=1).to_broadcast([36, NT, 128])
    T1v = T1[r36, :].rearrange("p (k t) -> p k t", t=128)
    T3v = T3[r36, :].rearrange("p (k t) -> p k t", t=128)
    S.op("dve", TT(T3v, T3v, MPb, ALU.subtract), reads=["T3g", "MPt"], writes=["T3g"])
    S.op("act", ACT(T3[r36, :], T3[r36, :], AF.Exp), reads=["T3g"], writes=["T3g"])
    S.op("dve", TT(T1v, T1v, MPb, ALU.subtract), reads=["T1g", "MPt"], writes=["T1g"])
    S.op("act", ACT(T1[r36, :], T1[r36, :], AF.Exp), reads=["T1g"], writes=["T1g"])
    if dstop("D4"):
        return nc
    for (src, skey, dst, dkey) in ((T1, "T1g", TOKU, "TOKU"), (T3, "T3g", TOKC, "TOKC")):
        for k0 in range(0, NT, 14):
            n = min(14, NT - k0)
            pt, pk = psrot.next()
            S.op("pe", [TR(pt[:, j * 36:(j + 1) * 36], src[0:36, (k0 + j) * 128:(k0 + j + 1) * 128], identf[0:36, 0:36]) for j in range(n)],
                 reads=[skey, "identf"], writes=[pk])
            S.op("dve", CP(dst[:, k0:k0 + n, :], pt[:, 0:n * 36].rearrange("p (a b) -> p a b", b=36)), reads=[pk], writes=[dkey])
    pt, pk = psrot.next()
    S.op("pe", [MM(pt[:, j * NT:(j + 1) * NT], selt[0:36, j, :], DECt[0:36, :]) for j in range(8)],
         reads=["selt", "DECt"], writes=[pk])
    S.op("dve", CP(DECB[:], pt[:, 0:8 * NT].rearrange("p (a b) -> p a b", b=NT)), reads=[pk], writes=["DECB"])
    if debug:
        S.dma("sp", "S_dbg", dbg["TOKU"][:, :, :], TOKU[:], reads=["TOKU"])
        S.dma("sp", "S_dbg", dbg["TOKC"][:, :, :], TOKC[:], reads=["TOKC"])
        S.dma("sp", "S_dbg", dbg["DECB"][:, :, :], DECB[:], reads=["DECB"])
    S.barrier()
    if stop_after == "D":
        S.emit()
        return nc

    S.op("pool", [MS(UA[:, 0:1], 0.0), MS(UA[:, S_ + 1:S_ + 2], 0.0), MS(UB[:, 0:1], 0.0), MS(UB[:, S_ + 1:S_ + 2], 0.0)],
         writes=[("U", 0), ("U", 1)])

    def load_w(g):
        wt, wk = wrot.next()
        S.dma("pool", "L_Wb%d" % wk[1], wt[:], win_d[g], writes=[wk])
        return wt, wk

    pend_tail = []

    def flush_tail():
        while pend_tail:
            inf = pend_tail.pop(0)
            ob, ok = orot.next()
            S.op("act", ACT(ob[:], ACC[:], AF.Silu), reads=["ACC"], writes=[ok])
            S.dma("sp", "S_obuf%d" % ok[1], inf["dst"], ob[:], reads=[ok], writes=[inf["dkey"]])

    def fm_group(g, kind, sub_info):
        wt, wk = load_w(g)
        for sub in range(4):
            info = sub_info(sub)
            if kind == "conv":
                U, uk = urot.next()
            else:
                ob, ok = orot.next()
            for tb in range(NBLK):
                pt, pk = psrot.next()
                S.op("pe", [MM(pt[:, :], wt[:, kc, sub * 128:(sub + 1) * 128], hT[:, kc, tb * 512:(tb + 1) * 512],
                               start=(kc == 0), stop=(kc == 7)) for kc in range(8)],
                     reads=[wk] + hkeys(tb), writes=[pk])
                if kind == "conv":
                    S.op("act", ACT(U[:, 1 + tb * 512:1 + (tb + 1) * 512], pt[:, :], AF.Copy), reads=[pk], writes=[uk])
                elif kind == "silu":
                    S.op("act", ACT(ob[:, tb * 512:(tb + 1) * 512], pt[:, :], AF.Silu), reads=[pk], writes=[ok])
                elif kind == "sigb":
                    S.op("act", ACT(ob[:, tb * 512:(tb + 1) * 512], pt[:, :], AF.Sigmoid, bias=info["bias"]), reads=[pk, "cols"], writes=[ok])
                elif kind == "copy":
                    evtog[0] ^= 1
                    if evtog[0]:
                        S.op("dve", TS(ob[:, tb * 512:(tb + 1) * 512], pt[:, :], info["scale"], None, ALU.mult), reads=[pk], writes=[ok])
                    else:
                        S.op("act", ACT(ob[:, tb * 512:(tb + 1) * 512], pt[:, :], AF.Copy, scale=info["scale"]), reads=[pk], writes=[ok])
            if kind == "conv":
                flush_tail()
                cg = info["cg"]
                w0 = cols[:, cg * 3 + 0:cg * 3 + 1]
                w1 = cols[:, cg * 3 + 1:cg * 3 + 2]
                w2 = cols[:, cg * 3 + 2:cg * 3 + 3]
                cb = cols[:, 48 + cg:49 + cg]
                S.op("dve", TS(ACC[:], U[:, 1:S_ + 1], w1, cb, ALU.mult, ALU.add), reads=[uk, "cols"], writes=["ACC"])
                S.op("dve", STT(ACC[:], U[:, 0:S_], w0, ACC[:], ALU.mult, ALU.add), reads=[uk, "cols", "ACC"], writes=["ACC"])
                S.op("dve", STT(ACC[:], U[:, 2:S_ + 2], w2, ACC[:], ALU.mult, ALU.add), reads=[uk, "cols", "ACC"], writes=["ACC"])
                pend_tail.append(info)
            else:
                S.dma("sp", "S_obuf%d" % ok[1], info["dst"], ob[:], reads=[ok], writes=[info["dkey"]])

    def tm_group(g, kind, col0):
        wt, wk = load_w(g)
        for t4 in range(NT // 4):
            st, sk = tmrot.next()
            if kind == "va":
                sv = st[:, 0:4 * 2 * 258].rearrange("p (t h c) -> p t h c", t=4, h=2)
                S.op("pool", [MS(sv[:, :, :, 256:257], 1.0), MS(sv[:, :, :, 257:258], 0.0)], writes=[sk])
            elif kind == "vb":
                sv = st[:, 0:4 * 8 * 66].rearrange("p (t h c) -> p t h c", t=4, h=8)
                S.op("pool", [MS(sv[:, :, :, 64:65], 1.0), MS(sv[:, :, :, 65:66], 0.0)], writes=[sk])
            else:
                sv = st[:, 0:2048].rearrange("p (t c) -> p t c", t=4)
            for j in range(4):
                tt = t4 * 4 + j
                pt, pk = psrot.next()
                S.op("pe", [MM(pt[:, :], hT[:, kc, tt * 128:(tt + 1) * 128], wt[:, kc, :], start=(kc == 0), stop=(kc == 7)) for kc in range(8)],
                     reads=[wk, ("hT", tt)], writes=[pk])
                if kind == "va":
                    S.op("dve", CP(sv[:, j, :, 0:256], pt[:, :].rearrange("p (h c) -> p h c", h=2)), reads=[pk], writes=[sk])
                elif kind == "vb":
                    S.op("dve", CP(sv[:, j, :, 0:64], pt[:, :].rearrange("p (h c) -> p h c", h=8)), reads=[pk], writes=[sk])
                else:
                    S.op("act", ACT(sv[:, j, :], pt[:, :], AF.Sigmoid), reads=[pk], writes=[sk])
            tsl = slice(t4 * 4, t4 * 4 + 4)
            if kind == "va":
                hd0 = col0
                dst = VA[tsl, :, hd0 * 258:(hd0 + 2) * 258].rearrange("t p c -> p t c")
                S.dma("sp", "S_tm%d" % sk[1], dst, st[:, 0:4 * 516].rearrange("p (t c) -> p t c", t=4), reads=[sk], writes=[("VA", t4, hd0)])
            elif kind == "vb":
                dst = VBA[tsl, :, :].rearrange("t p c -> p t c")
                S.dma("sp", "S_tm%d" % sk[1], dst, st[:, 0:4 * 528].rearrange("p (t c) -> p t c", t=4), reads=[sk], writes=[("VBA", t4)])
            else:
                dst = OG[t4 * 512:(t4 + 1) * 512, col0:col0 + 512].rearrange("(t p) c -> p t c", p=128)
                S.dma("sp", "S_tm%d" % sk[1], dst, sv, reads=[sk], writes=[("OG", t4, col0)])

    for g in (0, 1):
        fm_group(g, "conv", lambda sub, g=g: dict(cg=g * 4 + sub, dst=QT[(g * 4 + sub) // 2, :, (g * 4 + sub) % 2, :],
                                                 dkey=("QT", g * 4 + sub)))
    for g in (2, 3):
        fm_group(g, "conv", lambda sub, g=g: dict(cg=8 + (g - 2) * 4 + sub, dst=KT[((g - 2) * 4 + sub) // 2, :, ((g - 2) * 4 + sub) % 2, :],
                                                 dkey=("KT", (g - 2) * 4 + sub)))
    flush_tail()
    for g in (8, 9):
        fm_group(g, "silu", lambda sub, g=g: dict(dst=ZAT[(g - 8) * 4 + sub], dkey=("ZAT", (g - 8) * 4 + sub)))
    fm_group(13, "silu", lambda sub: dict(dst=ZBT[sub], dkey=("ZBT", sub)))
    fm_group(10, "copy", lambda sub: dict(scale=0.125, dst=QBT[sub], dkey=("QBT", sub)))
    fm_group(11, "copy", lambda sub: dict(scale=1.0, dst=KBT[sub], dkey=("KBT", sub)))
    tm_group(4, "va", 0)
    tm_group(5, "va", 2)
    tm_group(12, "vb", 0)
    tm_group(6, "og", 0)
    tm_group(7, "og", 512)
    for g in (14, 15, 16, 17):
        fm_group(g, "sigb", lambda sub, g=g: dict(bias=cols[:, 64 + (g - 14) * 4 + sub:65 + (g - 14) * 4 + sub],
                                                 dst=GMT[(g - 14) * 4 + sub], dkey=("GMT", (g - 14) * 4 + sub)))
    S.barrier()
    if stop_after == "C":
        S.emit()
        return nc

    if build_mlstm(nc, S, locals()):
        return nc
    if stop_after == "M":
        S.emit()
        return nc
    build_na(nc, S, locals())
    if stop_after == "N":
        S.emit()
        return nc
    build_final(nc, S, locals())
    S.emit()
    return nc


def build_mlstm(nc, S, E):
    T, cur, ps, psb = E["T"], E["cur"], E["ps"], E["psb"]
    identb, maskT, cols, TOKU, TOKC, DECB = E["identb"], E["maskT"], E["cols"], E["TOKU"], E["TOKC"], E["DECB"]
    QT, KT, VA, OG, ZAT, YAT = E["QT"], E["KT"], E["VA"], E["OG"], E["ZAT"], E["YAT"]
    dbg, debug = E["dbg"], E["debug"]
    cur[0] = E["REGION0"]
    qT = T("m_qT", [128, 2, S_], BF16)
    kT = T("m_kT", [128, 2, S_], BF16)
    ktok = T("m_ktok", [128, NT, 256], BF16)
    Vaug = T("m_Vaug", [128, NT, 258], BF16)
    Hacc = T("m_Hacc", [128, NT, 256], F32)
    ZATh = T("m_ZATh", [128, 2, S_], BF16)
    yaT = T("m_yaT", [128, 2, S_], BF16)
    OGt = Rot("OGt", [T("m_OGt%d" % i, [128, 4, 256], BF16) for i in range(3)])
    UVr = [Rot("UV%d" % d, [T("m_UV%d_%d" % (d, i), [128, 258], BF16) for i in range(4)]) for d in range(2)]
    Smr = [Rot("Sm%d" % d, [T("m_Sm%d_%d" % (d, i), [128, 128], BF16) for i in range(3)]) for d in range(2)]
    Zs = [T("m_Z%d" % d, [128, 2, 258], F32) for d in range(2)]
    Cbr = [Rot("Cb%d" % d, [T("m_Cb%d_%d" % (d, i), [128, 2, 258], BF16) for i in range(3)]) for d in range(2)]
    Htr = Rot("Htmp", [T("m_Htmp%d" % i, [128, 256], F32) for i in range(3)])
    hgr = Rot("hg", [T("m_hg%d" % i, [128, 256], F32) for i in range(4)])
    ytr = Rot("yatok", [T("m_yatok%d" % i, [128, 256], BF16) for i in range(4)])
    junk = T("m_junk", [128, 256], BF16)
    rcs = T("m_rcs", [128, 8, 4], F32)
    pcs = T("m_pcs", [128, 8, 4], F32)
    rci = [0]
    pci = [0]
    dcp = [((ps[4], ps[5]), [("ps", 4), ("ps", 5)]), ((ps[6], ps[7]), [("ps", 6), ("ps", 7)])]

    def loads(hd):
        S.dma("sp", "L_mq", qT[:], QT[hd], writes=["qT"])
        S.dma("sp", "L_mk", kT[:], KT[hd], writes=["kT"])
        for j0 in range(0, NT, 8):
            S.dma("sp", "L_mv", Vaug[:, j0:j0 + 8, :], VA[j0:j0 + 8, :, hd * 258:(hd + 1) * 258].rearrange("t p c -> p t c"),
                  writes=[("Vaug", j) for j in range(j0, j0 + 8)])

    loads(0)
    for hd in range(4):
        S.dma("sp", "L_mz", ZATh[:], ZAT[2 * hd:2 * hd + 2].rearrange("g p t -> p g t"), writes=["ZATh"])
        for k4 in range(8):
            kb = 6 + (k4 % 2)
            S.op("pe", [TR(psb(kb)[:, (kk * 2 + c) * 128:(kk * 2 + c + 1) * 128], kT[:, c, (k4 * 4 + kk) * 128:(k4 * 4 + kk + 1) * 128], identb[:])
                        for kk in range(4) for c in range(2)], reads=["kT", "identb"], writes=[("ps", kb)])
            if k4 % 2 == 0:
                S.op("act", ACT(ktok[:, k4 * 4:(k4 + 1) * 4, :].rearrange("p a b -> p (a b)"), psb(kb)[:, 0:1024], AF.Copy, scale=1.0 / 16),
                     reads=[("ps", kb)], writes=[("ktok", k4)])
            else:
                S.op("dve", TS(ktok[:, k4 * 4:(k4 + 1) * 4, :].rearrange("p a b -> p (a b)"), psb(kb)[:, 0:1024], 1.0 / 16, None, ALU.mult),
                     reads=[("ps", kb)], writes=[("ktok", k4)])
        if E["stop_after"] == "M0":
            S.emit()
            return True

        ctx = {}
        chain = [dict(kprev=None) for _ in range(2)]

        def kof(d, i):
            return i if d == 0 else NT - 1 - i

        def opUV(d, i):
            k = kof(d, i)
            ucol = TOKU[:, k, d * 32 + hd:d * 32 + hd + 1]
            UVb, uvk = UVr[d].next()
            S.op("dve", TS(UVb[:], Vaug[:, k, :], ucol, None, ALU.mult), reads=[("Vaug", k), "TOKU"], writes=[uvk])
            ctx[(d, i)] = dict(k=k, ch=slice(k * 128, (k + 1) * 128), UVb=UVb, uvk=uvk, cb=None, cbk=None)

        def opST(d, i):
            c_ = ctx[(d, i)]
            stp = ps[d][:, 0:128]
            S.op("pe", [MM(stp, kT[:, c, c_["ch"]], qT[:, c, c_["ch"]], start=(c == 0), stop=(c == 1)) for c in range(2)],
                 reads=["kT", "qT"], writes=[("ps", d)])

        def opMASK(d, i):
            c_ = ctx[(d, i)]
            Sm, smk = Smr[d].next()
            S.op("dve", TT(Sm[:], ps[d][:, 0:128], maskT[:, d, :], ALU.mult), reads=[("ps", d), "maskT"], writes=[smk])
            c_["Sm"], c_["smk"] = Sm, smk

        def opDC(d, i):
            c_ = ctx[(d, i)]
            (b0, b1), dks = dcp[d]
            k = c_["k"]
            S.op("pe", [MM(b0[:, 0:258], ktok[:, k, 0:128], c_["UVb"][:]), MM(b1[:, 0:258], ktok[:, k, 128:256], c_["UVb"][:])],
                 reads=[("ktok", k // 4), c_["uvk"]], writes=dks)

        def opZ(d, i):
            c_ = ctx[(d, i)]
            (b0, b1), dks = dcp[d]
            series = d * 4 + hd
            k = c_["k"]
            Z = Zs[d]
            zk = ("Z", d)
            if i == 0:
                S.op("dve", [CP(Z[:, 0, :], b0[:, 0:258]), CP(Z[:, 1, :], b1[:, 0:258])], reads=dks, writes=[zk])
            else:
                kp = chain[d]["kprev"]
                dprev = DECB[:, series, kp:kp + 1]
                S.op("dve", [STT(Z[:, 0, :], Z[:, 0, :], dprev, b0[:, 0:258], ALU.mult, ALU.add),
                             STT(Z[:, 1, :], Z[:, 1, :], dprev, b1[:, 0:258], ALU.mult, ALU.add)],
                     reads=dks + [zk, "DECB"], writes=[zk])
            cbn, cbk = Cbr[d].next()
            dcur = DECB[:, series, k:k + 1]
            S.op("act", ACT(cbn[:].rearrange("p a b -> p (a b)"), Z[:].rearrange("p a b -> p (a b)"), AF.Copy, scale=dcur), reads=[zk, "DECB"], writes=[cbk])
            c_["cb"], c_["cbk"] = cbn, cbk
            chain[d]["kprev"] = k

        def opNP(d, i):
            c_ = ctx[(d, i)]
            first = (i == 0)
            npb, npk = ps[2 + d], ("ps", 2 + d)
            npa = npb[:, 0:258]
            specs = [MM(npa, c_["Sm"][:], c_["UVb"][:], start=True, stop=first)]
            rd = [c_["smk"], c_["uvk"]]
            if not first:
                pv = ctx[(d, i - 1)]
                specs += [MM(npa, qT[:, c, c_["ch"]], pv["cb"][:, c, :], start=False, stop=(c == 1)) for c in range(2)]
                rd += ["qT", pv["cbk"]]
            S.op("pe", specs, reads=rd, writes=[npk])

        def opOUT(d, i):
            c_ = ctx[(d, i)]
            k = c_["k"]
            npb, npk = ps[2 + d], ("ps", 2 + d)
            j = rci[0] % 8
            rci[0] += 1
            rc = rcs[:, j, :]
            rck = ("rc", j)
            den = npb[:, 256:257]
            ccol = TOKC[:, k, d * 32 + hd:d * 32 + hd + 1]
            S.op("dve", TT(rc[:, 0:1], den, ccol, ALU.max), reads=[npk, "TOKC"], writes=[rck])
            S.op("dve", STT(rc[:, 1:2], den, -1.0, rc[:, 0:1], ALU.mult, ALU.max), reads=[npk, rck], writes=[rck])
            S.op("dve", ("reciprocal", dict(out=rc[:, 2:3], in_=rc[:, 1:2])), reads=[rck], writes=[rck])
            if i < NT // 2:
                S.op("act", ACT(Hacc[:, k, :], npb[:, 0:256], AF.Copy, scale=rc[:, 2:3]), reads=[npk, rck], writes=[("Hacc", k)])
            else:
                ht, htk = Htr.next()
                S.op("act", ACT(ht[:], npb[:, 0:256], AF.Copy, scale=rc[:, 2:3]), reads=[npk, rck], writes=[htk])
                S.op("pool", TT(Hacc[:, k, :], Hacc[:, k, :], ht[:], ALU.add), reads=[htk, ("Hacc", k)], writes=[("Hacc", k)])
            if i >= 1:
                ctx.pop((d, i - 1))

        for d in range(2):
            opUV(d, 0)
        for d in range(2):
            opUV(d, 1)
        for d in range(2):
            opST(d, 0)
        for d in range(2):
            opMASK(d, 0)
        for d in range(2):
            opDC(d, 0)
        for d in range(2):
            opZ(d, 0)
        for i in range(NT):
            nx = i + 1
            if i + 2 < NT:
                opUV(0, i + 2)
                opUV(1, i + 2)
            if nx < NT:
                opST(0, nx)
                opST(1, nx)
                opMASK(0, nx)
                opMASK(1, nx)
                if nx < NT - 1:
                    opDC(0, nx)
                    opDC(1, nx)
            opNP(0, i)
            opNP(1, i)
            if nx < NT - 1:
                opZ(0, nx)
            opOUT(0, i)
            if nx < NT - 1:
                opZ(1, nx)
            opOUT(1, i)

        if debug:
            S.dma("sp", "S_dbg", dbg["Hacc"][hd], Hacc[:], reads=[("Hacc", k) for k in range(NT)])
        if hd + 1 < 4:
            loads(hd + 1)

        pctx = {}

        def P1(k):
            k4, j = k // 4, k % 4
            if j == 0:
                ogt, ogk = OGt.next()
                S.dma("pool", "L_og%d" % ogk[1], ogt[:], OG[k4 * 512:(k4 + 1) * 512, hd * 256:(hd + 1) * 256].rearrange("(t p) c -> p t c", p=128),
                      writes=[ogk])
                pctx["og"] = (ogt, ogk)
            ogt, ogk = pctx["og"]
            hg, hgk = hgr.next()
            S.op("dve", TT(hg[:], Hacc[:, k, :], ogt[:, j, :], ALU.mult), reads=[("Hacc", k), ogk], writes=[hgk])
            q = pci[0] % 8
            pci[0] += 1
            pc = pcs[:, q, :]
            pck = ("pc", q)
            S.op("act", ACT(junk[:], hg[:], AF.Square, accum_out=pc[:, 0:1]), reads=[hgk], writes=["m_junk", pck])
            S.op("act", ACT(pc[:, 1:2], pc[:, 0:1], AF.Sqrt, scale=1.0 / DH, bias=EPS), reads=[pck], writes=[pck])
            pctx[k] = (hg, hgk, pc, pck)

        def P2(k):
            k4, j = k // 4, k % 4
            hg, hgk, pc, pck = pctx.pop(k)
            S.op("dve", ("reciprocal", dict(out=pc[:, 2:3], in_=pc[:, 1:2])), reads=[pck], writes=[pck])
            yt, ytk = ytr.next()
            S.op("dve", TS(yt[:], hg[:], pc[:, 2:3], None, ALU.mult), reads=[hgk, pck], writes=[ytk])
            S.op("pe", [TR(psb(7)[:, c * 512 + j * 128:c * 512 + (j + 1) * 128], yt[:, c * 128:(c + 1) * 128], identb[:]) for c in range(2)],
                 reads=[ytk, "identb"], writes=[("ps", 7)])
            if j == 3:
                for c in range(2):
                    S.op("dve", STT(yaT[:, c, k4 * 512:(k4 + 1) * 512], psb(7)[:, c * 512:(c + 1) * 512], cols[:, 80 + hd * 2 + c:81 + hd * 2 + c],
                                    ZATh[:, c, k4 * 512:(k4 + 1) * 512], ALU.mult, ALU.mult),
                         reads=[("ps", 7), "cols", "ZATh"], writes=[("yaT", c)])

        for k in range(NT + 2):
            if k < NT:
                P1(k)
            if k >= 2:
                P2(k - 2)
        for c in range(2):
            S.dma("pool", "S_yaT", YAT[hd * 2 + c], yaT[:, c, :], reads=[("yaT", c)])
        if E["stop_after"] == "M2":
            S.emit()
            return True
    S.barrier()


def build_na(nc, S, E):
    T, cur, ps, psb = E["T"], E["cur"], E["ps"], E["psb"]
    identb = E["identb"]
    QBT, KBT, VBA, ZBT, YBT, bt2_d = E["QBT"], E["KBT"], E["VBA"], E["ZBT"], E["YBT"], E["bt2_d"]
    cur[0] = E["REGION0"]
    bt2b = T("n_bt2", [128, 4, 14, 2, 64], BF16)
    QBD = T("n_QBD", [128, 2, 2, S_], BF16)
    kbT = T("n_kbT", [128, 2, S_], BF16)
    ZBh = T("n_ZBh", [128, 2, S_], BF16)
    ybT = T("n_ybT", [128, 2, S_], BF16)
    VE = T("n_VE", [128, NT, 4, 66], BF16)
    VO = T("n_VO", [128, NT - 1, 4, 66], BF16)
    PTr = Rot("PT", [T("n_PT%d" % i, [128, 1024], BF16) for i in range(2)])
    otr = Rot("otok", [T("n_otok%d" % i, [64, 4, 64], BF16) for i in range(3)])
    recs = T("n_rec", [64, 4, 4], F32)
    str_ = Rot("psS", [(ps[0], ps[1]), (ps[2], ps[3])])
    pvr = Rot("psPV", [ps[4], ps[5]])
    trr = Rot("psT", [6, 7])
    ri = [0]
    NA_END = cur[0]
    wpab = T("f_wpa", [128, 8, 1024], BF16)
    wpbb = T("f_wpb", [128, 4, 1024], BF16)
    woutb = T("f_wout", [128, 8, 1024], BF16)
    wsr = Rot("f_wst", [T("f_wst%d" % i, [128, 2, 1024], F32) for i in range(1)])
    S.shared_fw = (NA_END, wpab, wpbb, woutb)
    gate_bc = E["gate_bc"]
    S.dma("pool", "L_bt2", bt2b[:], bt2_d[:, :, :, :, :], writes=["bt2b"])
    S.op("pool", [MS(QBD[0:64, :, 1, :], 0.0), MS(QBD[64:128, :, 0, :], 0.0)], writes=["QBDz"])
    for half in range(2):
        c0 = half * 4 * 66
        for cq in range(4):
            tsl = slice(cq * 1024, (cq + 1) * 1024)
            S.dma("sp", "L_nq%d" % cq, QBD[0:64, :, 0, tsl], QBT[2 * half:2 * half + 2, 0:64, tsl].rearrange("g p t -> p g t"), reads=["QBDz"], writes=[("qbT", cq)])
            S.dma("sp", "L_nq%d" % cq, QBD[64:128, :, 1, tsl], QBT[2 * half:2 * half + 2, 64:128, tsl].rearrange("g p t -> p g t"), reads=["QBDz"], writes=[("qbT", cq)])
            S.dma("sp", "L_nk%d" % cq, kbT[:, :, tsl], KBT[2 * half:2 * half + 2, :, tsl].rearrange("g p t -> p g t"), writes=[("kbT", cq)])
            j0 = cq * 8
            S.dma("sp", "L_nve%d" % cq, VE[:, j0:j0 + 8, :, :].rearrange("p t h c -> p t (h c)"),
                  VBA[j0:j0 + 8, :, c0:c0 + 264].rearrange("t p c -> p t c"), writes=[("VE", cq)])
            j1 = min(j0 + 8, NT - 1)
            S.dma("sp", "L_nvo%d" % cq, VO[0:64, j0:j1, :, :].rearrange("p t h c -> p t (h c)"),
                  VBA[j0:j1, 64:128, c0:c0 + 264].rearrange("t p c -> p t c"), writes=[("VO", cq)])
            S.dma("sp", "L_nvo%d" % cq, VO[64:128, j0:j1, :, :].rearrange("p t h c -> p t (h c)"),
                  VBA[j0 + 1:j1 + 1, 0:64, c0:c0 + 264].rearrange("t p c -> p t c"), writes=[("VO", cq)])
            S.dma("sp", "L_nz%d" % cq, ZBh[:, :, tsl], ZBT[2 * half:2 * half + 2, :, tsl].rearrange("g p t -> p g t"), writes=[("ZBh", cq)])

        def n_stage1(r):
            rs = min(max(r - 4, 0), 56)
            j0b = rs - r + 7
            qs = slice(r * 64, (r + 1) * 64)
            pair, sk = str_.next()
            specs = []
            for gi in range(2):
                bank = pair[gi]
                hp = half * 2 + gi
                specs.append(MM(bank[:, 0:512], identb[:], bt2b[:, hp, j0b:j0b + 7:2, :, :].rearrange("p i j q -> p i (j q)"), start=True, stop=False))
                for i in range(4):
                    tok = rs * 64 + i * 128
                    specs.append(MM(bank[:, i * 128:(i + 1) * 128], kbT[:, gi, tok:tok + 128], QBD[:, gi, :, qs], start=False, stop=(i == 3)))
            S.op("pe", specs, reads=["identb", "bt2b", ("qbT", (r * 64) // 1024)] + [("kbT", cc) for cc in sorted({(rs * 64) // 1024, (rs * 64 + 511) // 1024})], writes=[sk])
            PT, ptk = PTr.next()
            S.op("act", [ACT(PT[:, 0:512], pair[0][:, :], AF.Exp), ACT(PT[:, 512:1024], pair[1][:, :], AF.Exp)], reads=[sk], writes=[ptk])
            return dict(r=r, rs=rs, qs=qs, PT=PT, ptk=ptk)

        def n_stage2(c):
            rs, PT, ptk = c["rs"], c["PT"], c["ptk"]
            if rs % 2 == 0:
                Vx, vnm, tbase = VE, "VE", rs // 2
            else:
                Vx, vnm, tbase = VO, "VO", (rs - 1) // 2
            vkeys = [(vnm, cc) for cc in sorted({tbase // 8, (tbase + 3) // 8})]
            ob, ok = pvr.next()
            specs = []
            for hh in range(4):
                for i in range(4):
                    specs.append(MM(ob[0:64, hh * 66:(hh + 1) * 66], PT[:, (hh // 2) * 512 + i * 128 + (hh % 2) * 64:(hh // 2) * 512 + i * 128 + (hh % 2) * 64 + 64], Vx[:, tbase + i, hh, :],
                                    start=(i == 0), stop=(i == 3)))
            S.op("pe", specs, reads=[ptk] + vkeys, writes=[ok])
            q = ri[0] % 4
            ri[0] += 1
            rec = recs[:, q, :]
            rk = ("rec", q)
            ov = ob[0:64, 0:264].rearrange("p (h c) -> p h c", c=66)
            S.op("dve", ("reciprocal", dict(out=rec, in_=ov[:, :, 64])), reads=[ok], writes=[rk])
            ot, otk = otr.next()
            S.op("dve", TT(ot[:], ov[:, :, 0:64], rec.rearrange("p (h o) -> p h o", o=1).to_broadcast([64, 4, 64]), ALU.mult),
                 reads=[ok, rk], writes=[otk])
            c["ot"], c["otk"] = ot, otk

        def n_stage3(c):
            ot, otk, qs = c["ot"], c["otk"], c["qs"]
            tb_, tk = trr.next()
            otf = ot[:].rearrange("p h c -> p (h c)")
            S.op("pe", [TR(psb(tb_)[:, g * 64:(g + 1) * 64], otf[:, g * 128:(g + 1) * 128], identb[0:64, 0:64]) for g in range(2)],
                 reads=[otk, "identb"], writes=[tk])
            S.op("dve", TT(ybT[:, :, qs], psb(tb_)[:, 0:128].rearrange("p (g q) -> p g q", g=2), ZBh[:, :, qs], ALU.mult),
                 reads=[tk, ("ZBh", c["r"] * 64 // 1024)], writes=["ybT"])

        nctx = {}
        if half == 1:
            S.dma("pool", "L_fwa", wpab[:], E["wpa_d"][:, :, :], writes=["wpab"])
            S.dma("pool", "L_fwb", wpbb[:], E["wpb_d"][:, :, :], writes=["wpbb"])
            for j in range(4):
                wt, wk = wsr.next()
                S.dma("sp", "L_fws%d" % wk[1], wt[:], E["wout_d"][:, 2 * j:2 * j + 2, :], writes=[wk])
                S.op("dve", TT(woutb[:, 2 * j:2 * j + 2, :], wt[:], gate_bc[:].rearrange("p (o n) -> p o n", o=1).to_broadcast([128, 2, 1024]), ALU.mult),
                     reads=[wk, "gate_bc"], writes=["woutb"])
        for r in range(64 + 2):
            if r < 64:
                nctx[r] = n_stage1(r)
            if 0 <= r - 1 < 64:
                n_stage2(nctx[r - 1])
            if 0 <= r - 2 < 64:
                n_stage3(nctx.pop(r - 2))
        for g in range(2):
            S.dma("sp", "S_ybT", YBT[2 * half + g], ybT[:, g, :], reads=["ybT"])
    S.barrier()


def build_final(nc, S, E):
    T, cur, ps = E["T"], E["cur"], E["ps"]
    gate_bc, fg_bc, smallc = E["gate_bc"], E["fg_bc"], E["smallc"]
    YAT, YBT, GMT, x_d, y_d = E["YAT"], E["YBT"], E["GMT"], E["x_d"], E["y_d"]
    wpa_d, wpb_d, wout_d = E["wpa_d"], E["wpb_d"], E["wout_d"]
    cur[0] = E["REGION0"]
    NA_END, wpab, wpbb, woutb = S.shared_fw
    yar = Rot("f_ya", [T("f_ya%d" % i, [128, 8, 512], BF16) for i in range(2)])
    ybr = Rot("f_yb", [T("f_yb%d" % i, [128, 4, 512], BF16) for i in range(2)])
    gmr = Rot("f_gm", [T("f_gm%d" % i, [128, 16, 512], BF16) for i in range(2)])
    t1r = Rot("f_t1", [T("f_t1%d" % i, [128, 512], F32) for i in range(2)])
    t2r = Rot("f_t2", [T("f_t2%d" % i, [128, 512], F32) for i in range(2)])
    mgr = Rot("f_mg", [T("f_mg%d" % i, [128, 8, 512], BF16) for i in range(2)])
    xr = Rot("f_xt", [T("f_xt%d" % i, [128, 1024], F32) for i in range(5)])
    x2r = Rot("f_x2", [T("f_x2%d" % i, [128, 1024], F32) for i in range(2)])
    otr = Rot("f_ot", [T("f_ot%d" % i, [128, 1024], F32) for i in range(2)])
    junk = T("f_junk", [128, 1024], BF16)
    par = Rot("psP", [ps[0], ps[1], ps[2], ps[3]])
    outr = Rot("psO", [(ps[4], ps[5]), (ps[6], ps[7])])
    assert cur[0] <= NA_END, (cur[0], NA_END)
    def f_loads(tb):
        ts = slice(tb * 512, (tb + 1) * 512)
        ya, yak = yar.next()
        yb, ybk = ybr.next()
        gm, gmk = gmr.next()
        S.dma("sp", "L_fya%d" % yak[1], ya[:], YAT[:, :, ts].rearrange("g p t -> p g t"), writes=[yak])
        S.dma("sp", "L_fyb%d" % ybk[1], yb[:], YBT[:, :, ts].rearrange("g p t -> p g t"), writes=[ybk])
        S.dma("sp", "L_fgm%d" % gmk[1], gm[:], GMT[:, :, ts].rearrange("g p t -> p g t"), writes=[gmk])
        return ya, yak, yb, ybk, gm, gmk

    fl = {0: f_loads(0)}
    mgs = {}

    def stageP(tb):
        ya, yak, yb, ybk, gm, gmk = fl.pop(tb)
        if tb + 1 < NBLK:
            fl[tb + 1] = f_loads(tb + 1)
        mg, mgk = mgr.next()
        for fg in range(8):
            fs = slice(fg * 128, (fg + 1) * 128)
            pa, pak = par.next()
            S.op("pe", [MM(pa[:, :], wpab[:, kc, fs], ya[:, kc, :], start=(kc == 0), stop=(kc == 7)) for kc in range(8)],
                 reads=["wpab", yak], writes=[pak])
            pb_, pbk = par.next()
            S.op("pe", [MM(pb_[:, :], wpbb[:, kc, fs], yb[:, kc, :], start=(kc == 0), stop=(kc == 3)) for kc in range(4)],
                 reads=["wpbb", ybk], writes=[pbk])
            t1, t1k = t1r.next()
            t2, t2k = t2r.next()
            S.op("dve", TT(t1[:], pa[:, :], gm[:, fg, :], ALU.mult), reads=[pak, gmk], writes=[t1k])
            S.op("dve", TT(t2[:], pb_[:, :], gm[:, 8 + fg, :], ALU.mult), reads=[pbk, gmk], writes=[t2k])
            S.op("pool", TT(mg[:, fg, :], t1[:], t2[:], ALU.add), reads=[t1k, t2k], writes=[(mgk, fg)])
        mgs[tb] = (mg, mgk)

    def stageO(tb):
        mg, mgk = mgs.pop(tb)
        xtl = []
        for tt in range(4):
            tile = tb * 4 + tt
            xt, xk = xr.next()
            S.dma("sp", "L_fx%d" % xk[1], xt[:], x_d[tile * 128:(tile + 1) * 128, :], writes=[xk])
            xtl.append((xt, xk))
        for tt in range(4):
            tile = tb * 4 + tt
            xt, xk = xtl[tt]
            (o0, o1), ok = outr.next()
            specs = []
            for nh, ob in enumerate((o0, o1)):
                for fg in range(8):
                    specs.append(MM(ob[:, :], mg[:, fg, tt * 128:(tt + 1) * 128], woutb[:, fg, nh * 512:(nh + 1) * 512], start=(fg == 0), stop=(fg == 7)))
            S.op("pe", specs, reads=[(mgk, fg) for fg in range(8)] + ["woutb"], writes=[ok])
            x2, x2k = x2r.next()
            S.op("dve", [TT(x2[:, 0:512], o0[:, :], xt[:, 0:512], ALU.add), TT(x2[:, 512:1024], o1[:, :], xt[:, 512:1024], ALU.add)],
                 reads=[ok, xk], writes=[x2k])
            q = tile % 4
            sc = smallc[:, q * 4:q * 4 + 4]
            sck = ("smallc", q)
            S.op("act", ACT(junk[:], x2[:], AF.Square, accum_out=sc[:, 0:1]), reads=[x2k], writes=["f_junk", sck])
            S.op("act", ACT(sc[:, 1:2], sc[:, 0:1], AF.Sqrt, scale=1.0 / D, bias=EPS), reads=[sck], writes=[sck])
            S.op("dve", ("reciprocal", dict(out=sc[:, 2:3], in_=sc[:, 1:2])), reads=[sck], writes=[sck])
            ot, otk = otr.next()
            S.op("act", ACT(ot[:], x2[:], AF.Copy, scale=sc[:, 2:3]), reads=[x2k, sck], writes=[otk])
            S.op("pool", TT(ot[:], ot[:], fg_bc[:], ALU.mult), reads=[otk, "fg_bc"], writes=[otk])
            S.dma("pool", "S_fo%d" % otk[1], y_d[tile * 128:(tile + 1) * 128, :], ot[:], reads=[otk])

    for tb in range(NBLK + 1):
        if tb < NBLK:
            stageP(tb)
        if tb >= 1:
            stageO(tb - 1)


def _shared_layouts(inp):
    f = np.float32
    w_ada = np.asarray(inp["w_ada"], f)[0]
    w_in = np.asarray(inp["w_in"], f)[0]
    sh = {}
    sh["wada"] = np.ascontiguousarray(w_ada.reshape(8, 128, 3072).transpose(1, 0, 2))
    sh["rows"] = np.ascontiguousarray(np.concatenate(
        [np.asarray(inp["b_ada"], f)[0], np.asarray(inp["norm_gain"], f)[0], np.asarray(inp["final_gain"], f)])[None, :])
    wg = np.zeros((1024, 72), f)
    gc = w_in[:, 5120:5136].reshape(1024, 2, 2, 4)
    wg[:, 0:4] = gc[:, 0, 0]
    wg[:, 32:36] = gc[:, 1, 0]
    wg[:, 36:40] = gc[:, 0, 1]
    wg[:, 68:72] = gc[:, 1, 1]
    sh["wgate"] = np.ascontiguousarray(wg.reshape(8, 128, 72).transpose(1, 0, 2))
    wl = np.concatenate([w_in[:, 0:5120], w_in[:, 5136:9232]], axis=1)
    sh["win"] = np.ascontiguousarray(wl.reshape(8, 128, 18, 512).transpose(2, 1, 0, 3))
    cols = np.zeros((128, 88), f)
    cw = np.asarray(inp["conv_w"], f)[0]
    cb = np.asarray(inp["conv_b"], f)[0]
    cols[:, 0:48] = cw.reshape(3, 16, 128).transpose(2, 1, 0).reshape(128, 48)
    cols[:, 48:64] = cb.reshape(16, 128).T
    cols[:, 64:80] = np.asarray(inp["b_merge"], f)[0].reshape(16, 128).T
    cols[:, 80:88] = np.asarray(inp["mlstm_norm_gain"], f)[0].reshape(8, 128).T
    sh["cols"] = cols
    gb = np.zeros((36, 2), f)
    bi = np.asarray(inp["b_igate"], f)[0]
    bfg = np.asarray(inp["b_fgate"], f)[0]
    gb[0:4, 0] = bi[0]
    gb[32:36, 0] = bi[1]
    gb[0:4, 1] = bfg[0]
    gb[32:36, 1] = bfg[1]
    sh["gb"] = gb
    rpb = np.asarray(inp["rpb"], f)[0]
    kc = np.arange(64)[:, None]
    qc = np.arange(64)[None, :]
    ws = np.clip(qc - 8, 0, 48)
    colok = (kc >= ws) & (kc < ws + 16)
    dcidx = np.clip(kc - qc + 15, 0, 30)
    bt2 = np.full((128, 8, 14, 64), NEG, f)
    for j in range(14):
        for half in range(2):
            dr = j - 7 + half
            tab = np.where(colok[None], rpb[:, dr + 7][:, dcidx], f(NEG))
            bt2[half * 64:(half + 1) * 64, :, j, :] = tab.transpose(1, 0, 2)
    sh["bt2"] = np.ascontiguousarray(bt2.reshape(128, 4, 2, 14, 64).transpose(0, 1, 3, 2, 4))
    sh["wpa"] = np.ascontiguousarray(np.asarray(inp["w_proj_a"], f)[0].reshape(8, 128, 1024).transpose(1, 0, 2))
    sh["wpb"] = np.ascontiguousarray(np.asarray(inp["w_proj_b"], f)[0].reshape(4, 128, 1024).transpose(1, 0, 2))
    sh["wout"] = np.ascontiguousarray(np.asarray(inp["w_out"], f)[0].reshape(8, 128, 1024).transpose(1, 0, 2))
    sh["ident"] = np.eye(128, dtype=f)
    s_i = np.arange(128)[:, None]
    t_i = np.arange(128)[None, :]
    masks = np.zeros((128, 2, 128), f)
    masks[:, 0, :] = np.where(s_i <= t_i, 1.0 / 16, 0.0)
    masks[:, 1, :] = np.where(s_i >= t_i, 1.0 / 16, 0.0)
    sh["masks"] = masks
    sel = np.zeros((36, 8, 128), f)
    for j in range(8):
        sel[(j % 4) + 32 * (j // 4), j, :] = 1.0
    sh["sel"] = sel
    return sh


def make_in_maps(inp):
    sh = _shared_layouts(inp)
    x = np.asarray(inp["x"], np.float32)
    c = np.asarray(inp["c"], np.float32)
    maps = []
    for b in range(8):
        m = dict(sh)
        m["x"] = np.ascontiguousarray(x[b])
        m["c_l"] = np.ascontiguousarray(c[b].reshape(8, 128).T)
        maps.append(m)
    return maps


_NC_CACHE = {}


def kernel(**inputs):
    if "nc" not in _NC_CACHE:
        _NC_CACHE["nc"] = build_program()
    nc = _NC_CACHE["nc"]
    in_maps = make_in_maps(inputs)
    res = run_bass_kernel_spmd(nc, in_maps, core_ids=list(range(8)))
    return np.stack([np.asarray(r["y"], np.float32) for r in res.results], axis=0)
```
